# Optimizing a Trainium2 kernel written in Bass

```python
import jax, jax.numpy as jnp
from jax import lax
import numpy as np

D_MODEL = 2048
BATCH = 4
SEQ = 4096
DEPTH = 1

MIX_WIDTH = D_MODEL
HEAD_DIM = 128
DSWA_HEADS = (MIX_WIDTH // 2) // HEAD_DIM
DSWA_CONFIGS = ((128, 1), (512, 4), (2048, 16))
ROT_DIM = HEAD_DIM // 4
ROPE_THETA = 500000.0
MLA_HEADS = (MIX_WIDTH // 2) // HEAD_DIM
Q_LORA_RANK = 512
KV_LORA_RANK = 512
QK_NOPE_DIM = 128
QK_ROPE_DIM = 64
V_HEAD_DIM = 128
D_FF = 4 * D_MODEL
Q_BLOCK = 128
NORM_EPS = 1e-6
NEG_INF = -1e30

A_WIDTH = DSWA_HEADS * HEAD_DIM
IN_SPLITS = (A_WIDTH, 2 * A_WIDTH, 3 * A_WIDTH,
             3 * A_WIDTH + Q_LORA_RANK,
             3 * A_WIDTH + Q_LORA_RANK + KV_LORA_RANK)
IN_COLS = 3 * A_WIDTH + Q_LORA_RANK + KV_LORA_RANK + QK_ROPE_DIM
OUT_ROWS = DSWA_HEADS * HEAD_DIM + MLA_HEADS * V_HEAD_DIM

kernel_name = "hybrid_dilated_swa_mla_sandwich"


def _rmsnorm(x, gain):
    xf = x.astype(jnp.float32)
    xf = xf * lax.rsqrt(jnp.mean(xf * xf, axis=-1, keepdims=True) + NORM_EPS)
    return xf.astype(x.dtype) * gain


def _rope(x, positions, rot_dim):
    inv_freq = ROPE_THETA ** (-jnp.arange(0, rot_dim, 2, dtype=jnp.float32) / rot_dim)
    ang = positions.astype(jnp.float32)[..., None] * inv_freq
    cos = jnp.cos(ang)[:, :, None, :]
    sin = jnp.sin(ang)[:, :, None, :]
    xr = x[..., :rot_dim].astype(jnp.float32)
    x1, x2 = xr[..., : rot_dim // 2], xr[..., rot_dim // 2:]
    rot = jnp.concatenate([x1 * cos - x2 * sin, x2 * cos + x1 * sin], axis=-1)
    return jnp.concatenate([rot.astype(x.dtype), x[..., rot_dim:]], axis=-1)


def _dilated_window_attention(q, k, v, window, dilation):
    B, S, H, D = q.shape
    steps = window // dilation
    span = dilation * Q_BLOCK
    s_pad = -(-S // span) * span
    pad = ((0, 0), (0, s_pad - S), (0, 0), (0, 0))
    q, k, v = jnp.pad(q, pad), jnp.pad(k, pad), jnp.pad(v, pad)
    nb = s_pad // span
    qb = q.reshape(B, nb, Q_BLOCK, dilation, H, D)
    kb = k.reshape(B, nb, Q_BLOCK, dilation, H, D)
    vb = v.reshape(B, nb, Q_BLOCK, dilation, H, D)

    def with_prev(t):
        prev = jnp.pad(t[:, :-1], ((0, 0), (1, 0), (0, 0), (0, 0), (0, 0), (0, 0)))
        return jnp.concatenate([prev, t], axis=2)

    kc, vc = with_prev(kb), with_prev(vb)
    s = jnp.einsum('bniphd,bnjphd->bnphij', qb, kc).astype(jnp.float32) * (D ** -0.5)
    i = jnp.arange(Q_BLOCK)[:, None]
    j = jnp.arange(2 * Q_BLOCK)[None, :]
    dist = i + Q_BLOCK - j
    band = (dist >= 0) & (dist <= steps)
    first = (jnp.arange(nb) == 0)[:, None, None] & (j < Q_BLOCK)[None]
    mask = band[None] & ~first
    s = jnp.where(mask[None, :, None, None], s, NEG_INF)
    m = jnp.max(s, axis=-1, keepdims=True)
    p = jnp.exp(s - m)
    denom = jnp.sum(p, axis=-1, keepdims=True)
    o = jnp.einsum('bnphij,bnjphd->bniphd', (p / denom).astype(v.dtype), vc)
    lse = (m + jnp.log(denom))[..., 0]
    o = o.reshape(B, s_pad, H, D)[:, :S]
    lse = jnp.transpose(lse, (0, 1, 4, 2, 3)).reshape(B, s_pad, H)[:, :S]
    return o, lse


def _causal_block_attention(q, k, v, scale):
    B, S, H, Dk = q.shape
    nb = S // Q_BLOCK
    qb = q.reshape(B, nb, Q_BLOCK, H, Dk).transpose(1, 0, 2, 3, 4)
    key_pos = jnp.arange(S)

    def one_block(args):
        n, qn = args
        s = jnp.einsum('bqhd,bkhd->bhqk', qn, k).astype(jnp.float32) * scale
        q_pos = n * Q_BLOCK + jnp.arange(Q_BLOCK)
        s = jnp.where((key_pos[None, :] <= q_pos[:, None])[None, None], s, NEG_INF)
        p = jax.nn.softmax(s, axis=-1)
        return jnp.einsum('bhqk,bkhd->bqhd', p.astype(v.dtype), v)

    o = lax.map(one_block, (jnp.arange(nb), qb))
    return o.transpose(1, 0, 2, 3, 4).reshape(B, S, H, v.shape[-1])


def setup_inputs(seed: int = 0) -> dict:
    key = jax.random.key(seed)
    ks = jax.random.split(key, 16)
    f32 = jnp.float32

    def w(k, shape, fan_in):
        return jax.random.normal(k, shape, f32) * (fan_in ** -0.5)

    def gain(k, n):
        return 1.0 + 0.05 * jax.random.normal(k, (DEPTH, n), f32)

    x = jax.random.normal(ks[0], (BATCH, SEQ, D_MODEL), f32)
    offset = jax.random.randint(ks[1], (BATCH, 1), 0, 2048, dtype=jnp.int32)
    positions = offset + jnp.arange(SEQ, dtype=jnp.int32)[None, :]
    return {
        "x": x,
        "positions": positions,
        "norm_attn_pre": gain(ks[2], D_MODEL),
        "norm_attn_post": gain(ks[3], D_MODEL),
        "w_in": w(ks[4], (DEPTH, D_MODEL, IN_COLS), D_MODEL),
        "q_latent_norm": gain(ks[5], Q_LORA_RANK),
        "kv_latent_norm": gain(ks[6], KV_LORA_RANK),
        "w_uq": w(ks[7], (DEPTH, Q_LORA_RANK, MLA_HEADS * (QK_NOPE_DIM + QK_ROPE_DIM)), Q_LORA_RANK),
        "w_ukv": w(ks[8], (DEPTH, KV_LORA_RANK, MLA_HEADS * (QK_NOPE_DIM + V_HEAD_DIM)), KV_LORA_RANK),
        "w_out": w(ks[9], (DEPTH, OUT_ROWS, D_MODEL), OUT_ROWS),
        "norm_mlp_pre": gain(ks[10], D_MODEL),
        "norm_mlp_post": gain(ks[11], D_MODEL),
        "w_up": w(ks[12], (DEPTH, D_MODEL, D_FF), D_MODEL),
        "w_down": w(ks[13], (DEPTH, D_FF, D_MODEL), D_FF),
    }


def reference(x, positions, norm_attn_pre, norm_attn_post, w_in, q_latent_norm,
              kv_latent_norm, w_uq, w_ukv, w_out, norm_mlp_pre, norm_mlp_post,
              w_up, w_down):
    B, S, _ = x.shape
    for layer in range(DEPTH):
        h = _rmsnorm(x, norm_attn_pre[layer])
        proj = h @ w_in[layer]
        a_q, a_k, a_v, c_q, c_kv, k_r = jnp.split(proj, IN_SPLITS, axis=-1)

        a_q = _rope(a_q.reshape(B, S, DSWA_HEADS, HEAD_DIM), positions, ROT_DIM)
        a_k = _rope(a_k.reshape(B, S, DSWA_HEADS, HEAD_DIM), positions, ROT_DIM)
        a_v = a_v.reshape(B, S, DSWA_HEADS, HEAD_DIM)
        outs, lses = [], []
        for window, dilation in DSWA_CONFIGS:
            o, lse = _dilated_window_attention(a_q, a_k, a_v, window, dilation)
            outs.append(o)
            lses.append(lse)
        alpha = jax.nn.softmax(jnp.stack(lses, axis=0), axis=0)
        a_out = jnp.sum(alpha[..., None].astype(a_v.dtype) * jnp.stack(outs, axis=0), axis=0)

        c_q = _rmsnorm(c_q, q_latent_norm[layer])
        q_b = (c_q @ w_uq[layer]).reshape(B, S, MLA_HEADS, QK_NOPE_DIM + QK_ROPE_DIM)
        q_nope, q_rope = q_b[..., :QK_NOPE_DIM], q_b[..., QK_NOPE_DIM:]
        q_rope = _rope(q_rope, positions, QK_ROPE_DIM)
        c_kv = _rmsnorm(c_kv, kv_latent_norm[layer])
        kv = (c_kv @ w_ukv[layer]).reshape(B, S, MLA_HEADS, QK_NOPE_DIM + V_HEAD_DIM)
        k_nope, v_b = kv[..., :QK_NOPE_DIM], kv[..., QK_NOPE_DIM:]
        k_rope = _rope(k_r[:, :, None, :], positions, QK_ROPE_DIM)
        q_full = jnp.concatenate([q_nope, q_rope], axis=-1)
        k_full = jnp.concatenate(
            [k_nope, jnp.broadcast_to(k_rope, (B, S, MLA_HEADS, QK_ROPE_DIM))], axis=-1)
        b_out = _causal_block_attention(q_full, k_full, v_b,
                                        (QK_NOPE_DIM + QK_ROPE_DIM) ** -0.5)

        mixed = jnp.concatenate([a_out.reshape(B, S, DSWA_HEADS * HEAD_DIM),
                                 b_out.reshape(B, S, MLA_HEADS * V_HEAD_DIM)], axis=-1)
        x = x + _rmsnorm(mixed @ w_out[layer], norm_attn_post[layer])

        h = _rmsnorm(x, norm_mlp_pre[layer])
        u = jnp.square(jax.nn.relu(h @ w_up[layer]))
        x = x + _rmsnorm(u @ w_down[layer], norm_mlp_post[layer])
    return x
```

```python
import numpy as np
from contextlib import ExitStack
import concourse.bass as bass
import concourse.mybir as mybir
from concourse.bass_utils import run_bass_kernel_spmd

F32 = mybir.dt.float32
BF16 = mybir.dt.bfloat16
I32 = mybir.dt.int32
AF = mybir.ActivationFunctionType
ALU = mybir.AluOpType
AX = mybir.AxisListType

ENGS = ("pe", "act", "dve", "pool", "sp")

D = 2048
SEQ = 4096
NB = 4
TL = 4096
TO = 2048
HD = 128
NH = 8
DFF = 8192
INC = 4160
EPS = 1e-6
NEG = -30000.0
THETA = 500000.0
TWO_PI = float(2.0 * np.pi)
PI = float(np.pi)


class Buf:
    __slots__ = ("name", "last_w", "readers", "sem", "sem_val", "last_dma", "excl")

    def __init__(self, name, excl=False):
        self.name = name
        self.excl = excl
        self.last_w = None
        self.readers = []
        self.sem = None
        self.sem_val = 0
        self.last_dma = None


class Op:
    __slots__ = ("eng", "idx", "fn", "deps", "signal", "is_dma", "sem", "val", "waits")

    def __init__(self, eng, idx, fn):
        self.eng = eng
        self.idx = idx
        self.fn = fn
        self.deps = []
        self.signal = False
        self.is_dma = False
        self.sem = None
        self.val = None
        self.waits = None


def K(name, *args, **kw):
    return lambda e: getattr(e, name)(*args, **kw)


class Prog:
    def __init__(self, nc, stack):
        self.nc = nc
        self.stack = stack
        self.ops = {e: [] for e in ENGS}
        self.eng_sem = {e: stack.enter_context(nc.semaphore("S_" + e)) for e in ENGS}
        self.dma_bufs = []
        self.same_engine_sync = True

    def _add(self, eng, fn, reads, writes, acc=False):
        op = Op(eng, len(self.ops[eng]), fn)
        self.ops[eng].append(op)
        deps = []
        for b in reads:
            if b.last_w is not None:
                deps.append(b.last_w)
            if b.excl:
                deps.extend(r for r in b.readers if r.eng != eng)
        for b in writes:
            if b.last_w is not None and not (acc and b.last_w.eng == eng):
                deps.append(b.last_w)
            deps.extend(b.readers)
        op.deps = deps
        for b in reads:
            b.readers.append(op)
        for b in writes:
            b.last_w = op
            b.readers = []
        return op

    def op(self, eng, fn, reads=(), writes=(), acc=False):
        return self._add(eng, fn, list(reads), list(writes), acc)

    def dma(self, eng, out_ap, in_ap, sbuf_buf, reads=(), writes=()):
        b = sbuf_buf
        if b.sem is None:
            b.sem = self.stack.enter_context(self.nc.semaphore("D_" + b.name))
            self.dma_bufs.append(b)
        op = self._add(eng, None, list(reads), list(writes))
        if b.last_dma is not None:
            op.deps.append(b.last_dma)
        b.sem_val += 16
        op.is_dma = True
        op.sem = b.sem
        op.val = b.sem_val
        op.fn = (out_ap, in_ap)
        b.last_dma = op
        return op

    def load(self, dst, src, buf, eng="sp"):
        return self.dma(eng, dst, src, buf, writes=[buf])

    def store(self, dst, src, buf, eng="pool"):
        return self.dma(eng, dst, src, buf, reads=[buf])

    def barrier(self):
        deps = []
        for e in ENGS:
            if self.ops[e]:
                deps.append(self.ops[e][-1])
        for b in self.dma_bufs:
            if b.last_dma is not None:
                deps.append(b.last_dma)
        for e in ENGS:
            op = self._add(e, None, [], [])
            op.deps = list(deps)

    def finalize(self):
        for e in ENGS:
            known_idx = {x: -1 for x in ENGS}
            known_dma = {}
            for op in self.ops[e]:
                waits = []
                best = {}
                for d in op.deps:
                    if d.is_dma:
                        k = id(d.sem)
                        if known_dma.get(k, 0) >= d.val:
                            continue
                        known_dma[k] = d.val
                        waits.append(d)
                    else:
                        if d.fn is None:
                            continue
                        if d.eng == e and (e == "pe" or not self.same_engine_sync):
                            continue
                        if d.idx <= known_idx[d.eng]:
                            continue
                        if d.eng not in best or best[d.eng].idx < d.idx:
                            best[d.eng] = d
                for x, d in best.items():
                    known_idx[x] = d.idx
                    d.signal = True
                    waits.append(d)
                op.waits = waits
        for e in ENGS:
            c = 0
            for op in self.ops[e]:
                if not op.is_dma and op.signal:
                    c += 1
                    op.sem = self.eng_sem[e]
                    op.val = c

    def emit(self, block):
        self.finalize()
        P = self

        def run(e, eng):
            for op in P.ops[e]:
                seen = {}
                for d in op.waits:
                    k = id(d.sem)
                    if k not in seen or seen[k][1] < d.val:
                        seen[k] = (d.sem, d.val)
                for sem, val in seen.values():
                    eng.wait_ge(sem, val)
                if op.is_dma:
                    o, i = op.fn
                    eng.dma_start(out=o, in_=i).then_inc(op.sem, 16)
                elif op.fn is not None:
                    ins = op.fn(eng)
                    if op.signal:
                        ins.then_inc(op.sem, 1)

        @block.tensor
        def _(eng):
            run("pe", eng)

        @block.scalar
        def _(eng):
            run("act", eng)

        @block.vector
        def _(eng):
            run("dve", eng)

        @block.gpsimd
        def _(eng):
            run("pool", eng)

        @block.sync
        def _(eng):
            run("sp", eng)


class Ring:
    def __init__(self, name, aps):
        self.items = [(ap, Buf("%s%d" % (name, i))) for i, ap in enumerate(aps)]
        self.i = 0

    def next(self):
        it = self.items[self.i % len(self.items)]
        self.i += 1
        return it


class Arena:
    def __init__(self, t, nbytes):
        self.t = t
        self.n = nbytes
        self.base = 0
        self.off = 0

    def persist(self):
        self.base = self.off

    def reset(self):
        self.off = self.base

    def take(self, nelem, dt, parts=128):
        sz = 2 if dt == BF16 else 4
        nb = (nelem * sz + 63) // 64 * 64
        o = self.off
        self.off += nb
        assert self.off <= self.n, "SBUF arena overflow %d > %d" % (self.off, self.n)
        ap = self.t[0:parts, o // 4:(o + nelem * sz + 3) // 4]
        if dt != F32:
            ap = ap.bitcast(dt)
        return ap


DEBUG_DUMP = False

C1 = 6.28125
C2 = float(2.0 * np.pi - 6.28125)


def rope_table(P, posf, invcol, ang, ni, nf, mm_, sin_out, cos_out, tb):
    w = dict(reads=[tb], writes=[tb])
    P.op("dve", K("tensor_scalar", out=ang, in0=posf, scalar1=invcol, scalar2=None, op0=ALU.mult), **w)
    P.op("dve", K("tensor_scalar", out=ni, in0=ang, scalar1=1.0 / TWO_PI, scalar2=None, op0=ALU.mult), **w)
    P.op("dve", K("tensor_copy", out=nf, in_=ni), **w)
    P.op("dve", K("scalar_tensor_tensor", out=ang, in0=nf, scalar=-C1, in1=ang, op0=ALU.mult, op1=ALU.add), **w)
    P.op("dve", K("scalar_tensor_tensor", out=ang, in0=nf, scalar=-C2, in1=ang, op0=ALU.mult, op1=ALU.add), **w)

    def wrap(x):
        P.op("dve", K("tensor_scalar", out=mm_, in0=x, scalar1=PI, scalar2=-TWO_PI, op0=ALU.is_gt, op1=ALU.mult), **w)
        P.op("dve", K("tensor_tensor", out=x, in0=x, in1=mm_, op=ALU.add), **w)
        P.op("dve", K("tensor_scalar", out=mm_, in0=x, scalar1=-PI, scalar2=TWO_PI, op0=ALU.is_lt, op1=ALU.mult), **w)
        P.op("dve", K("tensor_tensor", out=x, in0=x, in1=mm_, op=ALU.add), **w)
    wrap(ang)
    P.op("act", K("activation", out=sin_out, in_=ang, func=AF.Sin), **w)
    P.op("dve", K("tensor_scalar", out=ang, in0=ang, scalar1=PI / 2, scalar2=None, op0=ALU.add), **w)
    wrap(ang)
    P.op("act", K("activation", out=cos_out, in_=ang, func=AF.Sin), **w)


def build_program(debug=False, stop_after=None):
    nc = bass.Bass("TRN2", target_bir_lowering=False)

    def din(name, shape, dt=F32):
        return nc.dram_tensor(name, list(shape), dt, kind="ExternalInput").ap()

    def dscr(name, shape, dt=BF16):
        if debug:
            return nc.dram_tensor(name, list(shape), dt, kind="ExternalOutput").ap()
        return nc.dram_tensor(name, list(shape), dt).ap()

    xo = din("xo", [TO, D])
    xp = din("xp", [TO, D])
    posb = din("posb", [128, TL], I32)
    pvt = din("pvt", [128, 128])
    cmask = din("cmask", [128, 256])
    cRA = din("cRA", [128, 128])
    cRB = din("cRB", [128, 128])
    cinvf = din("cinvf", [128, 2])
    g_in = din("g_in", [128, 16])
    g_up = din("g_up", [128, 16])
    g_q = din("g_q", [128, 4])
    g_kv = din("g_kv", [128, 4])
    g_post = din("g_post", [128, D])
    g_post2 = din("g_post2", [128, D])
    w_in = din("w_in", [D, INC])
    w_uq = din("w_uq", [512, 1536])
    w_ukv = din("w_ukv", [512, 2048])
    w_out = din("w_out", [D, D])
    w_up = din("w_up", [D, DFF])
    w_down = din("w_down", [DFF, D])
    out = nc.dram_tensor("out", [TO, D], F32, kind="ExternalOutput").ap()

    wb_in = dscr("wb_in", [D, INC])
    wb_uq = dscr("wb_uq", [512, 1536])
    wb_ukv = dscr("wb_ukv", [512, 2048])
    wb_out = dscr("wb_out", [D, D])
    wb_up = dscr("wb_up", [D, DFF])
    wb_down = dscr("wb_down", [DFF, D])
    hT = dscr("hT", [16, 128, TL])
    qA = dscr("qA", [NH, 128, TO])
    kA = dscr("kA", [NH, 128, TL])
    vA = dscr("vA", [TL, NH * HD])
    cq = dscr("cq", [4, 128, TO])
    ckv = dscr("ckv", [4, 128, TL])
    kr = dscr("kr", [64, TL])
    qn = dscr("qn", [NH, 128, TO])
    qr = dscr("qr", [NH, 64, TO])
    kn = dscr("kn", [NH, 128, TL])
    vB = dscr("vB", [TL, NH * HD])
    mixT = dscr("mixT", [16, 128, TO])
    x1 = dscr("x1", [TO, D], F32)
    h2T = dscr("h2T", [16, 128, TO])

    st = ExitStack()
    with st:
        ARENA_BYTES = 204800
        arena_t = st.enter_context(nc.sbuf_tensor("arena", [128, ARENA_BYTES // 4], F32))
        A = Arena(arena_t, ARENA_BYTES)
        psum = []
        for i in range(8):
            t = st.enter_context(nc.psum_tensor("ps%d" % i, [128, 512], F32))
            psum.append((t[:, :], Buf("ps%d" % i, excl=True)))
        P = Prog(nc, st)
        psring = Ring("psr", [None] * 8)
        psring.items = [(t, b) for t, b in psum]

        cst = Buf("const")
        ident = A.take(128, BF16)
        ones_b = A.take(128, BF16)
        zeros_b = A.take(512, BF16)
        pv_b = A.take(128, BF16)
        mask_b = A.take(256, BF16)
        RA_b = A.take(128, BF16)
        RB_b = A.take(128, BF16)
        ones_f = A.take(128, F32)
        invf = A.take(2, F32)
        gc_in = A.take(16, F32)
        gc_up = A.take(16, F32)
        gc_q = A.take(4, F32)
        gc_kv = A.take(4, F32)
        pv_col = A.take(1, F32)
        A.persist()
        stg = A.take(128 * 6, F32)
        stgb = Buf("stg")
        P.load(stg[:, 0:128], pvt, stgb)
        P.load(stg[:, 128:384], cmask, stgb)
        P.load(stg[:, 384:512], cRA, stgb)
        P.load(stg[:, 512:640], cRB, stgb)
        gb = Buf("gl")
        P.load(invf, cinvf, gb)
        P.load(gc_in, g_in, gb)
        P.load(gc_up, g_up, gb)
        P.load(gc_q, g_q, gb)
        P.load(gc_kv, g_kv, gb)
        P.op("pool", K("memset", stg[:, 640:768], 0.0), writes=[cst])
        P.op("pool", K("affine_select", out=stg[:, 640:768], in_=stg[:, 640:768], pattern=[[-1, 128]],
                       compare_op=ALU.not_equal, fill=1.0, base=0, channel_multiplier=1),
             reads=[cst], writes=[cst])
        P.op("pool", K("tensor_copy", out=ident, in_=stg[:, 640:768]), reads=[cst], writes=[cst])
        P.op("pool", K("memset", ones_f, 1.0), writes=[cst])
        P.op("pool", K("memset", ones_b, 1.0), writes=[cst])
        P.op("pool", K("memset", zeros_b, 0.0), writes=[cst])
        P.op("dve", K("tensor_copy", out=pv_b, in_=stg[:, 0:128]), reads=[stgb], writes=[cst])
        P.op("dve", K("tensor_copy", out=pv_col, in_=stg[:, 0:1]), reads=[stgb], writes=[cst])
        P.op("dve", K("tensor_copy", out=mask_b, in_=stg[:, 128:384]), reads=[stgb], writes=[cst])
        P.op("dve", K("tensor_copy", out=RA_b, in_=stg[:, 384:512]), reads=[stgb], writes=[cst])
        P.op("dve", K("tensor_copy", out=RB_b, in_=stg[:, 512:640]), reads=[stgb], writes=[cst])
        P.barrier()
        mask_prev = mask_b[:, 0:128]
        mask_cur = mask_b[:, 128:256]

        def kcv(ap, k):
            return ap.rearrange("p (k n) -> p k n", k=k)

        A.reset()
        wst = Ring("wst", [A.take(2048, F32) for _ in range(3)])
        wob = Ring("wob", [A.take(2048, BF16) for _ in range(3)])
        cast_engs = ["pool", "dve", "act"]
        ci = 0
        wlist = [
            (w_in, wb_in, D, INC, gc_in), (w_uq, wb_uq, 512, 1536, gc_q), (w_ukv, wb_ukv, 512, 2048, gc_kv),
            (w_out, wb_out, D, D, None), (w_up, wb_up, D, DFF, gc_up), (w_down, wb_down, DFF, D, None),
        ]
        for (src, dst, R, C, gcol) in wlist:
            for kc in range(R // 128):
                for c0 in range(0, C, 2048):
                    cw = min(2048, C - c0)
                    sap, sb = wst.next()
                    oap, ob = wob.next()
                    P.load(sap[:, 0:cw], src[kc * 128:(kc + 1) * 128, c0:c0 + cw], sb)
                    eng = cast_engs[ci % 3]
                    ci += 1
                    if gcol is None:
                        if eng == "act":
                            P.op("act", K("activation", out=oap[:, 0:cw], in_=sap[:, 0:cw], func=AF.Copy),
                                 reads=[sb], writes=[ob])
                        else:
                            P.op(eng, K("tensor_copy", out=oap[:, 0:cw], in_=sap[:, 0:cw]), reads=[sb], writes=[ob])
                    else:
                        if eng == "act":
                            P.op("act", K("activation", out=oap[:, 0:cw], in_=sap[:, 0:cw], func=AF.Copy,
                                          scale=gcol[:, kc:kc + 1]), reads=[sb], writes=[ob])
                        else:
                            P.op(eng, K("tensor_scalar", out=oap[:, 0:cw], in0=sap[:, 0:cw],
                                        scalar1=gcol[:, kc:kc + 1], scalar2=None, op0=ALU.mult),
                                 reads=[sb], writes=[ob])
                    P.store(dst[kc * 128:(kc + 1) * 128, c0:c0 + cw], oap[:, 0:cw], ob)
        P.barrier()

        A.reset()
        base_const = A.base
        cosA = A.take(TL, F32)
        sinA = A.take(TL, F32)
        cosB = A.take(TL, F32)
        sinB = A.take(TL, F32)
        A.persist()
        posi = A.take(TL, I32)
        posf = A.take(TL, F32)
        ang = A.take(TL, F32)
        ni_t = A.take(TL, I32)
        nf_t = A.take(TL, F32)
        mm_t = A.take(TL, F32)
        tb = Buf("ropet")
        P.load(posi, posb, tb)
        P.op("dve", K("tensor_copy", out=posf, in_=posi), reads=[tb], writes=[tb])
        rope_table(P, posf, invf[:, 0:1], ang, ni_t, nf_t, mm_t, sinA, cosA, tb)
        rope_table(P, posf, invf[:, 1:2], ang, ni_t, nf_t, mm_t, sinB, cosB, tb)
        P.barrier()

        def norm_transpose(xt_ap, xb, dst_tile, dst_buf, sub, scr_bf, scr_buf, small, small_buf):
            ss = small[:, 0:1]
            rs = small[:, 1:2]
            P.op("act", K("activation", out=scr_bf, in_=xt_ap, func=AF.Square, accum_out=ss),
                 reads=[xb], writes=[scr_buf, small_buf])
            P.op("act", K("activation", out=rs, in_=ss, func=AF.Sqrt, scale=1.0 / D, bias=EPS),
                 reads=[small_buf], writes=[small_buf])
            P.op("dve", K("reciprocal", out=rs, in_=rs), reads=[small_buf], writes=[small_buf])
            P.op("dve", K("tensor_scalar", out=scr_bf, in0=xt_ap, scalar1=rs, scalar2=None, op0=ALU.mult),
                 reads=[xb, small_buf], writes=[scr_buf])
            for half in range(2):
                pt, pb_ = psring.next()
                ptb = pt[:, :].bitcast(BF16)
                for j in range(8):
                    kc = half * 8 + j
                    P.op("pe", K("transpose", out=ptb[:, j * 128:(j + 1) * 128], in_=scr_bf[:, kc * 128:(kc + 1) * 128],
                                 identity=ident), reads=[scr_buf, cst], writes=[pb_], acc=True)
                eng = "act" if half == 0 else "dve"
                dst = dst_tile[:, half * 8:(half + 1) * 8, sub * 128:(sub + 1) * 128]
                src = ptb.rearrange("p (k n) -> p k n", k=8)
                if eng == "act":
                    P.op("act", K("activation", out=dst, in_=src, func=AF.Copy), reads=[pb_], writes=[dst_buf])
                else:
                    P.op("dve", K("tensor_copy", out=dst, in_=src), reads=[pb_], writes=[dst_buf])

        A.reset()
        xr = Ring("x", [A.take(D, F32) for _ in range(2)])
        scr_r = Ring("xn", [A.take(D, BF16) for _ in range(2)])
        sm_r = Ring("sm", [A.take(2, F32) for _ in range(2)])
        hr = Ring("ht", [kcv(A.take(16 * 512, BF16), 16) for _ in range(2)])
        hT_v = hT.rearrange("k p t -> p k t")
        for tt in range(8):
            htile, hb = hr.next()
            for sub in range(4):
                xt, xb = xr.next()
                r0 = (tt % 4) * 512 + sub * 128
                srcx = xp if tt < 4 else xo
                P.load(xt, srcx[r0:r0 + 128, :], xb)
                sc, scb = scr_r.next()
                sm, smb = sm_r.next()
                norm_transpose(xt, xb, htile, hb, sub, sc, scb, sm, smb)
            P.store(hT_v[:, :, tt * 512:(tt + 1) * 512], htile, hb)
        P.barrier()
        if stop_after == 1:
            return finish(nc, P, st, out)

        A.reset()
        Wg = kcv(A.take(16 * 1024, BF16), 16)
        Wgb = Buf("Wg")
        hr = Ring("h2", [kcv(A.take(16 * 512, BF16), 16) for _ in range(2)])
        qs_r = Ring("qs", [A.take(512, BF16) for _ in range(3)])
        t1_r = Ring("t1", [A.take(512, F32) for _ in range(2)])
        t2_r = Ring("t2", [A.take(512, F32) for _ in range(2)])
        sq_r = Ring("sq", [A.take(512, F32) for _ in range(4)])
        rr_r = Ring("rr", [A.take(512, F32) for _ in range(2)])
        cn_r = Ring("cn", [kcv(A.take(4 * 512, BF16), 4) for _ in range(2)])
        wb_in_v = wb_in.rearrange("(k p) c -> p k c", p=128)

        def rope_epilogue(pt, pb_, nrow, Rm, cosT, sinT, tok0, dst_dram):
            qs, qb = qs_r.next()
            npart = pt.shape[0]
            P.op("act", K("activation", out=qs[0:npart, :], in_=pt, func=AF.Copy), reads=[pb_], writes=[qb])
            p2, p2b = psring.next()
            P.op("pe", K("matmul", p2[0:nrow, :], lhsT=Rm[0:npart, 0:nrow], rhs=qs[0:npart, :], start=True, stop=True),
                 reads=[qb, cst], writes=[p2b])
            t1, t1b = t1_r.next()
            t2, t2b = t2_r.next()
            P.op("dve", K("tensor_tensor", out=t1[0:nrow, :], in0=p2[0:nrow, :], in1=sinT[0:nrow, tok0:tok0 + 512],
                          op=ALU.mult), reads=[p2b], writes=[t1b])
            P.op("pool", K("tensor_tensor", out=t2[0:nrow, :], in0=qs[0:nrow, :], in1=cosT[0:nrow, tok0:tok0 + 512],
                           op=ALU.mult), reads=[qb], writes=[t2b])
            P.op("dve", K("tensor_tensor", out=qs[0:nrow, :], in0=t1[0:nrow, :], in1=t2[0:nrow, :], op=ALU.add),
                 reads=[t1b, t2b], writes=[qb])
            P.store(dst_dram, qs[0:npart, :], qb)

        def latent_epilogue(cps, tok0, dst3):
            sqs = []
            for (pt, pb_) in cps:
                sq, sqb = sq_r.next()
                P.op("act", K("activation", out=sq, in_=pt, func=AF.Square), reads=[pb_], writes=[sqb])
                sqs.append((sq, sqb))
            p5, p5b = psring.next()
            for i, (sq, sqb) in enumerate(sqs):
                P.op("pe", K("matmul", p5, lhsT=ones_f, rhs=sq, start=(i == 0), stop=(i == 3)),
                     reads=[sqb, cst], writes=[p5b], acc=(i > 0))
            rr, rrb = rr_r.next()
            P.op("act", K("activation", out=rr, in_=p5, func=AF.Sqrt, scale=1.0 / 512, bias=EPS), reads=[p5b], writes=[rrb])
            P.op("dve", K("reciprocal", out=rr, in_=rr), reads=[rrb], writes=[rrb])
            cn, cnb = cn_r.next()
            for i, (pt, pb_) in enumerate(cps):
                P.op("dve", K("tensor_tensor", out=cn[:, i, :], in0=pt, in1=rr, op=ALU.mult), reads=[pb_, rrb], writes=[cnb])
            P.store(dst3, cn, cnb)

        groups = [
            ("aq", 0, 1024, "own"), ("ak", 1024, 1024, "all"), ("av", 2048, 1024, "all"),
            ("cq", 3072, 512, "own"), ("ckv", 3584, 576, "all"),
        ]
        for (gname, c0, ncol, which) in groups:
            P.load(Wg[:, :, 0:ncol], wb_in_v[:, :, c0:c0 + ncol], Wgb)
            tts = range(4, 8) if which == "own" else range(8)
            for tt in tts:
                htile, hb = hr.next()
                P.load(htile, hT_v[:, :, tt * 512:(tt + 1) * 512], hb)
                tok0 = tt * 512
                otok0 = tok0 - TO
                if gname in ("aq", "ak"):
                    for ct in range(8):
                        pt, pb_ = psring.next()
                        for kc in range(16):
                            P.op("pe", K("matmul", pt, lhsT=Wg[:, kc, ct * 128:(ct + 1) * 128], rhs=htile[:, kc, :],
                                         start=(kc == 0), stop=(kc == 15)), reads=[Wgb, hb], writes=[pb_], acc=(kc > 0))
                        dst = qA[ct, :, otok0:otok0 + 512] if gname == "aq" else kA[ct, :, tok0:tok0 + 512]
                        rope_epilogue(pt, pb_, 32, RA_b, cosA, sinA, tok0, dst)
                elif gname == "av":
                    for sub in range(4):
                        for hf in range(2):
                            pt, pb_ = psring.next()
                            for kc in range(16):
                                P.op("pe", K("matmul", pt, lhsT=htile[:, kc, sub * 128:(sub + 1) * 128],
                                             rhs=Wg[:, kc, hf * 512:(hf + 1) * 512], start=(kc == 0), stop=(kc == 15)),
                                     reads=[Wgb, hb], writes=[pb_], acc=(kc > 0))
                            qs, qb = qs_r.next()
                            eng = "act" if hf == 0 else "dve"
                            if eng == "act":
                                P.op("act", K("activation", out=qs, in_=pt, func=AF.Copy), reads=[pb_], writes=[qb])
                            else:
                                P.op("dve", K("tensor_copy", out=qs, in_=pt), reads=[pb_], writes=[qb])
                            P.store(vA[tok0 + sub * 128:tok0 + (sub + 1) * 128, hf * 512:(hf + 1) * 512], qs, qb)
                else:
                    cps = []
                    for ct in range(4):
                        pt, pb_ = psring.next()
                        for kc in range(16):
                            P.op("pe", K("matmul", pt, lhsT=Wg[:, kc, ct * 128:(ct + 1) * 128], rhs=htile[:, kc, :],
                                         start=(kc == 0), stop=(kc == 15)), reads=[Wgb, hb], writes=[pb_], acc=(kc > 0))
                        cps.append((pt, pb_))
                    if gname == "cq":
                        latent_epilogue(cps, tok0, cq.rearrange("k p t -> p k t")[:, :, otok0:otok0 + 512])
                    else:
                        latent_epilogue(cps, tok0, ckv.rearrange("k p t -> p k t")[:, :, tok0:tok0 + 512])
                        pt, pb_ = psring.next()
                        for kc in range(16):
                            P.op("pe", K("matmul", pt[0:64, :], lhsT=Wg[:, kc, 512:576], rhs=htile[:, kc, :],
                                         start=(kc == 0), stop=(kc == 15)), reads=[Wgb, hb], writes=[pb_], acc=(kc > 0))
                        rope_epilogue(pt[0:64, :], pb_, 64, RB_b, cosB, sinB, tok0, kr[:, tok0:tok0 + 512])
        P.barrier()
        if stop_after == 2:
            return finish(nc, P, st, out)

        A.reset()
        Wq = kcv(A.take(4 * 1536, BF16), 4)
        Wqb = Buf("Wq")
        Wkv = kcv(A.take(4 * 2048, BF16), 4)
        Wkvb = Buf("Wkv")
        cr = Ring("c3", [kcv(A.take(4 * 512, BF16), 4) for _ in range(2)])
        qs_r = Ring("qs3", [A.take(512, BF16) for _ in range(3)])
        t1_r = Ring("t13", [A.take(512, F32) for _ in range(2)])
        t2_r = Ring("t23", [A.take(512, F32) for _ in range(2)])
        P.load(Wq, wb_uq.rearrange("(k p) c -> p k c", p=128), Wqb)
        P.load(Wkv, wb_ukv.rearrange("(k p) c -> p k c", p=128), Wkvb)
        cq_v = cq.rearrange("k p t -> p k t")
        ckv_v = ckv.rearrange("k p t -> p k t")
        cpy = [0]

        def copy_out(pt, pb_, dst_dram, npart=128):
            qs, qb = qs_r.next()
            cpy[0] += 1
            if cpy[0] % 2:
                P.op("act", K("activation", out=qs[0:npart, :], in_=pt, func=AF.Copy), reads=[pb_], writes=[qb])
            else:
                P.op("dve", K("tensor_copy", out=qs[0:npart, :], in_=pt), reads=[pb_], writes=[qb])
            P.store(dst_dram, qs[0:npart, :], qb)

        for tt in range(4):
            ctile, cb = cr.next()
            P.load(ctile, cq_v[:, :, tt * 512:(tt + 1) * 512], cb)
            tok0 = TO + tt * 512
            for h in range(NH):
                pt, pb_ = psring.next()
                for kc in range(4):
                    P.op("pe", K("matmul", pt, lhsT=Wq[:, kc, h * 192:h * 192 + 128], rhs=ctile[:, kc, :],
                                 start=(kc == 0), stop=(kc == 3)), reads=[Wqb, cb], writes=[pb_], acc=(kc > 0))
                copy_out(pt, pb_, qn[h, :, tt * 512:(tt + 1) * 512])
                pt, pb_ = psring.next()
                for kc in range(4):
                    P.op("pe", K("matmul", pt[0:64, :], lhsT=Wq[:, kc, h * 192 + 128:h * 192 + 192], rhs=ctile[:, kc, :],
                                 start=(kc == 0), stop=(kc == 3)), reads=[Wqb, cb], writes=[pb_], acc=(kc > 0))
                rope_epilogue(pt[0:64, :], pb_, 64, RB_b, cosB, sinB, tok0, qr[h, :, tt * 512:(tt + 1) * 512])
        for tt in range(8):
            ctile, cb = cr.next()
            P.load(ctile, ckv_v[:, :, tt * 512:(tt + 1) * 512], cb)
            for h in range(NH):
                pt, pb_ = psring.next()
                for kc in range(4):
                    P.op("pe", K("matmul", pt, lhsT=Wkv[:, kc, h * 256:h * 256 + 128], rhs=ctile[:, kc, :],
                                 start=(kc == 0), stop=(kc == 3)), reads=[Wkvb, cb], writes=[pb_], acc=(kc > 0))
                copy_out(pt, pb_, kn[h, :, tt * 512:(tt + 1) * 512])
            for sub in range(4):
                for hf in range(2):
                    pt, pb_ = psring.next()
                    for kc in range(4):
                        rhs = Wkv[:, kc, hf * 1024:(hf + 1) * 1024].rearrange("p (h c) -> p h c", c=256)[:, :, 128:256]
                        P.op("pe", K("matmul", pt.rearrange("p (h c) -> p h c", c=128), lhsT=ctile[:, kc, sub * 128:(sub + 1) * 128],
                                     rhs=rhs, start=(kc == 0), stop=(kc == 3)), reads=[Wkvb, cb], writes=[pb_], acc=(kc > 0))
                    r0 = tt * 512 + sub * 128
                    copy_out(pt, pb_, vB[r0:r0 + 128, hf * 512:(hf + 1) * 512])
        P.barrier()
        if stop_after == 3:
            return finish(nc, P, st, out)

        A.base = base_const
        A.reset()
        Qr = Ring("Qa", [A.take(TO, BF16) for _ in range(2)])
        Kr = Ring("Ka", [A.take(TL, BF16) for _ in range(2)])
        Vr = Ring("Va", [A.take(32 * 128, BF16).rearrange("p (b c) -> p b c", c=128) for _ in range(2)])
        AZr = Ring("AZ", [A.take(2 * TO, F32).rearrange("p (a t) -> p a t", a=2) for _ in range(2)])
        Ptr = Ring("Pt", [A.take(256, BF16) for _ in range(3)])
        Mxr = Ring("Mxa", [A.take(TO, BF16) for _ in range(2)])
        zr_t = A.take(TO, F32)
        zrb = Buf("zr")
        sc_a = float(HD ** -0.5)
        for h in range(NH):
            Q, Qb = Qr.next()
            Kt, Kb = Kr.next()
            AZ, AZb = AZr.next()
            P.load(Q, qA[h], Qb)
            P.load(Kt, kA[h], Kb)
            for ci_, d in enumerate((1, 4, 16)):
                Vt, Vb = Vr.next()
                nblk_n = TL // (128 * d)
                srcv = vA[:, h * 128:(h + 1) * 128].rearrange("(n i r) c -> i n r c", i=128, r=d)
                dstv = Vt.rearrange("p (n r) c -> p n r c", r=d)
                for n in range(nblk_n):
                    P.load(dstv[:, n], srcv[:, n], Vb)
                for blk in range(16, 32):
                    n, r = blk // d, blk % d
                    prev = blk - d
                    qs0 = n * 128 * d + r - TO
                    Qblk = Q[:, qs0:qs0 + 127 * d + 1:d]

                    def ksl(b_):
                        n_, r_ = b_ // d, b_ % d
                        s_ = n_ * 128 * d + r_
                        return Kt[:, s_:s_ + 127 * d + 1:d]
                    S, Sb = psring.next()
                    P.op("pe", K("matmul", S[:, 0:128], lhsT=ident, rhs=mask_prev, start=True, stop=False),
                         reads=[cst], writes=[Sb])
                    P.op("pe", K("matmul", S[:, 0:128], lhsT=ksl(prev), rhs=Qblk, start=False, stop=True),
                         reads=[Kb, Qb], writes=[Sb], acc=True)
                    P.op("pe", K("matmul", S[:, 128:256], lhsT=ident, rhs=mask_cur, start=True, stop=False),
                         reads=[cst], writes=[Sb], acc=True)
                    P.op("pe", K("matmul", S[:, 128:256], lhsT=ksl(blk), rhs=Qblk, start=False, stop=True),
                         reads=[Kb, Qb], writes=[Sb], acc=True)
                    Pt, Ptb = Ptr.next()
                    P.op("act", K("activation", out=Pt, in_=S[:, 0:256], func=AF.Exp, scale=sc_a), reads=[Sb], writes=[Ptb])
                    O, Ob = psring.next()
                    P.op("pe", K("matmul", O[:, 0:128], lhsT=Vt[:, prev, :], rhs=Pt[:, 0:128], start=True, stop=False),
                         reads=[Vb, Ptb], writes=[Ob])
                    P.op("pe", K("matmul", O[:, 0:128], lhsT=Vt[:, blk, :], rhs=Pt[:, 128:256], start=False, stop=True),
                         reads=[Vb, Ptb], writes=[Ob], acc=True)
                    P.op("pe", K("matmul", O[:, 128:256], lhsT=(pv_b if prev < 16 else ones_b), rhs=Pt[:, 0:128],
                                 start=True, stop=False), reads=[cst, Ptb], writes=[Ob], acc=True)
                    P.op("pe", K("matmul", O[:, 128:256], lhsT=ones_b, rhs=Pt[:, 128:256], start=False, stop=True),
                         reads=[cst, Ptb], writes=[Ob], acc=True)
                    azv = AZ[:, :, qs0:qs0 + 127 * d + 1:d]
                    osrc = O[:, 0:256].rearrange("p (a t) -> p a t", a=2)
                    if ci_ == 0:
                        P.op("dve", K("tensor_copy", out=azv, in_=osrc), reads=[Ob], writes=[AZb])
                    else:
                        P.op("dve", K("tensor_tensor", out=azv, in0=azv, in1=osrc, op=ALU.add), reads=[Ob, AZb], writes=[AZb])
            Mx, Mxb = Mxr.next()
            P.op("dve", K("reciprocal", out=zr_t, in_=AZ[:, 1, :]), reads=[AZb], writes=[zrb])
            P.op("pool", K("tensor_tensor", out=Mx, in0=AZ[:, 0, :], in1=zr_t, op=ALU.mult), reads=[AZb, zrb], writes=[Mxb])
            P.store(mixT[h], Mx, Mxb)
        P.barrier()
        if stop_after == 4:
            return finish(nc, P, st, out)

        A.reset()
        Qnr = Ring("Qn", [A.take(TO, BF16) for _ in range(2)])
        Qrr = Ring("Qr", [A.take(TO, BF16) for _ in range(2)])
        Knr = Ring("Kn", [A.take(TL, BF16) for _ in range(2)])
        KR = A.take(TL, BF16)
        KRb = Buf("KR")
        Vbr = Ring("Vb", [A.take(32 * 132, BF16).rearrange("p (b c) -> p b c", c=132) for _ in range(2)])
        Ptr = Ring("Pt5", [A.take(512, BF16) for _ in range(3)])
        Onr = Ring("On", [A.take(128, BF16) for _ in range(2)])
        rzr = Ring("rz", [A.take(1, F32) for _ in range(2)])
        Mxr = Ring("Mx5", [A.take(512, BF16) for _ in range(2)])
        sc_b = float(192 ** -0.5)
        Oring = Ring("O5", [None, None])
        Oring.items = [psum[0], psum[1]]
        Sring = Ring("S5", [None] * 4)
        Sring.items = [psum[2], psum[3], psum[4], psum[5]]
        Tring = Ring("T5", [None] * 2)
        Tring.items = [psum[6], psum[7]]
        P.op("pool", K("memset", KR, 0.0), writes=[KRb])
        P.load(KR[0:64, :], kr, KRb)
        for (Qr_t, Qrb) in Qrr.items:
            P.op("pool", K("memset", Qr_t, 0.0), writes=[Qrb])
        for (Vb_t, Vbb) in Vbr.items:
            P.op("pool", K("memset", Vb_t[:, 16:32, 128:129], 1.0), writes=[Vbb])
            P.op("pool", K("tensor_copy", out=Vb_t[:, 0:16, 128:129], in_=pv_b[:, 0:16].rearrange("p (b c) -> p b c", c=1)),
                 reads=[cst], writes=[Vbb])
        for h in range(NH):
            Qn_t, Qnb = Qnr.next()
            Qr_t, Qrb = Qrr.next()
            Kn_t, Knb = Knr.next()
            Vb_t, Vbb = Vbr.next()
            P.load(Qn_t, qn[h], Qnb)
            P.load(Qr_t[0:64, :], qr[h], Qrb)
            P.load(Kn_t, kn[h], Knb)
            srcv = vB[:, h * 128:(h + 1) * 128].rearrange("(b i) c -> i b c", i=128)
            for b4 in range(4):
                P.load(Vb_t[:, b4 * 8:(b4 + 1) * 8, 0:128], srcv[:, b4 * 8:(b4 + 1) * 8, :], Vbb)
            for g in range(4):
                OA, OAb = Oring.next()
                OB, OBb = Oring.next()
                for (Ot, Otb) in ((OA, OAb), (OB, OBb)):
                    P.op("pe", K("matmul", Ot, lhsT=zeros_b[:, 0:128], rhs=zeros_b, start=True, stop=False),
                         reads=[cst], writes=[Otb])
                Mx, Mxb = Mxr.next()
                nkb = 16 + 4 * g + 4
                for kb in range(nkb):
                    j0 = max(0, kb - 16 - 4 * g)
                    c0 = j0 * 128
                    diag = kb >= 16 + 4 * g
                    S, Sb = Sring.next()
                    q0 = g * 512 + c0
                    if diag:
                        P.op("pe", K("matmul", S[:, c0:c0 + 128], lhsT=ident, rhs=mask_cur, start=True, stop=False),
                             reads=[cst], writes=[Sb])
                    P.op("pe", K("matmul", S[:, c0:512], lhsT=Kn_t[:, kb * 128:(kb + 1) * 128], rhs=Qn_t[:, q0:g * 512 + 512],
                                 start=(not diag), stop=False), reads=[Knb, Qnb], writes=[Sb], acc=diag)
                    P.op("pe", K("matmul", S[:, c0:512], lhsT=KR[:, kb * 128:(kb + 1) * 128], rhs=Qr_t[:, q0:g * 512 + 512],
                                 start=False, stop=True), reads=[KRb, Qrb], writes=[Sb], acc=True)
                    Pt, Ptb = Ptr.next()
                    P.op("act", K("activation", out=Pt[:, c0:512], in_=S[:, c0:512], func=AF.Exp, scale=sc_b),
                         reads=[Sb], writes=[Ptb])
                    for j in range(j0, 4):
                        Ot, Otb = (OA, OAb) if j < 2 else (OB, OBb)
                        oc = (j % 2) * 256
                        last = (kb == 16 + 4 * g + j)
                        P.op("pe", K("matmul", Ot[:, oc:oc + 129], lhsT=Pt[:, j * 128:(j + 1) * 128], rhs=Vb_t[:, kb, 0:129],
                                     start=False, stop=last), reads=[Ptb, Vbb], writes=[Otb], acc=True)
                        if last:
                            rz, rzb = rzr.next()
                            On, Onb = Onr.next()
                            P.op("dve", K("reciprocal", out=rz, in_=Ot[:, oc + 128:oc + 129]), reads=[Otb], writes=[rzb])
                            P.op("dve", K("tensor_scalar", out=On, in0=Ot[:, oc:oc + 128], scalar1=rz, scalar2=None, op0=ALU.mult),
                                 reads=[Otb, rzb], writes=[Onb])
                            T, Tb = Tring.next()
                            Tb16 = T[:, :].bitcast(BF16)
                            P.op("pe", K("transpose", out=Tb16[:, 0:128], in_=On, identity=ident), reads=[Onb, cst], writes=[Tb])
                            P.op("act", K("activation", out=Mx[:, j * 128:(j + 1) * 128], in_=Tb16[:, 0:128], func=AF.Copy),
                                 reads=[Tb], writes=[Mxb])
                P.store(mixT[8 + h, :, g * 512:(g + 1) * 512], Mx, Mxb)
        P.barrier()
        if stop_after == 5:
            return finish(nc, P, st, out)

        A.reset()
        Wo = kcv(A.take(16 * D, BF16), 16)
        Wob = Buf("Wo")
        gp = A.take(D, F32)
        gpb = Buf("gp")
        mr = Ring("m6", [kcv(A.take(16 * 512, BF16), 16) for _ in range(2)])
        xr = Ring("x6", [A.take(D, F32) for _ in range(2)])
        yr = Ring("y6", [A.take(D, F32) for _ in range(2)])
        x1r = Ring("x16", [A.take(D, F32) for _ in range(2)])
        scr_r = Ring("xn6", [A.take(D, BF16) for _ in range(2)])
        sm_r = Ring("sm6", [A.take(8, F32) for _ in range(2)])
        hr = Ring("ht6", [kcv(A.take(16 * 512, BF16), 16) for _ in range(2)])
        P.load(Wo, wb_out.rearrange("(k p) c -> p k c", p=128), Wob)
        P.load(gp, g_post, gpb)
        mixT_v = mixT.rearrange("k p t -> p k t")
        h2T_v = h2T.rearrange("k p t -> p k t")
        for tt in range(4):
            mt, mb = mr.next()
            P.load(mt, mixT_v[:, :, tt * 512:(tt + 1) * 512], mb)
            htile, hb = hr.next()
            for sub in range(4):
                r0 = tt * 512 + sub * 128
                xt, xb = xr.next()
                P.load(xt, xo[r0:r0 + 128, :], xb)
                ys = []
                for dt_ in range(4):
                    pt, pb_ = psring.next()
                    for kc in range(16):
                        P.op("pe", K("matmul", pt, lhsT=mt[:, kc, sub * 128:(sub + 1) * 128], rhs=Wo[:, kc, dt_ * 512:(dt_ + 1) * 512],
                                     start=(kc == 0), stop=(kc == 15)), reads=[mb, Wob], writes=[pb_], acc=(kc > 0))
                    ys.append((pt, pb_))
                sm, smb = sm_r.next()
                sc, scb = scr_r.next()
                for dt_, (pt, pb_) in enumerate(ys):
                    P.op("act", K("activation", out=sc[:, dt_ * 512:(dt_ + 1) * 512], in_=pt, func=AF.Square,
                                  accum_out=sm[:, dt_:dt_ + 1]), reads=[pb_], writes=[scb, smb])
                P.op("dve", K("tensor_tensor", out=sm[:, 4:6], in0=sm[:, 0:2], in1=sm[:, 2:4], op=ALU.add), reads=[smb], writes=[smb])
                P.op("dve", K("tensor_tensor", out=sm[:, 6:7], in0=sm[:, 4:5], in1=sm[:, 5:6], op=ALU.add), reads=[smb], writes=[smb])
                P.op("act", K("activation", out=sm[:, 7:8], in_=sm[:, 6:7], func=AF.Sqrt, scale=1.0 / D, bias=EPS),
                     reads=[smb], writes=[smb])
                P.op("dve", K("reciprocal", out=sm[:, 7:8], in_=sm[:, 7:8]), reads=[smb], writes=[smb])
                yt, yb = yr.next()
                for dt_, (pt, pb_) in enumerate(ys):
                    P.op("act", K("activation", out=yt[:, dt_ * 512:(dt_ + 1) * 512], in_=pt, func=AF.Copy, scale=sm[:, 7:8]),
                         reads=[pb_, smb], writes=[yb])
                P.op("pool", K("tensor_tensor", out=yt, in0=yt, in1=gp, op=ALU.mult), reads=[yb, gpb], writes=[yb])
                x1t, x1b = x1r.next()
                P.op("pool", K("tensor_tensor", out=x1t, in0=yt, in1=xt, op=ALU.add), reads=[yb, xb], writes=[x1b])
                P.store(x1[r0:r0 + 128, :], x1t, x1b)
                sm2, sm2b = sm_r.next()
                norm_transpose(x1t, x1b, htile, hb, sub, sc, scb, sm2, sm2b)
            P.store(h2T_v[:, :, tt * 512:(tt + 1) * 512], htile, hb)
        P.barrier()
        if stop_after == 6:
            return finish(nc, P, st, out)

        A.reset()
        gp2 = A.take(D, F32)
        gp2b = Buf("gp2")
        P.load(gp2, g_post2, gp2b)
        h2 = kcv(A.take(16 * 512, BF16), 16)
        h2b = Buf("h2t")
        U = kcv(A.take(64 * 512, BF16), 64)
        Ub = Buf("U")
        Wur = Ring("Wu", [kcv(A.take(16 * 512, BF16), 16) for _ in range(2)])
        Wdr = Ring("Wd", [kcv(A.take(4 * 1024, BF16), 4) for _ in range(2)])
        rl_r = Ring("rl", [A.take(512, F32) for _ in range(2)])
        Y = A.take(4 * D, F32).rearrange("p (s c) -> p s c", s=4)
        Yb = Buf("Y")
        sm7 = A.take(32, F32)
        sm7b = Buf("sm7")
        junk = A.take(1024, BF16)
        junkb = Buf("junk")
        x1r = Ring("x17", [A.take(D, F32) for _ in range(1)])
        wb_up_v = wb_up.rearrange("(k p) c -> p k c", p=128)
        wb_down_v = wb_down.rearrange("(k p) c -> p k c", p=128)
        out_ops = []
        import os as _os
        _ntt = int(_os.environ.get("F7_NTT", "4"))
        _mode = _os.environ.get("F7_MODE", "full")
        for tt in range(_ntt):
            P.load(h2, h2T_v[:, :, tt * 512:(tt + 1) * 512], h2b)
            for fg in range(16):
                Wu, Wub = Wur.next()
                P.load(Wu, wb_up_v[:, :, fg * 512:(fg + 1) * 512], Wub)
                for f in range(4):
                    ft = fg * 4 + f
                    pt, pb_ = psring.next()
                    for kc in range(16):
                        P.op("pe", K("matmul", pt, lhsT=Wu[:, kc, f * 128:(f + 1) * 128], rhs=h2[:, kc, :],
                                     start=(kc == 0), stop=(kc == 15)), reads=[Wub, h2b], writes=[pb_], acc=(kc > 0))
                    rl, rlb = rl_r.next()
                    P.op("act", K("activation", out=rl, in_=pt, func=AF.Relu), reads=[pb_], writes=[rlb])
                    P.op("pool", K("tensor_tensor", out=U[:, ft, :], in0=rl, in1=rl, op=ALU.mult), reads=[rlb], writes=[Ub])
            for dh in range(2 if _mode != "up" else 0):
                accs = [psring.next() for _ in range(8)]
                for fg in range(16):
                    Wd, Wdb = Wdr.next()
                    P.load(Wd, wb_down_v[:, fg * 4:(fg + 1) * 4, dh * 1024:(dh + 1) * 1024], Wdb)
                    for f in range(4):
                        ft = fg * 4 + f
                        for sub in range(4):
                            for dt_ in range(2):
                                pt, pb_ = accs[sub * 2 + dt_]
                                P.op("pe", K("matmul", pt, lhsT=U[:, ft, sub * 128:(sub + 1) * 128],
                                             rhs=Wd[:, f, dt_ * 512:(dt_ + 1) * 512], start=(ft == 0), stop=(ft == 63)),
                                     reads=[Ub, Wdb], writes=[pb_], acc=(ft > 0))
                for sub in range(4):
                    for dt_ in range(2):
                        pt, pb_ = accs[sub * 2 + dt_]
                        col = dh * 2 + dt_
                        _ev = _os.environ.get("F7_EV", "both")
                        if _ev in ("act", "both"):
                            P.op("act", K("activation", out=junk[:, 0:512], in_=pt, func=AF.Square,
                                          accum_out=sm7[:, sub * 4 + col:sub * 4 + col + 1]), reads=[pb_], writes=[junkb, sm7b])
                        if _ev in ("dve", "both"):
                            P.op("dve", K("tensor_copy", out=Y[:, sub, col * 512:(col + 1) * 512], in_=pt), reads=[pb_], writes=[Yb])
            for sub in range(4 if _mode == "full" else 0):
                r0 = tt * 512 + sub * 128
                x1t, x1b = x1r.next()
                P.load(x1t, x1[r0:r0 + 128, :], x1b)
                s4 = sm7[:, sub * 4:sub * 4 + 4]
                t2 = sm7[:, 16 + sub * 4:16 + sub * 4 + 2]
                ssum = sm7[:, 16 + sub * 4 + 2:16 + sub * 4 + 3]
                rs7 = sm7[:, 16 + sub * 4 + 3:16 + sub * 4 + 4]
                P.op("dve", K("tensor_tensor", out=t2, in0=s4[:, 0:2], in1=s4[:, 2:4], op=ALU.add), reads=[sm7b], writes=[sm7b])
                P.op("dve", K("tensor_tensor", out=ssum, in0=t2[:, 0:1], in1=t2[:, 1:2], op=ALU.add), reads=[sm7b], writes=[sm7b])
                P.op("act", K("activation", out=rs7, in_=ssum, func=AF.Sqrt, scale=1.0 / D, bias=EPS), reads=[sm7b], writes=[sm7b])
                P.op("dve", K("reciprocal", out=rs7, in_=rs7), reads=[sm7b], writes=[sm7b])
                P.op("act", K("activation", out=Y[:, sub, :], in_=Y[:, sub, :], func=AF.Copy, scale=rs7), reads=[Yb, sm7b], writes=[Yb])
                P.op("pool", K("tensor_tensor", out=Y[:, sub, :], in0=Y[:, sub, :], in1=gp2, op=ALU.mult), reads=[Yb, gp2b], writes=[Yb])
                P.op("pool", K("tensor_tensor", out=x1t, in0=Y[:, sub, :], in1=x1t, op=ALU.add), reads=[Yb, x1b], writes=[x1b])
                out_ops.append(P.store(out[r0:r0 + 128, :], x1t, x1b))
        P.barrier()
        return finish(nc, P, st, out)


def finish(nc, P, st, out):
    P.barrier()
    with nc.Block() as block:
        P.emit(block)
    return nc


def _consts():
    j = np.arange(128)[:, None]
    i = np.arange(128)[None, :]
    mask_prev = np.where(j >= i, 0.0, NEG).astype(np.float32)
    mask_cur = np.where(j <= i, 0.0, NEG).astype(np.float32)
    cmask = np.concatenate([mask_prev, mask_cur], axis=1)
    RA = np.zeros((128, 128), np.float32)
    for m in range(16):
        RA[m + 16, m] = -1.0
        RA[m, m + 16] = 1.0
    RB = np.zeros((128, 128), np.float32)
    for m in range(32):
        RB[m + 32, m] = -1.0
        RB[m, m + 32] = 1.0
    p = np.arange(128)
    invA = (THETA ** (-(2.0 * (p % 16)) / 32.0)).astype(np.float32)
    invB = (THETA ** (-(2.0 * (p % 32)) / 64.0)).astype(np.float32)
    invf = np.stack([invA, invB], axis=1).astype(np.float32)
    return cmask, RA, RB, invf


def make_in_maps(x, positions, norm_attn_pre, norm_attn_post, w_in, q_latent_norm, kv_latent_norm,
                 w_uq, w_ukv, w_out, norm_mlp_pre, norm_mlp_post, w_up, w_down):
    x = np.asarray(x, np.float32)
    positions = np.asarray(positions, np.int32)
    cmask, RA, RB, invf = _consts()

    def col(g, k):
        return np.ascontiguousarray(np.asarray(g, np.float32).reshape(k, 128).T)

    def bc(g):
        return np.ascontiguousarray(np.broadcast_to(np.asarray(g, np.float32).reshape(1, -1), (128, D)))

    shared = {
        "cmask": cmask, "cRA": RA, "cRB": RB, "cinvf": invf,
        "g_in": col(norm_attn_pre[0], 16), "g_up": col(norm_mlp_pre[0], 16),
        "g_q": col(q_latent_norm[0], 4), "g_kv": col(kv_latent_norm[0], 4),
        "g_post": bc(norm_attn_post[0]), "g_post2": bc(norm_mlp_post[0]),
        "w_in": np.ascontiguousarray(np.asarray(w_in[0], np.float32)),
        "w_uq": np.ascontiguousarray(np.asarray(w_uq[0], np.float32)),
        "w_ukv": np.ascontiguousarray(np.asarray(w_ukv[0], np.float32)),
        "w_out": np.ascontiguousarray(np.asarray(w_out[0], np.float32)),
        "w_up": np.ascontiguousarray(np.asarray(w_up[0], np.float32)),
        "w_down": np.ascontiguousarray(np.asarray(w_down[0], np.float32)),
    }
    maps = []
    for c in range(8):
        b, half = c // 2, c % 2
        m = dict(shared)
        m["xo"] = np.ascontiguousarray(x[b, half * TO:(half + 1) * TO])
        m["xp"] = np.ascontiguousarray(x[b, 0:TO]) if half else np.zeros((TO, D), np.float32)
        pl = np.concatenate([positions[b, 0:TO], positions[b, half * TO:(half + 1) * TO]])
        m["posb"] = np.ascontiguousarray(np.broadcast_to(pl.reshape(1, TL), (128, TL))).astype(np.int32)
        m["pvt"] = np.full((128, 128), float(half), np.float32)
        maps.append(m)
    return maps


_NC_CACHE = {}


def kernel(**inputs):
    maps = make_in_maps(**inputs)
    if "nc" not in _NC_CACHE:
        _NC_CACHE["nc"] = build_program()
    nc = _NC_CACHE["nc"]
    res = run_bass_kernel_spmd(nc, maps, core_ids=list(range(8)))
    outp = np.empty((NB, SEQ, D), np.float32)
    for c in range(8):
        b, half = c // 2, c % 2
        outp[b, half * TO:(half + 1) * TO] = np.asarray(res.results[c]["out"], np.float32)
    return outp
```

```python
import numpy as np
from contextlib import ExitStack
import concourse.bass as bass
import concourse.mybir as mybir
from concourse.bass_utils import run_bass_kernel_spmd

F32 = mybir.dt.float32
BF16 = mybir.dt.bfloat16
I32 = mybir.dt.int32
AF = mybir.ActivationFunctionType
ALU = mybir.AluOpType
AX = mybir.AxisListType

ENGS = ("pe", "act", "dve", "pool", "sp")

D = 2048
SEQ = 4096
NB = 4
TL = 4096
TO = 2048
HD = 128
NH = 8
DFF = 8192
INC = 4160
EPS = 1e-6
NEG = -30000.0
THETA = 500000.0
TWO_PI = float(2.0 * np.pi)
PI = float(np.pi)


class Buf:
    __slots__ = ("name", "last_w", "readers", "sem", "sem_val", "last_dma", "excl")

    def __init__(self, name, excl=False):
        self.name = name
        self.excl = excl
        self.last_w = None
        self.readers = []
        self.sem = None
        self.sem_val = 0
        self.last_dma = None


class Op:
    __slots__ = ("eng", "idx", "fn", "deps", "signal", "is_dma", "sem", "val", "waits")

    def __init__(self, eng, idx, fn):
        self.eng = eng
        self.idx = idx
        self.fn = fn
        self.deps = []
        self.signal = False
        self.is_dma = False
        self.sem = None
        self.val = None
        self.waits = None


def K(name, *args, **kw):
    return lambda e: getattr(e, name)(*args, **kw)


class Prog:
    def __init__(self, nc, stack):
        self.nc = nc
        self.stack = stack
        self.ops = {e: [] for e in ENGS}
        self.eng_sem = {e: stack.enter_context(nc.semaphore("S_" + e)) for e in ENGS}
        self.dma_bufs = []
        self.same_engine_sync = True

    def _add(self, eng, fn, reads, writes, acc=False):
        op = Op(eng, len(self.ops[eng]), fn)
        self.ops[eng].append(op)
        deps = []
        for b in reads:
            if b.last_w is not None:
                deps.append(b.last_w)
            if b.excl:
                deps.extend(r for r in b.readers if r.eng != eng)
        for b in writes:
            if b.last_w is not None and not (acc and b.last_w.eng == eng):
                deps.append(b.last_w)
            deps.extend(b.readers)
        op.deps = deps
        for b in reads:
            b.readers.append(op)
        for b in writes:
            b.last_w = op
            b.readers = []
        return op

    def op(self, eng, fn, reads=(), writes=(), acc=False):
        return self._add(eng, fn, list(reads), list(writes), acc)

    def dma(self, eng, out_ap, in_ap, sbuf_buf, reads=(), writes=()):
        b = sbuf_buf
        if b.sem is None:
            b.sem = self.stack.enter_context(self.nc.semaphore("D_" + b.name))
            self.dma_bufs.append(b)
        op = self._add(eng, None, list(reads), list(writes))
        if b.last_dma is not None:
            op.deps.append(b.last_dma)
        b.sem_val += 16
        op.is_dma = True
        op.sem = b.sem
        op.val = b.sem_val
        op.fn = (out_ap, in_ap)
        b.last_dma = op
        return op

    def load(self, dst, src, buf, eng="sp"):
        return self.dma(eng, dst, src, buf, writes=[buf])

    def store(self, dst, src, buf, eng="pool"):
        return self.dma(eng, dst, src, buf, reads=[buf])

    def barrier(self):
        deps = []
        for e in ENGS:
            if self.ops[e]:
                deps.append(self.ops[e][-1])
        for b in self.dma_bufs:
            if b.last_dma is not None:
                deps.append(b.last_dma)
        for e in ENGS:
            op = self._add(e, None, [], [])
            op.deps = list(deps)

    def finalize(self):
        for e in ENGS:
            known_idx = {x: -1 for x in ENGS}
            known_dma = {}
            for op in self.ops[e]:
                waits = []
                best = {}
                for d in op.deps:
                    if d.is_dma:
                        k = id(d.sem)
                        if known_dma.get(k, 0) >= d.val:
                            continue
                        known_dma[k] = d.val
                        waits.append(d)
                    else:
                        if d.fn is None:
                            continue
                        if d.eng == e and (e == "pe" or not self.same_engine_sync):
                            continue
                        if d.idx <= known_idx[d.eng]:
                            continue
                        if d.eng not in best or best[d.eng].idx < d.idx:
                            best[d.eng] = d
                for x, d in best.items():
                    known_idx[x] = d.idx
                    d.signal = True
                    waits.append(d)
                op.waits = waits
        for e in ENGS:
            c = 0
            for op in self.ops[e]:
                if not op.is_dma and op.signal:
                    c += 1
                    op.sem = self.eng_sem[e]
                    op.val = c

    def emit(self, block):
        self.finalize()
        P = self

        def run(e, eng):
            for op in P.ops[e]:
                seen = {}
                for d in op.waits:
                    k = id(d.sem)
                    if k not in seen or seen[k][1] < d.val:
                        seen[k] = (d.sem, d.val)
                for sem, val in seen.values():
                    eng.wait_ge(sem, val)
                if op.is_dma:
                    o, i = op.fn
                    eng.dma_start(out=o, in_=i).then_inc(op.sem, 16)
                elif op.fn is not None:
                    ins = op.fn(eng)
                    if op.signal:
                        ins.then_inc(op.sem, 1)

        @block.tensor
        def _(eng):
            run("pe", eng)

        @block.scalar
        def _(eng):
            run("act", eng)

        @block.vector
        def _(eng):
            run("dve", eng)

        @block.gpsimd
        def _(eng):
            run("pool", eng)

        @block.sync
        def _(eng):
            run("sp", eng)


class Ring:
    def __init__(self, name, aps):
        self.items = [(ap, Buf("%s%d" % (name, i))) for i, ap in enumerate(aps)]
        self.i = 0

    def next(self):
        it = self.items[self.i % len(self.items)]
        self.i += 1
        return it


class Arena:
    def __init__(self, t, nbytes):
        self.t = t
        self.n = nbytes
        self.base = 0
        self.off = 0

    def persist(self):
        self.base = self.off

    def reset(self):
        self.off = self.base

    def take(self, nelem, dt, parts=128):
        sz = 2 if dt == BF16 else 4
        nb = (nelem * sz + 63) // 64 * 64
        o = self.off
        self.off += nb
        assert self.off <= self.n, "SBUF arena overflow %d > %d" % (self.off, self.n)
        ap = self.t[0:parts, o // 4:(o + nelem * sz + 3) // 4]
        if dt != F32:
            ap = ap.bitcast(dt)
        return ap


DEBUG_DUMP = False

class BG:
    def __init__(self, P, wst, wob, wlist):
        self.P = P
        self.wst = wst
        self.wob = wob
        self.tiles = []
        for (src, dst, R, C, gcol) in wlist:
            for kc in range(R // 128):
                for c0 in range(0, C, 2048):
                    cw = min(2048, C - c0)
                    self.tiles.append((src[kc * 128:(kc + 1) * 128, c0:c0 + cw], dst[kc * 128:(kc + 1) * 128, c0:c0 + cw], cw,
                                       None if gcol is None else gcol[:, kc:kc + 1]))
        self.il = 0
        self.ic = 0
        self.slots = {}

    def step(self, n=1):
        P = self.P
        for _ in range(n):
            while self.il < min(len(self.tiles), self.ic + 3):
                src, dst, cw, g = self.tiles[self.il]
                sap, sb = self.wst.next()
                self.slots[self.il] = (sap, sb)
                P.load(sap[:, 0:cw], src, sb)
                self.il += 1
            if self.ic < len(self.tiles):
                src, dst, cw, g = self.tiles[self.ic]
                sap, sb = self.slots.pop(self.ic)
                oap, ob = self.wob.next()
                if g is None:
                    P.op("act", K("activation", out=oap[:, 0:cw], in_=sap[:, 0:cw], func=AF.Copy), reads=[sb], writes=[ob])
                else:
                    P.op("act", K("activation", out=oap[:, 0:cw], in_=sap[:, 0:cw], func=AF.Copy, scale=g), reads=[sb], writes=[ob])
                P.store(dst, oap[:, 0:cw], ob)
                self.ic += 1

    def flush(self):
        while self.ic < len(self.tiles):
            self.step()


C1 = 6.28125
C2 = float(2.0 * np.pi - 6.28125)


def rope_table(P, posf, invcol, ang, ni, nf, mm_, sin_out, cos_out, tb):
    w = dict(reads=[tb], writes=[tb])
    P.op("dve", K("tensor_scalar", out=ang, in0=posf, scalar1=invcol, scalar2=None, op0=ALU.mult), **w)
    P.op("dve", K("tensor_scalar", out=ni, in0=ang, scalar1=1.0 / TWO_PI, scalar2=None, op0=ALU.mult), **w)
    P.op("dve", K("tensor_copy", out=nf, in_=ni), **w)
    P.op("dve", K("scalar_tensor_tensor", out=ang, in0=nf, scalar=-C1, in1=ang, op0=ALU.mult, op1=ALU.add), **w)
    P.op("dve", K("scalar_tensor_tensor", out=ang, in0=nf, scalar=-C2, in1=ang, op0=ALU.mult, op1=ALU.add), **w)

    def wrap(x):
        P.op("dve", K("tensor_scalar", out=mm_, in0=x, scalar1=PI, scalar2=-TWO_PI, op0=ALU.is_gt, op1=ALU.mult), **w)
        P.op("dve", K("tensor_tensor", out=x, in0=x, in1=mm_, op=ALU.add), **w)
        P.op("dve", K("tensor_scalar", out=mm_, in0=x, scalar1=-PI, scalar2=TWO_PI, op0=ALU.is_lt, op1=ALU.mult), **w)
        P.op("dve", K("tensor_tensor", out=x, in0=x, in1=mm_, op=ALU.add), **w)
    wrap(ang)
    P.op("act", K("activation", out=sin_out, in_=ang, func=AF.Sin), **w)
    P.op("dve", K("tensor_scalar", out=ang, in0=ang, scalar1=PI / 2, scalar2=None, op0=ALU.add), **w)
    wrap(ang)
    P.op("act", K("activation", out=cos_out, in_=ang, func=AF.Sin), **w)


def build_program(debug=False, stop_after=None):
    nc = bass.Bass("TRN2", target_bir_lowering=False)

    def din(name, shape, dt=F32):
        return nc.dram_tensor(name, list(shape), dt, kind="ExternalInput").ap()

    def dscr(name, shape, dt=BF16):
        if debug:
            return nc.dram_tensor(name, list(shape), dt, kind="ExternalOutput").ap()
        return nc.dram_tensor(name, list(shape), dt).ap()

    xo = din("xo", [TO, D])
    xp = din("xp", [TO, D])
    posb = din("posb", [128, TL], I32)
    pvt = din("pvt", [128, 128])
    cmask = din("cmask", [128, 256])
    cRA = din("cRA", [128, 128])
    cRB = din("cRB", [128, 128])
    cinvf = din("cinvf", [128, 2])
    g_in = din("g_in", [128, 16])
    g_up = din("g_up", [128, 16])
    g_q = din("g_q", [128, 4])
    g_kv = din("g_kv", [128, 4])
    g_post = din("g_post", [128, D])
    g_post2 = din("g_post2", [128, D])
    w_in = din("w_in", [D, INC])
    w_uq = din("w_uq", [512, 1536])
    w_ukv = din("w_ukv", [512, 2048])
    w_out = din("w_out", [D, D])
    w_up = din("w_up", [D, DFF])
    w_down = din("w_down", [DFF, D])
    out = nc.dram_tensor("out", [TO, D], F32, kind="ExternalOutput").ap()

    wb_in = dscr("wb_in", [D, INC])
    wb_uq = dscr("wb_uq", [512, 1536])
    wb_ukv = dscr("wb_ukv", [512, 2048])
    wb_out = dscr("wb_out", [D, D])
    wb_up = dscr("wb_up", [D, DFF])
    wb_down = dscr("wb_down", [DFF, D])
    hT = dscr("hT", [16, 128, TL])
    qA = dscr("qA", [NH, 128, TO])
    kA = dscr("kA", [NH, 128, TL])
    vA = dscr("vA", [TL, NH * HD])
    cq = dscr("cq", [4, 128, TO])
    ckv = dscr("ckv", [4, 128, TL])
    kr = dscr("kr", [64, TL])
    qn = dscr("qn", [NH, 128, TO])
    qr = dscr("qr", [NH, 64, TO])
    kn = dscr("kn", [NH, 128, TL])
    vB = dscr("vB", [TL, NH * HD])
    mixT = dscr("mixT", [16, 128, TO])
    x1 = dscr("x1", [TO, D], F32)
    h2T = dscr("h2T", [16, 128, TO])

    st = ExitStack()
    with st:
        ARENA_BYTES = 204800
        arena_t = st.enter_context(nc.sbuf_tensor("arena", [128, ARENA_BYTES // 4], F32))
        A = Arena(arena_t, ARENA_BYTES)
        psum = []
        for i in range(8):
            t = st.enter_context(nc.psum_tensor("ps%d" % i, [128, 512], F32))
            psum.append((t[:, :], Buf("ps%d" % i, excl=True)))
        P = Prog(nc, st)
        psring = Ring("psr", [None] * 8)
        psring.items = [(t, b) for t, b in psum]

        cst = Buf("const")
        ident = A.take(128, BF16)
        ones_b = A.take(128, BF16)
        zeros_b = A.take(512, BF16)
        pv_b = A.take(128, BF16)
        mask_b = A.take(256, BF16)
        RA_b = A.take(128, BF16)
        RB_b = A.take(128, BF16)
        ones_f = A.take(128, F32)
        invf = A.take(2, F32)
        gc_in = A.take(16, F32)
        gc_up = A.take(16, F32)
        gc_q = A.take(4, F32)
        gc_kv = A.take(4, F32)
        pv_col = A.take(1, F32)
        A.persist()
        stg = A.take(128 * 6, F32)
        stgb = Buf("stg")
        P.load(stg[:, 0:128], pvt, stgb)
        P.load(stg[:, 128:384], cmask, stgb)
        P.load(stg[:, 384:512], cRA, stgb)
        P.load(stg[:, 512:640], cRB, stgb)
        gb = Buf("gl")
        P.load(invf, cinvf, gb)
        P.load(gc_in, g_in, gb)
        P.load(gc_up, g_up, gb)
        P.load(gc_q, g_q, gb)
        P.load(gc_kv, g_kv, gb)
        P.op("pool", K("memset", stg[:, 640:768], 0.0), writes=[cst])
        P.op("pool", K("affine_select", out=stg[:, 640:768], in_=stg[:, 640:768], pattern=[[-1, 128]],
                       compare_op=ALU.not_equal, fill=1.0, base=0, channel_multiplier=1),
             reads=[cst], writes=[cst])
        P.op("pool", K("tensor_copy", out=ident, in_=stg[:, 640:768]), reads=[cst], writes=[cst])
        P.op("pool", K("memset", ones_f, 1.0), writes=[cst])
        P.op("pool", K("memset", ones_b, 1.0), writes=[cst])
        P.op("pool", K("memset", zeros_b, 0.0), writes=[cst])
        P.op("dve", K("tensor_copy", out=pv_b, in_=stg[:, 0:128]), reads=[stgb], writes=[cst])
        P.op("dve", K("tensor_copy", out=pv_col, in_=stg[:, 0:1]), reads=[stgb], writes=[cst])
        P.op("dve", K("tensor_copy", out=mask_b, in_=stg[:, 128:384]), reads=[stgb], writes=[cst])
        P.op("dve", K("tensor_copy", out=RA_b, in_=stg[:, 384:512]), reads=[stgb], writes=[cst])
        P.op("dve", K("tensor_copy", out=RB_b, in_=stg[:, 512:640]), reads=[stgb], writes=[cst])
        P.barrier()
        mask_prev = mask_b[:, 0:128]
        mask_cur = mask_b[:, 128:256]

        def kcv(ap, k):
            return ap.rearrange("p (k n) -> p k n", k=k)

        A.reset()
        base_const = A.base
        wst = Ring("wst", [A.take(2048, F32) for _ in range(3)])
        wob = Ring("wob", [A.take(2048, BF16) for _ in range(3)])
        A.persist()
        base_bg = A.base
        bg1 = BG(P, wst, wob, [(w_in, wb_in, D, INC, gc_in)])
        bg2 = BG(P, wst, wob, [(w_uq, wb_uq, 512, 1536, gc_q), (w_ukv, wb_ukv, 512, 2048, gc_kv),
                               (w_out, wb_out, D, D, None), (w_up, wb_up, D, DFF, gc_up), (w_down, wb_down, DFF, D, None)])

        cosA = A.take(TL, F32)
        sinA = A.take(TL, F32)
        cosB = A.take(TL, F32)
        sinB = A.take(TL, F32)
        A.persist()
        posi = A.take(TL, I32)
        posf = A.take(TL, F32)
        ang = A.take(TL, F32)
        ni_t = posi
        nf_t = A.take(TL, F32)
        mm_t = A.take(TL, F32)
        tb = Buf("ropet")
        P.load(posi, posb, tb)
        P.op("dve", K("tensor_copy", out=posf, in_=posi), reads=[tb], writes=[tb])
        bg1.step(4)
        rope_table(P, posf, invf[:, 0:1], ang, ni_t, nf_t, mm_t, sinA, cosA, tb)
        bg1.step(4)
        rope_table(P, posf, invf[:, 1:2], ang, ni_t, nf_t, mm_t, sinB, cosB, tb)
        P.barrier()

        def norm_transpose(xt_ap, xb, dst_tile, dst_buf, sub, scr_bf, scr_buf, small, small_buf):
            ss = small[:, 0:1]
            rs = small[:, 1:2]
            P.op("act", K("activation", out=scr_bf, in_=xt_ap, func=AF.Square, accum_out=ss),
                 reads=[xb], writes=[scr_buf, small_buf])
            P.op("act", K("activation", out=rs, in_=ss, func=AF.Sqrt, scale=1.0 / D, bias=EPS),
                 reads=[small_buf], writes=[small_buf])
            P.op("dve", K("reciprocal", out=rs, in_=rs), reads=[small_buf], writes=[small_buf])
            P.op("dve", K("tensor_scalar", out=scr_bf, in0=xt_ap, scalar1=rs, scalar2=None, op0=ALU.mult),
                 reads=[xb, small_buf], writes=[scr_buf])
            for half in range(2):
                pt, pb_ = psring.next()
                ptb = pt[:, :].bitcast(BF16)
                for j in range(8):
                    kc = half * 8 + j
                    P.op("pe", K("transpose", out=ptb[:, j * 128:(j + 1) * 128], in_=scr_bf[:, kc * 128:(kc + 1) * 128],
                                 identity=ident), reads=[scr_buf, cst], writes=[pb_], acc=True)
                eng = "act" if half == 0 else "dve"
                dst = dst_tile[:, half * 8:(half + 1) * 8, sub * 128:(sub + 1) * 128]
                src = ptb.rearrange("p (k n) -> p k n", k=8)
                if eng == "act":
                    P.op("act", K("activation", out=dst, in_=src, func=AF.Copy), reads=[pb_], writes=[dst_buf])
                else:
                    P.op("dve", K("tensor_copy", out=dst, in_=src), reads=[pb_], writes=[dst_buf])

        A.reset()
        xr = Ring("x", [A.take(D, F32) for _ in range(2)])
        scr_r = Ring("xn", [A.take(D, BF16) for _ in range(2)])
        sm_r = Ring("sm", [A.take(2, F32) for _ in range(2)])
        hr = Ring("ht", [kcv(A.take(16 * 512, BF16), 16) for _ in range(2)])
        hT_v = hT.rearrange("k p t -> p k t")
        for tt in range(8):
            htile, hb = hr.next()
            for sub in range(4):
                xt, xb = xr.next()
                r0 = (tt % 4) * 512 + sub * 128
                srcx = xp if tt < 4 else xo
                P.load(xt, srcx[r0:r0 + 128, :], xb)
                sc, scb = scr_r.next()
                sm, smb = sm_r.next()
                norm_transpose(xt, xb, htile, hb, sub, sc, scb, sm, smb)
                bg1.step(2 if sub % 2 == 0 else 1)
            P.store(hT_v[:, :, tt * 512:(tt + 1) * 512], htile, hb)
        bg1.flush()
        P.barrier()
        if stop_after == 1:
            return finish(nc, P, st, out)

        A.reset()
        Wg = kcv(A.take(16 * 1024, BF16), 16)
        Wgb = Buf("Wg")
        hr = Ring("h2", [kcv(A.take(16 * 512, BF16), 16) for _ in range(2)])
        qs_r = Ring("qs", [A.take(512, BF16) for _ in range(3)])
        t1_r = Ring("t1", [A.take(512, F32) for _ in range(2)])
        t2_r = Ring("t2", [A.take(512, F32) for _ in range(2)])
        sq_r = Ring("sq", [A.take(512, F32) for _ in range(4)])
        rr_r = Ring("rr", [A.take(512, F32) for _ in range(2)])
        cn_r = Ring("cn", [kcv(A.take(4 * 512, BF16), 4) for _ in range(2)])
        wb_in_v = wb_in.rearrange("(k p) c -> p k c", p=128)

        def rope_epilogue(pt, pb_, nrow, Rm, cosT, sinT, tok0, dst_dram):
            qs, qb = qs_r.next()
            npart = pt.shape[0]
            P.op("act", K("activation", out=qs[0:npart, :], in_=pt, func=AF.Copy), reads=[pb_], writes=[qb])
            p2, p2b = psring.next()
            P.op("pe", K("matmul", p2[0:nrow, :], lhsT=Rm[0:npart, 0:nrow], rhs=qs[0:npart, :], start=True, stop=True),
                 reads=[qb, cst], writes=[p2b])
            t1, t1b = t1_r.next()
            t2, t2b = t2_r.next()
            P.op("dve", K("tensor_tensor", out=t1[0:nrow, :], in0=p2[0:nrow, :], in1=sinT[0:nrow, tok0:tok0 + 512],
                          op=ALU.mult), reads=[p2b], writes=[t1b])
            P.op("pool", K("tensor_tensor", out=t2[0:nrow, :], in0=qs[0:nrow, :], in1=cosT[0:nrow, tok0:tok0 + 512],
                           op=ALU.mult), reads=[qb], writes=[t2b])
            P.op("dve", K("tensor_tensor", out=qs[0:nrow, :], in0=t1[0:nrow, :], in1=t2[0:nrow, :], op=ALU.add),
                 reads=[t1b, t2b], writes=[qb])
            P.store(dst_dram, qs[0:npart, :], qb)

        def latent_epilogue(cps, tok0, dst3):
            sqs = []
            for (pt, pb_) in cps:
                sq, sqb = sq_r.next()
                P.op("act", K("activation", out=sq, in_=pt, func=AF.Square), reads=[pb_], writes=[sqb])
                sqs.append((sq, sqb))
            p5, p5b = psring.next()
            for i, (sq, sqb) in enumerate(sqs):
                P.op("pe", K("matmul", p5, lhsT=ones_f, rhs=sq, start=(i == 0), stop=(i == 3)),
                     reads=[sqb, cst], writes=[p5b], acc=(i > 0))
            rr, rrb = rr_r.next()
            P.op("act", K("activation", out=rr, in_=p5, func=AF.Sqrt, scale=1.0 / 512, bias=EPS), reads=[p5b], writes=[rrb])
            P.op("dve", K("reciprocal", out=rr, in_=rr), reads=[rrb], writes=[rrb])
            cn, cnb = cn_r.next()
            for i, (pt, pb_) in enumerate(cps):
                P.op("dve", K("tensor_tensor", out=cn[:, i, :], in0=pt, in1=rr, op=ALU.mult), reads=[pb_, rrb], writes=[cnb])
            P.store(dst3, cn, cnb)

        groups = [
            ("aq", 0, 1024, "own"), ("ak", 1024, 1024, "all"), ("av", 2048, 1024, "all"),
            ("cq", 3072, 512, "own"), ("ckv", 3584, 576, "all"),
        ]
        for (gname, c0, ncol, which) in groups:
            P.load(Wg[:, :, 0:ncol], wb_in_v[:, :, c0:c0 + ncol], Wgb)
            tts = range(4, 8) if which == "own" else range(8)
            for tt in tts:
                htile, hb = hr.next()
                P.load(htile, hT_v[:, :, tt * 512:(tt + 1) * 512], hb)
                bg2.step(2)
                tok0 = tt * 512
                otok0 = tok0 - TO
                if gname in ("aq", "ak"):
                    for ct in range(8):
                        pt, pb_ = psring.next()
                        for kc in range(16):
                            P.op("pe", K("matmul", pt, lhsT=Wg[:, kc, ct * 128:(ct + 1) * 128], rhs=htile[:, kc, :],
                                         start=(kc == 0), stop=(kc == 15)), reads=[Wgb, hb], writes=[pb_], acc=(kc > 0))
                        dst = qA[ct, :, otok0:otok0 + 512] if gname == "aq" else kA[ct, :, tok0:tok0 + 512]
                        rope_epilogue(pt, pb_, 32, RA_b, cosA, sinA, tok0, dst)
                elif gname == "av":
                    for sub in range(4):
                        for hf in range(2):
                            pt, pb_ = psring.next()
                            for kc in range(16):
                                P.op("pe", K("matmul", pt, lhsT=htile[:, kc, sub * 128:(sub + 1) * 128],
                                             rhs=Wg[:, kc, hf * 512:(hf + 1) * 512], start=(kc == 0), stop=(kc == 15)),
                                     reads=[Wgb, hb], writes=[pb_], acc=(kc > 0))
                            qs, qb = qs_r.next()
                            eng = "act" if hf == 0 else "dve"
                            if eng == "act":
                                P.op("act", K("activation", out=qs, in_=pt, func=AF.Copy), reads=[pb_], writes=[qb])
                            else:
                                P.op("dve", K("tensor_copy", out=qs, in_=pt), reads=[pb_], writes=[qb])
                            P.store(vA[tok0 + sub * 128:tok0 + (sub + 1) * 128, hf * 512:(hf + 1) * 512], qs, qb)
                else:
                    cps = []
                    for ct in range(4):
                        pt, pb_ = psring.next()
                        for kc in range(16):
                            P.op("pe", K("matmul", pt, lhsT=Wg[:, kc, ct * 128:(ct + 1) * 128], rhs=htile[:, kc, :],
                                         start=(kc == 0), stop=(kc == 15)), reads=[Wgb, hb], writes=[pb_], acc=(kc > 0))
                        cps.append((pt, pb_))
                    if gname == "cq":
                        latent_epilogue(cps, tok0, cq.rearrange("k p t -> p k t")[:, :, otok0:otok0 + 512])
                    else:
                        latent_epilogue(cps, tok0, ckv.rearrange("k p t -> p k t")[:, :, tok0:tok0 + 512])
                        pt, pb_ = psring.next()
                        for kc in range(16):
                            P.op("pe", K("matmul", pt[0:64, :], lhsT=Wg[:, kc, 512:576], rhs=htile[:, kc, :],
                                         start=(kc == 0), stop=(kc == 15)), reads=[Wgb, hb], writes=[pb_], acc=(kc > 0))
                        rope_epilogue(pt[0:64, :], pb_, 64, RB_b, cosB, sinB, tok0, kr[:, tok0:tok0 + 512])
        P.barrier()
        if stop_after == 2:
            return finish(nc, P, st, out)

        A.reset()
        Wq = kcv(A.take(4 * 1536, BF16), 4)
        Wqb = Buf("Wq")
        Wkv = kcv(A.take(4 * 2048, BF16), 4)
        Wkvb = Buf("Wkv")
        cr = Ring("c3", [kcv(A.take(4 * 512, BF16), 4) for _ in range(2)])
        qs_r = Ring("qs3", [A.take(512, BF16) for _ in range(3)])
        t1_r = Ring("t13", [A.take(512, F32) for _ in range(2)])
        t2_r = Ring("t23", [A.take(512, F32) for _ in range(2)])
        P.load(Wq, wb_uq.rearrange("(k p) c -> p k c", p=128), Wqb)
        P.load(Wkv, wb_ukv.rearrange("(k p) c -> p k c", p=128), Wkvb)
        cq_v = cq.rearrange("k p t -> p k t")
        ckv_v = ckv.rearrange("k p t -> p k t")
        cpy = [0]

        def copy_out(pt, pb_, dst_dram, npart=128):
            qs, qb = qs_r.next()
            cpy[0] += 1
            if cpy[0] % 2:
                P.op("act", K("activation", out=qs[0:npart, :], in_=pt, func=AF.Copy), reads=[pb_], writes=[qb])
            else:
                P.op("dve", K("tensor_copy", out=qs[0:npart, :], in_=pt), reads=[pb_], writes=[qb])
            P.store(dst_dram, qs[0:npart, :], qb)

        for tt in range(4):
            ctile, cb = cr.next()
            P.load(ctile, cq_v[:, :, tt * 512:(tt + 1) * 512], cb)
            bg2.step(2)
            tok0 = TO + tt * 512
            for h in range(NH):
                pt, pb_ = psring.next()
                for kc in range(4):
                    P.op("pe", K("matmul", pt, lhsT=Wq[:, kc, h * 192:h * 192 + 128], rhs=ctile[:, kc, :],
                                 start=(kc == 0), stop=(kc == 3)), reads=[Wqb, cb], writes=[pb_], acc=(kc > 0))
                copy_out(pt, pb_, qn[h, :, tt * 512:(tt + 1) * 512])
                pt, pb_ = psring.next()
                for kc in range(4):
                    P.op("pe", K("matmul", pt[0:64, :], lhsT=Wq[:, kc, h * 192 + 128:h * 192 + 192], rhs=ctile[:, kc, :],
                                 start=(kc == 0), stop=(kc == 3)), reads=[Wqb, cb], writes=[pb_], acc=(kc > 0))
                rope_epilogue(pt[0:64, :], pb_, 64, RB_b, cosB, sinB, tok0, qr[h, :, tt * 512:(tt + 1) * 512])
        for tt in range(8):
            ctile, cb = cr.next()
            P.load(ctile, ckv_v[:, :, tt * 512:(tt + 1) * 512], cb)
            bg2.step(2)
            for h in range(NH):
                pt, pb_ = psring.next()
                for kc in range(4):
                    P.op("pe", K("matmul", pt, lhsT=Wkv[:, kc, h * 256:h * 256 + 128], rhs=ctile[:, kc, :],
                                 start=(kc == 0), stop=(kc == 3)), reads=[Wkvb, cb], writes=[pb_], acc=(kc > 0))
                copy_out(pt, pb_, kn[h, :, tt * 512:(tt + 1) * 512])
            for sub in range(4):
                for hf in range(2):
                    pt, pb_ = psring.next()
                    for kc in range(4):
                        rhs = Wkv[:, kc, hf * 1024:(hf + 1) * 1024].rearrange("p (h c) -> p h c", c=256)[:, :, 128:256]
                        P.op("pe", K("matmul", pt.rearrange("p (h c) -> p h c", c=128), lhsT=ctile[:, kc, sub * 128:(sub + 1) * 128],
                                     rhs=rhs, start=(kc == 0), stop=(kc == 3)), reads=[Wkvb, cb], writes=[pb_], acc=(kc > 0))
                    r0 = tt * 512 + sub * 128
                    copy_out(pt, pb_, vB[r0:r0 + 128, hf * 512:(hf + 1) * 512])
        P.barrier()
        if stop_after == 3:
            return finish(nc, P, st, out)

        A.base = base_bg
        A.reset()
        Qr = Ring("Qa", [A.take(TO, BF16) for _ in range(2)])
        Kr = Ring("Ka", [A.take(TL, BF16) for _ in range(2)])
        Vr = Ring("Va", [A.take(32 * 128, BF16).rearrange("p (b c) -> p b c", c=128) for _ in range(2)])
        AZr = Ring("AZ", [A.take(2 * TO, F32).rearrange("p (a t) -> p a t", a=2) for _ in range(2)])
        Ptr = Ring("Pt", [A.take(256, BF16) for _ in range(4)])
        Mxr = Ring("Mxa", [A.take(TO, BF16) for _ in range(2)])
        zr_t = A.take(TO, F32)
        zrb = Buf("zr")
        sc_a = float(HD ** -0.5)
        Sring = Ring("S4", [None] * 4)
        Sring.items = [psum[0], psum[1], psum[2], psum[3]]
        Oring = Ring("O4", [None] * 4)
        Oring.items = [psum[4], psum[5], psum[6], psum[7]]
        CFG = (1, 4, 16)
        LOOK = 2

        def load_head4(h):
            Q, Qb = Qr.next()
            Kt, Kb = Kr.next()
            P.load(Q, qA[h], Qb)
            P.load(Kt, kA[h], Kb)
            return (Q, Qb, Kt, Kb)

        def load_v4(h, d):
            Vt, Vb = Vr.next()
            nblk_n = TL // (128 * d)
            srcv = vA[:, h * 128:(h + 1) * 128].rearrange("(n i r) c -> i n r c", i=128, r=d)
            dstv = Vt.rearrange("p (n r) c -> p n r c", r=d)
            for n in range(nblk_n):
                P.load(dstv[:, n], srcv[:, n], Vb)
            return (Vt, Vb)

        def s_stage4(ctx, d, blk):
            Q, Qb, Kt, Kb = ctx
            n, r = blk // d, blk % d
            prev = blk - d
            qs0 = n * 128 * d + r - TO
            Qblk = Q[:, qs0:qs0 + 127 * d + 1:d]

            def ksl(b_):
                n_, r_ = b_ // d, b_ % d
                s_ = n_ * 128 * d + r_
                return Kt[:, s_:s_ + 127 * d + 1:d]
            S, Sb = Sring.next()
            P.op("pe", K("matmul", S[:, 0:128], lhsT=ident, rhs=mask_prev, start=True, stop=False), reads=[cst], writes=[Sb])
            P.op("pe", K("matmul", S[:, 0:128], lhsT=ksl(prev), rhs=Qblk, start=False, stop=True),
                 reads=[Kb, Qb], writes=[Sb], acc=True)
            P.op("pe", K("matmul", S[:, 128:256], lhsT=ident, rhs=mask_cur, start=True, stop=False),
                 reads=[cst], writes=[Sb], acc=True)
            P.op("pe", K("matmul", S[:, 128:256], lhsT=ksl(blk), rhs=Qblk, start=False, stop=True),
                 reads=[Kb, Qb], writes=[Sb], acc=True)
            Pt, Ptb = Ptr.next()
            P.op("act", K("activation", out=Pt, in_=S[:, 0:256], func=AF.Exp, scale=sc_a), reads=[Sb], writes=[Ptb])
            return (Pt, Ptb, prev, qs0)

        def o_stage4(vt, AZc, d, blk, first, st_):
            Vt, Vb = vt
            AZ, AZb = AZc
            Pt, Ptb, prev, qs0 = st_
            O, Ob = Oring.next()
            P.op("pe", K("matmul", O[:, 0:128], lhsT=Vt[:, prev, :], rhs=Pt[:, 0:128], start=True, stop=False),
                 reads=[Vb, Ptb], writes=[Ob])
            P.op("pe", K("matmul", O[:, 0:128], lhsT=Vt[:, blk, :], rhs=Pt[:, 128:256], start=False, stop=True),
                 reads=[Vb, Ptb], writes=[Ob], acc=True)
            P.op("pe", K("matmul", O[:, 128:256], lhsT=(pv_b if prev < 16 else ones_b), rhs=Pt[:, 0:128],
                         start=True, stop=False), reads=[cst, Ptb], writes=[Ob], acc=True)
            P.op("pe", K("matmul", O[:, 128:256], lhsT=ones_b, rhs=Pt[:, 128:256], start=False, stop=True),
                 reads=[cst, Ptb], writes=[Ob], acc=True)
            azv = AZ[:, :, qs0:qs0 + 127 * d + 1:d]
            osrc = O[:, 0:256].rearrange("p (a t) -> p a t", a=2)
            if first:
                P.op("dve", K("tensor_copy", out=azv, in_=osrc), reads=[Ob], writes=[AZb])
            else:
                P.op("dve", K("tensor_tensor", out=azv, in0=azv, in1=osrc, op=ALU.add), reads=[Ob, AZb], writes=[AZb])

        def fin_head4(h, AZc):
            AZ, AZb = AZc
            Mx, Mxb = Mxr.next()
            P.op("dve", K("reciprocal", out=zr_t, in_=AZ[:, 1, :]), reads=[AZb], writes=[zrb])
            P.op("pool", K("tensor_tensor", out=Mx, in0=AZ[:, 0, :], in1=zr_t, op=ALU.mult), reads=[AZb, zrb], writes=[Mxb])
            P.store(mixT[h], Mx, Mxb)

        pend = []
        nxt_ctx = load_head4(0)
        nxt_v = load_v4(0, CFG[0])
        for h in range(NH):
            ctx = nxt_ctx
            AZc = AZr.next()
            for ci_, d in enumerate(CFG):
                vt = nxt_v
                bg2.step(2)
                for blk in range(16, 32):
                    if blk == 16 + LOOK + 1:
                        if ci_ < 2:
                            nxt_v = load_v4(h, CFG[ci_ + 1])
                        elif h + 1 < NH:
                            nxt_ctx = load_head4(h + 1)
                            nxt_v = load_v4(h + 1, CFG[0])
                    st_ = s_stage4(ctx, d, blk)
                    pend.append((vt, AZc, d, blk, ci_ == 0, st_, (h if (ci_ == 2 and blk == 31) else None)))
                    if len(pend) > LOOK:
                        it = pend.pop(0)
                        o_stage4(*it[:6])
                        if it[6] is not None:
                            fin_head4(it[6], it[1])
        while pend:
            it = pend.pop(0)
            o_stage4(*it[:6])
            if it[6] is not None:
                fin_head4(it[6], it[1])
        P.barrier()
        if stop_after == 4:
            return finish(nc, P, st, out)

        A.reset()
        Qnr = Ring("Qn", [A.take(TO, BF16) for _ in range(2)])
        Qrr = Ring("Qr", [A.take(TO, BF16) for _ in range(2)])
        Knr = Ring("Kn", [A.take(TL, BF16) for _ in range(2)])
        KR = A.take(TL, BF16)
        KRb = Buf("KR")
        Vbr = Ring("Vb", [A.take(32 * 132, BF16).rearrange("p (b c) -> p b c", c=132) for _ in range(2)])
        Ptr = Ring("Pt5", [A.take(512, BF16) for _ in range(4)])
        Onr = Ring("On", [A.take(128, BF16) for _ in range(2)])
        rzr = Ring("rz", [A.take(1, F32) for _ in range(2)])
        Mxr = Ring("Mx5", [A.take(512, BF16) for _ in range(2)])
        sc_b = float(192 ** -0.5)
        Opairs = [(psum[0], psum[1]), (psum[2], psum[3])]
        Sring = Ring("S5", [None] * 3)
        Sring.items = [psum[4], psum[5], psum[6]]
        Tring = Ring("T5", [None])
        Tring.items = [psum[7]]
        P.op("pool", K("memset", KR, 0.0), writes=[KRb])
        P.load(KR[0:64, :], kr, KRb)
        for (Qr_t, Qrb) in Qrr.items:
            P.op("pool", K("memset", Qr_t, 0.0), writes=[Qrb])
        for (Vb_t, Vbb) in Vbr.items:
            P.op("pool", K("memset", Vb_t[:, 16:32, 128:129], 1.0), writes=[Vbb])
            P.op("pool", K("tensor_copy", out=Vb_t[:, 0:16, 128:129], in_=pv_b[:, 0:16].rearrange("p (b c) -> p b c", c=1)),
                 reads=[cst], writes=[Vbb])

        def load_head5(h):
            Qn_t, Qnb = Qnr.next()
            Qr_t, Qrb = Qrr.next()
            Kn_t, Knb = Knr.next()
            Vb_t, Vbb = Vbr.next()
            P.load(Qn_t, qn[h], Qnb)
            P.load(Qr_t[0:64, :], qr[h], Qrb)
            P.load(Kn_t, kn[h], Knb)
            srcv = vB[:, h * 128:(h + 1) * 128].rearrange("(b i) c -> i b c", i=128)
            for b4 in range(4):
                P.load(Vb_t[:, b4 * 8:(b4 + 1) * 8, 0:128], srcv[:, b4 * 8:(b4 + 1) * 8, :], Vbb)
            return (Qn_t, Qnb, Qr_t, Qrb, Kn_t, Knb, Vb_t, Vbb)

        def s_stage5(ctx, g, kb):
            Qn_t, Qnb, Qr_t, Qrb, Kn_t, Knb, Vb_t, Vbb = ctx
            j0 = max(0, kb - 16 - 4 * g)
            c0 = j0 * 128
            diag = kb >= 16 + 4 * g
            S, Sb = Sring.next()
            q0 = g * 512 + c0
            if diag:
                P.op("pe", K("matmul", S[:, c0:c0 + 128], lhsT=ident, rhs=mask_cur, start=True, stop=False),
                     reads=[cst], writes=[Sb])
            P.op("pe", K("matmul", S[:, c0:512], lhsT=Kn_t[:, kb * 128:(kb + 1) * 128], rhs=Qn_t[:, q0:g * 512 + 512],
                         start=(not diag), stop=False), reads=[Knb, Qnb], writes=[Sb], acc=diag)
            P.op("pe", K("matmul", S[:, c0:512], lhsT=KR[:, kb * 128:(kb + 1) * 128], rhs=Qr_t[:, q0:g * 512 + 512],
                         start=False, stop=True), reads=[KRb, Qrb], writes=[Sb], acc=True)
            Pt, Ptb = Ptr.next()
            P.op("act", K("activation", out=Pt[:, c0:512], in_=S[:, c0:512], func=AF.Exp, scale=sc_b),
                 reads=[Sb], writes=[Ptb])
            return (Pt, Ptb, j0)

        def pv_stage5(ctx, h, g, kb, opair, Mxc, st_):
            Vb_t, Vbb = ctx[6], ctx[7]
            Pt, Ptb, j0 = st_
            Mx, Mxb = Mxc
            for j in range(j0, 4):
                Ot, Otb = opair[0] if j < 2 else opair[1]
                oc = (j % 2) * 256
                last = (kb == 16 + 4 * g + j)
                P.op("pe", K("matmul", Ot[:, oc:oc + 129], lhsT=Pt[:, j * 128:(j + 1) * 128], rhs=Vb_t[:, kb, 0:129],
                             start=False, stop=last), reads=[Ptb, Vbb], writes=[Otb], acc=True)
                if last:
                    rz, rzb = rzr.next()
                    On, Onb = Onr.next()
                    P.op("dve", K("reciprocal", out=rz, in_=Ot[:, oc + 128:oc + 129]), reads=[Otb], writes=[rzb])
                    P.op("dve", K("tensor_scalar", out=On, in0=Ot[:, oc:oc + 128], scalar1=rz, scalar2=None, op0=ALU.mult),
                         reads=[Otb, rzb], writes=[Onb])
                    T, Tb = Tring.next()
                    Tb16 = T[:, :].bitcast(BF16)
                    P.op("pe", K("transpose", out=Tb16[:, 0:128], in_=On, identity=ident), reads=[Onb, cst], writes=[Tb])
                    P.op("act", K("activation", out=Mx[:, j * 128:(j + 1) * 128], in_=Tb16[:, 0:128], func=AF.Copy),
                         reads=[Tb], writes=[Mxb])
                    if j == 3:
                        P.store(mixT[8 + h, :, g * 512:(g + 1) * 512], Mx, Mxb)

        pend = []
        gi = 0
        nxt_ctx = load_head5(0)
        for h in range(NH):
            ctx = nxt_ctx
            for g in range(4):
                if g == 1 and h + 1 < NH:
                    nxt_ctx = load_head5(h + 1)
                bg2.step(1)
                opair = Opairs[gi % 2]
                gi += 1
                Mxc = Mxr.next()
                for (Ot, Otb) in opair:
                    P.op("pe", K("matmul", Ot, lhsT=zeros_b[:, 0:128], rhs=zeros_b, start=True, stop=False),
                         reads=[cst], writes=[Otb])
                for kb in range(16 + 4 * g + 4):
                    st_ = s_stage5(ctx, g, kb)
                    pend.append((ctx, h, g, kb, opair, Mxc, st_))
                    if len(pend) > LOOK:
                        pv_stage5(*pend.pop(0))
        while pend:
            pv_stage5(*pend.pop(0))
        bg2.flush()
        P.barrier()
        if stop_after == 5:
            return finish(nc, P, st, out)

        A.base = base_const
        A.reset()
        Wo = kcv(A.take(16 * D, BF16), 16)
        Wob = Buf("Wo")
        gp = A.take(D, F32)
        gpb = Buf("gp")
        mr = Ring("m6", [kcv(A.take(16 * 512, BF16), 16) for _ in range(2)])
        xr = Ring("x6", [A.take(D, F32) for _ in range(2)])
        yr = Ring("y6", [A.take(D, F32) for _ in range(2)])
        x1r = Ring("x16", [A.take(D, F32) for _ in range(2)])
        scr_r = Ring("xn6", [A.take(D, BF16) for _ in range(2)])
        sm_r = Ring("sm6", [A.take(8, F32) for _ in range(2)])
        hr = Ring("ht6", [kcv(A.take(16 * 512, BF16), 16) for _ in range(2)])
        P.load(Wo, wb_out.rearrange("(k p) c -> p k c", p=128), Wob)
        P.load(gp, g_post, gpb)
        mixT_v = mixT.rearrange("k p t -> p k t")
        h2T_v = h2T.rearrange("k p t -> p k t")
        for tt in range(4):
            mt, mb = mr.next()
            P.load(mt, mixT_v[:, :, tt * 512:(tt + 1) * 512], mb)
            htile, hb = hr.next()
            for sub in range(4):
                r0 = tt * 512 + sub * 128
                xt, xb = xr.next()
                P.load(xt, xo[r0:r0 + 128, :], xb)
                ys = []
                for dt_ in range(4):
                    pt, pb_ = psring.next()
                    for kc in range(16):
                        P.op("pe", K("matmul", pt, lhsT=mt[:, kc, sub * 128:(sub + 1) * 128], rhs=Wo[:, kc, dt_ * 512:(dt_ + 1) * 512],
                                     start=(kc == 0), stop=(kc == 15)), reads=[mb, Wob], writes=[pb_], acc=(kc > 0))
                    ys.append((pt, pb_))
                sm, smb = sm_r.next()
                sc, scb = scr_r.next()
                for dt_, (pt, pb_) in enumerate(ys):
                    P.op("act", K("activation", out=sc[:, dt_ * 512:(dt_ + 1) * 512], in_=pt, func=AF.Square,
                                  accum_out=sm[:, dt_:dt_ + 1]), reads=[pb_], writes=[scb, smb])
                P.op("dve", K("tensor_tensor", out=sm[:, 4:6], in0=sm[:, 0:2], in1=sm[:, 2:4], op=ALU.add), reads=[smb], writes=[smb])
                P.op("dve", K("tensor_tensor", out=sm[:, 6:7], in0=sm[:, 4:5], in1=sm[:, 5:6], op=ALU.add), reads=[smb], writes=[smb])
                P.op("act", K("activation", out=sm[:, 7:8], in_=sm[:, 6:7], func=AF.Sqrt, scale=1.0 / D, bias=EPS),
                     reads=[smb], writes=[smb])
                P.op("dve", K("reciprocal", out=sm[:, 7:8], in_=sm[:, 7:8]), reads=[smb], writes=[smb])
                yt, yb = yr.next()
                for dt_, (pt, pb_) in enumerate(ys):
                    P.op("act", K("activation", out=yt[:, dt_ * 512:(dt_ + 1) * 512], in_=pt, func=AF.Copy, scale=sm[:, 7:8]),
                         reads=[pb_, smb], writes=[yb])
                P.op("pool", K("tensor_tensor", out=yt, in0=yt, in1=gp, op=ALU.mult), reads=[yb, gpb], writes=[yb])
                x1t, x1b = x1r.next()
                P.op("pool", K("tensor_tensor", out=x1t, in0=yt, in1=xt, op=ALU.add), reads=[yb, xb], writes=[x1b])
                P.store(x1[r0:r0 + 128, :], x1t, x1b)
                sm2, sm2b = sm_r.next()
                norm_transpose(x1t, x1b, htile, hb, sub, sc, scb, sm2, sm2b)
            P.store(h2T_v[:, :, tt * 512:(tt + 1) * 512], htile, hb)
        P.barrier()
        if stop_after == 6:
            return finish(nc, P, st, out)

        A.reset()
        gp2 = A.take(D, F32)
        gp2b = Buf("gp2")
        P.load(gp2, g_post2, gp2b)
        h2 = kcv(A.take(16 * 512, BF16), 16)
        h2b = Buf("h2t")
        U = kcv(A.take(64 * 512, BF16), 64)
        Ub = Buf("U")
        Wur = Ring("Wu", [kcv(A.take(16 * 512, BF16), 16) for _ in range(2)])
        Wdr = Ring("Wd", [kcv(A.take(4 * 1024, BF16), 4) for _ in range(2)])
        rl_r = Ring("rl", [A.take(512, F32) for _ in range(2)])
        Y = A.take(4 * D, F32).rearrange("p (s c) -> p s c", s=4)
        Yb = Buf("Y")
        sm7 = A.take(32, F32)
        sm7b = Buf("sm7")
        junk = A.take(1024, BF16)
        junkb = Buf("junk")
        x1r = Ring("x17", [A.take(D, F32) for _ in range(1)])
        wb_up_v = wb_up.rearrange("(k p) c -> p k c", p=128)
        wb_down_v = wb_down.rearrange("(k p) c -> p k c", p=128)
        out_ops = []
        import os as _os
        _ntt = int(_os.environ.get("F7_NTT", "4"))
        _mode = _os.environ.get("F7_MODE", "full")
        for tt in range(_ntt):
            P.load(h2, h2T_v[:, :, tt * 512:(tt + 1) * 512], h2b)
            for fg in range(16):
                Wu, Wub = Wur.next()
                P.load(Wu, wb_up_v[:, :, fg * 512:(fg + 1) * 512], Wub)
                for f in range(4):
                    ft = fg * 4 + f
                    pt, pb_ = psring.next()
                    for kc in range(16):
                        P.op("pe", K("matmul", pt, lhsT=Wu[:, kc, f * 128:(f + 1) * 128], rhs=h2[:, kc, :],
                                     start=(kc == 0), stop=(kc == 15)), reads=[Wub, h2b], writes=[pb_], acc=(kc > 0))
                    rl, rlb = rl_r.next()
                    P.op("act", K("activation", out=rl, in_=pt, func=AF.Relu), reads=[pb_], writes=[rlb])
                    P.op("pool", K("tensor_tensor", out=U[:, ft, :], in0=rl, in1=rl, op=ALU.mult), reads=[rlb], writes=[Ub])
            for dh in range(2 if _mode != "up" else 0):
                accs = [psring.next() for _ in range(8)]
                for fg in range(16):
                    Wd, Wdb = Wdr.next()
                    P.load(Wd, wb_down_v[:, fg * 4:(fg + 1) * 4, dh * 1024:(dh + 1) * 1024], Wdb)
                    for f in range(4):
                        ft = fg * 4 + f
                        for sub in range(4):
                            for dt_ in range(2):
                                pt, pb_ = accs[sub * 2 + dt_]
                                P.op("pe", K("matmul", pt, lhsT=U[:, ft, sub * 128:(sub + 1) * 128],
                                             rhs=Wd[:, f, dt_ * 512:(dt_ + 1) * 512], start=(ft == 0), stop=(ft == 63)),
                                     reads=[Ub, Wdb], writes=[pb_], acc=(ft > 0))
                for sub in range(4):
                    for dt_ in range(2):
                        pt, pb_ = accs[sub * 2 + dt_]
                        col = dh * 2 + dt_
                        _ev = _os.environ.get("F7_EV", "both")
                        if _ev in ("act", "both"):
                            P.op("act", K("activation", out=junk[:, 0:512], in_=pt, func=AF.Square,
                                          accum_out=sm7[:, sub * 4 + col:sub * 4 + col + 1]), reads=[pb_], writes=[junkb, sm7b])
                        if _ev in ("dve", "both"):
                            P.op("dve", K("tensor_copy", out=Y[:, sub, col * 512:(col + 1) * 512], in_=pt), reads=[pb_], writes=[Yb])
            for sub in range(4 if _mode == "full" else 0):
                r0 = tt * 512 + sub * 128
                x1t, x1b = x1r.next()
                P.load(x1t, x1[r0:r0 + 128, :], x1b)
                s4 = sm7[:, sub * 4:sub * 4 + 4]
                t2 = sm7[:, 16 + sub * 4:16 + sub * 4 + 2]
                ssum = sm7[:, 16 + sub * 4 + 2:16 + sub * 4 + 3]
                rs7 = sm7[:, 16 + sub * 4 + 3:16 + sub * 4 + 4]
                P.op("dve", K("tensor_tensor", out=t2, in0=s4[:, 0:2], in1=s4[:, 2:4], op=ALU.add), reads=[sm7b], writes=[sm7b])
                P.op("dve", K("tensor_tensor", out=ssum, in0=t2[:, 0:1], in1=t2[:, 1:2], op=ALU.add), reads=[sm7b], writes=[sm7b])
                P.op("act", K("activation", out=rs7, in_=ssum, func=AF.Sqrt, scale=1.0 / D, bias=EPS), reads=[sm7b], writes=[sm7b])
                P.op("dve", K("reciprocal", out=rs7, in_=rs7), reads=[sm7b], writes=[sm7b])
                P.op("act", K("activation", out=Y[:, sub, :], in_=Y[:, sub, :], func=AF.Copy, scale=rs7), reads=[Yb, sm7b], writes=[Yb])
                P.op("pool", K("tensor_tensor", out=Y[:, sub, :], in0=Y[:, sub, :], in1=gp2, op=ALU.mult), reads=[Yb, gp2b], writes=[Yb])
                P.op("pool", K("tensor_tensor", out=x1t, in0=Y[:, sub, :], in1=x1t, op=ALU.add), reads=[Yb, x1b], writes=[x1b])
                out_ops.append(P.store(out[r0:r0 + 128, :], x1t, x1b))
        P.barrier()
        return finish(nc, P, st, out)


def finish(nc, P, st, out):
    P.barrier()
    with nc.Block() as block:
        P.emit(block)
    return nc


def _consts():
    j = np.arange(128)[:, None]
    i = np.arange(128)[None, :]
    mask_prev = np.where(j >= i, 0.0, NEG).astype(np.float32)
    mask_cur = np.where(j <= i, 0.0, NEG).astype(np.float32)
    cmask = np.concatenate([mask_prev, mask_cur], axis=1)
    RA = np.zeros((128, 128), np.float32)
    for m in range(16):
        RA[m + 16, m] = -1.0
        RA[m, m + 16] = 1.0
    RB = np.zeros((128, 128), np.float32)
    for m in range(32):
        RB[m + 32, m] = -1.0
        RB[m, m + 32] = 1.0
    p = np.arange(128)
    invA = (THETA ** (-(2.0 * (p % 16)) / 32.0)).astype(np.float32)
    invB = (THETA ** (-(2.0 * (p % 32)) / 64.0)).astype(np.float32)
    invf = np.stack([invA, invB], axis=1).astype(np.float32)
    return cmask, RA, RB, invf


def make_in_maps(x, positions, norm_attn_pre, norm_attn_post, w_in, q_latent_norm, kv_latent_norm,
                 w_uq, w_ukv, w_out, norm_mlp_pre, norm_mlp_post, w_up, w_down):
    x = np.asarray(x, np.float32)
    positions = np.asarray(positions, np.int32)
    cmask, RA, RB, invf = _consts()

    def col(g, k):
        return np.ascontiguousarray(np.asarray(g, np.float32).reshape(k, 128).T)

    def bc(g):
        return np.ascontiguousarray(np.broadcast_to(np.asarray(g, np.float32).reshape(1, -1), (128, D)))

    shared = {
        "cmask": cmask, "cRA": RA, "cRB": RB, "cinvf": invf,
        "g_in": col(norm_attn_pre[0], 16), "g_up": col(norm_mlp_pre[0], 16),
        "g_q": col(q_latent_norm[0], 4), "g_kv": col(kv_latent_norm[0], 4),
        "g_post": bc(norm_attn_post[0]), "g_post2": bc(norm_mlp_post[0]),
        "w_in": np.ascontiguousarray(np.asarray(w_in[0], np.float32)),
        "w_uq": np.ascontiguousarray(np.asarray(w_uq[0], np.float32)),
        "w_ukv": np.ascontiguousarray(np.asarray(w_ukv[0], np.float32)),
        "w_out": np.ascontiguousarray(np.asarray(w_out[0], np.float32)),
        "w_up": np.ascontiguousarray(np.asarray(w_up[0], np.float32)),
        "w_down": np.ascontiguousarray(np.asarray(w_down[0], np.float32)),
    }
    maps = []
    for c in range(8):
        b, half = c // 2, c % 2
        m = dict(shared)
        m["xo"] = np.ascontiguousarray(x[b, half * TO:(half + 1) * TO])
        m["xp"] = np.ascontiguousarray(x[b, 0:TO]) if half else np.zeros((TO, D), np.float32)
        pl = np.concatenate([positions[b, 0:TO], positions[b, half * TO:(half + 1) * TO]])
        m["posb"] = np.ascontiguousarray(np.broadcast_to(pl.reshape(1, TL), (128, TL))).astype(np.int32)
        m["pvt"] = np.full((128, 128), float(half), np.float32)
        maps.append(m)
    return maps


_NC_CACHE = {}


def kernel(**inputs):
    maps = make_in_maps(**inputs)
    if "nc" not in _NC_CACHE:
        _NC_CACHE["nc"] = build_program()
    nc = _NC_CACHE["nc"]
    res = run_bass_kernel_spmd(nc, maps, core_ids=list(range(8)))
    outp = np.empty((NB, SEQ, D), np.float32)
    for c in range(8):
        b, half = c // 2, c % 2
        outp[b, half * TO:(half + 1) * TO] = np.asarray(res.results[c]["out"], np.float32)
    return outp
```

```python
import numpy as np
from contextlib import ExitStack
import concourse.bass as bass
import concourse.mybir as mybir
from concourse.bass_utils import run_bass_kernel_spmd

F32 = mybir.dt.float32
BF16 = mybir.dt.bfloat16
I32 = mybir.dt.int32
AF = mybir.ActivationFunctionType
ALU = mybir.AluOpType
AX = mybir.AxisListType

ENGS = ("pe", "act", "dve", "pool", "sp")

D = 2048
SEQ = 4096
NB = 4
TL = 4096
TO = 2048
HD = 128
NH = 8
DFF = 8192
INC = 4160
EPS = 1e-6
NEG = -30000.0
THETA = 500000.0
TWO_PI = float(2.0 * np.pi)
PI = float(np.pi)


class Buf:
    __slots__ = ("name", "last_w", "readers", "sem", "sem_val", "last_dma", "excl")

    def __init__(self, name, excl=False):
        self.name = name
        self.excl = excl
        self.last_w = None
        self.readers = []
        self.sem = None
        self.sem_val = 0
        self.last_dma = None


class Op:
    __slots__ = ("eng", "idx", "fn", "deps", "signal", "is_dma", "sem", "val", "waits")

    def __init__(self, eng, idx, fn):
        self.eng = eng
        self.idx = idx
        self.fn = fn
        self.deps = []
        self.signal = False
        self.is_dma = False
        self.sem = None
        self.val = None
        self.waits = None


def K(name, *args, **kw):
    return lambda e: getattr(e, name)(*args, **kw)


class Prog:
    def __init__(self, nc, stack):
        self.nc = nc
        self.stack = stack
        self.ops = {e: [] for e in ENGS}
        self.eng_sem = {e: stack.enter_context(nc.semaphore("S_" + e)) for e in ENGS}
        self.dma_bufs = []
        self.same_engine_sync = True

    def _add(self, eng, fn, reads, writes, acc=False):
        op = Op(eng, len(self.ops[eng]), fn)
        self.ops[eng].append(op)
        deps = []
        for b in reads:
            if b.last_w is not None:
                deps.append(b.last_w)
            if b.excl:
                deps.extend(r for r in b.readers if r.eng != eng)
        for b in writes:
            if b.last_w is not None and not (acc and b.last_w.eng == eng):
                deps.append(b.last_w)
            deps.extend(b.readers)
        op.deps = deps
        for b in reads:
            b.readers.append(op)
        for b in writes:
            b.last_w = op
            b.readers = []
        return op

    def op(self, eng, fn, reads=(), writes=(), acc=False):
        return self._add(eng, fn, list(reads), list(writes), acc)

    def dma(self, eng, out_ap, in_ap, sbuf_buf, reads=(), writes=()):
        b = sbuf_buf
        if b.sem is None:
            b.sem = self.stack.enter_context(self.nc.semaphore("D_" + b.name))
            self.dma_bufs.append(b)
        op = self._add(eng, None, list(reads), list(writes))
        if b.last_dma is not None:
            op.deps.append(b.last_dma)
        b.sem_val += 16
        op.is_dma = True
        op.sem = b.sem
        op.val = b.sem_val
        op.fn = (out_ap, in_ap)
        b.last_dma = op
        return op

    def load(self, dst, src, buf, eng="sp"):
        return self.dma(eng, dst, src, buf, writes=[buf])

    def store(self, dst, src, buf, eng="pool"):
        return self.dma(eng, dst, src, buf, reads=[buf])

    def barrier(self):
        deps = []
        for e in ENGS:
            if self.ops[e]:
                deps.append(self.ops[e][-1])
        for b in self.dma_bufs:
            if b.last_dma is not None:
                deps.append(b.last_dma)
        for e in ENGS:
            op = self._add(e, None, [], [])
            op.deps = list(deps)

    def finalize(self):
        for e in ENGS:
            known_idx = {x: -1 for x in ENGS}
            known_dma = {}
            for op in self.ops[e]:
                waits = []
                best = {}
                for d in op.deps:
                    if d.is_dma:
                        k = id(d.sem)
                        if known_dma.get(k, 0) >= d.val:
                            continue
                        known_dma[k] = d.val
                        waits.append(d)
                    else:
                        if d.fn is None:
                            continue
                        if d.eng == e and (e == "pe" or not self.same_engine_sync):
                            continue
                        if d.idx <= known_idx[d.eng]:
                            continue
                        if d.eng not in best or best[d.eng].idx < d.idx:
                            best[d.eng] = d
                for x, d in best.items():
                    known_idx[x] = d.idx
                    d.signal = True
                    waits.append(d)
                op.waits = waits
        for e in ENGS:
            c = 0
            for op in self.ops[e]:
                if not op.is_dma and op.signal:
                    c += 1
                    op.sem = self.eng_sem[e]
                    op.val = c

    def emit(self, block):
        self.finalize()
        P = self

        def run(e, eng):
            for op in P.ops[e]:
                seen = {}
                for d in op.waits:
                    k = id(d.sem)
                    if k not in seen or seen[k][1] < d.val:
                        seen[k] = (d.sem, d.val)
                for sem, val in seen.values():
                    eng.wait_ge(sem, val)
                if op.is_dma:
                    o, i = op.fn
                    eng.dma_start(out=o, in_=i).then_inc(op.sem, 16)
                elif op.fn is not None:
                    ins = op.fn(eng)
                    if op.signal:
                        ins.then_inc(op.sem, 1)

        @block.tensor
        def _(eng):
            run("pe", eng)

        @block.scalar
        def _(eng):
            run("act", eng)

        @block.vector
        def _(eng):
            run("dve", eng)

        @block.gpsimd
        def _(eng):
            run("pool", eng)

        @block.sync
        def _(eng):
            run("sp", eng)


class Ring:
    def __init__(self, name, aps):
        self.items = [(ap, Buf("%s%d" % (name, i))) for i, ap in enumerate(aps)]
        self.i = 0

    def next(self):
        it = self.items[self.i % len(self.items)]
        self.i += 1
        return it


class Arena:
    def __init__(self, t, nbytes):
        self.t = t
        self.n = nbytes
        self.base = 0
        self.off = 0

    def persist(self):
        self.base = self.off

    def reset(self):
        self.off = self.base

    def take(self, nelem, dt, parts=128):
        sz = 2 if dt == BF16 else 4
        nb = (nelem * sz + 63) // 64 * 64
        o = self.off
        self.off += nb
        assert self.off <= self.n, "SBUF arena overflow %d > %d" % (self.off, self.n)
        ap = self.t[0:parts, o // 4:(o + nelem * sz + 3) // 4]
        if dt != F32:
            ap = ap.bitcast(dt)
        return ap


DEBUG_DUMP = False

class BG:
    def __init__(self, P, wst, wob, wlist):
        self.P = P
        self.wst = wst
        self.wob = wob
        self.tiles = []
        for (src, dst, R, C, gcol) in wlist:
            for kc in range(R // 128):
                for c0 in range(0, C, 2048):
                    cw = min(2048, C - c0)
                    self.tiles.append((src[kc * 128:(kc + 1) * 128, c0:c0 + cw], dst[kc * 128:(kc + 1) * 128, c0:c0 + cw], cw,
                                       None if gcol is None else gcol[:, kc:kc + 1]))
        self.il = 0
        self.ic = 0
        self.slots = {}

    def step(self, n=1):
        P = self.P
        for _ in range(n):
            while self.il < min(len(self.tiles), self.ic + 3):
                src, dst, cw, g = self.tiles[self.il]
                sap, sb = self.wst.next()
                self.slots[self.il] = (sap, sb)
                P.load(sap[:, 0:cw], src, sb)
                self.il += 1
            if self.ic < len(self.tiles):
                src, dst, cw, g = self.tiles[self.ic]
                sap, sb = self.slots.pop(self.ic)
                oap, ob = self.wob.next()
                if g is None:
                    P.op("act", K("activation", out=oap[:, 0:cw], in_=sap[:, 0:cw], func=AF.Copy), reads=[sb], writes=[ob])
                else:
                    P.op("act", K("activation", out=oap[:, 0:cw], in_=sap[:, 0:cw], func=AF.Copy, scale=g), reads=[sb], writes=[ob])
                P.store(dst, oap[:, 0:cw], ob)
                self.ic += 1

    def flush(self):
        while self.ic < len(self.tiles):
            self.step()


C1 = 6.28125
C2 = float(2.0 * np.pi - 6.28125)


def rope_table(P, posf, invcol, ang, ni, nf, mm_, sin_out, cos_out, tb):
    w = dict(reads=[tb], writes=[tb])
    P.op("dve", K("tensor_scalar", out=ang, in0=posf, scalar1=invcol, scalar2=None, op0=ALU.mult), **w)
    P.op("dve", K("tensor_scalar", out=ni, in0=ang, scalar1=1.0 / TWO_PI, scalar2=None, op0=ALU.mult), **w)
    P.op("dve", K("tensor_copy", out=nf, in_=ni), **w)
    P.op("dve", K("scalar_tensor_tensor", out=ang, in0=nf, scalar=-C1, in1=ang, op0=ALU.mult, op1=ALU.add), **w)
    P.op("dve", K("scalar_tensor_tensor", out=ang, in0=nf, scalar=-C2, in1=ang, op0=ALU.mult, op1=ALU.add), **w)

    def wrap(x):
        P.op("dve", K("tensor_scalar", out=mm_, in0=x, scalar1=PI, scalar2=-TWO_PI, op0=ALU.is_gt, op1=ALU.mult), **w)
        P.op("dve", K("tensor_tensor", out=x, in0=x, in1=mm_, op=ALU.add), **w)
        P.op("dve", K("tensor_scalar", out=mm_, in0=x, scalar1=-PI, scalar2=TWO_PI, op0=ALU.is_lt, op1=ALU.mult), **w)
        P.op("dve", K("tensor_tensor", out=x, in0=x, in1=mm_, op=ALU.add), **w)
    wrap(ang)
    P.op("act", K("activation", out=sin_out, in_=ang, func=AF.Sin), **w)
    P.op("dve", K("tensor_scalar", out=ang, in0=ang, scalar1=PI / 2, scalar2=None, op0=ALU.add), **w)
    wrap(ang)
    P.op("act", K("activation", out=cos_out, in_=ang, func=AF.Sin), **w)


def build_program(debug=False, stop_after=None):
    nc = bass.Bass("TRN2", target_bir_lowering=False)

    def din(name, shape, dt=F32):
        return nc.dram_tensor(name, list(shape), dt, kind="ExternalInput").ap()

    def dscr(name, shape, dt=BF16):
        if debug:
            return nc.dram_tensor(name, list(shape), dt, kind="ExternalOutput").ap()
        return nc.dram_tensor(name, list(shape), dt).ap()

    xo = din("xo", [TO, D])
    xp = din("xp", [TO, D])
    posb = din("posb", [128, TL], I32)
    pvt = din("pvt", [128, 128])
    cmask = din("cmask", [128, 256])
    cRA = din("cRA", [128, 128])
    cRB = din("cRB", [128, 128])
    cinvf = din("cinvf", [128, 2])
    g_in = din("g_in", [128, 16])
    g_up = din("g_up", [128, 16])
    g_q = din("g_q", [128, 4])
    g_kv = din("g_kv", [128, 4])
    g_post = din("g_post", [128, D])
    g_post2 = din("g_post2", [128, D])
    w_in = din("w_in", [D, INC])
    w_uq = din("w_uq", [512, 1536])
    w_ukv = din("w_ukv", [512, 2048])
    w_out = din("w_out", [D, D])
    w_up = din("w_up", [D, DFF])
    w_down = din("w_down", [DFF, D])
    out = nc.dram_tensor("out", [TO, D], F32, kind="ExternalOutput").ap()

    wb_in = dscr("wb_in", [D, INC])
    wb_uq = dscr("wb_uq", [512, 1536])
    wb_ukv = dscr("wb_ukv", [512, 2048])
    wb_out = dscr("wb_out", [D, D])
    wb_up = dscr("wb_up", [D, DFF])
    wb_down = dscr("wb_down", [DFF, D])
    hT = dscr("hT", [16, 128, TL])
    qA = dscr("qA", [NH, 128, TO])
    kA = dscr("kA", [NH, 128, TL])
    vA = dscr("vA", [TL, NH * HD])
    cq = dscr("cq", [4, 128, TO])
    ckv = dscr("ckv", [4, 128, TL])
    kr = dscr("kr", [64, TL])
    qn = dscr("qn", [NH, 128, TO])
    qr = dscr("qr", [NH, 64, TO])
    kn = dscr("kn", [NH, 128, TL])
    vB = dscr("vB", [TL, NH * HD])
    mixT = dscr("mixT", [16, 128, TO])
    x1 = dscr("x1", [TO, D], F32)
    h2T = dscr("h2T", [16, 128, TO])

    st = ExitStack()
    with st:
        ARENA_BYTES = 212736
        arena_t = st.enter_context(nc.sbuf_tensor("arena", [128, ARENA_BYTES // 4], F32))
        A = Arena(arena_t, ARENA_BYTES)
        psum = []
        for i in range(8):
            t = st.enter_context(nc.psum_tensor("ps%d" % i, [128, 512], F32))
            psum.append((t[:, :], Buf("ps%d" % i, excl=True)))
        P = Prog(nc, st)
        psring = Ring("psr", [None] * 8)
        psring.items = [(t, b) for t, b in psum]

        cst = Buf("const")
        ident = A.take(128, BF16)
        ones_b = A.take(128, BF16)
        zeros_b = A.take(512, BF16)
        pv_b = A.take(128, BF16)
        mask_b = A.take(256, BF16)
        RA_b = A.take(128, BF16)
        RB_b = A.take(128, BF16)
        ones_f = A.take(128, F32)
        invf = A.take(2, F32)
        gc_in = A.take(16, F32)
        gc_up = A.take(16, F32)
        gc_q = A.take(4, F32)
        gc_kv = A.take(4, F32)
        pv_col = A.take(1, F32)
        A.persist()
        stg = A.take(128 * 6, F32)
        stgb = Buf("stg")
        P.load(stg[:, 0:128], pvt, stgb)
        P.load(stg[:, 128:384], cmask, stgb)
        P.load(stg[:, 384:512], cRA, stgb)
        P.load(stg[:, 512:640], cRB, stgb)
        gb = Buf("gl")
        P.load(invf, cinvf, gb)
        P.load(gc_in, g_in, gb)
        P.load(gc_up, g_up, gb)
        P.load(gc_q, g_q, gb)
        P.load(gc_kv, g_kv, gb)
        P.op("pool", K("memset", stg[:, 640:768], 0.0), writes=[cst])
        P.op("pool", K("affine_select", out=stg[:, 640:768], in_=stg[:, 640:768], pattern=[[-1, 128]],
                       compare_op=ALU.not_equal, fill=1.0, base=0, channel_multiplier=1),
             reads=[cst], writes=[cst])
        P.op("pool", K("tensor_copy", out=ident, in_=stg[:, 640:768]), reads=[cst], writes=[cst])
        P.op("pool", K("memset", ones_f, 1.0), writes=[cst])
        P.op("pool", K("memset", ones_b, 1.0), writes=[cst])
        P.op("pool", K("memset", zeros_b, 0.0), writes=[cst])
        P.op("dve", K("tensor_copy", out=pv_b, in_=stg[:, 0:128]), reads=[stgb], writes=[cst])
        P.op("dve", K("tensor_copy", out=pv_col, in_=stg[:, 0:1]), reads=[stgb], writes=[cst])
        P.op("dve", K("tensor_copy", out=mask_b, in_=stg[:, 128:384]), reads=[stgb], writes=[cst])
        P.op("dve", K("tensor_copy", out=RA_b, in_=stg[:, 384:512]), reads=[stgb], writes=[cst])
        P.op("dve", K("tensor_copy", out=RB_b, in_=stg[:, 512:640]), reads=[stgb], writes=[cst])
        P.barrier()
        mask_prev = mask_b[:, 0:128]
        mask_cur = mask_b[:, 128:256]

        def kcv(ap, k):
            return ap.rearrange("p (k n) -> p k n", k=k)

        A.reset()
        base_const = A.base
        wst = Ring("wst", [A.take(2048, F32) for _ in range(3)])
        wob = Ring("wob", [A.take(2048, BF16) for _ in range(3)])
        A.persist()
        base_bg = A.base
        bg1 = BG(P, wst, wob, [(w_in, wb_in, D, INC, gc_in)])
        bg2 = BG(P, wst, wob, [(w_uq, wb_uq, 512, 1536, gc_q), (w_ukv, wb_ukv, 512, 2048, gc_kv),
                               (w_out, wb_out, D, D, None), (w_up, wb_up, D, DFF, gc_up), (w_down, wb_down, DFF, D, None)])

        cosA = A.take(TL, F32)
        sinA = A.take(TL, F32)
        cosB = A.take(TL, F32)
        sinB = A.take(TL, F32)
        A.persist()
        posi = A.take(TL, I32)
        posf = A.take(TL, F32)
        ang = A.take(TL, F32)
        ni_t = posi
        nf_t = A.take(TL, F32)
        mm_t = A.take(TL, F32)
        tb = Buf("ropet")
        P.load(posi, posb, tb)
        P.op("dve", K("tensor_copy", out=posf, in_=posi), reads=[tb], writes=[tb])
        bg1.step(4)
        rope_table(P, posf, invf[:, 0:1], ang, ni_t, nf_t, mm_t, sinA, cosA, tb)
        bg1.step(4)
        rope_table(P, posf, invf[:, 1:2], ang, ni_t, nf_t, mm_t, sinB, cosB, tb)
        P.barrier()

        def norm_transpose(xt_ap, xb, dst_tile, dst_buf, sub, scr_bf, scr_buf, small, small_buf):
            ss = small[:, 0:1]
            rs = small[:, 1:2]
            P.op("act", K("activation", out=scr_bf, in_=xt_ap, func=AF.Square, accum_out=ss),
                 reads=[xb], writes=[scr_buf, small_buf])
            P.op("act", K("activation", out=rs, in_=ss, func=AF.Sqrt, scale=1.0 / D, bias=EPS),
                 reads=[small_buf], writes=[small_buf])
            P.op("dve", K("reciprocal", out=rs, in_=rs), reads=[small_buf], writes=[small_buf])
            P.op("dve", K("tensor_scalar", out=scr_bf, in0=xt_ap, scalar1=rs, scalar2=None, op0=ALU.mult),
                 reads=[xb, small_buf], writes=[scr_buf])
            for half in range(2):
                pt, pb_ = psring.next()
                ptb = pt[:, :].bitcast(BF16)
                for j in range(8):
                    kc = half * 8 + j
                    P.op("pe", K("transpose", out=ptb[:, j * 128:(j + 1) * 128], in_=scr_bf[:, kc * 128:(kc + 1) * 128],
                                 identity=ident), reads=[scr_buf, cst], writes=[pb_], acc=True)
                eng = "act" if half == 0 else "dve"
                dst = dst_tile[:, half * 8:(half + 1) * 8, sub * 128:(sub + 1) * 128]
                src = ptb.rearrange("p (k n) -> p k n", k=8)
                if eng == "act":
                    P.op("act", K("activation", out=dst, in_=src, func=AF.Copy), reads=[pb_], writes=[dst_buf])
                else:
                    P.op("dve", K("tensor_copy", out=dst, in_=src), reads=[pb_], writes=[dst_buf])

        A.reset()
        xr = Ring("x", [A.take(D, F32) for _ in range(2)])
        scr_r = Ring("xn", [A.take(D, BF16) for _ in range(2)])
        sm_r = Ring("sm", [A.take(2, F32) for _ in range(2)])
        hr = Ring("ht", [kcv(A.take(16 * 512, BF16), 16) for _ in range(2)])
        hT_v = hT.rearrange("k p t -> p k t")
        for tt in range(8):
            htile, hb = hr.next()
            for sub in range(4):
                xt, xb = xr.next()
                r0 = (tt % 4) * 512 + sub * 128
                srcx = xp if tt < 4 else xo
                P.load(xt, srcx[r0:r0 + 128, :], xb)
                sc, scb = scr_r.next()
                sm, smb = sm_r.next()
                norm_transpose(xt, xb, htile, hb, sub, sc, scb, sm, smb)
                bg1.step(2 if sub % 2 == 0 else 1)
            P.store(hT_v[:, :, tt * 512:(tt + 1) * 512], htile, hb)
        bg1.flush()
        P.barrier()
        if stop_after == 1:
            return finish(nc, P, st, out)

        A.reset()
        Wg = kcv(A.take(16 * 1024, BF16), 16)
        Wgb = Buf("Wg")
        hr = Ring("h2", [kcv(A.take(16 * 512, BF16), 16) for _ in range(2)])
        qs_r = Ring("qs", [A.take(512, BF16) for _ in range(4)])
        t1_r = Ring("t1", [A.take(512, F32) for _ in range(2)])
        t2_r = Ring("t2", [A.take(512, F32) for _ in range(2)])
        sq_r = Ring("sq", [A.take(512, F32) for _ in range(4)])
        rr_r = Ring("rr", [A.take(512, F32) for _ in range(2)])
        cn_r = Ring("cn", [kcv(A.take(4 * 512, BF16), 4) for _ in range(2)])
        wb_in_v = wb_in.rearrange("(k p) c -> p k c", p=128)

        defq = []

        def run_deferred(keep=0):
            while len(defq) > keep:
                defq.pop(0)()

        def rope_epilogue(pt, pb_, nrow, Rm, cosT, sinT, tok0, dst_dram):
            qs, qb = qs_r.next()
            npart = pt.shape[0]
            P.op("act", K("activation", out=qs[0:npart, :], in_=pt, func=AF.Copy), reads=[pb_], writes=[qb])

            def part_b():
                p2, p2b = psring.next()
                P.op("pe", K("matmul", p2[0:nrow, :], lhsT=Rm[0:npart, 0:nrow], rhs=qs[0:npart, :], start=True, stop=True),
                     reads=[qb, cst], writes=[p2b])
                t1, t1b = t1_r.next()
                t2, t2b = t2_r.next()
                P.op("dve", K("tensor_tensor", out=t1[0:nrow, :], in0=p2[0:nrow, :], in1=sinT[0:nrow, tok0:tok0 + 512],
                              op=ALU.mult), reads=[p2b], writes=[t1b])
                P.op("pool", K("tensor_tensor", out=t2[0:nrow, :], in0=qs[0:nrow, :], in1=cosT[0:nrow, tok0:tok0 + 512],
                               op=ALU.mult), reads=[qb], writes=[t2b])
                P.op("dve", K("tensor_tensor", out=qs[0:nrow, :], in0=t1[0:nrow, :], in1=t2[0:nrow, :], op=ALU.add),
                     reads=[t1b, t2b], writes=[qb])
                P.store(dst_dram, qs[0:npart, :], qb)
            defq.append(part_b)
            run_deferred(keep=1)

        def latent_epilogue(cps, tok0, dst3):
            sqs = []
            for (pt, pb_) in cps:
                sq, sqb = sq_r.next()
                P.op("act", K("activation", out=sq, in_=pt, func=AF.Square), reads=[pb_], writes=[sqb])
                sqs.append((sq, sqb))
            p5, p5b = psring.next()
            for i, (sq, sqb) in enumerate(sqs):
                P.op("pe", K("matmul", p5, lhsT=ones_f, rhs=sq, start=(i == 0), stop=(i == 3)),
                     reads=[sqb, cst], writes=[p5b], acc=(i > 0))
            rr, rrb = rr_r.next()
            P.op("act", K("activation", out=rr, in_=p5, func=AF.Sqrt, scale=1.0 / 512, bias=EPS), reads=[p5b], writes=[rrb])
            P.op("dve", K("reciprocal", out=rr, in_=rr), reads=[rrb], writes=[rrb])
            cn, cnb = cn_r.next()
            for i, (pt, pb_) in enumerate(cps):
                P.op("dve", K("tensor_tensor", out=cn[:, i, :], in0=pt, in1=rr, op=ALU.mult), reads=[pb_, rrb], writes=[cnb])
            P.store(dst3, cn, cnb)

        groups = [
            ("aq", 0, 1024, "own"), ("ak", 1024, 1024, "all"), ("av", 2048, 1024, "all"),
            ("cq", 3072, 512, "own"), ("ckv", 3584, 576, "all"),
        ]
        for (gname, c0, ncol, which) in groups:
            P.load(Wg[:, :, 0:ncol], wb_in_v[:, :, c0:c0 + ncol], Wgb)
            tts = range(4, 8) if which == "own" else range(8)
            for tt in tts:
                htile, hb = hr.next()
                P.load(htile, hT_v[:, :, tt * 512:(tt + 1) * 512], hb)
                bg2.step(2)
                tok0 = tt * 512
                otok0 = tok0 - TO
                if gname in ("aq", "ak"):
                    for ct in range(8):
                        pt, pb_ = psring.next()
                        for kc in range(16):
                            P.op("pe", K("matmul", pt, lhsT=Wg[:, kc, ct * 128:(ct + 1) * 128], rhs=htile[:, kc, :],
                                         start=(kc == 0), stop=(kc == 15)), reads=[Wgb, hb], writes=[pb_], acc=(kc > 0))
                        dst = qA[ct, :, otok0:otok0 + 512] if gname == "aq" else kA[ct, :, tok0:tok0 + 512]
                        rope_epilogue(pt, pb_, 32, RA_b, cosA, sinA, tok0, dst)
                elif gname == "av":
                    for sub in range(4):
                        for hf in range(2):
                            pt, pb_ = psring.next()
                            for kc in range(16):
                                P.op("pe", K("matmul", pt, lhsT=htile[:, kc, sub * 128:(sub + 1) * 128],
                                             rhs=Wg[:, kc, hf * 512:(hf + 1) * 512], start=(kc == 0), stop=(kc == 15)),
                                     reads=[Wgb, hb], writes=[pb_], acc=(kc > 0))
                            qs, qb = qs_r.next()
                            eng = "act" if hf == 0 else "dve"
                            if eng == "act":
                                P.op("act", K("activation", out=qs, in_=pt, func=AF.Copy), reads=[pb_], writes=[qb])
                            else:
                                P.op("dve", K("tensor_copy", out=qs, in_=pt), reads=[pb_], writes=[qb])
                            P.store(vA[tok0 + sub * 128:tok0 + (sub + 1) * 128, hf * 512:(hf + 1) * 512], qs, qb)
                else:
                    cps = []
                    for ct in range(4):
                        pt, pb_ = psring.next()
                        for kc in range(16):
                            P.op("pe", K("matmul", pt, lhsT=Wg[:, kc, ct * 128:(ct + 1) * 128], rhs=htile[:, kc, :],
                                         start=(kc == 0), stop=(kc == 15)), reads=[Wgb, hb], writes=[pb_], acc=(kc > 0))
                        cps.append((pt, pb_))
                    if gname == "cq":
                        latent_epilogue(cps, tok0, cq.rearrange("k p t -> p k t")[:, :, otok0:otok0 + 512])
                    else:
                        latent_epilogue(cps, tok0, ckv.rearrange("k p t -> p k t")[:, :, tok0:tok0 + 512])
                        pt, pb_ = psring.next()
                        for kc in range(16):
                            P.op("pe", K("matmul", pt[0:64, :], lhsT=Wg[:, kc, 512:576], rhs=htile[:, kc, :],
                                         start=(kc == 0), stop=(kc == 15)), reads=[Wgb, hb], writes=[pb_], acc=(kc > 0))
                        rope_epilogue(pt[0:64, :], pb_, 64, RB_b, cosB, sinB, tok0, kr[:, tok0:tok0 + 512])
            run_deferred()
        run_deferred()
        P.barrier()
        if stop_after == 2:
            return finish(nc, P, st, out)

        A.reset()
        Wq = kcv(A.take(4 * 1536, BF16), 4)
        Wqb = Buf("Wq")
        Wkv = kcv(A.take(4 * 2048, BF16), 4)
        Wkvb = Buf("Wkv")
        cr = Ring("c3", [kcv(A.take(4 * 512, BF16), 4) for _ in range(2)])
        qs_r = Ring("qs3", [A.take(512, BF16) for _ in range(4)])
        t1_r = Ring("t13", [A.take(512, F32) for _ in range(2)])
        t2_r = Ring("t23", [A.take(512, F32) for _ in range(2)])
        P.load(Wq, wb_uq.rearrange("(k p) c -> p k c", p=128), Wqb)
        P.load(Wkv, wb_ukv.rearrange("(k p) c -> p k c", p=128), Wkvb)
        cq_v = cq.rearrange("k p t -> p k t")
        ckv_v = ckv.rearrange("k p t -> p k t")
        cpy = [0]

        def copy_out(pt, pb_, dst_dram, npart=128):
            qs, qb = qs_r.next()
            cpy[0] += 1
            if cpy[0] % 2:
                P.op("act", K("activation", out=qs[0:npart, :], in_=pt, func=AF.Copy), reads=[pb_], writes=[qb])
            else:
                P.op("dve", K("tensor_copy", out=qs[0:npart, :], in_=pt), reads=[pb_], writes=[qb])
            P.store(dst_dram, qs[0:npart, :], qb)

        for tt in range(4):
            ctile, cb = cr.next()
            P.load(ctile, cq_v[:, :, tt * 512:(tt + 1) * 512], cb)
            bg2.step(2)
            tok0 = TO + tt * 512
            for h in range(NH):
                pt, pb_ = psring.next()
                for kc in range(4):
                    P.op("pe", K("matmul", pt, lhsT=Wq[:, kc, h * 192:h * 192 + 128], rhs=ctile[:, kc, :],
                                 start=(kc == 0), stop=(kc == 3)), reads=[Wqb, cb], writes=[pb_], acc=(kc > 0))
                copy_out(pt, pb_, qn[h, :, tt * 512:(tt + 1) * 512])
                pt, pb_ = psring.next()
                for kc in range(4):
                    P.op("pe", K("matmul", pt[0:64, :], lhsT=Wq[:, kc, h * 192 + 128:h * 192 + 192], rhs=ctile[:, kc, :],
                                 start=(kc == 0), stop=(kc == 3)), reads=[Wqb, cb], writes=[pb_], acc=(kc > 0))
                rope_epilogue(pt[0:64, :], pb_, 64, RB_b, cosB, sinB, tok0, qr[h, :, tt * 512:(tt + 1) * 512])
        run_deferred()
        for tt in range(8):
            ctile, cb = cr.next()
            P.load(ctile, ckv_v[:, :, tt * 512:(tt + 1) * 512], cb)
            bg2.step(2)
            for h in range(NH):
                pt, pb_ = psring.next()
                for kc in range(4):
                    P.op("pe", K("matmul", pt, lhsT=Wkv[:, kc, h * 256:h * 256 + 128], rhs=ctile[:, kc, :],
                                 start=(kc == 0), stop=(kc == 3)), reads=[Wkvb, cb], writes=[pb_], acc=(kc > 0))
                copy_out(pt, pb_, kn[h, :, tt * 512:(tt + 1) * 512])
            for sub in range(4):
                for hf in range(2):
                    pt, pb_ = psring.next()
                    for kc in range(4):
                        rhs = Wkv[:, kc, hf * 1024:(hf + 1) * 1024].rearrange("p (h c) -> p h c", c=256)[:, :, 128:256]
                        P.op("pe", K("matmul", pt.rearrange("p (h c) -> p h c", c=128), lhsT=ctile[:, kc, sub * 128:(sub + 1) * 128],
                                     rhs=rhs, start=(kc == 0), stop=(kc == 3)), reads=[Wkvb, cb], writes=[pb_], acc=(kc > 0))
                    r0 = tt * 512 + sub * 128
                    copy_out(pt, pb_, vB[r0:r0 + 128, hf * 512:(hf + 1) * 512])
        run_deferred()
        P.barrier()
        if stop_after == 3:
            return finish(nc, P, st, out)

        A.base = base_bg
        A.reset()
        Qr = Ring("Qa", [A.take(TO, BF16) for _ in range(2)])
        Kr = Ring("Ka", [A.take(TL, BF16) for _ in range(2)])
        Vr = Ring("Va", [A.take(32 * 128, BF16).rearrange("p (b c) -> p b c", c=128) for _ in range(2)])
        AZr = Ring("AZ", [A.take(2 * TO, F32).rearrange("p (a t) -> p a t", a=2) for _ in range(2)])
        Ptr = Ring("Pt", [A.take(256, BF16) for _ in range(4)])
        Mxr = Ring("Mxa", [A.take(TO, BF16) for _ in range(2)])
        zr_t = A.take(TO, F32)
        zrb = Buf("zr")
        sc_a = float(HD ** -0.5)
        Sring = Ring("S4", [None] * 4)
        Sring.items = [psum[0], psum[1], psum[2], psum[3]]
        Oring = Ring("O4", [None] * 4)
        Oring.items = [psum[4], psum[5], psum[6], psum[7]]
        CFG = (1, 4, 16)
        LOOK = 2

        def load_head4(h):
            Q, Qb = Qr.next()
            Kt, Kb = Kr.next()
            P.load(Q, qA[h], Qb)
            P.load(Kt, kA[h], Kb)
            return (Q, Qb, Kt, Kb)

        def load_v4(h, d):
            Vt, Vb = Vr.next()
            nblk_n = TL // (128 * d)
            srcv = vA[:, h * 128:(h + 1) * 128].rearrange("(n i r) c -> i n r c", i=128, r=d)
            dstv = Vt.rearrange("p (n r) c -> p n r c", r=d)
            for n in range(nblk_n):
                P.load(dstv[:, n], srcv[:, n], Vb)
            return (Vt, Vb)

        def s_stage4(ctx, d, blk):
            Q, Qb, Kt, Kb = ctx
            n, r = blk // d, blk % d
            prev = blk - d
            qs0 = n * 128 * d + r - TO
            Qblk = Q[:, qs0:qs0 + 127 * d + 1:d]

            def ksl(b_):
                n_, r_ = b_ // d, b_ % d
                s_ = n_ * 128 * d + r_
                return Kt[:, s_:s_ + 127 * d + 1:d]
            S, Sb = Sring.next()
            P.op("pe", K("matmul", S[:, 0:128], lhsT=ident, rhs=mask_prev, start=True, stop=False), reads=[cst], writes=[Sb])
            P.op("pe", K("matmul", S[:, 0:128], lhsT=ksl(prev), rhs=Qblk, start=False, stop=True),
                 reads=[Kb, Qb], writes=[Sb], acc=True)
            P.op("pe", K("matmul", S[:, 128:256], lhsT=ident, rhs=mask_cur, start=True, stop=False),
                 reads=[cst], writes=[Sb], acc=True)
            P.op("pe", K("matmul", S[:, 128:256], lhsT=ksl(blk), rhs=Qblk, start=False, stop=True),
                 reads=[Kb, Qb], writes=[Sb], acc=True)
            Pt, Ptb = Ptr.next()
            P.op("act", K("activation", out=Pt, in_=S[:, 0:256], func=AF.Exp, scale=sc_a), reads=[Sb], writes=[Ptb])
            return (Pt, Ptb, prev, qs0)

        def o_stage4(vt, AZc, d, blk, first, st_):
            Vt, Vb = vt
            AZ, AZb = AZc
            Pt, Ptb, prev, qs0 = st_
            O, Ob = Oring.next()
            P.op("pe", K("matmul", O[:, 0:128], lhsT=Vt[:, prev, :], rhs=Pt[:, 0:128], start=True, stop=False),
                 reads=[Vb, Ptb], writes=[Ob])
            P.op("pe", K("matmul", O[:, 0:128], lhsT=Vt[:, blk, :], rhs=Pt[:, 128:256], start=False, stop=True),
                 reads=[Vb, Ptb], writes=[Ob], acc=True)
            P.op("pe", K("matmul", O[:, 128:256], lhsT=(pv_b if prev < 16 else ones_b), rhs=Pt[:, 0:128],
                         start=True, stop=False), reads=[cst, Ptb], writes=[Ob], acc=True)
            P.op("pe", K("matmul", O[:, 128:256], lhsT=ones_b, rhs=Pt[:, 128:256], start=False, stop=True),
                 reads=[cst, Ptb], writes=[Ob], acc=True)
            azv = AZ[:, :, qs0:qs0 + 127 * d + 1:d]
            osrc = O[:, 0:256].rearrange("p (a t) -> p a t", a=2)
            if first:
                P.op("dve", K("tensor_copy", out=azv, in_=osrc), reads=[Ob], writes=[AZb])
            else:
                P.op("dve", K("tensor_tensor", out=azv, in0=azv, in1=osrc, op=ALU.add), reads=[Ob, AZb], writes=[AZb])

        def fin_head4(h, AZc):
            AZ, AZb = AZc
            Mx, Mxb = Mxr.next()
            P.op("dve", K("reciprocal", out=zr_t, in_=AZ[:, 1, :]), reads=[AZb], writes=[zrb])
            P.op("pool", K("tensor_tensor", out=Mx, in0=AZ[:, 0, :], in1=zr_t, op=ALU.mult), reads=[AZb, zrb], writes=[Mxb])
            P.store(mixT[h], Mx, Mxb)

        pend = []
        nxt_ctx = load_head4(0)
        nxt_v = load_v4(0, CFG[0])
        for h in range(NH):
            ctx = nxt_ctx
            AZc = AZr.next()
            for ci_, d in enumerate(CFG):
                vt = nxt_v
                bg2.step(2)
                for blk in range(16, 32):
                    if blk == 16 + LOOK + 1:
                        if ci_ < 2:
                            nxt_v = load_v4(h, CFG[ci_ + 1])
                        elif h + 1 < NH:
                            nxt_ctx = load_head4(h + 1)
                            nxt_v = load_v4(h + 1, CFG[0])
                    st_ = s_stage4(ctx, d, blk)
                    pend.append((vt, AZc, d, blk, ci_ == 0, st_, (h if (ci_ == 2 and blk == 31) else None)))
                    if len(pend) > LOOK:
                        it = pend.pop(0)
                        o_stage4(*it[:6])
                        if it[6] is not None:
                            fin_head4(it[6], it[1])
        while pend:
            it = pend.pop(0)
            o_stage4(*it[:6])
            if it[6] is not None:
                fin_head4(it[6], it[1])
        P.barrier()
        if stop_after == 4:
            return finish(nc, P, st, out)

        A.reset()
        Qnr = Ring("Qn", [A.take(TO, BF16) for _ in range(2)])
        Qrr = Ring("Qr", [A.take(TO, BF16) for _ in range(2)])
        Knr = Ring("Kn", [A.take(TL, BF16) for _ in range(2)])
        KR = A.take(TL, BF16)
        KRb = Buf("KR")
        Vbr = Ring("Vb", [A.take(32 * 132, BF16).rearrange("p (b c) -> p b c", c=132) for _ in range(2)])
        Ptr = Ring("Pt5", [A.take(512, BF16) for _ in range(4)])
        Onr = Ring("On", [A.take(128, BF16) for _ in range(2)])
        rzr = Ring("rz", [A.take(1, F32) for _ in range(2)])
        Mxr = Ring("Mx5", [A.take(512, BF16) for _ in range(2)])
        sc_b = float(192 ** -0.5)
        Opairs = [(psum[0], psum[1]), (psum[2], psum[3])]
        Sring = Ring("S5", [None] * 3)
        Sring.items = [psum[4], psum[5], psum[6]]
        Tring = Ring("T5", [None])
        Tring.items = [psum[7]]
        P.op("pool", K("memset", KR, 0.0), writes=[KRb])
        P.load(KR[0:64, :], kr, KRb)
        for (Qr_t, Qrb) in Qrr.items:
            P.op("pool", K("memset", Qr_t, 0.0), writes=[Qrb])
        for (Vb_t, Vbb) in Vbr.items:
            P.op("pool", K("memset", Vb_t[:, 16:32, 128:129], 1.0), writes=[Vbb])
            P.op("pool", K("tensor_copy", out=Vb_t[:, 0:16, 128:129], in_=pv_b[:, 0:16].rearrange("p (b c) -> p b c", c=1)),
                 reads=[cst], writes=[Vbb])

        def load_head5(h):
            Qn_t, Qnb = Qnr.next()
            Qr_t, Qrb = Qrr.next()
            Kn_t, Knb = Knr.next()
            Vb_t, Vbb = Vbr.next()
            P.load(Qn_t, qn[h], Qnb)
            P.load(Qr_t[0:64, :], qr[h], Qrb)
            P.load(Kn_t, kn[h], Knb)
            srcv = vB[:, h * 128:(h + 1) * 128].rearrange("(b i) c -> i b c", i=128)
            for b4 in range(4):
                P.load(Vb_t[:, b4 * 8:(b4 + 1) * 8, 0:128], srcv[:, b4 * 8:(b4 + 1) * 8, :], Vbb)
            return (Qn_t, Qnb, Qr_t, Qrb, Kn_t, Knb, Vb_t, Vbb)

        def s_stage5(ctx, g, kb):
            Qn_t, Qnb, Qr_t, Qrb, Kn_t, Knb, Vb_t, Vbb = ctx
            j0 = max(0, kb - 16 - 4 * g)
            c0 = j0 * 128
            diag = kb >= 16 + 4 * g
            S, Sb = Sring.next()
            q0 = g * 512 + c0
            if diag:
                P.op("pe", K("matmul", S[:, c0:c0 + 128], lhsT=ident, rhs=mask_cur, start=True, stop=False),
                     reads=[cst], writes=[Sb])
            P.op("pe", K("matmul", S[:, c0:512], lhsT=Kn_t[:, kb * 128:(kb + 1) * 128], rhs=Qn_t[:, q0:g * 512 + 512],
                         start=(not diag), stop=False), reads=[Knb, Qnb], writes=[Sb], acc=diag)
            P.op("pe", K("matmul", S[:, c0:512], lhsT=KR[:, kb * 128:(kb + 1) * 128], rhs=Qr_t[:, q0:g * 512 + 512],
                         start=False, stop=True), reads=[KRb, Qrb], writes=[Sb], acc=True)
            Pt, Ptb = Ptr.next()
            P.op("act", K("activation", out=Pt[:, c0:512], in_=S[:, c0:512], func=AF.Exp, scale=sc_b),
                 reads=[Sb], writes=[Ptb])
            return (Pt, Ptb, j0)

        def pv_stage5(ctx, h, g, kb, opair, Mxc, st_):
            Vb_t, Vbb = ctx[6], ctx[7]
            Pt, Ptb, j0 = st_
            Mx, Mxb = Mxc
            for j in range(j0, 4):
                Ot, Otb = opair[0] if j < 2 else opair[1]
                oc = (j % 2) * 256
                last = (kb == 16 + 4 * g + j)
                P.op("pe", K("matmul", Ot[:, oc:oc + 129], lhsT=Pt[:, j * 128:(j + 1) * 128], rhs=Vb_t[:, kb, 0:129],
                             start=False, stop=last), reads=[Ptb, Vbb], writes=[Otb], acc=True)
                if last:
                    rz, rzb = rzr.next()
                    On, Onb = Onr.next()
                    P.op("dve", K("reciprocal", out=rz, in_=Ot[:, oc + 128:oc + 129]), reads=[Otb], writes=[rzb])
                    P.op("dve", K("tensor_scalar", out=On, in0=Ot[:, oc:oc + 128], scalar1=rz, scalar2=None, op0=ALU.mult),
                         reads=[Otb, rzb], writes=[Onb])
                    T, Tb = Tring.next()
                    Tb16 = T[:, :].bitcast(BF16)
                    P.op("pe", K("transpose", out=Tb16[:, 0:128], in_=On, identity=ident), reads=[Onb, cst], writes=[Tb])
                    P.op("act", K("activation", out=Mx[:, j * 128:(j + 1) * 128], in_=Tb16[:, 0:128], func=AF.Copy),
                         reads=[Tb], writes=[Mxb])
                    if j == 3:
                        P.store(mixT[8 + h, :, g * 512:(g + 1) * 512], Mx, Mxb)

        pend = []
        gi = 0
        nxt_ctx = load_head5(0)
        for h in range(NH):
            ctx = nxt_ctx
            for g in range(4):
                if g == 1 and h + 1 < NH:
                    nxt_ctx = load_head5(h + 1)
                bg2.step(1)
                opair = Opairs[gi % 2]
                gi += 1
                Mxc = Mxr.next()
                for (Ot, Otb) in opair:
                    P.op("pe", K("matmul", Ot, lhsT=zeros_b[:, 0:128], rhs=zeros_b, start=True, stop=False),
                         reads=[cst], writes=[Otb])
                for kb in range(16 + 4 * g + 4):
                    st_ = s_stage5(ctx, g, kb)
                    pend.append((ctx, h, g, kb, opair, Mxc, st_))
                    if len(pend) > LOOK:
                        pv_stage5(*pend.pop(0))
        while pend:
            pv_stage5(*pend.pop(0))
        bg2.flush()
        P.barrier()
        if stop_after == 5:
            return finish(nc, P, st, out)

        A.base = base_const
        A.reset()
        Wo = kcv(A.take(16 * D, BF16), 16)
        Wob = Buf("Wo")
        gp = A.take(D, F32)
        gpb = Buf("gp")
        mr = Ring("m6", [kcv(A.take(16 * 512, BF16), 16) for _ in range(2)])
        xr = Ring("x6", [A.take(D, F32) for _ in range(2)])
        yr = Ring("y6", [A.take(D, F32) for _ in range(2)])
        x1r = Ring("x16", [A.take(D, F32) for _ in range(2)])
        scr_r = Ring("xn6", [A.take(D, BF16) for _ in range(4)])
        sm_r = Ring("sm6", [A.take(8, F32) for _ in range(4)])
        hr = Ring("ht6", [kcv(A.take(16 * 512, BF16), 16) for _ in range(2)])
        P.load(Wo, wb_out.rearrange("(k p) c -> p k c", p=128), Wob)
        P.load(gp, g_post, gpb)
        mixT_v = mixT.rearrange("k p t -> p k t")
        h2T_v = h2T.rearrange("k p t -> p k t")
        defq = []

        def transposes6(sc, scb, htile, hb, sub, tt, last):
            def f():
                for half in range(2):
                    pt, pb_ = psring.next()
                    ptb = pt[:, :].bitcast(BF16)
                    for j in range(8):
                        kc = half * 8 + j
                        P.op("pe", K("transpose", out=ptb[:, j * 128:(j + 1) * 128], in_=sc[:, kc * 128:(kc + 1) * 128],
                                     identity=ident), reads=[scb, cst], writes=[pb_], acc=True)
                    dst = htile[:, half * 8:(half + 1) * 8, sub * 128:(sub + 1) * 128]
                    src = ptb.rearrange("p (k n) -> p k n", k=8)
                    if half == 0:
                        P.op("act", K("activation", out=dst, in_=src, func=AF.Copy), reads=[pb_], writes=[hb])
                    else:
                        P.op("dve", K("tensor_copy", out=dst, in_=src), reads=[pb_], writes=[hb])
                if last:
                    P.store(h2T_v[:, :, tt * 512:(tt + 1) * 512], htile, hb)
            return f

        for tt in range(4):
            mt, mb = mr.next()
            P.load(mt, mixT_v[:, :, tt * 512:(tt + 1) * 512], mb)
            htile, hb = hr.next()
            for sub in range(4):
                r0 = tt * 512 + sub * 128
                xt, xb = xr.next()
                P.load(xt, xo[r0:r0 + 128, :], xb)
                yt, yb = yr.next()
                for dt_ in range(4):
                    pt, pb_ = psring.next()
                    for kc in range(16):
                        P.op("pe", K("matmul", pt, lhsT=mt[:, kc, sub * 128:(sub + 1) * 128], rhs=Wo[:, kc, dt_ * 512:(dt_ + 1) * 512],
                                     start=(kc == 0), stop=(kc == 15)), reads=[mb, Wob], writes=[pb_], acc=(kc > 0))
                    P.op("dve", K("tensor_copy", out=yt[:, dt_ * 512:(dt_ + 1) * 512], in_=pt), reads=[pb_], writes=[yb])
                while len(defq) > 1:
                    defq.pop(0)()
                sm, smb = sm_r.next()
                sc, scb = scr_r.next()
                P.op("act", K("activation", out=sc, in_=yt, func=AF.Square, accum_out=sm[:, 0:1]), reads=[yb], writes=[scb, smb])
                P.op("act", K("activation", out=sm[:, 1:2], in_=sm[:, 0:1], func=AF.Sqrt, scale=1.0 / D, bias=EPS),
                     reads=[smb], writes=[smb])
                P.op("dve", K("reciprocal", out=sm[:, 1:2], in_=sm[:, 1:2]), reads=[smb], writes=[smb])
                P.op("act", K("activation", out=yt, in_=yt, func=AF.Copy, scale=sm[:, 1:2]), reads=[yb, smb], writes=[yb])
                P.op("dve", K("tensor_tensor", out=yt, in0=yt, in1=gp, op=ALU.mult), reads=[yb, gpb], writes=[yb])
                x1t, x1b = x1r.next()
                P.op("pool", K("tensor_tensor", out=x1t, in0=yt, in1=xt, op=ALU.add), reads=[yb, xb], writes=[x1b])
                P.store(x1[r0:r0 + 128, :], x1t, x1b)
                P.op("act", K("activation", out=sc, in_=x1t, func=AF.Square, accum_out=sm[:, 2:3]), reads=[x1b], writes=[scb, smb])
                P.op("act", K("activation", out=sm[:, 3:4], in_=sm[:, 2:3], func=AF.Sqrt, scale=1.0 / D, bias=EPS),
                     reads=[smb], writes=[smb])
                P.op("dve", K("reciprocal", out=sm[:, 3:4], in_=sm[:, 3:4]), reads=[smb], writes=[smb])
                P.op("dve", K("tensor_scalar", out=sc, in0=x1t, scalar1=sm[:, 3:4], scalar2=None, op0=ALU.mult),
                     reads=[x1b, smb], writes=[scb])
                defq.append(transposes6(sc, scb, htile, hb, sub, tt, sub == 3))
        while defq:
            defq.pop(0)()
        P.barrier()
        if stop_after == 6:
            return finish(nc, P, st, out)

        A.reset()
        gp2 = A.take(D, F32)
        gp2b = Buf("gp2")
        P.load(gp2, g_post2, gp2b)
        h2r = Ring("h2t", [kcv(A.take(16 * 512, BF16), 16) for _ in range(2)])
        U = kcv(A.take(64 * 512, BF16), 64)
        Ub = Buf("U")
        Wur = Ring("Wu", [kcv(A.take(16 * 256, BF16), 16) for _ in range(3)])
        Wdr = Ring("Wd", [kcv(A.take(4 * 1024, BF16), 4) for _ in range(3)])
        rl_r = Ring("rl", [A.take(512, F32) for _ in range(3)])
        Y = A.take(4 * D, F32).rearrange("p (s c) -> p s c", s=4)
        Yb = Buf("Y")
        sm7 = A.take(32, F32)
        sm7b = Buf("sm7")
        junk = A.take(1024, BF16)
        junkb = Buf("junk")
        x1r = Ring("x17", [A.take(D, F32) for _ in range(1)])
        wb_up_v = wb_up.rearrange("(k p) c -> p k c", p=128)
        wb_down_v = wb_down.rearrange("(k p) c -> p k c", p=128)
        out_ops = []
        import os as _os
        _ntt = int(_os.environ.get("F7_NTT", "4"))
        _mode = _os.environ.get("F7_MODE", "full")
        nxt_h2 = h2r.next()
        P.load(nxt_h2[0], h2T_v[:, :, 0:512], nxt_h2[1])
        for tt in range(_ntt):
            h2, h2b = nxt_h2
            for fg in range(32):
                Wu, Wub = Wur.next()
                P.load(Wu, wb_up_v[:, :, fg * 256:(fg + 1) * 256], Wub)
                for f in range(2):
                    ft = fg * 2 + f
                    pt, pb_ = psring.next()
                    for kc in range(16):
                        P.op("pe", K("matmul", pt, lhsT=Wu[:, kc, f * 128:(f + 1) * 128], rhs=h2[:, kc, :],
                                     start=(kc == 0), stop=(kc == 15)), reads=[Wub, h2b], writes=[pb_], acc=(kc > 0))
                    rl, rlb = rl_r.next()
                    P.op("act", K("activation", out=rl, in_=pt, func=AF.Relu), reads=[pb_], writes=[rlb])
                    P.op("dve", K("tensor_tensor", out=U[:, ft, :], in0=rl, in1=rl, op=ALU.mult), reads=[rlb], writes=[Ub])
            if tt + 1 < _ntt:
                nxt_h2 = h2r.next()
                P.load(nxt_h2[0], h2T_v[:, :, (tt + 1) * 512:(tt + 2) * 512], nxt_h2[1])
            for dh in range(2 if _mode != "up" else 0):
                accs = [psring.next() for _ in range(8)]
                for fg in range(16):
                    Wd, Wdb = Wdr.next()
                    P.load(Wd, wb_down_v[:, fg * 4:(fg + 1) * 4, dh * 1024:(dh + 1) * 1024], Wdb)
                    for f in range(4):
                        ft = fg * 4 + f
                        for sub in range(4):
                            for dt_ in range(2):
                                pt, pb_ = accs[sub * 2 + dt_]
                                P.op("pe", K("matmul", pt, lhsT=U[:, ft, sub * 128:(sub + 1) * 128],
                                             rhs=Wd[:, f, dt_ * 512:(dt_ + 1) * 512], start=(ft == 0), stop=(ft == 63)),
                                     reads=[Ub, Wdb], writes=[pb_], acc=(ft > 0))
                for sub in range(4):
                    for dt_ in range(2):
                        pt, pb_ = accs[sub * 2 + dt_]
                        col = dh * 2 + dt_
                        _ev = _os.environ.get("F7_EV", "both")
                        P.op("dve", K("tensor_copy", out=Y[:, sub, col * 512:(col + 1) * 512], in_=pt), reads=[pb_], writes=[Yb])
                        P.op("act", K("activation", out=junk[:, 0:512], in_=Y[:, sub, col * 512:(col + 1) * 512], func=AF.Square,
                                      accum_out=sm7[:, sub * 4 + col:sub * 4 + col + 1]), reads=[Yb], writes=[junkb, sm7b])
            for sub in range(4 if _mode == "full" else 0):
                r0 = tt * 512 + sub * 128
                x1t, x1b = x1r.next()
                P.load(x1t, x1[r0:r0 + 128, :], x1b)
                s4 = sm7[:, sub * 4:sub * 4 + 4]
                t2 = sm7[:, 16 + sub * 4:16 + sub * 4 + 2]
                ssum = sm7[:, 16 + sub * 4 + 2:16 + sub * 4 + 3]
                rs7 = sm7[:, 16 + sub * 4 + 3:16 + sub * 4 + 4]
                P.op("dve", K("tensor_tensor", out=t2, in0=s4[:, 0:2], in1=s4[:, 2:4], op=ALU.add), reads=[sm7b], writes=[sm7b])
                P.op("dve", K("tensor_tensor", out=ssum, in0=t2[:, 0:1], in1=t2[:, 1:2], op=ALU.add), reads=[sm7b], writes=[sm7b])
                P.op("act", K("activation", out=rs7, in_=ssum, func=AF.Sqrt, scale=1.0 / D, bias=EPS), reads=[sm7b], writes=[sm7b])
                P.op("dve", K("reciprocal", out=rs7, in_=rs7), reads=[sm7b], writes=[sm7b])
                P.op("act", K("activation", out=Y[:, sub, :], in_=Y[:, sub, :], func=AF.Copy, scale=rs7), reads=[Yb, sm7b], writes=[Yb])
                P.op("pool", K("tensor_tensor", out=Y[:, sub, :], in0=Y[:, sub, :], in1=gp2, op=ALU.mult), reads=[Yb, gp2b], writes=[Yb])
                P.op("pool", K("tensor_tensor", out=x1t, in0=Y[:, sub, :], in1=x1t, op=ALU.add), reads=[Yb, x1b], writes=[x1b])
                out_ops.append(P.store(out[r0:r0 + 128, :], x1t, x1b))
        P.barrier()
        return finish(nc, P, st, out)


def finish(nc, P, st, out):
    P.barrier()
    with nc.Block() as block:
        P.emit(block)
    return nc


def _consts():
    j = np.arange(128)[:, None]
    i = np.arange(128)[None, :]
    mask_prev = np.where(j >= i, 0.0, NEG).astype(np.float32)
    mask_cur = np.where(j <= i, 0.0, NEG).astype(np.float32)
    cmask = np.concatenate([mask_prev, mask_cur], axis=1)
    RA = np.zeros((128, 128), np.float32)
    for m in range(16):
        RA[m + 16, m] = -1.0
        RA[m, m + 16] = 1.0
    RB = np.zeros((128, 128), np.float32)
    for m in range(32):
        RB[m + 32, m] = -1.0
        RB[m, m + 32] = 1.0
    p = np.arange(128)
    invA = (THETA ** (-(2.0 * (p % 16)) / 32.0)).astype(np.float32)
    invB = (THETA ** (-(2.0 * (p % 32)) / 64.0)).astype(np.float32)
    invf = np.stack([invA, invB], axis=1).astype(np.float32)
    return cmask, RA, RB, invf


def make_in_maps(x, positions, norm_attn_pre, norm_attn_post, w_in, q_latent_norm, kv_latent_norm,
                 w_uq, w_ukv, w_out, norm_mlp_pre, norm_mlp_post, w_up, w_down):
    x = np.asarray(x, np.float32)
    positions = np.asarray(positions, np.int32)
    cmask, RA, RB, invf = _consts()

    def col(g, k):
        return np.ascontiguousarray(np.asarray(g, np.float32).reshape(k, 128).T)

    def bc(g):
        return np.ascontiguousarray(np.broadcast_to(np.asarray(g, np.float32).reshape(1, -1), (128, D)))

    shared = {
        "cmask": cmask, "cRA": RA, "cRB": RB, "cinvf": invf,
        "g_in": col(norm_attn_pre[0], 16), "g_up": col(norm_mlp_pre[0], 16),
        "g_q": col(q_latent_norm[0], 4), "g_kv": col(kv_latent_norm[0], 4),
        "g_post": bc(norm_attn_post[0]), "g_post2": bc(norm_mlp_post[0]),
        "w_in": np.ascontiguousarray(np.asarray(w_in[0], np.float32)),
        "w_uq": np.ascontiguousarray(np.asarray(w_uq[0], np.float32)),
        "w_ukv": np.ascontiguousarray(np.asarray(w_ukv[0], np.float32)),
        "w_out": np.ascontiguousarray(np.asarray(w_out[0], np.float32)),
        "w_up": np.ascontiguousarray(np.asarray(w_up[0], np.float32)),
        "w_down": np.ascontiguousarray(np.asarray(w_down[0], np.float32)),
    }
    maps = []
    for c in range(8):
        b, half = c // 2, c % 2
        m = dict(shared)
        m["xo"] = np.ascontiguousarray(x[b, half * TO:(half + 1) * TO])
        m["xp"] = np.ascontiguousarray(x[b, 0:TO]) if half else np.zeros((TO, D), np.float32)
        pl = np.concatenate([positions[b, 0:TO], positions[b, half * TO:(half + 1) * TO]])
        m["posb"] = np.ascontiguousarray(np.broadcast_to(pl.reshape(1, TL), (128, TL))).astype(np.int32)
        m["pvt"] = np.full((128, 128), float(half), np.float32)
        maps.append(m)
    return maps


_NC_CACHE = {}


def kernel(**inputs):
    maps = make_in_maps(**inputs)
    if "nc" not in _NC_CACHE:
        _NC_CACHE["nc"] = build_program()
    nc = _NC_CACHE["nc"]
    res = run_bass_kernel_spmd(nc, maps, core_ids=list(range(8)))
    outp = np.empty((NB, SEQ, D), np.float32)
    for c in range(8):
        b, half = c // 2, c % 2
        outp[b, half * TO:(half + 1) * TO] = np.asarray(res.results[c]["out"], np.float32)
    return outp
```

```python
import numpy as np
from contextlib import ExitStack
import concourse.bass as bass
import concourse.mybir as mybir
from concourse.bass_utils import run_bass_kernel_spmd

F32 = mybir.dt.float32
BF16 = mybir.dt.bfloat16
I32 = mybir.dt.int32
AF = mybir.ActivationFunctionType
ALU = mybir.AluOpType
AX = mybir.AxisListType

ENGS = ("pe", "act", "dve", "pool", "sp")

D = 2048
SEQ = 4096
NB = 4
TL = 4096
TO = 2048
HD = 128
NH = 8
DFF = 8192
INC = 4160
EPS = 1e-6
NEG = -30000.0
THETA = 500000.0
TWO_PI = float(2.0 * np.pi)
PI = float(np.pi)


class Buf:
    __slots__ = ("name", "last_w", "readers", "sem", "sem_val", "last_dma", "excl")

    def __init__(self, name, excl=False):
        self.name = name
        self.excl = excl
        self.last_w = None
        self.readers = []
        self.sem = None
        self.sem_val = 0
        self.last_dma = None


class Op:
    __slots__ = ("eng", "idx", "fn", "deps", "signal", "is_dma", "sem", "val", "waits")

    def __init__(self, eng, idx, fn):
        self.eng = eng
        self.idx = idx
        self.fn = fn
        self.deps = []
        self.signal = False
        self.is_dma = False
        self.sem = None
        self.val = None
        self.waits = None


def K(name, *args, **kw):
    return lambda e: getattr(e, name)(*args, **kw)


class Prog:
    def __init__(self, nc, stack):
        self.nc = nc
        self.stack = stack
        self.ops = {e: [] for e in ENGS}
        self.eng_sem = {e: stack.enter_context(nc.semaphore("S_" + e)) for e in ENGS}
        self.dma_bufs = []
        self.same_engine_sync = True

    def _add(self, eng, fn, reads, writes, acc=False):
        op = Op(eng, len(self.ops[eng]), fn)
        self.ops[eng].append(op)
        deps = []
        for b in reads:
            if b.last_w is not None:
                deps.append(b.last_w)
            if b.excl:
                deps.extend(r for r in b.readers if r.eng != eng)
        for b in writes:
            if b.last_w is not None and not (acc and b.last_w.eng == eng):
                deps.append(b.last_w)
            deps.extend(b.readers)
        op.deps = deps
        for b in reads:
            b.readers.append(op)
        for b in writes:
            b.last_w = op
            b.readers = []
        return op

    def op(self, eng, fn, reads=(), writes=(), acc=False):
        return self._add(eng, fn, list(reads), list(writes), acc)

    def dma(self, eng, out_ap, in_ap, sbuf_buf, reads=(), writes=()):
        b = sbuf_buf
        if b.sem is None:
            b.sem = self.stack.enter_context(self.nc.semaphore("D_" + b.name))
            self.dma_bufs.append(b)
        op = self._add(eng, None, list(reads), list(writes))
        if b.last_dma is not None:
            op.deps.append(b.last_dma)
        b.sem_val += 16
        op.is_dma = True
        op.sem = b.sem
        op.val = b.sem_val
        op.fn = (out_ap, in_ap)
        b.last_dma = op
        return op

    def load(self, dst, src, buf, eng="sp"):
        return self.dma(eng, dst, src, buf, writes=[buf])

    def store(self, dst, src, buf, eng="pool"):
        return self.dma(eng, dst, src, buf, reads=[buf])

    def barrier(self):
        deps = []
        for e in ENGS:
            if self.ops[e]:
                deps.append(self.ops[e][-1])
        for b in self.dma_bufs:
            if b.last_dma is not None:
                deps.append(b.last_dma)
        for e in ENGS:
            op = self._add(e, None, [], [])
            op.deps = list(deps)

    def finalize(self):
        for e in ENGS:
            known_idx = {x: -1 for x in ENGS}
            known_dma = {}
            for op in self.ops[e]:
                waits = []
                best = {}
                for d in op.deps:
                    if d.is_dma:
                        k = id(d.sem)
                        if known_dma.get(k, 0) >= d.val:
                            continue
                        known_dma[k] = d.val
                        waits.append(d)
                    else:
                        if d.fn is None:
                            continue
                        if d.eng == e and (e == "pe" or not self.same_engine_sync):
                            continue
                        if d.idx <= known_idx[d.eng]:
                            continue
                        if d.eng not in best or best[d.eng].idx < d.idx:
                            best[d.eng] = d
                for x, d in best.items():
                    known_idx[x] = d.idx
                    d.signal = True
                    waits.append(d)
                op.waits = waits
        for e in ENGS:
            c = 0
            for op in self.ops[e]:
                if not op.is_dma and op.signal:
                    c += 1
                    op.sem = self.eng_sem[e]
                    op.val = c

    def emit(self, block):
        self.finalize()
        P = self

        def run(e, eng):
            for op in P.ops[e]:
                seen = {}
                for d in op.waits:
                    k = id(d.sem)
                    if k not in seen or seen[k][1] < d.val:
                        seen[k] = (d.sem, d.val)
                for sem, val in seen.values():
                    eng.wait_ge(sem, val)
                if op.is_dma:
                    o, i = op.fn
                    eng.dma_start(out=o, in_=i).then_inc(op.sem, 16)
                elif op.fn is not None:
                    ins = op.fn(eng)
                    if op.signal:
                        ins.then_inc(op.sem, 1)

        @block.tensor
        def _(eng):
            run("pe", eng)

        @block.scalar
        def _(eng):
            run("act", eng)

        @block.vector
        def _(eng):
            run("dve", eng)

        @block.gpsimd
        def _(eng):
            run("pool", eng)

        @block.sync
        def _(eng):
            run("sp", eng)


class Ring:
    def __init__(self, name, aps):
        self.items = [(ap, Buf("%s%d" % (name, i))) for i, ap in enumerate(aps)]
        self.i = 0

    def next(self):
        it = self.items[self.i % len(self.items)]
        self.i += 1
        return it


class Arena:
    def __init__(self, t, nbytes):
        self.t = t
        self.n = nbytes
        self.base = 0
        self.off = 0

    def persist(self):
        self.base = self.off

    def reset(self):
        self.off = self.base

    def take(self, nelem, dt, parts=128):
        sz = 2 if dt == BF16 else 4
        nb = (nelem * sz + 63) // 64 * 64
        o = self.off
        self.off += nb
        assert self.off <= self.n, "SBUF arena overflow %d > %d" % (self.off, self.n)
        ap = self.t[0:parts, o // 4:(o + nelem * sz + 3) // 4]
        if dt != F32:
            ap = ap.bitcast(dt)
        return ap


DEBUG_DUMP = False

class BG:
    def __init__(self, P, wst, wob, wlist):
        self.P = P
        self.wst = wst
        self.wob = wob
        self.tiles = []
        for (src, dst, R, C, gcol) in wlist:
            for kc in range(R // 128):
                for c0 in range(0, C, 2048):
                    cw = min(2048, C - c0)
                    self.tiles.append((src[kc * 128:(kc + 1) * 128, c0:c0 + cw], dst[kc * 128:(kc + 1) * 128, c0:c0 + cw], cw,
                                       None if gcol is None else gcol[:, kc:kc + 1]))
        self.il = 0
        self.ic = 0
        self.slots = {}

    def step(self, n=1):
        P = self.P
        for _ in range(n):
            while self.il < min(len(self.tiles), self.ic + 3):
                src, dst, cw, g = self.tiles[self.il]
                sap, sb = self.wst.next()
                self.slots[self.il] = (sap, sb)
                P.load(sap[:, 0:cw], src, sb)
                self.il += 1
            if self.ic < len(self.tiles):
                src, dst, cw, g = self.tiles[self.ic]
                sap, sb = self.slots.pop(self.ic)
                oap, ob = self.wob.next()
                if g is None:
                    P.op("act", K("activation", out=oap[:, 0:cw], in_=sap[:, 0:cw], func=AF.Copy), reads=[sb], writes=[ob])
                else:
                    P.op("act", K("activation", out=oap[:, 0:cw], in_=sap[:, 0:cw], func=AF.Copy, scale=g), reads=[sb], writes=[ob])
                P.store(dst, oap[:, 0:cw], ob)
                self.ic += 1

    def flush(self):
        while self.ic < len(self.tiles):
            self.step()


C1 = 6.28125
C2 = float(2.0 * np.pi - 6.28125)


def rope_table(P, posf, invcol, ang, ni, nf, mm_, sin_out, cos_out, tb):
    w = dict(reads=[tb], writes=[tb])
    P.op("dve", K("tensor_scalar", out=ang, in0=posf, scalar1=invcol, scalar2=None, op0=ALU.mult), **w)
    P.op("dve", K("tensor_scalar", out=ni, in0=ang, scalar1=1.0 / TWO_PI, scalar2=None, op0=ALU.mult), **w)
    P.op("dve", K("tensor_copy", out=nf, in_=ni), **w)
    P.op("dve", K("scalar_tensor_tensor", out=ang, in0=nf, scalar=-C1, in1=ang, op0=ALU.mult, op1=ALU.add), **w)
    P.op("dve", K("scalar_tensor_tensor", out=ang, in0=nf, scalar=-C2, in1=ang, op0=ALU.mult, op1=ALU.add), **w)

    def wrap(x):
        P.op("dve", K("tensor_scalar", out=mm_, in0=x, scalar1=PI, scalar2=-TWO_PI, op0=ALU.is_gt, op1=ALU.mult), **w)
        P.op("dve", K("tensor_tensor", out=x, in0=x, in1=mm_, op=ALU.add), **w)
        P.op("dve", K("tensor_scalar", out=mm_, in0=x, scalar1=-PI, scalar2=TWO_PI, op0=ALU.is_lt, op1=ALU.mult), **w)
        P.op("dve", K("tensor_tensor", out=x, in0=x, in1=mm_, op=ALU.add), **w)
    wrap(ang)
    P.op("act", K("activation", out=sin_out, in_=ang, func=AF.Sin), **w)
    P.op("dve", K("tensor_scalar", out=ang, in0=ang, scalar1=PI / 2, scalar2=None, op0=ALU.add), **w)
    wrap(ang)
    P.op("act", K("activation", out=cos_out, in_=ang, func=AF.Sin), **w)


def build_program(debug=False, stop_after=None):
    nc = bass.Bass("TRN2", target_bir_lowering=False)

    def din(name, shape, dt=F32):
        return nc.dram_tensor(name, list(shape), dt, kind="ExternalInput").ap()

    def dscr(name, shape, dt=BF16):
        if debug:
            return nc.dram_tensor(name, list(shape), dt, kind="ExternalOutput").ap()
        return nc.dram_tensor(name, list(shape), dt).ap()

    xo = din("xo", [TO, D])
    xp = din("xp", [TO, D])
    posb = din("posb", [128, TL], I32)
    pvt = din("pvt", [128, 128])
    cmask = din("cmask", [128, 256])
    cRA = din("cRA", [128, 128])
    cRB = din("cRB", [128, 128])
    cinvf = din("cinvf", [128, 2])
    g_in = din("g_in", [128, 16])
    g_up = din("g_up", [128, 16])
    g_q = din("g_q", [128, 4])
    g_kv = din("g_kv", [128, 4])
    g_post = din("g_post", [128, D])
    g_post2 = din("g_post2", [128, D])
    w_in = din("w_in", [D, INC])
    w_uq = din("w_uq", [512, 1536])
    w_ukv = din("w_ukv", [512, 2048])
    w_out = din("w_out", [D, D])
    w_up = din("w_up", [D, DFF])
    w_down = din("w_down", [DFF, D])
    out = nc.dram_tensor("out", [TO, D], F32, kind="ExternalOutput").ap()

    wb_in = dscr("wb_in", [D, INC])
    wb_uq = dscr("wb_uq", [512, 1536])
    wb_ukv = dscr("wb_ukv", [512, 2048])
    wb_out = dscr("wb_out", [D, D])
    wb_up = dscr("wb_up", [D, DFF])
    wb_down = dscr("wb_down", [DFF, D])
    hT = dscr("hT", [16, 128, TL])
    qA = dscr("qA", [NH, 128, TO])
    kA = dscr("kA", [NH, 128, TL])
    vA = dscr("vA", [TL, NH * HD])
    cq = dscr("cq", [4, 128, TO])
    ckv = dscr("ckv", [4, 128, TL])
    kr = dscr("kr", [64, TL])
    qn = dscr("qn", [NH, 128, TO])
    qr = dscr("qr", [NH, 64, TO])
    kn = dscr("kn", [NH, 128, TL])
    vB = dscr("vB", [TL, NH * HD])
    mixT = dscr("mixT", [16, 128, TO])
    x1 = dscr("x1", [TO, D], F32)
    h2T = dscr("h2T", [16, 128, TO])

    st = ExitStack()
    with st:
        ARENA_BYTES = 212736
        arena_t = st.enter_context(nc.sbuf_tensor("arena", [128, ARENA_BYTES // 4], F32))
        A = Arena(arena_t, ARENA_BYTES)
        psum = []
        for i in range(8):
            t = st.enter_context(nc.psum_tensor("ps%d" % i, [128, 512], F32))
            psum.append((t[:, :], Buf("ps%d" % i, excl=True)))
        P = Prog(nc, st)
        psring = Ring("psr", [None] * 8)
        psring.items = [(t, b) for t, b in psum]

        cst = Buf("const")
        ident = A.take(128, BF16)
        ones_b = A.take(128, BF16)
        zeros_b = A.take(512, BF16)
        pv_b = A.take(128, BF16)
        mask_b = A.take(384, BF16)
        RA_b = A.take(128, BF16)
        RB_b = A.take(128, BF16)
        ones_f = A.take(128, F32)
        invf = A.take(2, F32)
        gc_in = A.take(16, F32)
        gc_up = A.take(16, F32)
        gc_q = A.take(4, F32)
        gc_kv = A.take(4, F32)
        pv_col = A.take(1, F32)
        A.persist()
        stg = A.take(128 * 6, F32)
        stgb = Buf("stg")
        P.load(stg[:, 0:128], pvt, stgb)
        P.load(stg[:, 128:384], cmask, stgb)
        P.load(stg[:, 384:512], cRA, stgb)
        P.load(stg[:, 512:640], cRB, stgb)
        gb = Buf("gl")
        P.load(invf, cinvf, gb)
        P.load(gc_in, g_in, gb)
        P.load(gc_up, g_up, gb)
        P.load(gc_q, g_q, gb)
        P.load(gc_kv, g_kv, gb)
        P.op("pool", K("memset", stg[:, 640:768], 0.0), writes=[cst])
        P.op("pool", K("affine_select", out=stg[:, 640:768], in_=stg[:, 640:768], pattern=[[-1, 128]],
                       compare_op=ALU.not_equal, fill=1.0, base=0, channel_multiplier=1),
             reads=[cst], writes=[cst])
        P.op("pool", K("tensor_copy", out=ident, in_=stg[:, 640:768]), reads=[cst], writes=[cst])
        P.op("pool", K("memset", ones_f, 1.0), writes=[cst])
        P.op("pool", K("memset", ones_b, 1.0), writes=[cst])
        P.op("pool", K("memset", zeros_b, 0.0), writes=[cst])
        P.op("dve", K("tensor_copy", out=pv_b, in_=stg[:, 0:128]), reads=[stgb], writes=[cst])
        P.op("dve", K("tensor_copy", out=pv_col, in_=stg[:, 0:1]), reads=[stgb], writes=[cst])
        P.op("dve", K("tensor_copy", out=mask_b[:, 0:256], in_=stg[:, 128:384]), reads=[stgb], writes=[cst])
        P.op("dve", K("tensor_copy", out=mask_b[:, 256:384], in_=stg[:, 128:256]), reads=[stgb], writes=[cst])
        P.op("dve", K("tensor_copy", out=RA_b, in_=stg[:, 384:512]), reads=[stgb], writes=[cst])
        P.op("dve", K("tensor_copy", out=RB_b, in_=stg[:, 512:640]), reads=[stgb], writes=[cst])
        P.barrier()
        mask_prev = mask_b[:, 0:128]
        mask_cur = mask_b[:, 128:256]

        def kcv(ap, k):
            return ap.rearrange("p (k n) -> p k n", k=k)

        A.reset()
        base_const = A.base
        wst = Ring("wst", [A.take(2048, F32) for _ in range(3)])
        wob = Ring("wob", [A.take(2048, BF16) for _ in range(3)])
        A.persist()
        base_bg = A.base
        bg1 = BG(P, wst, wob, [(w_in, wb_in, D, INC, gc_in)])
        bg2 = BG(P, wst, wob, [(w_uq, wb_uq, 512, 1536, gc_q), (w_ukv, wb_ukv, 512, 2048, gc_kv),
                               (w_out, wb_out, D, D, None), (w_up, wb_up, D, DFF, gc_up), (w_down, wb_down, DFF, D, None)])

        cosA = A.take(TL, F32)
        sinA = A.take(TL, F32)
        cosB = A.take(TL, F32)
        sinB = A.take(TL, F32)
        A.persist()
        posi = A.take(TL, I32)
        posf = A.take(TL, F32)
        ang = A.take(TL, F32)
        ni_t = posi
        nf_t = A.take(TL, F32)
        mm_t = A.take(TL, F32)
        tb = Buf("ropet")
        P.load(posi, posb, tb)
        P.op("dve", K("tensor_copy", out=posf, in_=posi), reads=[tb], writes=[tb])
        bg1.step(4)
        rope_table(P, posf, invf[:, 0:1], ang, ni_t, nf_t, mm_t, sinA, cosA, tb)
        bg1.step(4)
        rope_table(P, posf, invf[:, 1:2], ang, ni_t, nf_t, mm_t, sinB, cosB, tb)
        P.barrier()

        def norm_transpose(xt_ap, xb, dst_tile, dst_buf, sub, scr_bf, scr_buf, small, small_buf):
            ss = small[:, 0:1]
            rs = small[:, 1:2]
            P.op("act", K("activation", out=scr_bf, in_=xt_ap, func=AF.Square, accum_out=ss),
                 reads=[xb], writes=[scr_buf, small_buf])
            P.op("act", K("activation", out=rs, in_=ss, func=AF.Sqrt, scale=1.0 / D, bias=EPS),
                 reads=[small_buf], writes=[small_buf])
            P.op("dve", K("reciprocal", out=rs, in_=rs), reads=[small_buf], writes=[small_buf])
            P.op("dve", K("tensor_scalar", out=scr_bf, in0=xt_ap, scalar1=rs, scalar2=None, op0=ALU.mult),
                 reads=[xb, small_buf], writes=[scr_buf])
            for half in range(2):
                pt, pb_ = psring.next()
                ptb = pt[:, :].bitcast(BF16)
                for j in range(8):
                    kc = half * 8 + j
                    P.op("pe", K("transpose", out=ptb[:, j * 128:(j + 1) * 128], in_=scr_bf[:, kc * 128:(kc + 1) * 128],
                                 identity=ident), reads=[scr_buf, cst], writes=[pb_], acc=True)
                eng = "act" if half == 0 else "dve"
                dst = dst_tile[:, half * 8:(half + 1) * 8, sub * 128:(sub + 1) * 128]
                src = ptb.rearrange("p (k n) -> p k n", k=8)
                if eng == "act":
                    P.op("act", K("activation", out=dst, in_=src, func=AF.Copy), reads=[pb_], writes=[dst_buf])
                else:
                    P.op("dve", K("tensor_copy", out=dst, in_=src), reads=[pb_], writes=[dst_buf])

        A.reset()
        xr = Ring("x", [A.take(D, F32) for _ in range(2)])
        scr_r = Ring("xn", [A.take(D, BF16) for _ in range(2)])
        sm_r = Ring("sm", [A.take(2, F32) for _ in range(2)])
        hr = Ring("ht", [kcv(A.take(16 * 512, BF16), 16) for _ in range(2)])
        hT_v = hT.rearrange("k p t -> p k t")
        for tt in range(8):
            htile, hb = hr.next()
            for sub in range(4):
                xt, xb = xr.next()
                r0 = (tt % 4) * 512 + sub * 128
                srcx = xp if tt < 4 else xo
                P.load(xt, srcx[r0:r0 + 128, :], xb)
                sc, scb = scr_r.next()
                sm, smb = sm_r.next()
                norm_transpose(xt, xb, htile, hb, sub, sc, scb, sm, smb)
                bg1.step(2 if sub % 2 == 0 else 1)
            P.store(hT_v[:, :, tt * 512:(tt + 1) * 512], htile, hb)
        bg1.flush()
        P.barrier()
        if stop_after == 1:
            return finish(nc, P, st, out)

        A.reset()
        Wg = kcv(A.take(16 * 1024, BF16), 16)
        Wgb = Buf("Wg")
        hr = Ring("h2", [kcv(A.take(16 * 512, BF16), 16) for _ in range(2)])
        qs_r = Ring("qs", [A.take(512, BF16) for _ in range(6)])
        t1_r = Ring("t1", [A.take(512, F32) for _ in range(3)])
        t2_r = Ring("t2", [A.take(512, F32) for _ in range(3)])
        sq_r = Ring("sq", [A.take(512, F32) for _ in range(4)])
        rr_r = Ring("rr", [A.take(512, F32) for _ in range(2)])
        cn_r = Ring("cn", [kcv(A.take(4 * 512, BF16), 4) for _ in range(2)])
        wb_in_v = wb_in.rearrange("(k p) c -> p k c", p=128)

        defq = []

        def run_deferred(keep=0):
            while len(defq) > keep:
                defq.pop(0)()

        def rope_epilogue(pt, pb_, nrow, Rm, cosT, sinT, tok0, dst_dram):
            qs, qb = qs_r.next()
            npart = pt.shape[0]
            P.op("act", K("activation", out=qs[0:npart, :], in_=pt, func=AF.Copy), reads=[pb_], writes=[qb])

            def part_b():
                p2, p2b = psring.next()
                P.op("pe", K("matmul", p2[0:nrow, :], lhsT=Rm[0:npart, 0:nrow], rhs=qs[0:npart, :], start=True, stop=True),
                     reads=[qb, cst], writes=[p2b])
                t1, t1b = t1_r.next()
                t2, t2b = t2_r.next()
                P.op("dve", K("tensor_tensor", out=t1[0:nrow, :], in0=p2[0:nrow, :], in1=sinT[0:nrow, tok0:tok0 + 512],
                              op=ALU.mult), reads=[p2b], writes=[t1b])
                P.op("pool", K("tensor_tensor", out=t2[0:nrow, :], in0=qs[0:nrow, :], in1=cosT[0:nrow, tok0:tok0 + 512],
                               op=ALU.mult), reads=[qb], writes=[t2b])
                P.op("dve", K("tensor_tensor", out=qs[0:nrow, :], in0=t1[0:nrow, :], in1=t2[0:nrow, :], op=ALU.add),
                     reads=[t1b, t2b], writes=[qb])
                P.store(dst_dram, qs[0:npart, :], qb)
            defq.append(part_b)
            run_deferred(keep=1)

        def latent_epilogue(cps, tok0, dst3):
            sqs = []
            for (pt, pb_) in cps:
                sq, sqb = sq_r.next()
                P.op("act", K("activation", out=sq, in_=pt, func=AF.Square), reads=[pb_], writes=[sqb])
                sqs.append((sq, sqb))
            p5, p5b = psring.next()
            for i, (sq, sqb) in enumerate(sqs):
                P.op("pe", K("matmul", p5, lhsT=ones_f, rhs=sq, start=(i == 0), stop=(i == 3)),
                     reads=[sqb, cst], writes=[p5b], acc=(i > 0))
            rr, rrb = rr_r.next()
            P.op("act", K("activation", out=rr, in_=p5, func=AF.Sqrt, scale=1.0 / 512, bias=EPS), reads=[p5b], writes=[rrb])
            P.op("dve", K("reciprocal", out=rr, in_=rr), reads=[rrb], writes=[rrb])
            cn, cnb = cn_r.next()
            for i, (pt, pb_) in enumerate(cps):
                P.op("dve", K("tensor_tensor", out=cn[:, i, :], in0=pt, in1=rr, op=ALU.mult), reads=[pb_, rrb], writes=[cnb])
            P.store(dst3, cn, cnb)

        groups = [
            ("aq", 0, 1024, "own"), ("ak", 1024, 1024, "all"), ("av", 2048, 1024, "all"),
            ("cq", 3072, 512, "own"), ("ckv", 3584, 576, "all"),
        ]
        for (gname, c0, ncol, which) in groups:
            P.load(Wg[:, :, 0:ncol], wb_in_v[:, :, c0:c0 + ncol], Wgb)
            tts = range(4, 8) if which == "own" else range(8)
            for tt in tts:
                htile, hb = hr.next()
                P.load(htile, hT_v[:, :, tt * 512:(tt + 1) * 512], hb)
                bg2.step(2)
                tok0 = tt * 512
                otok0 = tok0 - TO
                if gname in ("aq", "ak"):
                    for ct in range(8):
                        pt, pb_ = psring.next()
                        for kc in range(16):
                            P.op("pe", K("matmul", pt, lhsT=Wg[:, kc, ct * 128:(ct + 1) * 128], rhs=htile[:, kc, :],
                                         start=(kc == 0), stop=(kc == 15)), reads=[Wgb, hb], writes=[pb_], acc=(kc > 0))
                        dst = qA[ct, :, otok0:otok0 + 512] if gname == "aq" else kA[ct, :, tok0:tok0 + 512]
                        rope_epilogue(pt, pb_, 32, RA_b, cosA, sinA, tok0, dst)
                elif gname == "av":
                    for sub in range(4):
                        for hf in range(2):
                            pt, pb_ = psring.next()
                            for kc in range(16):
                                P.op("pe", K("matmul", pt, lhsT=htile[:, kc, sub * 128:(sub + 1) * 128],
                                             rhs=Wg[:, kc, hf * 512:(hf + 1) * 512], start=(kc == 0), stop=(kc == 15)),
                                     reads=[Wgb, hb], writes=[pb_], acc=(kc > 0))
                            qs, qb = qs_r.next()
                            eng = "act" if hf == 0 else "dve"
                            if eng == "act":
                                P.op("act", K("activation", out=qs, in_=pt, func=AF.Copy), reads=[pb_], writes=[qb])
                            else:
                                P.op("dve", K("tensor_copy", out=qs, in_=pt), reads=[pb_], writes=[qb])
                            P.store(vA[tok0 + sub * 128:tok0 + (sub + 1) * 128, hf * 512:(hf + 1) * 512], qs, qb)
                else:
                    cps = []
                    for ct in range(4):
                        pt, pb_ = psring.next()
                        for kc in range(16):
                            P.op("pe", K("matmul", pt, lhsT=Wg[:, kc, ct * 128:(ct + 1) * 128], rhs=htile[:, kc, :],
                                         start=(kc == 0), stop=(kc == 15)), reads=[Wgb, hb], writes=[pb_], acc=(kc > 0))
                        cps.append((pt, pb_))
                    if gname == "cq":
                        latent_epilogue(cps, tok0, cq.rearrange("k p t -> p k t")[:, :, otok0:otok0 + 512])
                    else:
                        latent_epilogue(cps, tok0, ckv.rearrange("k p t -> p k t")[:, :, tok0:tok0 + 512])
                        pt, pb_ = psring.next()
                        for kc in range(16):
                            P.op("pe", K("matmul", pt[0:64, :], lhsT=Wg[:, kc, 512:576], rhs=htile[:, kc, :],
                                         start=(kc == 0), stop=(kc == 15)), reads=[Wgb, hb], writes=[pb_], acc=(kc > 0))
                        rope_epilogue(pt[0:64, :], pb_, 64, RB_b, cosB, sinB, tok0, kr[:, tok0:tok0 + 512])
            run_deferred()
        run_deferred()
        P.barrier()
        if stop_after == 2:
            return finish(nc, P, st, out)

        A.reset()
        Wq = kcv(A.take(4 * 1536, BF16), 4)
        Wqb = Buf("Wq")
        Wkv = kcv(A.take(4 * 2048, BF16), 4)
        Wkvb = Buf("Wkv")
        cr = Ring("c3", [kcv(A.take(4 * 512, BF16), 4) for _ in range(2)])
        qs_r = Ring("qs3", [A.take(512, BF16) for _ in range(8)])
        t1_r = Ring("t13", [A.take(512, F32) for _ in range(3)])
        t2_r = Ring("t23", [A.take(512, F32) for _ in range(3)])
        P.load(Wq, wb_uq.rearrange("(k p) c -> p k c", p=128), Wqb)
        P.load(Wkv, wb_ukv.rearrange("(k p) c -> p k c", p=128), Wkvb)
        cq_v = cq.rearrange("k p t -> p k t")
        ckv_v = ckv.rearrange("k p t -> p k t")
        cpy = [0]

        def copy_out(pt, pb_, dst_dram, npart=128):
            qs, qb = qs_r.next()
            cpy[0] += 1
            if cpy[0] % 2:
                P.op("act", K("activation", out=qs[0:npart, :], in_=pt, func=AF.Copy), reads=[pb_], writes=[qb])
            else:
                P.op("dve", K("tensor_copy", out=qs[0:npart, :], in_=pt), reads=[pb_], writes=[qb])
            P.store(dst_dram, qs[0:npart, :], qb)

        for tt in range(4):
            ctile, cb = cr.next()
            P.load(ctile, cq_v[:, :, tt * 512:(tt + 1) * 512], cb)
            bg2.step(2)
            tok0 = TO + tt * 512
            for h in range(NH):
                pt, pb_ = psring.next()
                for kc in range(4):
                    P.op("pe", K("matmul", pt, lhsT=Wq[:, kc, h * 192:h * 192 + 128], rhs=ctile[:, kc, :],
                                 start=(kc == 0), stop=(kc == 3)), reads=[Wqb, cb], writes=[pb_], acc=(kc > 0))
                copy_out(pt, pb_, qn[h, :, tt * 512:(tt + 1) * 512])
                pt, pb_ = psring.next()
                for kc in range(4):
                    P.op("pe", K("matmul", pt[0:64, :], lhsT=Wq[:, kc, h * 192 + 128:h * 192 + 192], rhs=ctile[:, kc, :],
                                 start=(kc == 0), stop=(kc == 3)), reads=[Wqb, cb], writes=[pb_], acc=(kc > 0))
                rope_epilogue(pt[0:64, :], pb_, 64, RB_b, cosB, sinB, tok0, qr[h, :, tt * 512:(tt + 1) * 512])
        run_deferred()
        for tt in range(8):
            ctile, cb = cr.next()
            P.load(ctile, ckv_v[:, :, tt * 512:(tt + 1) * 512], cb)
            bg2.step(2)
            for h in range(NH):
                pt, pb_ = psring.next()
                for kc in range(4):
                    P.op("pe", K("matmul", pt, lhsT=Wkv[:, kc, h * 256:h * 256 + 128], rhs=ctile[:, kc, :],
                                 start=(kc == 0), stop=(kc == 3)), reads=[Wkvb, cb], writes=[pb_], acc=(kc > 0))
                copy_out(pt, pb_, kn[h, :, tt * 512:(tt + 1) * 512])
            for sub in range(4):
                for hf in range(2):
                    pt, pb_ = psring.next()
                    for kc in range(4):
                        rhs = Wkv[:, kc, hf * 1024:(hf + 1) * 1024].rearrange("p (h c) -> p h c", c=256)[:, :, 128:256]
                        P.op("pe", K("matmul", pt.rearrange("p (h c) -> p h c", c=128), lhsT=ctile[:, kc, sub * 128:(sub + 1) * 128],
                                     rhs=rhs, start=(kc == 0), stop=(kc == 3)), reads=[Wkvb, cb], writes=[pb_], acc=(kc > 0))
                    r0 = tt * 512 + sub * 128
                    copy_out(pt, pb_, vB[r0:r0 + 128, hf * 512:(hf + 1) * 512])
        run_deferred()
        P.barrier()
        if stop_after == 3:
            return finish(nc, P, st, out)

        A.base = base_bg
        A.reset()
        Qr = Ring("Qa", [A.take(TO, BF16) for _ in range(2)])
        Kr = Ring("Ka", [A.take(TL, BF16) for _ in range(2)])
        Vr = Ring("Va", [A.take(32 * 128, BF16).rearrange("p (b c) -> p b c", c=128) for _ in range(2)])
        AZr = Ring("AZ", [A.take(2 * TO, F32).rearrange("p (a t) -> p a t", a=2) for _ in range(2)])
        Ptr = Ring("Pt", [A.take(256, BF16) for _ in range(4)])
        Mxr = Ring("Mxa", [A.take(TO, BF16) for _ in range(2)])
        zr_t = A.take(TO, F32)
        zrb = Buf("zr")
        sc_a = float(HD ** -0.5)
        Sring = Ring("S4", [None] * 4)
        Sring.items = [psum[0], psum[1], psum[2], psum[3]]
        Oring = Ring("O4", [None] * 4)
        Oring.items = [psum[4], psum[5], psum[6], psum[7]]
        CFG = (1, 4, 16)
        LOOK = 2

        def load_head4(h):
            Q, Qb = Qr.next()
            Kt, Kb = Kr.next()
            P.load(Q, qA[h], Qb)
            P.load(Kt, kA[h], Kb)
            return (Q, Qb, Kt, Kb)

        def load_v4(h, d):
            Vt, Vb = Vr.next()
            nblk_n = TL // (128 * d)
            srcv = vA[:, h * 128:(h + 1) * 128].rearrange("(n i r) c -> i n r c", i=128, r=d)
            dstv = Vt.rearrange("p (n r) c -> p n r c", r=d)
            for n in range(nblk_n):
                P.load(dstv[:, n], srcv[:, n], Vb)
            return (Vt, Vb)

        def s_stage4(ctx, d, kb):
            Q, Qb, Kt, Kb = ctx
            has_cur = kb >= 16
            has_next = kb + d <= 31
            n, r = kb // d, kb % d
            ks0 = n * 128 * d + r
            Kblk = Kt[:, ks0:ks0 + 127 * d + 1:d]
            if has_cur and has_next:
                N = 256
                qs0 = ks0 - TO
                msk = mask_b[:, 128:384]
            elif has_cur:
                N = 128
                qs0 = ks0 - TO
                msk = mask_cur
            else:
                N = 128
                qs0 = ks0 + 128 * d - TO
                msk = mask_prev
            Qsl = Q[:, qs0:qs0 + (N - 1) * d + 1:d]
            S, Sb = Sring.next()
            P.op("pe", K("matmul", S[:, 0:N], lhsT=ident, rhs=msk, start=True, stop=False), reads=[cst], writes=[Sb])
            P.op("pe", K("matmul", S[:, 0:N], lhsT=Kblk, rhs=Qsl, start=False, stop=True),
                 reads=[Kb, Qb], writes=[Sb], acc=True)
            Pt, Ptb = Ptr.next()
            P.op("act", K("activation", out=Pt[:, 0:N], in_=S[:, 0:N], func=AF.Exp, scale=sc_a), reads=[Sb], writes=[Ptb])
            return (Pt, Ptb, N, qs0)

        def o_stage4(vt, AZc, d, kb, st_):
            Vt, Vb = vt
            AZ, AZb = AZc
            Pt, Ptb, N, qs0 = st_
            O, Ob = Oring.next()
            P.op("pe", K("matmul", O[:, 0:N], lhsT=Vt[:, kb, :], rhs=Pt[:, 0:N], start=True, stop=True),
                 reads=[Vb, Ptb], writes=[Ob])
            P.op("pe", K("matmul", O[:, 256:256 + N], lhsT=(pv_b if kb < 16 else ones_b), rhs=Pt[:, 0:N],
                         start=True, stop=True), reads=[cst, Ptb], writes=[Ob], acc=True)
            azv = AZ[:, :, qs0:qs0 + (N - 1) * d + 1:d]
            osrc = O[:, 0:512].rearrange("p (a t) -> p a t", a=2)[:, :, 0:N]
            P.op("dve", K("tensor_tensor", out=azv, in0=azv, in1=osrc, op=ALU.add), reads=[Ob, AZb], writes=[AZb])

        def fin_head4(h, AZc):
            AZ, AZb = AZc
            Mx, Mxb = Mxr.next()
            P.op("dve", K("reciprocal", out=zr_t, in_=AZ[:, 1, :]), reads=[AZb], writes=[zrb])
            P.op("pool", K("tensor_tensor", out=Mx, in0=AZ[:, 0, :], in1=zr_t, op=ALU.mult), reads=[AZb, zrb], writes=[Mxb])
            P.store(mixT[h], Mx, Mxb)

        pend = []
        nxt_ctx = load_head4(0)
        nxt_v = load_v4(0, CFG[0])
        for h in range(NH):
            ctx = nxt_ctx
            AZc = AZr.next()
            P.op("pool", K("memset", AZc[0], 0.0), writes=[AZc[1]])
            for ci_, d in enumerate(CFG):
                vt = nxt_v
                bg2.step(2)
                kbs = list(range(16 - d, 32))
                for i_, kb in enumerate(kbs):
                    if i_ == LOOK + 1:
                        if ci_ < 2:
                            nxt_v = load_v4(h, CFG[ci_ + 1])
                        elif h + 1 < NH:
                            nxt_ctx = load_head4(h + 1)
                            nxt_v = load_v4(h + 1, CFG[0])
                    st_ = s_stage4(ctx, d, kb)
                    pend.append((vt, AZc, d, kb, st_, (h if (ci_ == 2 and kb == 31) else None)))
                    if len(pend) > LOOK:
                        it = pend.pop(0)
                        o_stage4(*it[:5])
                        if it[5] is not None:
                            fin_head4(it[5], it[1])
        while pend:
            it = pend.pop(0)
            o_stage4(*it[:5])
            if it[5] is not None:
                fin_head4(it[5], it[1])
        P.barrier()
        if stop_after == 4:
            return finish(nc, P, st, out)

        A.reset()
        Qnr = Ring("Qn", [A.take(TO, BF16) for _ in range(2)])
        Qrr = Ring("Qr", [A.take(TO, BF16) for _ in range(2)])
        Knr = Ring("Kn", [A.take(TL, BF16) for _ in range(2)])
        KR = A.take(TL, BF16)
        KRb = Buf("KR")
        Vbr = Ring("Vb", [A.take(32 * 132, BF16).rearrange("p (b c) -> p b c", c=132) for _ in range(2)])
        Ptr = Ring("Pt5", [A.take(512, BF16) for _ in range(4)])
        Onr = Ring("On", [A.take(128, BF16) for _ in range(2)])
        rzr = Ring("rz", [A.take(1, F32) for _ in range(2)])
        Mxr = Ring("Mx5", [A.take(512, BF16) for _ in range(2)])
        sc_b = float(192 ** -0.5)
        Opairs = [(psum[0], psum[1]), (psum[2], psum[3])]
        Sring = Ring("S5", [None] * 3)
        Sring.items = [psum[4], psum[5], psum[6]]
        Tring = Ring("T5", [None])
        Tring.items = [psum[7]]
        P.op("pool", K("memset", KR, 0.0), writes=[KRb])
        P.load(KR[0:64, :], kr, KRb)
        for (Qr_t, Qrb) in Qrr.items:
            P.op("pool", K("memset", Qr_t, 0.0), writes=[Qrb])
        for (Vb_t, Vbb) in Vbr.items:
            P.op("pool", K("memset", Vb_t[:, 16:32, 128:129], 1.0), writes=[Vbb])
            P.op("pool", K("tensor_copy", out=Vb_t[:, 0:16, 128:129], in_=pv_b[:, 0:16].rearrange("p (b c) -> p b c", c=1)),
                 reads=[cst], writes=[Vbb])

        def load_head5(h):
            Qn_t, Qnb = Qnr.next()
            Qr_t, Qrb = Qrr.next()
            Kn_t, Knb = Knr.next()
            Vb_t, Vbb = Vbr.next()
            P.load(Qn_t, qn[h], Qnb)
            P.load(Qr_t[0:64, :], qr[h], Qrb)
            P.load(Kn_t, kn[h], Knb)
            srcv = vB[:, h * 128:(h + 1) * 128].rearrange("(b i) c -> i b c", i=128)
            for b4 in range(4):
                P.load(Vb_t[:, b4 * 8:(b4 + 1) * 8, 0:128], srcv[:, b4 * 8:(b4 + 1) * 8, :], Vbb)
            return (Qn_t, Qnb, Qr_t, Qrb, Kn_t, Knb, Vb_t, Vbb)

        def s_stage5(ctx, g, kb):
            Qn_t, Qnb, Qr_t, Qrb, Kn_t, Knb, Vb_t, Vbb = ctx
            j0 = max(0, kb - 16 - 4 * g)
            c0 = j0 * 128
            diag = kb >= 16 + 4 * g
            S, Sb = Sring.next()
            q0 = g * 512 + c0
            if diag:
                P.op("pe", K("matmul", S[:, c0:c0 + 128], lhsT=ident, rhs=mask_cur, start=True, stop=False),
                     reads=[cst], writes=[Sb])
            P.op("pe", K("matmul", S[:, c0:512], lhsT=Kn_t[:, kb * 128:(kb + 1) * 128], rhs=Qn_t[:, q0:g * 512 + 512],
                         start=(not diag), stop=False), reads=[Knb, Qnb], writes=[Sb], acc=diag)
            P.op("pe", K("matmul", S[:, c0:512], lhsT=KR[:, kb * 128:(kb + 1) * 128], rhs=Qr_t[:, q0:g * 512 + 512],
                         start=False, stop=True), reads=[KRb, Qrb], writes=[Sb], acc=True)
            Pt, Ptb = Ptr.next()
            P.op("act", K("activation", out=Pt[:, c0:512], in_=S[:, c0:512], func=AF.Exp, scale=sc_b),
                 reads=[Sb], writes=[Ptb])
            return (Pt, Ptb, j0)

        def pv_stage5(ctx, h, g, kb, opair, Mxc, st_):
            Vb_t, Vbb = ctx[6], ctx[7]
            Pt, Ptb, j0 = st_
            Mx, Mxb = Mxc
            for j in range(j0, 4):
                Ot, Otb = opair[0] if j < 2 else opair[1]
                oc = (j % 2) * 256
                last = (kb == 16 + 4 * g + j)
                P.op("pe", K("matmul", Ot[:, oc:oc + 129], lhsT=Pt[:, j * 128:(j + 1) * 128], rhs=Vb_t[:, kb, 0:129],
                             start=False, stop=last), reads=[Ptb, Vbb], writes=[Otb], acc=True)
                if last:
                    rz, rzb = rzr.next()
                    On, Onb = Onr.next()
                    P.op("dve", K("reciprocal", out=rz, in_=Ot[:, oc + 128:oc + 129]), reads=[Otb], writes=[rzb])
                    P.op("dve", K("tensor_scalar", out=On, in0=Ot[:, oc:oc + 128], scalar1=rz, scalar2=None, op0=ALU.mult),
                         reads=[Otb, rzb], writes=[Onb])
                    T, Tb = Tring.next()
                    Tb16 = T[:, :].bitcast(BF16)
                    P.op("pe", K("transpose", out=Tb16[:, 0:128], in_=On, identity=ident), reads=[Onb, cst], writes=[Tb])
                    P.op("act", K("activation", out=Mx[:, j * 128:(j + 1) * 128], in_=Tb16[:, 0:128], func=AF.Copy),
                         reads=[Tb], writes=[Mxb])
                    if j == 3:
                        P.store(mixT[8 + h, :, g * 512:(g + 1) * 512], Mx, Mxb)

        pend = []
        gi = 0
        nxt_ctx = load_head5(0)
        for h in range(NH):
            ctx = nxt_ctx
            for g in range(4):
                if g == 1 and h + 1 < NH:
                    nxt_ctx = load_head5(h + 1)
                bg2.step(1)
                opair = Opairs[gi % 2]
                gi += 1
                Mxc = Mxr.next()
                for (Ot, Otb) in opair:
                    P.op("pe", K("matmul", Ot, lhsT=zeros_b[:, 0:128], rhs=zeros_b, start=True, stop=False),
                         reads=[cst], writes=[Otb])
                for kb in range(16 + 4 * g + 4):
                    st_ = s_stage5(ctx, g, kb)
                    pend.append((ctx, h, g, kb, opair, Mxc, st_))
                    if len(pend) > LOOK:
                        pv_stage5(*pend.pop(0))
        while pend:
            pv_stage5(*pend.pop(0))
        bg2.flush()
        P.barrier()
        if stop_after == 5:
            return finish(nc, P, st, out)

        A.base = base_const
        A.reset()
        Wo = kcv(A.take(16 * D, BF16), 16)
        Wob = Buf("Wo")
        gp = A.take(D, F32)
        gpb = Buf("gp")
        mr = Ring("m6", [kcv(A.take(16 * 512, BF16), 16) for _ in range(2)])
        xr = Ring("x6", [A.take(D, F32) for _ in range(2)])
        yr = Ring("y6", [A.take(D, F32) for _ in range(2)])
        x1r = Ring("x16", [A.take(D, F32) for _ in range(2)])
        scr_r = Ring("xn6", [A.take(D, BF16) for _ in range(4)])
        sm_r = Ring("sm6", [A.take(8, F32) for _ in range(4)])
        hr = Ring("ht6", [kcv(A.take(16 * 512, BF16), 16) for _ in range(2)])
        P.load(Wo, wb_out.rearrange("(k p) c -> p k c", p=128), Wob)
        P.load(gp, g_post, gpb)
        mixT_v = mixT.rearrange("k p t -> p k t")
        h2T_v = h2T.rearrange("k p t -> p k t")
        defq = []

        def transposes6(sc, scb, htile, hb, sub, tt, last):
            def f():
                for half in range(2):
                    pt, pb_ = psring.next()
                    ptb = pt[:, :].bitcast(BF16)
                    for j in range(8):
                        kc = half * 8 + j
                        P.op("pe", K("transpose", out=ptb[:, j * 128:(j + 1) * 128], in_=sc[:, kc * 128:(kc + 1) * 128],
                                     identity=ident), reads=[scb, cst], writes=[pb_], acc=True)
                    dst = htile[:, half * 8:(half + 1) * 8, sub * 128:(sub + 1) * 128]
                    src = ptb.rearrange("p (k n) -> p k n", k=8)
                    if half == 0:
                        P.op("act", K("activation", out=dst, in_=src, func=AF.Copy), reads=[pb_], writes=[hb])
                    else:
                        P.op("dve", K("tensor_copy", out=dst, in_=src), reads=[pb_], writes=[hb])
                if last:
                    P.store(h2T_v[:, :, tt * 512:(tt + 1) * 512], htile, hb)
            return f

        for tt in range(4):
            mt, mb = mr.next()
            P.load(mt, mixT_v[:, :, tt * 512:(tt + 1) * 512], mb)
            htile, hb = hr.next()
            for sub in range(4):
                r0 = tt * 512 + sub * 128
                xt, xb = xr.next()
                P.load(xt, xo[r0:r0 + 128, :], xb)
                yt, yb = yr.next()
                for dt_ in range(4):
                    pt, pb_ = psring.next()
                    for kc in range(16):
                        P.op("pe", K("matmul", pt, lhsT=mt[:, kc, sub * 128:(sub + 1) * 128], rhs=Wo[:, kc, dt_ * 512:(dt_ + 1) * 512],
                                     start=(kc == 0), stop=(kc == 15)), reads=[mb, Wob], writes=[pb_], acc=(kc > 0))
                    P.op("dve", K("tensor_copy", out=yt[:, dt_ * 512:(dt_ + 1) * 512], in_=pt), reads=[pb_], writes=[yb])
                while len(defq) > 1:
                    defq.pop(0)()
                sm, smb = sm_r.next()
                sc, scb = scr_r.next()
                P.op("act", K("activation", out=sc, in_=yt, func=AF.Square, accum_out=sm[:, 0:1]), reads=[yb], writes=[scb, smb])
                P.op("act", K("activation", out=sm[:, 1:2], in_=sm[:, 0:1], func=AF.Sqrt, scale=1.0 / D, bias=EPS),
                     reads=[smb], writes=[smb])
                P.op("dve", K("reciprocal", out=sm[:, 1:2], in_=sm[:, 1:2]), reads=[smb], writes=[smb])
                P.op("act", K("activation", out=yt, in_=yt, func=AF.Copy, scale=sm[:, 1:2]), reads=[yb, smb], writes=[yb])
                P.op("dve", K("tensor_tensor", out=yt, in0=yt, in1=gp, op=ALU.mult), reads=[yb, gpb], writes=[yb])
                x1t, x1b = x1r.next()
                P.op("pool", K("tensor_tensor", out=x1t, in0=yt, in1=xt, op=ALU.add), reads=[yb, xb], writes=[x1b])
                P.store(x1[r0:r0 + 128, :], x1t, x1b)
                P.op("act", K("activation", out=sc, in_=x1t, func=AF.Square, accum_out=sm[:, 2:3]), reads=[x1b], writes=[scb, smb])
                P.op("act", K("activation", out=sm[:, 3:4], in_=sm[:, 2:3], func=AF.Sqrt, scale=1.0 / D, bias=EPS),
                     reads=[smb], writes=[smb])
                P.op("dve", K("reciprocal", out=sm[:, 3:4], in_=sm[:, 3:4]), reads=[smb], writes=[smb])
                P.op("dve", K("tensor_scalar", out=sc, in0=x1t, scalar1=sm[:, 3:4], scalar2=None, op0=ALU.mult),
                     reads=[x1b, smb], writes=[scb])
                defq.append(transposes6(sc, scb, htile, hb, sub, tt, sub == 3))
        while defq:
            defq.pop(0)()
        P.barrier()
        if stop_after == 6:
            return finish(nc, P, st, out)

        A.reset()
        gp2 = A.take(D, F32)
        gp2b = Buf("gp2")
        P.load(gp2, g_post2, gp2b)
        h2r = Ring("h2t", [kcv(A.take(16 * 512, BF16), 16) for _ in range(2)])
        U = kcv(A.take(64 * 512, BF16), 64)
        Ub = Buf("U")
        Wur = Ring("Wu", [kcv(A.take(16 * 256, BF16), 16) for _ in range(3)])
        Wdr = Ring("Wd", [kcv(A.take(4 * 1024, BF16), 4) for _ in range(3)])
        rl_r = Ring("rl", [A.take(512, F32) for _ in range(3)])
        Y = A.take(4 * D, F32).rearrange("p (s c) -> p s c", s=4)
        Yb = Buf("Y")
        sm7 = A.take(32, F32)
        sm7b = Buf("sm7")
        junk = A.take(1024, BF16)
        junkb = Buf("junk")
        x1r = Ring("x17", [A.take(D, F32) for _ in range(1)])
        wb_up_v = wb_up.rearrange("(k p) c -> p k c", p=128)
        wb_down_v = wb_down.rearrange("(k p) c -> p k c", p=128)
        out_ops = []
        import os as _os
        _ntt = int(_os.environ.get("F7_NTT", "4"))
        _mode = _os.environ.get("F7_MODE", "full")
        nxt_h2 = h2r.next()
        P.load(nxt_h2[0], h2T_v[:, :, 0:512], nxt_h2[1])
        for tt in range(_ntt):
            h2, h2b = nxt_h2
            for fg in range(32):
                Wu, Wub = Wur.next()
                P.load(Wu, wb_up_v[:, :, fg * 256:(fg + 1) * 256], Wub)
                for f in range(2):
                    ft = fg * 2 + f
                    pt, pb_ = psring.next()
                    for kc in range(16):
                        P.op("pe", K("matmul", pt, lhsT=Wu[:, kc, f * 128:(f + 1) * 128], rhs=h2[:, kc, :],
                                     start=(kc == 0), stop=(kc == 15)), reads=[Wub, h2b], writes=[pb_], acc=(kc > 0))
                    rl, rlb = rl_r.next()
                    P.op("act", K("activation", out=rl, in_=pt, func=AF.Relu), reads=[pb_], writes=[rlb])
                    P.op("dve", K("tensor_tensor", out=U[:, ft, :], in0=rl, in1=rl, op=ALU.mult), reads=[rlb], writes=[Ub])
            if tt + 1 < _ntt:
                nxt_h2 = h2r.next()
                P.load(nxt_h2[0], h2T_v[:, :, (tt + 1) * 512:(tt + 2) * 512], nxt_h2[1])
            for dh in range(2 if _mode != "up" else 0):
                accs = [psring.next() for _ in range(8)]
                for fg in range(16):
                    Wd, Wdb = Wdr.next()
                    P.load(Wd, wb_down_v[:, fg * 4:(fg + 1) * 4, dh * 1024:(dh + 1) * 1024], Wdb)
                    for f in range(4):
                        ft = fg * 4 + f
                        for sub in range(4):
                            for dt_ in range(2):
                                pt, pb_ = accs[sub * 2 + dt_]
                                P.op("pe", K("matmul", pt, lhsT=U[:, ft, sub * 128:(sub + 1) * 128],
                                             rhs=Wd[:, f, dt_ * 512:(dt_ + 1) * 512], start=(ft == 0), stop=(ft == 63)),
                                     reads=[Ub, Wdb], writes=[pb_], acc=(ft > 0))
                for sub in range(4):
                    for dt_ in range(2):
                        pt, pb_ = accs[sub * 2 + dt_]
                        col = dh * 2 + dt_
                        _ev = _os.environ.get("F7_EV", "both")
                        P.op("dve", K("tensor_copy", out=Y[:, sub, col * 512:(col + 1) * 512], in_=pt), reads=[pb_], writes=[Yb])
                        P.op("act", K("activation", out=junk[:, 0:512], in_=Y[:, sub, col * 512:(col + 1) * 512], func=AF.Square,
                                      accum_out=sm7[:, sub * 4 + col:sub * 4 + col + 1]), reads=[Yb], writes=[junkb, sm7b])
            for sub in range(4 if _mode == "full" else 0):
                r0 = tt * 512 + sub * 128
                x1t, x1b = x1r.next()
                P.load(x1t, x1[r0:r0 + 128, :], x1b, eng="pool")
                s4 = sm7[:, sub * 4:sub * 4 + 4]
                t2 = sm7[:, 16 + sub * 4:16 + sub * 4 + 2]
                ssum = sm7[:, 16 + sub * 4 + 2:16 + sub * 4 + 3]
                rs7 = sm7[:, 16 + sub * 4 + 3:16 + sub * 4 + 4]
                P.op("dve", K("tensor_tensor", out=t2, in0=s4[:, 0:2], in1=s4[:, 2:4], op=ALU.add), reads=[sm7b], writes=[sm7b])
                P.op("dve", K("tensor_tensor", out=ssum, in0=t2[:, 0:1], in1=t2[:, 1:2], op=ALU.add), reads=[sm7b], writes=[sm7b])
                P.op("act", K("activation", out=rs7, in_=ssum, func=AF.Sqrt, scale=1.0 / D, bias=EPS), reads=[sm7b], writes=[sm7b])
                P.op("dve", K("reciprocal", out=rs7, in_=rs7), reads=[sm7b], writes=[sm7b])
                P.op("act", K("activation", out=Y[:, sub, :], in_=Y[:, sub, :], func=AF.Copy, scale=rs7), reads=[Yb, sm7b], writes=[Yb])
                ee = "dve" if tt == _ntt - 1 else "pool"
                P.op(ee, K("tensor_tensor", out=Y[:, sub, :], in0=Y[:, sub, :], in1=gp2, op=ALU.mult), reads=[Yb, gp2b], writes=[Yb])
                P.op(ee, K("tensor_tensor", out=x1t, in0=Y[:, sub, :], in1=x1t, op=ALU.add), reads=[Yb, x1b], writes=[x1b])
                out_ops.append(P.store(out[r0:r0 + 128, :], x1t, x1b))
        P.barrier()
        return finish(nc, P, st, out)


def finish(nc, P, st, out):
    P.barrier()
    with nc.Block() as block:
        P.emit(block)
    return nc


def _consts():
    j = np.arange(128)[:, None]
    i = np.arange(128)[None, :]
    mask_prev = np.where(j >= i, 0.0, NEG).astype(np.float32)
    mask_cur = np.where(j <= i, 0.0, NEG).astype(np.float32)
    cmask = np.concatenate([mask_prev, mask_cur], axis=1)
    RA = np.zeros((128, 128), np.float32)
    for m in range(16):
        RA[m + 16, m] = -1.0
        RA[m, m + 16] = 1.0
    RB = np.zeros((128, 128), np.float32)
    for m in range(32):
        RB[m + 32, m] = -1.0
        RB[m, m + 32] = 1.0
    p = np.arange(128)
    invA = (THETA ** (-(2.0 * (p % 16)) / 32.0)).astype(np.float32)
    invB = (THETA ** (-(2.0 * (p % 32)) / 64.0)).astype(np.float32)
    invf = np.stack([invA, invB], axis=1).astype(np.float32)
    return cmask, RA, RB, invf


def make_in_maps(x, positions, norm_attn_pre, norm_attn_post, w_in, q_latent_norm, kv_latent_norm,
                 w_uq, w_ukv, w_out, norm_mlp_pre, norm_mlp_post, w_up, w_down):
    x = np.asarray(x, np.float32)
    positions = np.asarray(positions, np.int32)
    cmask, RA, RB, invf = _consts()

    def col(g, k):
        return np.ascontiguousarray(np.asarray(g, np.float32).reshape(k, 128).T)

    def bc(g):
        return np.ascontiguousarray(np.broadcast_to(np.asarray(g, np.float32).reshape(1, -1), (128, D)))

    shared = {
        "cmask": cmask, "cRA": RA, "cRB": RB, "cinvf": invf,
        "g_in": col(norm_attn_pre[0], 16), "g_up": col(norm_mlp_pre[0], 16),
        "g_q": col(q_latent_norm[0], 4), "g_kv": col(kv_latent_norm[0], 4),
        "g_post": bc(norm_attn_post[0]), "g_post2": bc(norm_mlp_post[0]),
        "w_in": np.ascontiguousarray(np.asarray(w_in[0], np.float32)),
        "w_uq": np.ascontiguousarray(np.asarray(w_uq[0], np.float32)),
        "w_ukv": np.ascontiguousarray(np.asarray(w_ukv[0], np.float32)),
        "w_out": np.ascontiguousarray(np.asarray(w_out[0], np.float32)),
        "w_up": np.ascontiguousarray(np.asarray(w_up[0], np.float32)),
        "w_down": np.ascontiguousarray(np.asarray(w_down[0], np.float32)),
    }
    maps = []
    for c in range(8):
        b, half = c // 2, c % 2
        m = dict(shared)
        m["xo"] = np.ascontiguousarray(x[b, half * TO:(half + 1) * TO])
        m["xp"] = np.ascontiguousarray(x[b, 0:TO]) if half else np.zeros((TO, D), np.float32)
        pl = np.concatenate([positions[b, 0:TO], positions[b, half * TO:(half + 1) * TO]])
        m["posb"] = np.ascontiguousarray(np.broadcast_to(pl.reshape(1, TL), (128, TL))).astype(np.int32)
        m["pvt"] = np.full((128, 128), float(half), np.float32)
        maps.append(m)
    return maps


_NC_CACHE = {}


def kernel(**inputs):
    maps = make_in_maps(**inputs)
    if "nc" not in _NC_CACHE:
        _NC_CACHE["nc"] = build_program()
    nc = _NC_CACHE["nc"]
    res = run_bass_kernel_spmd(nc, maps, core_ids=list(range(8)))
    outp = np.empty((NB, SEQ, D), np.float32)
    for c in range(8):
        b, half = c // 2, c % 2
        outp[b, half * TO:(half + 1) * TO] = np.asarray(res.results[c]["out"], np.float32)
    return outp
```

```python
import numpy as np
from contextlib import ExitStack
import concourse.bass as bass
import concourse.mybir as mybir
from concourse.bass_utils import run_bass_kernel_spmd

F32 = mybir.dt.float32
BF16 = mybir.dt.bfloat16
I32 = mybir.dt.int32
AF = mybir.ActivationFunctionType
ALU = mybir.AluOpType
AX = mybir.AxisListType

ENGS = ("pe", "act", "dve", "pool", "sp")

D = 2048
SEQ = 4096
NB = 4
TL = 4096
TO = 2048
HD = 128
NH = 8
DFF = 8192
INC = 4160
EPS = 1e-6
NEG = -30000.0
THETA = 500000.0
TWO_PI = float(2.0 * np.pi)
PI = float(np.pi)


class Buf:
    __slots__ = ("name", "last_w", "readers", "sem", "sem_val", "last_dma", "excl", "_last_was_load")

    def __init__(self, name, excl=False):
        self.name = name
        self._last_was_load = False
        self.excl = excl
        self.last_w = None
        self.readers = []
        self.sem = None
        self.sem_val = 0
        self.last_dma = None


class Op:
    __slots__ = ("eng", "idx", "fn", "deps", "signal", "is_dma", "sem", "val", "waits")

    def __init__(self, eng, idx, fn):
        self.eng = eng
        self.idx = idx
        self.fn = fn
        self.deps = []
        self.signal = False
        self.is_dma = False
        self.sem = None
        self.val = None
        self.waits = None


def K(name, *args, **kw):
    return lambda e: getattr(e, name)(*args, **kw)


class Prog:
    def __init__(self, nc, stack):
        self.nc = nc
        self.stack = stack
        self.ops = {e: [] for e in ENGS}
        self.eng_sem = {e: stack.enter_context(nc.semaphore("S_" + e)) for e in ENGS}
        self.dma_bufs = []
        self.same_engine_sync = True

    def _add(self, eng, fn, reads, writes, acc=False):
        op = Op(eng, len(self.ops[eng]), fn)
        self.ops[eng].append(op)
        deps = []
        for b in reads:
            if b.last_w is not None:
                deps.append(b.last_w)
            if b.excl:
                deps.extend(r for r in b.readers if r.eng != eng)
        for b in writes:
            if b.last_w is not None and not (acc and b.last_w.eng == eng):
                deps.append(b.last_w)
            deps.extend(b.readers)
        op.deps = deps
        for b in reads:
            b.readers.append(op)
        for b in writes:
            b.last_w = op
            b.readers = []
        return op

    def op(self, eng, fn, reads=(), writes=(), acc=False):
        return self._add(eng, fn, list(reads), list(writes), acc)

    def dma(self, eng, out_ap, in_ap, sbuf_buf, reads=(), writes=()):
        b = sbuf_buf
        if b.sem is None:
            b.sem = self.stack.enter_context(self.nc.semaphore("D_" + b.name))
            self.dma_bufs.append(b)
        op = self._add(eng, None, list(reads), list(writes))
        if b.last_dma is not None and not (writes and not reads and b.last_dma.fn is not None and b.last_dma.waits is None and getattr(b, "_last_was_load", False)):
            op.deps.append(b.last_dma)
        b._last_was_load = bool(writes) and not reads
        b.sem_val += 16
        op.is_dma = True
        op.sem = b.sem
        op.val = b.sem_val
        op.fn = (out_ap, in_ap)
        b.last_dma = op
        return op

    def load(self, dst, src, buf, eng="sp"):
        return self.dma(eng, dst, src, buf, writes=[buf])

    def store(self, dst, src, buf, eng="pool"):
        return self.dma(eng, dst, src, buf, reads=[buf])

    def barrier(self):
        deps = []
        for e in ENGS:
            if self.ops[e]:
                deps.append(self.ops[e][-1])
        for b in self.dma_bufs:
            if b.last_dma is not None:
                deps.append(b.last_dma)
        for e in ENGS:
            op = self._add(e, None, [], [])
            op.deps = list(deps)

    def finalize(self):
        for e in ENGS:
            known_idx = {x: -1 for x in ENGS}
            known_dma = {}
            for op in self.ops[e]:
                waits = []
                best = {}
                for d in op.deps:
                    if d.is_dma:
                        k = id(d.sem)
                        if known_dma.get(k, 0) >= d.val:
                            continue
                        known_dma[k] = d.val
                        waits.append(d)
                    else:
                        if d.fn is None:
                            continue
                        if d.eng == e and (e == "pe" or not self.same_engine_sync):
                            continue
                        if d.idx <= known_idx[d.eng]:
                            continue
                        if d.eng not in best or best[d.eng].idx < d.idx:
                            best[d.eng] = d
                for x, d in best.items():
                    known_idx[x] = d.idx
                    d.signal = True
                    waits.append(d)
                op.waits = waits
        for e in ENGS:
            c = 0
            for op in self.ops[e]:
                if not op.is_dma and op.signal:
                    c += 1
                    op.sem = self.eng_sem[e]
                    op.val = c

    def emit(self, block):
        self.finalize()
        P = self

        def run(e, eng):
            for op in P.ops[e]:
                seen = {}
                for d in op.waits:
                    k = id(d.sem)
                    if k not in seen or seen[k][1] < d.val:
                        seen[k] = (d.sem, d.val)
                for sem, val in seen.values():
                    eng.wait_ge(sem, val)
                if op.is_dma:
                    o, i = op.fn
                    eng.dma_start(out=o, in_=i).then_inc(op.sem, 16)
                elif op.fn is not None:
                    ins = op.fn(eng)
                    if op.signal:
                        ins.then_inc(op.sem, 1)

        @block.tensor
        def _(eng):
            run("pe", eng)

        @block.scalar
        def _(eng):
            run("act", eng)

        @block.vector
        def _(eng):
            run("dve", eng)

        @block.gpsimd
        def _(eng):
            run("pool", eng)

        @block.sync
        def _(eng):
            run("sp", eng)


class Ring:
    def __init__(self, name, aps):
        self.items = [(ap, Buf("%s%d" % (name, i))) for i, ap in enumerate(aps)]
        self.i = 0

    def next(self):
        it = self.items[self.i % len(self.items)]
        self.i += 1
        return it


class Arena:
    def __init__(self, t, nbytes):
        self.t = t
        self.n = nbytes
        self.base = 0
        self.off = 0

    def persist(self):
        self.base = self.off

    def reset(self):
        self.off = self.base

    def take(self, nelem, dt, parts=128):
        sz = 2 if dt == BF16 else 4
        nb = (nelem * sz + 63) // 64 * 64
        o = self.off
        self.off += nb
        assert self.off <= self.n, "SBUF arena overflow %d > %d" % (self.off, self.n)
        ap = self.t[0:parts, o // 4:(o + nelem * sz + 3) // 4]
        if dt != F32:
            ap = ap.bitcast(dt)
        return ap


DEBUG_DUMP = False

class BG:
    def __init__(self, P, wst, wob, wlist):
        self.P = P
        self.wst = wst
        self.wob = wob
        self.tiles = []
        for (src, dst, R, C, gcol) in wlist:
            for kc in range(R // 128):
                for c0 in range(0, C, 2048):
                    cw = min(2048, C - c0)
                    self.tiles.append((src[kc * 128:(kc + 1) * 128, c0:c0 + cw], dst[kc * 128:(kc + 1) * 128, c0:c0 + cw], cw,
                                       None if gcol is None else gcol[:, kc:kc + 1]))
        self.il = 0
        self.ic = 0
        self.slots = {}

    def step(self, n=1):
        P = self.P
        for _ in range(n):
            while self.il < min(len(self.tiles), self.ic + 3):
                src, dst, cw, g = self.tiles[self.il]
                sap, sb = self.wst.next()
                self.slots[self.il] = (sap, sb)
                P.load(sap[:, 0:cw], src, sb)
                self.il += 1
            if self.ic < len(self.tiles):
                src, dst, cw, g = self.tiles[self.ic]
                sap, sb = self.slots.pop(self.ic)
                oap, ob = self.wob.next()
                if g is None:
                    P.op("act", K("activation", out=oap[:, 0:cw], in_=sap[:, 0:cw], func=AF.Copy), reads=[sb], writes=[ob])
                else:
                    P.op("act", K("activation", out=oap[:, 0:cw], in_=sap[:, 0:cw], func=AF.Copy, scale=g), reads=[sb], writes=[ob])
                P.store(dst, oap[:, 0:cw], ob)
                self.ic += 1

    def flush(self):
        while self.ic < len(self.tiles):
            self.step()


C1 = 6.28125
C2 = float(2.0 * np.pi - 6.28125)


def rope_table(P, posf, invcol, ang, ni, nf, mm_, sin_out, cos_out, tb):
    w = dict(reads=[tb], writes=[tb])
    P.op("dve", K("tensor_scalar", out=ang, in0=posf, scalar1=invcol, scalar2=None, op0=ALU.mult), **w)
    P.op("dve", K("tensor_scalar", out=ni, in0=ang, scalar1=1.0 / TWO_PI, scalar2=None, op0=ALU.mult), **w)
    P.op("dve", K("tensor_copy", out=nf, in_=ni), **w)
    P.op("dve", K("scalar_tensor_tensor", out=ang, in0=nf, scalar=-C1, in1=ang, op0=ALU.mult, op1=ALU.add), **w)
    P.op("dve", K("scalar_tensor_tensor", out=ang, in0=nf, scalar=-C2, in1=ang, op0=ALU.mult, op1=ALU.add), **w)

    def wrap(x):
        P.op("dve", K("tensor_scalar", out=mm_, in0=x, scalar1=PI, scalar2=-TWO_PI, op0=ALU.is_gt, op1=ALU.mult), **w)
        P.op("dve", K("tensor_tensor", out=x, in0=x, in1=mm_, op=ALU.add), **w)
        P.op("dve", K("tensor_scalar", out=mm_, in0=x, scalar1=-PI, scalar2=TWO_PI, op0=ALU.is_lt, op1=ALU.mult), **w)
        P.op("dve", K("tensor_tensor", out=x, in0=x, in1=mm_, op=ALU.add), **w)
    wrap(ang)
    P.op("act", K("activation", out=sin_out, in_=ang, func=AF.Sin), **w)
    P.op("dve", K("tensor_scalar", out=ang, in0=ang, scalar1=PI / 2, scalar2=None, op0=ALU.add), **w)
    wrap(ang)
    P.op("act", K("activation", out=cos_out, in_=ang, func=AF.Sin), **w)


def build_program(debug=False, stop_after=None):
    nc = bass.Bass("TRN2", target_bir_lowering=False)

    def din(name, shape, dt=F32):
        return nc.dram_tensor(name, list(shape), dt, kind="ExternalInput").ap()

    def dscr(name, shape, dt=BF16):
        if debug:
            return nc.dram_tensor(name, list(shape), dt, kind="ExternalOutput").ap()
        return nc.dram_tensor(name, list(shape), dt).ap()

    xo = din("xo", [TO, D])
    xp = din("xp", [TO, D])
    posb = din("posb", [128, TL], I32)
    pvt = din("pvt", [128, 128])
    cmask = din("cmask", [128, 256])
    cRA = din("cRA", [128, 128])
    cRB = din("cRB", [128, 128])
    cinvf = din("cinvf", [128, 2])
    g_in = din("g_in", [128, 16])
    g_up = din("g_up", [128, 16])
    g_q = din("g_q", [128, 4])
    g_kv = din("g_kv", [128, 4])
    g_post = din("g_post", [128, D])
    g_post2 = din("g_post2", [128, D])
    w_in = din("w_in", [D, INC])
    w_uq = din("w_uq", [512, 1536])
    w_ukv = din("w_ukv", [512, 2048])
    w_out = din("w_out", [D, D])
    w_up = din("w_up", [D, DFF])
    w_down = din("w_down", [DFF, D])
    out = nc.dram_tensor("out", [TO, D], F32, kind="ExternalOutput").ap()

    wb_in = dscr("wb_in", [D, INC])
    wb_uq = dscr("wb_uq", [512, 1536])
    wb_ukv = dscr("wb_ukv", [512, 2048])
    wb_out = dscr("wb_out", [D, D])
    wb_up = dscr("wb_up", [D, DFF])
    wb_down = dscr("wb_down", [DFF, D])
    hT = dscr("hT", [16, 128, TL])
    qA = dscr("qA", [NH, 128, TO])
    kA = dscr("kA", [NH, 128, TL])
    vA = dscr("vA", [TL, NH * HD])
    cq = dscr("cq", [4, 128, TO])
    ckv = dscr("ckv", [4, 128, TL])
    kr = dscr("kr", [64, TL])
    qn = dscr("qn", [NH, 128, TO])
    qr = dscr("qr", [NH, 64, TO])
    kn = dscr("kn", [NH, 128, TL])
    vB = dscr("vB", [TL, NH * HD])
    mixT = dscr("mixT", [16, 128, TO])
    x1 = dscr("x1", [TO, D], F32)
    h2T = dscr("h2T", [16, 128, TO])

    st = ExitStack()
    with st:
        ARENA_BYTES = 212736
        arena_t = st.enter_context(nc.sbuf_tensor("arena", [128, ARENA_BYTES // 4], F32))
        A = Arena(arena_t, ARENA_BYTES)
        psum = []
        for i in range(8):
            t = st.enter_context(nc.psum_tensor("ps%d" % i, [128, 512], F32))
            psum.append((t[:, :], Buf("ps%d" % i, excl=True)))
        P = Prog(nc, st)
        psring = Ring("psr", [None] * 8)
        psring.items = [(t, b) for t, b in psum]

        cst = Buf("const")
        ident = A.take(128, BF16)
        ones_b = A.take(128, BF16)
        zeros_b = A.take(512, BF16)
        pv_b = A.take(128, BF16)
        mask_b = A.take(384, BF16)
        RA_b = A.take(128, BF16)
        RB_b = A.take(128, BF16)
        ones_f = A.take(128, F32)
        invf = A.take(2, F32)
        gc_in = A.take(16, F32)
        gc_up = A.take(16, F32)
        gc_q = A.take(4, F32)
        gc_kv = A.take(4, F32)
        pv_col = A.take(1, F32)
        A.persist()
        stg = A.take(128 * 6, F32)
        stgb = Buf("stg")
        P.load(stg[:, 0:128], pvt, stgb)
        P.load(stg[:, 128:384], cmask, stgb)
        P.load(stg[:, 384:512], cRA, stgb)
        P.load(stg[:, 512:640], cRB, stgb)
        gb = Buf("gl")
        P.load(invf, cinvf, gb)
        P.load(gc_in, g_in, gb)
        P.load(gc_up, g_up, gb)
        P.load(gc_q, g_q, gb)
        P.load(gc_kv, g_kv, gb)
        P.op("pool", K("memset", stg[:, 640:768], 0.0), writes=[cst])
        P.op("pool", K("affine_select", out=stg[:, 640:768], in_=stg[:, 640:768], pattern=[[-1, 128]],
                       compare_op=ALU.not_equal, fill=1.0, base=0, channel_multiplier=1),
             reads=[cst], writes=[cst])
        P.op("pool", K("tensor_copy", out=ident, in_=stg[:, 640:768]), reads=[cst], writes=[cst])
        P.op("pool", K("memset", ones_f, 1.0), writes=[cst])
        P.op("pool", K("memset", ones_b, 1.0), writes=[cst])
        P.op("pool", K("memset", zeros_b, 0.0), writes=[cst])
        P.op("dve", K("tensor_copy", out=pv_b, in_=stg[:, 0:128]), reads=[stgb], writes=[cst])
        P.op("dve", K("tensor_copy", out=pv_col, in_=stg[:, 0:1]), reads=[stgb], writes=[cst])
        P.op("dve", K("tensor_copy", out=mask_b[:, 0:256], in_=stg[:, 128:384]), reads=[stgb], writes=[cst])
        P.op("dve", K("tensor_copy", out=mask_b[:, 256:384], in_=stg[:, 128:256]), reads=[stgb], writes=[cst])
        P.op("dve", K("tensor_copy", out=RA_b, in_=stg[:, 384:512]), reads=[stgb], writes=[cst])
        P.op("dve", K("tensor_copy", out=RB_b, in_=stg[:, 512:640]), reads=[stgb], writes=[cst])
        P.barrier()
        mask_prev = mask_b[:, 0:128]
        mask_cur = mask_b[:, 128:256]

        def kcv(ap, k):
            return ap.rearrange("p (k n) -> p k n", k=k)

        A.reset()
        base_const = A.base
        wst = Ring("wst", [A.take(2048, F32) for _ in range(3)])
        wob = Ring("wob", [A.take(2048, BF16) for _ in range(3)])
        A.persist()
        base_bg = A.base
        bg1 = BG(P, wst, wob, [(w_in, wb_in, D, INC, gc_in)])
        bg2 = BG(P, wst, wob, [(w_uq, wb_uq, 512, 1536, gc_q), (w_ukv, wb_ukv, 512, 2048, gc_kv),
                               (w_out, wb_out, D, D, None), (w_up, wb_up, D, DFF, gc_up), (w_down, wb_down, DFF, D, None)])

        cosA = A.take(TL, F32)
        sinA = A.take(TL, F32)
        cosB = A.take(TL, F32)
        sinB = A.take(TL, F32)
        A.persist()
        posi = A.take(TL, I32)
        posf = A.take(TL, F32)
        ang = A.take(TL, F32)
        ni_t = posi
        nf_t = A.take(TL, F32)
        mm_t = A.take(TL, F32)
        tb = Buf("ropet")
        P.load(posi, posb, tb)
        P.op("dve", K("tensor_copy", out=posf, in_=posi), reads=[tb], writes=[tb])
        bg1.step(4)
        rope_table(P, posf, invf[:, 0:1], ang, ni_t, nf_t, mm_t, sinA, cosA, tb)
        bg1.step(4)
        rope_table(P, posf, invf[:, 1:2], ang, ni_t, nf_t, mm_t, sinB, cosB, tb)
        P.barrier()

        def norm_transpose(xt_ap, xb, dst_tile, dst_buf, sub, scr_bf, scr_buf, small, small_buf):
            ss = small[:, 0:1]
            rs = small[:, 1:2]
            P.op("act", K("activation", out=scr_bf, in_=xt_ap, func=AF.Square, accum_out=ss),
                 reads=[xb], writes=[scr_buf, small_buf])
            P.op("act", K("activation", out=rs, in_=ss, func=AF.Sqrt, scale=1.0 / D, bias=EPS),
                 reads=[small_buf], writes=[small_buf])
            P.op("dve", K("reciprocal", out=rs, in_=rs), reads=[small_buf], writes=[small_buf])
            P.op("dve", K("tensor_scalar", out=scr_bf, in0=xt_ap, scalar1=rs, scalar2=None, op0=ALU.mult),
                 reads=[xb, small_buf], writes=[scr_buf])
            for half in range(2):
                pt, pb_ = psring.next()
                ptb = pt[:, :].bitcast(BF16)
                for j in range(8):
                    kc = half * 8 + j
                    P.op("pe", K("transpose", out=ptb[:, j * 128:(j + 1) * 128], in_=scr_bf[:, kc * 128:(kc + 1) * 128],
                                 identity=ident), reads=[scr_buf, cst], writes=[pb_], acc=True)
                eng = "act" if half == 0 else "dve"
                dst = dst_tile[:, half * 8:(half + 1) * 8, sub * 128:(sub + 1) * 128]
                src = ptb.rearrange("p (k n) -> p k n", k=8)
                if eng == "act":
                    P.op("act", K("activation", out=dst, in_=src, func=AF.Copy), reads=[pb_], writes=[dst_buf])
                else:
                    P.op("dve", K("tensor_copy", out=dst, in_=src), reads=[pb_], writes=[dst_buf])

        A.reset()
        xr = Ring("x", [A.take(D, F32) for _ in range(2)])
        scr_r = Ring("xn", [A.take(D, BF16) for _ in range(2)])
        sm_r = Ring("sm", [A.take(2, F32) for _ in range(2)])
        hr = Ring("ht", [kcv(A.take(16 * 512, BF16), 16) for _ in range(2)])
        hT_v = hT.rearrange("k p t -> p k t")
        for tt in range(8):
            htile, hb = hr.next()
            for sub in range(4):
                xt, xb = xr.next()
                r0 = (tt % 4) * 512 + sub * 128
                srcx = xp if tt < 4 else xo
                P.load(xt, srcx[r0:r0 + 128, :], xb)
                sc, scb = scr_r.next()
                sm, smb = sm_r.next()
                norm_transpose(xt, xb, htile, hb, sub, sc, scb, sm, smb)
                bg1.step(2 if sub % 2 == 0 else 1)
            P.store(hT_v[:, :, tt * 512:(tt + 1) * 512], htile, hb)
        bg1.flush()
        P.barrier()
        if stop_after == 1:
            return finish(nc, P, st, out)

        A.reset()
        Wg = kcv(A.take(16 * 1024, BF16), 16)
        Wgb = Buf("Wg")
        hr = Ring("h2", [kcv(A.take(16 * 512, BF16), 16) for _ in range(2)])
        qs_r = Ring("qs", [A.take(512, BF16) for _ in range(6)])
        t1_r = Ring("t1", [A.take(512, F32) for _ in range(3)])
        t2_r = Ring("t2", [A.take(512, F32) for _ in range(3)])
        sq_r = Ring("sq", [A.take(512, F32) for _ in range(4)])
        rr_r = Ring("rr", [A.take(512, F32) for _ in range(2)])
        cn_r = Ring("cn", [kcv(A.take(4 * 512, BF16), 4) for _ in range(2)])
        wb_in_v = wb_in.rearrange("(k p) c -> p k c", p=128)

        defq = []

        def run_deferred(keep=0):
            while len(defq) > keep:
                defq.pop(0)()

        def rope_epilogue(pt, pb_, nrow, Rm, cosT, sinT, tok0, dst_dram):
            qs, qb = qs_r.next()
            npart = pt.shape[0]
            P.op("act", K("activation", out=qs[0:npart, :], in_=pt, func=AF.Copy), reads=[pb_], writes=[qb])

            def part_b():
                p2, p2b = psring.next()
                P.op("pe", K("matmul", p2[0:nrow, :], lhsT=Rm[0:npart, 0:nrow], rhs=qs[0:npart, :], start=True, stop=True),
                     reads=[qb, cst], writes=[p2b])
                t1, t1b = t1_r.next()
                t2, t2b = t2_r.next()
                P.op("dve", K("tensor_tensor", out=t1[0:nrow, :], in0=p2[0:nrow, :], in1=sinT[0:nrow, tok0:tok0 + 512],
                              op=ALU.mult), reads=[p2b], writes=[t1b])
                P.op("pool", K("tensor_tensor", out=t2[0:nrow, :], in0=qs[0:nrow, :], in1=cosT[0:nrow, tok0:tok0 + 512],
                               op=ALU.mult), reads=[qb], writes=[t2b])
                P.op("dve", K("tensor_tensor", out=qs[0:nrow, :], in0=t1[0:nrow, :], in1=t2[0:nrow, :], op=ALU.add),
                     reads=[t1b, t2b], writes=[qb])
                P.store(dst_dram, qs[0:npart, :], qb)
            defq.append(part_b)
            run_deferred(keep=1)

        def latent_epilogue(cps, tok0, dst3):
            sqs = []
            for (pt, pb_) in cps:
                sq, sqb = sq_r.next()
                P.op("act", K("activation", out=sq, in_=pt, func=AF.Square), reads=[pb_], writes=[sqb])
                sqs.append((sq, sqb))
            p5, p5b = psring.next()
            for i, (sq, sqb) in enumerate(sqs):
                P.op("pe", K("matmul", p5, lhsT=ones_f, rhs=sq, start=(i == 0), stop=(i == 3)),
                     reads=[sqb, cst], writes=[p5b], acc=(i > 0))
            rr, rrb = rr_r.next()
            P.op("act", K("activation", out=rr, in_=p5, func=AF.Sqrt, scale=1.0 / 512, bias=EPS), reads=[p5b], writes=[rrb])
            P.op("dve", K("reciprocal", out=rr, in_=rr), reads=[rrb], writes=[rrb])
            cn, cnb = cn_r.next()
            for i, (pt, pb_) in enumerate(cps):
                P.op("dve", K("tensor_tensor", out=cn[:, i, :], in0=pt, in1=rr, op=ALU.mult), reads=[pb_, rrb], writes=[cnb])
            P.store(dst3, cn, cnb)

        groups = [
            ("aq", 0, 1024, "own"), ("ak", 1024, 1024, "all"), ("av", 2048, 1024, "all"),
            ("cq", 3072, 512, "own"), ("ckv", 3584, 576, "all"),
        ]
        for (gname, c0, ncol, which) in groups:
            P.load(Wg[:, :, 0:ncol], wb_in_v[:, :, c0:c0 + ncol], Wgb)
            tts = range(4, 8) if which == "own" else range(8)
            for tt in tts:
                htile, hb = hr.next()
                P.load(htile, hT_v[:, :, tt * 512:(tt + 1) * 512], hb)
                bg2.step(2)
                tok0 = tt * 512
                otok0 = tok0 - TO
                if gname in ("aq", "ak"):
                    for ct in range(8):
                        pt, pb_ = psring.next()
                        for kc in range(16):
                            P.op("pe", K("matmul", pt, lhsT=Wg[:, kc, ct * 128:(ct + 1) * 128], rhs=htile[:, kc, :],
                                         start=(kc == 0), stop=(kc == 15)), reads=[Wgb, hb], writes=[pb_], acc=(kc > 0))
                        dst = qA[ct, :, otok0:otok0 + 512] if gname == "aq" else kA[ct, :, tok0:tok0 + 512]
                        rope_epilogue(pt, pb_, 32, RA_b, cosA, sinA, tok0, dst)
                elif gname == "av":
                    for sub in range(4):
                        for hf in range(2):
                            pt, pb_ = psring.next()
                            for kc in range(16):
                                P.op("pe", K("matmul", pt, lhsT=htile[:, kc, sub * 128:(sub + 1) * 128],
                                             rhs=Wg[:, kc, hf * 512:(hf + 1) * 512], start=(kc == 0), stop=(kc == 15)),
                                     reads=[Wgb, hb], writes=[pb_], acc=(kc > 0))
                            qs, qb = qs_r.next()
                            eng = "act" if hf == 0 else "dve"
                            if eng == "act":
                                P.op("act", K("activation", out=qs, in_=pt, func=AF.Copy), reads=[pb_], writes=[qb])
                            else:
                                P.op("dve", K("tensor_copy", out=qs, in_=pt), reads=[pb_], writes=[qb])
                            P.store(vA[tok0 + sub * 128:tok0 + (sub + 1) * 128, hf * 512:(hf + 1) * 512], qs, qb)
                else:
                    cps = []
                    for ct in range(4):
                        pt, pb_ = psring.next()
                        for kc in range(16):
                            P.op("pe", K("matmul", pt, lhsT=Wg[:, kc, ct * 128:(ct + 1) * 128], rhs=htile[:, kc, :],
                                         start=(kc == 0), stop=(kc == 15)), reads=[Wgb, hb], writes=[pb_], acc=(kc > 0))
                        cps.append((pt, pb_))
                    if gname == "cq":
                        latent_epilogue(cps, tok0, cq.rearrange("k p t -> p k t")[:, :, otok0:otok0 + 512])
                    else:
                        latent_epilogue(cps, tok0, ckv.rearrange("k p t -> p k t")[:, :, tok0:tok0 + 512])
                        pt, pb_ = psring.next()
                        for kc in range(16):
                            P.op("pe", K("matmul", pt[0:64, :], lhsT=Wg[:, kc, 512:576], rhs=htile[:, kc, :],
                                         start=(kc == 0), stop=(kc == 15)), reads=[Wgb, hb], writes=[pb_], acc=(kc > 0))
                        rope_epilogue(pt[0:64, :], pb_, 64, RB_b, cosB, sinB, tok0, kr[:, tok0:tok0 + 512])
            run_deferred()
        run_deferred()
        P.barrier()
        if stop_after == 2:
            return finish(nc, P, st, out)

        A.reset()
        Wq = kcv(A.take(4 * 1536, BF16), 4)
        Wqb = Buf("Wq")
        Wkv = kcv(A.take(4 * 2048, BF16), 4)
        Wkvb = Buf("Wkv")
        cr = Ring("c3", [kcv(A.take(4 * 512, BF16), 4) for _ in range(2)])
        qs_r = Ring("qs3", [A.take(512, BF16) for _ in range(8)])
        t1_r = Ring("t13", [A.take(512, F32) for _ in range(3)])
        t2_r = Ring("t23", [A.take(512, F32) for _ in range(3)])
        P.load(Wq, wb_uq.rearrange("(k p) c -> p k c", p=128), Wqb)
        P.load(Wkv, wb_ukv.rearrange("(k p) c -> p k c", p=128), Wkvb)
        cq_v = cq.rearrange("k p t -> p k t")
        ckv_v = ckv.rearrange("k p t -> p k t")
        cpy = [0]

        def copy_out(pt, pb_, dst_dram, npart=128):
            qs, qb = qs_r.next()
            cpy[0] += 1
            if cpy[0] % 2:
                P.op("act", K("activation", out=qs[0:npart, :], in_=pt, func=AF.Copy), reads=[pb_], writes=[qb])
            else:
                P.op("dve", K("tensor_copy", out=qs[0:npart, :], in_=pt), reads=[pb_], writes=[qb])
            P.store(dst_dram, qs[0:npart, :], qb)

        for tt in range(4):
            ctile, cb = cr.next()
            P.load(ctile, cq_v[:, :, tt * 512:(tt + 1) * 512], cb)
            bg2.step(2)
            tok0 = TO + tt * 512
            for h in range(NH):
                pt, pb_ = psring.next()
                for kc in range(4):
                    P.op("pe", K("matmul", pt, lhsT=Wq[:, kc, h * 192:h * 192 + 128], rhs=ctile[:, kc, :],
                                 start=(kc == 0), stop=(kc == 3)), reads=[Wqb, cb], writes=[pb_], acc=(kc > 0))
                copy_out(pt, pb_, qn[h, :, tt * 512:(tt + 1) * 512])
                pt, pb_ = psring.next()
                for kc in range(4):
                    P.op("pe", K("matmul", pt[0:64, :], lhsT=Wq[:, kc, h * 192 + 128:h * 192 + 192], rhs=ctile[:, kc, :],
                                 start=(kc == 0), stop=(kc == 3)), reads=[Wqb, cb], writes=[pb_], acc=(kc > 0))
                rope_epilogue(pt[0:64, :], pb_, 64, RB_b, cosB, sinB, tok0, qr[h, :, tt * 512:(tt + 1) * 512])
        run_deferred()
        for tt in range(8):
            ctile, cb = cr.next()
            P.load(ctile, ckv_v[:, :, tt * 512:(tt + 1) * 512], cb)
            bg2.step(2)
            for h in range(NH):
                pt, pb_ = psring.next()
                for kc in range(4):
                    P.op("pe", K("matmul", pt, lhsT=Wkv[:, kc, h * 256:h * 256 + 128], rhs=ctile[:, kc, :],
                                 start=(kc == 0), stop=(kc == 3)), reads=[Wkvb, cb], writes=[pb_], acc=(kc > 0))
                copy_out(pt, pb_, kn[h, :, tt * 512:(tt + 1) * 512])
            for sub in range(4):
                for hf in range(2):
                    pt, pb_ = psring.next()
                    for kc in range(4):
                        rhs = Wkv[:, kc, hf * 1024:(hf + 1) * 1024].rearrange("p (h c) -> p h c", c=256)[:, :, 128:256]
                        P.op("pe", K("matmul", pt.rearrange("p (h c) -> p h c", c=128), lhsT=ctile[:, kc, sub * 128:(sub + 1) * 128],
                                     rhs=rhs, start=(kc == 0), stop=(kc == 3)), reads=[Wkvb, cb], writes=[pb_], acc=(kc > 0))
                    r0 = tt * 512 + sub * 128
                    copy_out(pt, pb_, vB[r0:r0 + 128, hf * 512:(hf + 1) * 512])
        run_deferred()
        P.barrier()
        if stop_after == 3:
            return finish(nc, P, st, out)

        A.base = base_bg
        A.reset()
        Qr = Ring("Qa", [A.take(TO, BF16) for _ in range(2)])
        Kr = Ring("Ka", [A.take(TL, BF16) for _ in range(2)])
        Vr = Ring("Va", [A.take(32 * 128, BF16).rearrange("p (b c) -> p b c", c=128) for _ in range(2)])
        AZr = Ring("AZ", [A.take(2 * TO, F32).rearrange("p (a t) -> p a t", a=2) for _ in range(2)])
        Ptr = Ring("Pt", [A.take(256, BF16) for _ in range(4)])
        Mxr = Ring("Mxa", [A.take(TO, BF16) for _ in range(2)])
        zr_t = A.take(TO, F32)
        zrb = Buf("zr")
        sc_a = float(HD ** -0.5)
        Sring = Ring("S4", [None] * 4)
        Sring.items = [psum[0], psum[1], psum[2], psum[3]]
        Oring = Ring("O4", [None] * 4)
        Oring.items = [psum[4], psum[5], psum[6], psum[7]]
        CFG = (1, 4, 16)
        LOOK = 2

        def load_head4(h):
            Q, Qb = Qr.next()
            Kt, Kb = Kr.next()
            P.load(Q, qA[h], Qb)
            P.load(Kt, kA[h], Kb)
            return (Q, Qb, Kt, Kb)

        def load_v4(h, d):
            Vt, Vb = Vr.next()
            nblk_n = TL // (128 * d)
            srcv = vA[:, h * 128:(h + 1) * 128].rearrange("(n i r) c -> i n r c", i=128, r=d)
            dstv = Vt.rearrange("p (n r) c -> p n r c", r=d)
            for n in range(nblk_n):
                P.load(dstv[:, n], srcv[:, n], Vb)
            return (Vt, Vb)

        def s_stage4(ctx, d, kb):
            Q, Qb, Kt, Kb = ctx
            has_cur = kb >= 16
            has_next = kb + d <= 31
            n, r = kb // d, kb % d
            ks0 = n * 128 * d + r
            Kblk = Kt[:, ks0:ks0 + 127 * d + 1:d]
            if has_cur and has_next:
                N = 256
                qs0 = ks0 - TO
                msk = mask_b[:, 128:384]
            elif has_cur:
                N = 128
                qs0 = ks0 - TO
                msk = mask_cur
            else:
                N = 128
                qs0 = ks0 + 128 * d - TO
                msk = mask_prev
            Qsl = Q[:, qs0:qs0 + (N - 1) * d + 1:d]
            S, Sb = Sring.next()
            P.op("pe", K("matmul", S[:, 0:N], lhsT=ident, rhs=msk, start=True, stop=False), reads=[cst], writes=[Sb])
            P.op("pe", K("matmul", S[:, 0:N], lhsT=Kblk, rhs=Qsl, start=False, stop=True),
                 reads=[Kb, Qb], writes=[Sb], acc=True)
            Pt, Ptb = Ptr.next()
            P.op("act", K("activation", out=Pt[:, 0:N], in_=S[:, 0:N], func=AF.Exp, scale=sc_a), reads=[Sb], writes=[Ptb])
            return (Pt, Ptb, N, qs0)

        def o_stage4(vt, AZc, d, kb, st_):
            Vt, Vb = vt
            AZ, AZb = AZc
            Pt, Ptb, N, qs0 = st_
            O, Ob = Oring.next()
            P.op("pe", K("matmul", O[:, 0:N], lhsT=Vt[:, kb, :], rhs=Pt[:, 0:N], start=True, stop=True),
                 reads=[Vb, Ptb], writes=[Ob])
            P.op("pe", K("matmul", O[:, 256:256 + N], lhsT=(pv_b if kb < 16 else ones_b), rhs=Pt[:, 0:N],
                         start=True, stop=True), reads=[cst, Ptb], writes=[Ob], acc=True)
            azv = AZ[:, :, qs0:qs0 + (N - 1) * d + 1:d]
            osrc = O[:, 0:512].rearrange("p (a t) -> p a t", a=2)[:, :, 0:N]
            P.op("dve", K("tensor_tensor", out=azv, in0=azv, in1=osrc, op=ALU.add), reads=[Ob, AZb], writes=[AZb])

        def fin_head4(h, AZc):
            AZ, AZb = AZc
            Mx, Mxb = Mxr.next()
            P.op("dve", K("reciprocal", out=zr_t, in_=AZ[:, 1, :]), reads=[AZb], writes=[zrb])
            P.op("pool", K("tensor_tensor", out=Mx, in0=AZ[:, 0, :], in1=zr_t, op=ALU.mult), reads=[AZb, zrb], writes=[Mxb])
            P.store(mixT[h], Mx, Mxb)

        pend = []
        nxt_ctx = load_head4(0)
        nxt_v = load_v4(0, CFG[0])
        for h in range(NH):
            ctx = nxt_ctx
            AZc = AZr.next()
            P.op("pool", K("memset", AZc[0], 0.0), writes=[AZc[1]])
            for ci_, d in enumerate(CFG):
                vt = nxt_v
                bg2.step(2)
                kbs = list(range(16 - d, 32))
                for i_, kb in enumerate(kbs):
                    if i_ == LOOK + 1:
                        if ci_ < 2:
                            nxt_v = load_v4(h, CFG[ci_ + 1])
                        elif h + 1 < NH:
                            nxt_ctx = load_head4(h + 1)
                            nxt_v = load_v4(h + 1, CFG[0])
                    st_ = s_stage4(ctx, d, kb)
                    pend.append((vt, AZc, d, kb, st_, (h if (ci_ == 2 and kb == 31) else None)))
                    if len(pend) > LOOK:
                        it = pend.pop(0)
                        o_stage4(*it[:5])
                        if it[5] is not None:
                            fin_head4(it[5], it[1])
        while pend:
            it = pend.pop(0)
            o_stage4(*it[:5])
            if it[5] is not None:
                fin_head4(it[5], it[1])
        P.barrier()
        if stop_after == 4:
            return finish(nc, P, st, out)

        A.reset()
        Qnr = Ring("Qn", [A.take(TO, BF16) for _ in range(2)])
        Qrr = Ring("Qr", [A.take(TO, BF16) for _ in range(2)])
        Knr = Ring("Kn", [A.take(TL, BF16) for _ in range(2)])
        KR = A.take(TL, BF16)
        KRb = Buf("KR")
        Vbr = Ring("Vb", [A.take(32 * 132, BF16).rearrange("p (b c) -> p b c", c=132) for _ in range(2)])
        Ptr = Ring("Pt5", [A.take(512, BF16) for _ in range(4)])
        Onr = Ring("On", [A.take(128, BF16) for _ in range(2)])
        rzr = Ring("rz", [A.take(1, F32) for _ in range(2)])
        Mxr = Ring("Mx5", [A.take(512, BF16) for _ in range(2)])
        sc_b = float(192 ** -0.5)
        Opairs = [(psum[0], psum[1]), (psum[2], psum[3])]
        Sring = Ring("S5", [None] * 3)
        Sring.items = [psum[4], psum[5], psum[6]]
        Tring = Ring("T5", [None])
        Tring.items = [psum[7]]
        P.op("pool", K("memset", KR, 0.0), writes=[KRb])
        P.load(KR[0:64, :], kr, KRb)
        for (Qr_t, Qrb) in Qrr.items:
            P.op("pool", K("memset", Qr_t, 0.0), writes=[Qrb])
        for (Vb_t, Vbb) in Vbr.items:
            P.op("pool", K("memset", Vb_t[:, 16:32, 128:129], 1.0), writes=[Vbb])
            P.op("pool", K("tensor_copy", out=Vb_t[:, 0:16, 128:129], in_=pv_b[:, 0:16].rearrange("p (b c) -> p b c", c=1)),
                 reads=[cst], writes=[Vbb])

        def load_head5(h):
            Qn_t, Qnb = Qnr.next()
            Qr_t, Qrb = Qrr.next()
            Kn_t, Knb = Knr.next()
            Vb_t, Vbb = Vbr.next()
            P.load(Qn_t, qn[h], Qnb)
            P.load(Qr_t[0:64, :], qr[h], Qrb)
            P.load(Kn_t, kn[h], Knb)
            srcv = vB[:, h * 128:(h + 1) * 128].rearrange("(b i) c -> i b c", i=128)
            for b4 in range(4):
                P.load(Vb_t[:, b4 * 8:(b4 + 1) * 8, 0:128], srcv[:, b4 * 8:(b4 + 1) * 8, :], Vbb)
            return (Qn_t, Qnb, Qr_t, Qrb, Kn_t, Knb, Vb_t, Vbb)

        def s_stage5(ctx, g, kb):
            Qn_t, Qnb, Qr_t, Qrb, Kn_t, Knb, Vb_t, Vbb = ctx
            j0 = max(0, kb - 16 - 4 * g)
            c0 = j0 * 128
            diag = kb >= 16 + 4 * g
            S, Sb = Sring.next()
            q0 = g * 512 + c0
            if diag:
                P.op("pe", K("matmul", S[:, c0:c0 + 128], lhsT=ident, rhs=mask_cur, start=True, stop=False),
                     reads=[cst], writes=[Sb])
            P.op("pe", K("matmul", S[:, c0:512], lhsT=Kn_t[:, kb * 128:(kb + 1) * 128], rhs=Qn_t[:, q0:g * 512 + 512],
                         start=(not diag), stop=False), reads=[Knb, Qnb], writes=[Sb], acc=diag)
            P.op("pe", K("matmul", S[:, c0:512], lhsT=KR[:, kb * 128:(kb + 1) * 128], rhs=Qr_t[:, q0:g * 512 + 512],
                         start=False, stop=True), reads=[KRb, Qrb], writes=[Sb], acc=True)
            Pt, Ptb = Ptr.next()
            P.op("act", K("activation", out=Pt[:, c0:512], in_=S[:, c0:512], func=AF.Exp, scale=sc_b),
                 reads=[Sb], writes=[Ptb])
            return (Pt, Ptb, j0)

        def pv_stage5(ctx, h, g, kb, opair, Mxc, st_):
            Vb_t, Vbb = ctx[6], ctx[7]
            Pt, Ptb, j0 = st_
            Mx, Mxb = Mxc
            for j in range(j0, 4):
                Ot, Otb = opair[0] if j < 2 else opair[1]
                oc = (j % 2) * 256
                last = (kb == 16 + 4 * g + j)
                P.op("pe", K("matmul", Ot[:, oc:oc + 129], lhsT=Pt[:, j * 128:(j + 1) * 128], rhs=Vb_t[:, kb, 0:129],
                             start=False, stop=last), reads=[Ptb, Vbb], writes=[Otb], acc=True)
                if last:
                    rz, rzb = rzr.next()
                    On, Onb = Onr.next()
                    P.op("dve", K("reciprocal", out=rz, in_=Ot[:, oc + 128:oc + 129]), reads=[Otb], writes=[rzb])
                    P.op("dve", K("tensor_scalar", out=On, in0=Ot[:, oc:oc + 128], scalar1=rz, scalar2=None, op0=ALU.mult),
                         reads=[Otb, rzb], writes=[Onb])
                    T, Tb = Tring.next()
                    Tb16 = T[:, :].bitcast(BF16)
                    P.op("pe", K("transpose", out=Tb16[:, 0:128], in_=On, identity=ident), reads=[Onb, cst], writes=[Tb])
                    P.op("act", K("activation", out=Mx[:, j * 128:(j + 1) * 128], in_=Tb16[:, 0:128], func=AF.Copy),
                         reads=[Tb], writes=[Mxb])
                    if j == 3:
                        P.store(mixT[8 + h, :, g * 512:(g + 1) * 512], Mx, Mxb)

        pend = []
        gi = 0
        nxt_ctx = load_head5(0)
        for h in range(NH):
            ctx = nxt_ctx
            for g in range(4):
                if g == 1 and h + 1 < NH:
                    nxt_ctx = load_head5(h + 1)
                bg2.step(1)
                opair = Opairs[gi % 2]
                gi += 1
                Mxc = Mxr.next()
                for (Ot, Otb) in opair:
                    P.op("pe", K("matmul", Ot, lhsT=zeros_b[:, 0:128], rhs=zeros_b, start=True, stop=False),
                         reads=[cst], writes=[Otb])
                for kb in range(16 + 4 * g + 4):
                    st_ = s_stage5(ctx, g, kb)
                    pend.append((ctx, h, g, kb, opair, Mxc, st_))
                    if len(pend) > LOOK:
                        pv_stage5(*pend.pop(0))
        while pend:
            pv_stage5(*pend.pop(0))
        bg2.flush()
        P.barrier()
        if stop_after == 5:
            return finish(nc, P, st, out)

        A.base = base_const
        A.reset()
        Wo = kcv(A.take(16 * D, BF16), 16)
        Wob = Buf("Wo")
        gp = A.take(D, F32)
        gpb = Buf("gp")
        mr = Ring("m6", [kcv(A.take(16 * 512, BF16), 16) for _ in range(2)])
        xr = Ring("x6", [A.take(D, F32) for _ in range(2)])
        yr = Ring("y6", [A.take(D, F32) for _ in range(2)])
        x1r = Ring("x16", [A.take(D, F32) for _ in range(2)])
        scr_r = Ring("xn6", [A.take(D, BF16) for _ in range(4)])
        sm_r = Ring("sm6", [A.take(8, F32) for _ in range(4)])
        hr = Ring("ht6", [kcv(A.take(16 * 512, BF16), 16) for _ in range(2)])
        P.load(Wo, wb_out.rearrange("(k p) c -> p k c", p=128), Wob)
        P.load(gp, g_post, gpb)
        mixT_v = mixT.rearrange("k p t -> p k t")
        h2T_v = h2T.rearrange("k p t -> p k t")
        defq = []

        def transposes6(sc, scb, htile, hb, sub, tt, last):
            def f():
                for half in range(2):
                    pt, pb_ = psring.next()
                    ptb = pt[:, :].bitcast(BF16)
                    for j in range(8):
                        kc = half * 8 + j
                        P.op("pe", K("transpose", out=ptb[:, j * 128:(j + 1) * 128], in_=sc[:, kc * 128:(kc + 1) * 128],
                                     identity=ident), reads=[scb, cst], writes=[pb_], acc=True)
                    dst = htile[:, half * 8:(half + 1) * 8, sub * 128:(sub + 1) * 128]
                    src = ptb.rearrange("p (k n) -> p k n", k=8)
                    if half == 0:
                        P.op("act", K("activation", out=dst, in_=src, func=AF.Copy), reads=[pb_], writes=[hb])
                    else:
                        P.op("dve", K("tensor_copy", out=dst, in_=src), reads=[pb_], writes=[hb])
                if last:
                    P.store(h2T_v[:, :, tt * 512:(tt + 1) * 512], htile, hb)
            return f

        for tt in range(4):
            mt, mb = mr.next()
            P.load(mt, mixT_v[:, :, tt * 512:(tt + 1) * 512], mb)
            htile, hb = hr.next()
            for sub in range(4):
                r0 = tt * 512 + sub * 128
                xt, xb = xr.next()
                P.load(xt, xo[r0:r0 + 128, :], xb)
                yt, yb = yr.next()
                for dt_ in range(4):
                    pt, pb_ = psring.next()
                    for kc in range(16):
                        P.op("pe", K("matmul", pt, lhsT=mt[:, kc, sub * 128:(sub + 1) * 128], rhs=Wo[:, kc, dt_ * 512:(dt_ + 1) * 512],
                                     start=(kc == 0), stop=(kc == 15)), reads=[mb, Wob], writes=[pb_], acc=(kc > 0))
                    P.op("dve", K("tensor_copy", out=yt[:, dt_ * 512:(dt_ + 1) * 512], in_=pt), reads=[pb_], writes=[yb])
                while len(defq) > 1:
                    defq.pop(0)()
                sm, smb = sm_r.next()
                sc, scb = scr_r.next()
                P.op("act", K("activation", out=sc, in_=yt, func=AF.Square, accum_out=sm[:, 0:1]), reads=[yb], writes=[scb, smb])
                P.op("act", K("activation", out=sm[:, 1:2], in_=sm[:, 0:1], func=AF.Sqrt, scale=1.0 / D, bias=EPS),
                     reads=[smb], writes=[smb])
                P.op("dve", K("reciprocal", out=sm[:, 1:2], in_=sm[:, 1:2]), reads=[smb], writes=[smb])
                P.op("act", K("activation", out=yt, in_=yt, func=AF.Copy, scale=sm[:, 1:2]), reads=[yb, smb], writes=[yb])
                P.op("dve", K("tensor_tensor", out=yt, in0=yt, in1=gp, op=ALU.mult), reads=[yb, gpb], writes=[yb])
                x1t, x1b = x1r.next()
                P.op("pool", K("tensor_tensor", out=x1t, in0=yt, in1=xt, op=ALU.add), reads=[yb, xb], writes=[x1b])
                P.store(x1[r0:r0 + 128, :], x1t, x1b)
                P.op("act", K("activation", out=sc, in_=x1t, func=AF.Square, accum_out=sm[:, 2:3]), reads=[x1b], writes=[scb, smb])
                P.op("act", K("activation", out=sm[:, 3:4], in_=sm[:, 2:3], func=AF.Sqrt, scale=1.0 / D, bias=EPS),
                     reads=[smb], writes=[smb])
                P.op("dve", K("reciprocal", out=sm[:, 3:4], in_=sm[:, 3:4]), reads=[smb], writes=[smb])
                P.op("dve", K("tensor_scalar", out=sc, in0=x1t, scalar1=sm[:, 3:4], scalar2=None, op0=ALU.mult),
                     reads=[x1b, smb], writes=[scb])
                defq.append(transposes6(sc, scb, htile, hb, sub, tt, sub == 3))
        while defq:
            defq.pop(0)()
        P.barrier()
        if stop_after == 6:
            return finish(nc, P, st, out)

        A.reset()
        gp2 = A.take(D, F32)
        gp2b = Buf("gp2")
        P.load(gp2, g_post2, gp2b)
        h2r = Ring("h2t", [kcv(A.take(16 * 512, BF16), 16) for _ in range(2)])
        U = kcv(A.take(64 * 512, BF16), 64)
        Ub = Buf("U")
        Wur = Ring("Wu", [kcv(A.take(16 * 256, BF16), 16) for _ in range(3)])
        Wdr = Ring("Wd", [kcv(A.take(4 * 1024, BF16), 4) for _ in range(3)])
        rl_r = Ring("rl", [A.take(512, F32) for _ in range(3)])
        Y = A.take(4 * D, F32).rearrange("p (s c) -> p s c", s=4)
        Yb = Buf("Y")
        sm7 = A.take(32, F32)
        sm7b = Buf("sm7")
        junk = A.take(1024, BF16)
        junkb = Buf("junk")
        x1r = Ring("x17", [A.take(D, F32) for _ in range(1)])
        wb_up_v = wb_up.rearrange("(k p) c -> p k c", p=128)
        wb_down_v = wb_down.rearrange("(k p) c -> p k c", p=128)
        out_ops = []
        import os as _os
        _ntt = int(_os.environ.get("F7_NTT", "4"))
        _mode = _os.environ.get("F7_MODE", "full")
        nxt_h2 = h2r.next()
        P.load(nxt_h2[0], h2T_v[:, :, 0:512], nxt_h2[1])
        for tt in range(_ntt):
            h2, h2b = nxt_h2
            for fg in range(32):
                Wu, Wub = Wur.next()
                P.load(Wu, wb_up_v[:, :, fg * 256:(fg + 1) * 256], Wub)
                for f in range(2):
                    ft = fg * 2 + f
                    pt, pb_ = psring.next()
                    for kc in range(16):
                        P.op("pe", K("matmul", pt, lhsT=Wu[:, kc, f * 128:(f + 1) * 128], rhs=h2[:, kc, :],
                                     start=(kc == 0), stop=(kc == 15)), reads=[Wub, h2b], writes=[pb_], acc=(kc > 0))
                    rl, rlb = rl_r.next()
                    P.op("act", K("activation", out=rl, in_=pt, func=AF.Relu), reads=[pb_], writes=[rlb])
                    P.op("dve", K("tensor_tensor", out=U[:, ft, :], in0=rl, in1=rl, op=ALU.mult), reads=[rlb], writes=[Ub])
            if tt + 1 < _ntt:
                nxt_h2 = h2r.next()
                P.load(nxt_h2[0], h2T_v[:, :, (tt + 1) * 512:(tt + 2) * 512], nxt_h2[1])
            for dh in range(2 if _mode != "up" else 0):
                accs = [psring.next() for _ in range(8)]
                for fg in range(16):
                    Wd, Wdb = Wdr.next()
                    P.load(Wd, wb_down_v[:, fg * 4:(fg + 1) * 4, dh * 1024:(dh + 1) * 1024], Wdb)
                    for f in range(4):
                        ft = fg * 4 + f
                        for sub in range(4):
                            for dt_ in range(2):
                                pt, pb_ = accs[sub * 2 + dt_]
                                P.op("pe", K("matmul", pt, lhsT=U[:, ft, sub * 128:(sub + 1) * 128],
                                             rhs=Wd[:, f, dt_ * 512:(dt_ + 1) * 512], start=(ft == 0), stop=(ft == 63)),
                                     reads=[Ub, Wdb], writes=[pb_], acc=(ft > 0))
                for sub in range(4):
                    for dt_ in range(2):
                        pt, pb_ = accs[sub * 2 + dt_]
                        col = dh * 2 + dt_
                        _ev = _os.environ.get("F7_EV", "both")
                        P.op("dve", K("tensor_copy", out=Y[:, sub, col * 512:(col + 1) * 512], in_=pt), reads=[pb_], writes=[Yb])
                        P.op("act", K("activation", out=junk[:, 0:512], in_=Y[:, sub, col * 512:(col + 1) * 512], func=AF.Square,
                                      accum_out=sm7[:, sub * 4 + col:sub * 4 + col + 1]), reads=[Yb], writes=[junkb, sm7b])
            for sub in range(4 if _mode == "full" else 0):
                r0 = tt * 512 + sub * 128
                x1t, x1b = x1r.next()
                P.load(x1t, x1[r0:r0 + 128, :], x1b, eng="pool")
                s4 = sm7[:, sub * 4:sub * 4 + 4]
                t2 = sm7[:, 16 + sub * 4:16 + sub * 4 + 2]
                ssum = sm7[:, 16 + sub * 4 + 2:16 + sub * 4 + 3]
                rs7 = sm7[:, 16 + sub * 4 + 3:16 + sub * 4 + 4]
                P.op("dve", K("tensor_tensor", out=t2, in0=s4[:, 0:2], in1=s4[:, 2:4], op=ALU.add), reads=[sm7b], writes=[sm7b])
                P.op("dve", K("tensor_tensor", out=ssum, in0=t2[:, 0:1], in1=t2[:, 1:2], op=ALU.add), reads=[sm7b], writes=[sm7b])
                P.op("act", K("activation", out=rs7, in_=ssum, func=AF.Sqrt, scale=1.0 / D, bias=EPS), reads=[sm7b], writes=[sm7b])
                P.op("dve", K("reciprocal", out=rs7, in_=rs7), reads=[sm7b], writes=[sm7b])
                P.op("act", K("activation", out=Y[:, sub, :], in_=Y[:, sub, :], func=AF.Copy, scale=rs7), reads=[Yb, sm7b], writes=[Yb])
                ee = "dve" if tt == _ntt - 1 else "pool"
                P.op(ee, K("tensor_tensor", out=Y[:, sub, :], in0=Y[:, sub, :], in1=gp2, op=ALU.mult), reads=[Yb, gp2b], writes=[Yb])
                P.op(ee, K("tensor_tensor", out=x1t, in0=Y[:, sub, :], in1=x1t, op=ALU.add), reads=[Yb, x1b], writes=[x1b])
                out_ops.append(P.store(out[r0:r0 + 128, :], x1t, x1b))
        P.barrier()
        return finish(nc, P, st, out)


def finish(nc, P, st, out):
    P.barrier()
    with nc.Block() as block:
        P.emit(block)
    return nc


def _consts():
    j = np.arange(128)[:, None]
    i = np.arange(128)[None, :]
    mask_prev = np.where(j >= i, 0.0, NEG).astype(np.float32)
    mask_cur = np.where(j <= i, 0.0, NEG).astype(np.float32)
    cmask = np.concatenate([mask_prev, mask_cur], axis=1)
    RA = np.zeros((128, 128), np.float32)
    for m in range(16):
        RA[m + 16, m] = -1.0
        RA[m, m + 16] = 1.0
    RB = np.zeros((128, 128), np.float32)
    for m in range(32):
        RB[m + 32, m] = -1.0
        RB[m, m + 32] = 1.0
    p = np.arange(128)
    invA = (THETA ** (-(2.0 * (p % 16)) / 32.0)).astype(np.float32)
    invB = (THETA ** (-(2.0 * (p % 32)) / 64.0)).astype(np.float32)
    invf = np.stack([invA, invB], axis=1).astype(np.float32)
    return cmask, RA, RB, invf


def make_in_maps(x, positions, norm_attn_pre, norm_attn_post, w_in, q_latent_norm, kv_latent_norm,
                 w_uq, w_ukv, w_out, norm_mlp_pre, norm_mlp_post, w_up, w_down):
    x = np.asarray(x, np.float32)
    positions = np.asarray(positions, np.int32)
    cmask, RA, RB, invf = _consts()

    def col(g, k):
        return np.ascontiguousarray(np.asarray(g, np.float32).reshape(k, 128).T)

    def bc(g):
        return np.ascontiguousarray(np.broadcast_to(np.asarray(g, np.float32).reshape(1, -1), (128, D)))

    shared = {
        "cmask": cmask, "cRA": RA, "cRB": RB, "cinvf": invf,
        "g_in": col(norm_attn_pre[0], 16), "g_up": col(norm_mlp_pre[0], 16),
        "g_q": col(q_latent_norm[0], 4), "g_kv": col(kv_latent_norm[0], 4),
        "g_post": bc(norm_attn_post[0]), "g_post2": bc(norm_mlp_post[0]),
        "w_in": np.ascontiguousarray(np.asarray(w_in[0], np.float32)),
        "w_uq": np.ascontiguousarray(np.asarray(w_uq[0], np.float32)),
        "w_ukv": np.ascontiguousarray(np.asarray(w_ukv[0], np.float32)),
        "w_out": np.ascontiguousarray(np.asarray(w_out[0], np.float32)),
        "w_up": np.ascontiguousarray(np.asarray(w_up[0], np.float32)),
        "w_down": np.ascontiguousarray(np.asarray(w_down[0], np.float32)),
    }
    maps = []
    for c in range(8):
        b, half = c // 2, c % 2
        m = dict(shared)
        m["xo"] = np.ascontiguousarray(x[b, half * TO:(half + 1) * TO])
        m["xp"] = np.ascontiguousarray(x[b, 0:TO]) if half else np.zeros((TO, D), np.float32)
        pl = np.concatenate([positions[b, 0:TO], positions[b, half * TO:(half + 1) * TO]])
        m["posb"] = np.ascontiguousarray(np.broadcast_to(pl.reshape(1, TL), (128, TL))).astype(np.int32)
        m["pvt"] = np.full((128, 128), float(half), np.float32)
        maps.append(m)
    return maps


_NC_CACHE = {}


def kernel(**inputs):
    maps = make_in_maps(**inputs)
    if "nc" not in _NC_CACHE:
        _NC_CACHE["nc"] = build_program()
    nc = _NC_CACHE["nc"]
    res = run_bass_kernel_spmd(nc, maps, core_ids=list(range(8)))
    outp = np.empty((NB, SEQ, D), np.float32)
    for c in range(8):
        b, half = c // 2, c % 2
        outp[b, half * TO:(half + 1) * TO] = np.asarray(res.results[c]["out"], np.float32)
    return outp
```

```python
import numpy as np
from contextlib import ExitStack
import concourse.bass as bass
import concourse.mybir as mybir
from concourse.bass_utils import run_bass_kernel_spmd

F32 = mybir.dt.float32
BF16 = mybir.dt.bfloat16
I32 = mybir.dt.int32
AF = mybir.ActivationFunctionType
ALU = mybir.AluOpType
AX = mybir.AxisListType

ENGS = ("pe", "act", "dve", "pool", "sp")

D = 2048
SEQ = 4096
NB = 4
TL = 4096
TO = 2048
HD = 128
NH = 8
DFF = 8192
INC = 4160
EPS = 1e-6
NEG = -30000.0
THETA = 500000.0
TWO_PI = float(2.0 * np.pi)
PI = float(np.pi)


class Buf:
    __slots__ = ("name", "last_w", "readers", "sem", "sem_val", "last_dma", "excl", "_last_was_load")

    def __init__(self, name, excl=False):
        self.name = name
        self._last_was_load = False
        self.excl = excl
        self.last_w = None
        self.readers = []
        self.sem = None
        self.sem_val = 0
        self.last_dma = None


class Op:
    __slots__ = ("eng", "idx", "fn", "deps", "signal", "is_dma", "sem", "val", "waits")

    def __init__(self, eng, idx, fn):
        self.eng = eng
        self.idx = idx
        self.fn = fn
        self.deps = []
        self.signal = False
        self.is_dma = False
        self.sem = None
        self.val = None
        self.waits = None


def K(name, *args, **kw):
    return lambda e: getattr(e, name)(*args, **kw)


class Prog:
    def __init__(self, nc, stack):
        self.nc = nc
        self.stack = stack
        self.ops = {e: [] for e in ENGS}
        self.eng_sem = {e: stack.enter_context(nc.semaphore("S_" + e)) for e in ENGS}
        self.dma_bufs = []
        self.same_engine_sync = True

    def _add(self, eng, fn, reads, writes, acc=False):
        op = Op(eng, len(self.ops[eng]), fn)
        self.ops[eng].append(op)
        deps = []
        for b in reads:
            if b.last_w is not None:
                deps.append(b.last_w)
            if b.excl:
                deps.extend(r for r in b.readers if r.eng != eng)
        for b in writes:
            if b.last_w is not None and not (acc and b.last_w.eng == eng):
                deps.append(b.last_w)
            deps.extend(b.readers)
        op.deps = deps
        for b in reads:
            b.readers.append(op)
        for b in writes:
            b.last_w = op
            b.readers = []
        return op

    def op(self, eng, fn, reads=(), writes=(), acc=False):
        return self._add(eng, fn, list(reads), list(writes), acc)

    def dma(self, eng, out_ap, in_ap, sbuf_buf, reads=(), writes=()):
        b = sbuf_buf
        if b.sem is None:
            b.sem = self.stack.enter_context(self.nc.semaphore("D_" + b.name))
            self.dma_bufs.append(b)
        op = self._add(eng, None, list(reads), list(writes))
        if b.last_dma is not None and not (writes and not reads and b.last_dma.fn is not None and b.last_dma.waits is None and getattr(b, "_last_was_load", False)):
            op.deps.append(b.last_dma)
        b._last_was_load = bool(writes) and not reads
        b.sem_val += 16
        op.is_dma = True
        op.sem = b.sem
        op.val = b.sem_val
        op.fn = (out_ap, in_ap)
        b.last_dma = op
        return op

    def load(self, dst, src, buf, eng="sp"):
        return self.dma(eng, dst, src, buf, writes=[buf])

    def store(self, dst, src, buf, eng="pool"):
        return self.dma(eng, dst, src, buf, reads=[buf])

    def barrier(self):
        deps = []
        for e in ENGS:
            if self.ops[e]:
                deps.append(self.ops[e][-1])
        for b in self.dma_bufs:
            if b.last_dma is not None:
                deps.append(b.last_dma)
        for e in ENGS:
            op = self._add(e, None, [], [])
            op.deps = list(deps)

    def finalize(self):
        for e in ENGS:
            known_idx = {x: -1 for x in ENGS}
            known_dma = {}
            for op in self.ops[e]:
                waits = []
                best = {}
                for d in op.deps:
                    if d.is_dma:
                        k = id(d.sem)
                        if known_dma.get(k, 0) >= d.val:
                            continue
                        known_dma[k] = d.val
                        waits.append(d)
                    else:
                        if d.fn is None:
                            continue
                        if d.eng == e and (e == "pe" or not self.same_engine_sync):
                            continue
                        if d.idx <= known_idx[d.eng]:
                            continue
                        if d.eng not in best or best[d.eng].idx < d.idx:
                            best[d.eng] = d
                for x, d in best.items():
                    known_idx[x] = d.idx
                    d.signal = True
                    waits.append(d)
                op.waits = waits
        for e in ENGS:
            c = 0
            for op in self.ops[e]:
                if not op.is_dma and op.signal:
                    c += 1
                    op.sem = self.eng_sem[e]
                    op.val = c

    def emit(self, block):
        self.finalize()
        P = self

        def run(e, eng):
            for op in P.ops[e]:
                seen = {}
                for d in op.waits:
                    k = id(d.sem)
                    if k not in seen or seen[k][1] < d.val:
                        seen[k] = (d.sem, d.val)
                for sem, val in seen.values():
                    eng.wait_ge(sem, val)
                if op.is_dma:
                    o, i = op.fn
                    eng.dma_start(out=o, in_=i).then_inc(op.sem, 16)
                elif op.fn is not None:
                    ins = op.fn(eng)
                    if op.signal:
                        ins.then_inc(op.sem, 1)

        @block.tensor
        def _(eng):
            run("pe", eng)

        @block.scalar
        def _(eng):
            run("act", eng)

        @block.vector
        def _(eng):
            run("dve", eng)

        @block.gpsimd
        def _(eng):
            run("pool", eng)

        @block.sync
        def _(eng):
            run("sp", eng)


class Ring:
    def __init__(self, name, aps):
        self.items = [(ap, Buf("%s%d" % (name, i))) for i, ap in enumerate(aps)]
        self.i = 0

    def next(self):
        it = self.items[self.i % len(self.items)]
        self.i += 1
        return it


class Arena:
    def __init__(self, t, nbytes):
        self.t = t
        self.n = nbytes
        self.base = 0
        self.off = 0

    def persist(self):
        self.base = self.off

    def reset(self):
        self.off = self.base

    def take(self, nelem, dt, parts=128):
        sz = 2 if dt == BF16 else 4
        nb = (nelem * sz + 63) // 64 * 64
        o = self.off
        self.off += nb
        assert self.off <= self.n, "SBUF arena overflow %d > %d" % (self.off, self.n)
        ap = self.t[0:parts, o // 4:(o + nelem * sz + 3) // 4]
        if dt != F32:
            ap = ap.bitcast(dt)
        return ap


DEBUG_DUMP = False

class BG:
    def __init__(self, P, wst, wob, wlist):
        self.P = P
        self.wst = wst
        self.wob = wob
        self.tiles = []
        for (src, dst, R, C, gcol) in wlist:
            for kc in range(R // 128):
                for c0 in range(0, C, 2048):
                    cw = min(2048, C - c0)
                    self.tiles.append((src[kc * 128:(kc + 1) * 128, c0:c0 + cw], dst[kc * 128:(kc + 1) * 128, c0:c0 + cw], cw,
                                       None if gcol is None else gcol[:, kc:kc + 1]))
        self.il = 0
        self.ic = 0
        self.slots = {}

    def step(self, n=1):
        P = self.P
        for _ in range(n):
            while self.il < min(len(self.tiles), self.ic + 3):
                src, dst, cw, g = self.tiles[self.il]
                sap, sb = self.wst.next()
                self.slots[self.il] = (sap, sb)
                P.load(sap[:, 0:cw], src, sb)
                self.il += 1
            if self.ic < len(self.tiles):
                src, dst, cw, g = self.tiles[self.ic]
                sap, sb = self.slots.pop(self.ic)
                oap, ob = self.wob.next()
                if g is None:
                    P.op("act", K("activation", out=oap[:, 0:cw], in_=sap[:, 0:cw], func=AF.Copy), reads=[sb], writes=[ob])
                else:
                    P.op("act", K("activation", out=oap[:, 0:cw], in_=sap[:, 0:cw], func=AF.Copy, scale=g), reads=[sb], writes=[ob])
                P.store(dst, oap[:, 0:cw], ob)
                self.ic += 1

    def flush(self):
        while self.ic < len(self.tiles):
            self.step()


C1 = 6.28125
C2 = float(2.0 * np.pi - 6.28125)


def rope_table(P, posf, invcol, ang, ni, nf, mm_, sin_out, cos_out, tb):
    w = dict(reads=[tb], writes=[tb])
    P.op("dve", K("tensor_scalar", out=ang, in0=posf, scalar1=invcol, scalar2=None, op0=ALU.mult), **w)
    P.op("dve", K("tensor_scalar", out=ni, in0=ang, scalar1=1.0 / TWO_PI, scalar2=None, op0=ALU.mult), **w)
    P.op("dve", K("tensor_copy", out=nf, in_=ni), **w)
    P.op("dve", K("scalar_tensor_tensor", out=ang, in0=nf, scalar=-C1, in1=ang, op0=ALU.mult, op1=ALU.add), **w)
    P.op("dve", K("scalar_tensor_tensor", out=ang, in0=nf, scalar=-C2, in1=ang, op0=ALU.mult, op1=ALU.add), **w)

    def wrap(x):
        P.op("dve", K("tensor_scalar", out=mm_, in0=x, scalar1=PI, scalar2=-TWO_PI, op0=ALU.is_gt, op1=ALU.mult), **w)
        P.op("dve", K("tensor_tensor", out=x, in0=x, in1=mm_, op=ALU.add), **w)
        P.op("dve", K("tensor_scalar", out=mm_, in0=x, scalar1=-PI, scalar2=TWO_PI, op0=ALU.is_lt, op1=ALU.mult), **w)
        P.op("dve", K("tensor_tensor", out=x, in0=x, in1=mm_, op=ALU.add), **w)
    wrap(ang)
    P.op("act", K("activation", out=sin_out, in_=ang, func=AF.Sin), **w)
    P.op("dve", K("tensor_scalar", out=ang, in0=ang, scalar1=PI / 2, scalar2=None, op0=ALU.add), **w)
    wrap(ang)
    P.op("act", K("activation", out=cos_out, in_=ang, func=AF.Sin), **w)


def build_program(debug=False, stop_after=None):
    nc = bass.Bass("TRN2", target_bir_lowering=False)

    def din(name, shape, dt=F32):
        return nc.dram_tensor(name, list(shape), dt, kind="ExternalInput").ap()

    def dscr(name, shape, dt=BF16):
        if debug:
            return nc.dram_tensor(name, list(shape), dt, kind="ExternalOutput").ap()
        return nc.dram_tensor(name, list(shape), dt).ap()

    xo = din("xo", [TO, D])
    xp = din("xp", [TO, D])
    posb = din("posb", [128, TL], I32)
    pvt = din("pvt", [128, 128])
    cmask = din("cmask", [128, 256])
    cRA = din("cRA", [128, 128])
    cRB = din("cRB", [128, 128])
    cinvf = din("cinvf", [128, 2])
    g_in = din("g_in", [128, 16])
    g_up = din("g_up", [128, 16])
    g_q = din("g_q", [128, 4])
    g_kv = din("g_kv", [128, 4])
    g_post = din("g_post", [128, D])
    g_post2 = din("g_post2", [128, D])
    w_in = din("w_in", [D, INC])
    w_uq = din("w_uq", [512, 1536])
    w_ukv = din("w_ukv", [512, 2048])
    w_out = din("w_out", [D, D])
    w_up = din("w_up", [D, DFF])
    w_down = din("w_down", [DFF, D])
    out = nc.dram_tensor("out", [TO, D], F32, kind="ExternalOutput").ap()

    wb_in = dscr("wb_in", [D, INC])
    wb_uq = dscr("wb_uq", [512, 1536])
    wb_ukv = dscr("wb_ukv", [512, 2048])
    wb_out = dscr("wb_out", [D, D])
    wb_up = dscr("wb_up", [D, DFF])
    wb_down = dscr("wb_down", [DFF, D])
    hT = dscr("hT", [16, 128, TL])
    qA = dscr("qA", [NH, 128, TO])
    kA = dscr("kA", [NH, 128, TL])
    vA = dscr("vA", [TL, NH * HD])
    cq = dscr("cq", [4, 128, TO])
    ckv = dscr("ckv", [4, 128, TL])
    kr = dscr("kr", [64, TL])
    qn = dscr("qn", [NH, 128, TO])
    qr = dscr("qr", [NH, 64, TO])
    kn = dscr("kn", [NH, 128, TL])
    vB = dscr("vB", [TL, NH * HD])
    mixT = dscr("mixT", [16, 128, TO])
    x1 = dscr("x1", [TO, D], F32)
    h2T = dscr("h2T", [16, 128, TO])

    st = ExitStack()
    with st:
        ARENA_BYTES = 212736
        arena_t = st.enter_context(nc.sbuf_tensor("arena", [128, ARENA_BYTES // 4], F32))
        A = Arena(arena_t, ARENA_BYTES)
        psum = []
        for i in range(8):
            t = st.enter_context(nc.psum_tensor("ps%d" % i, [128, 512], F32))
            psum.append((t[:, :], Buf("ps%d" % i, excl=True)))
        P = Prog(nc, st)
        psring = Ring("psr", [None] * 8)
        psring.items = [(t, b) for t, b in psum]

        cst = Buf("const")
        ident = A.take(128, BF16)
        ones_b = A.take(128, BF16)
        zeros_b = A.take(512, BF16)
        pv_b = A.take(128, BF16)
        mask_b = A.take(384, BF16)
        RA_b = A.take(128, BF16)
        RB_b = A.take(128, BF16)
        ones_f = A.take(128, F32)
        invf = A.take(2, F32)
        gc_in = A.take(16, F32)
        gc_up = A.take(16, F32)
        gc_q = A.take(4, F32)
        gc_kv = A.take(4, F32)
        pv_col = A.take(1, F32)
        A.persist()
        stg = A.take(128 * 6, F32)
        stgb = Buf("stg")
        P.load(stg[:, 0:128], pvt, stgb)
        P.load(stg[:, 128:384], cmask, stgb)
        P.load(stg[:, 384:512], cRA, stgb)
        P.load(stg[:, 512:640], cRB, stgb)
        gb = Buf("gl")
        P.load(invf, cinvf, gb)
        P.load(gc_in, g_in, gb)
        P.load(gc_up, g_up, gb)
        P.load(gc_q, g_q, gb)
        P.load(gc_kv, g_kv, gb)
        P.op("pool", K("memset", stg[:, 640:768], 0.0), writes=[cst])
        P.op("pool", K("affine_select", out=stg[:, 640:768], in_=stg[:, 640:768], pattern=[[-1, 128]],
                       compare_op=ALU.not_equal, fill=1.0, base=0, channel_multiplier=1),
             reads=[cst], writes=[cst])
        P.op("pool", K("tensor_copy", out=ident, in_=stg[:, 640:768]), reads=[cst], writes=[cst])
        P.op("pool", K("memset", ones_f, 1.0), writes=[cst])
        P.op("pool", K("memset", ones_b, 1.0), writes=[cst])
        P.op("pool", K("memset", zeros_b, 0.0), writes=[cst])
        P.op("dve", K("tensor_copy", out=pv_b, in_=stg[:, 0:128]), reads=[stgb], writes=[cst])
        P.op("dve", K("tensor_copy", out=pv_col, in_=stg[:, 0:1]), reads=[stgb], writes=[cst])
        P.op("dve", K("tensor_copy", out=mask_b[:, 0:256], in_=stg[:, 128:384]), reads=[stgb], writes=[cst])
        P.op("dve", K("tensor_copy", out=mask_b[:, 256:384], in_=stg[:, 128:256]), reads=[stgb], writes=[cst])
        P.op("dve", K("tensor_copy", out=RA_b, in_=stg[:, 384:512]), reads=[stgb], writes=[cst])
        P.op("dve", K("tensor_copy", out=RB_b, in_=stg[:, 512:640]), reads=[stgb], writes=[cst])
        P.barrier()
        mask_prev = mask_b[:, 0:128]
        mask_cur = mask_b[:, 128:256]

        def kcv(ap, k):
            return ap.rearrange("p (k n) -> p k n", k=k)

        A.reset()
        base_const = A.base
        wst = Ring("wst", [A.take(2048, F32) for _ in range(3)])
        wob = Ring("wob", [A.take(2048, BF16) for _ in range(3)])
        A.persist()
        base_bg = A.base
        bg1 = BG(P, wst, wob, [(w_in, wb_in, D, INC, gc_in)])
        bg2 = BG(P, wst, wob, [(w_uq, wb_uq, 512, 1536, gc_q), (w_ukv, wb_ukv, 512, 2048, gc_kv),
                               (w_out, wb_out, D, D, None), (w_up, wb_up, D, DFF, gc_up), (w_down, wb_down, DFF, D, None)])

        cosA = A.take(TL, F32)
        sinA = A.take(TL, F32)
        cosB = A.take(TL, F32)
        sinB = A.take(TL, F32)
        A.persist()
        posi = A.take(TL, I32)
        posf = A.take(TL, F32)
        ang = A.take(TL, F32)
        ni_t = posi
        nf_t = A.take(TL, F32)
        mm_t = A.take(TL, F32)
        tb = Buf("ropet")
        P.load(posi, posb, tb)
        P.op("dve", K("tensor_copy", out=posf, in_=posi), reads=[tb], writes=[tb])
        bg1.step(4)
        rope_table(P, posf, invf[:, 0:1], ang, ni_t, nf_t, mm_t, sinA, cosA, tb)
        bg1.step(4)
        rope_table(P, posf, invf[:, 1:2], ang, ni_t, nf_t, mm_t, sinB, cosB, tb)
        P.barrier()

        def norm_transpose(xt_ap, xb, dst_tile, dst_buf, sub, scr_bf, scr_buf, small, small_buf):
            ss = small[:, 0:1]
            rs = small[:, 1:2]
            P.op("act", K("activation", out=scr_bf, in_=xt_ap, func=AF.Square, accum_out=ss),
                 reads=[xb], writes=[scr_buf, small_buf])
            P.op("act", K("activation", out=rs, in_=ss, func=AF.Sqrt, scale=1.0 / D, bias=EPS),
                 reads=[small_buf], writes=[small_buf])
            P.op("dve", K("reciprocal", out=rs, in_=rs), reads=[small_buf], writes=[small_buf])
            P.op("dve", K("tensor_scalar", out=scr_bf, in0=xt_ap, scalar1=rs, scalar2=None, op0=ALU.mult),
                 reads=[xb, small_buf], writes=[scr_buf])
            for half in range(2):
                pt, pb_ = psring.next()
                ptb = pt[:, :].bitcast(BF16)
                for j in range(8):
                    kc = half * 8 + j
                    P.op("pe", K("transpose", out=ptb[:, j * 128:(j + 1) * 128], in_=scr_bf[:, kc * 128:(kc + 1) * 128],
                                 identity=ident), reads=[scr_buf, cst], writes=[pb_], acc=True)
                eng = "act" if half == 0 else "dve"
                dst = dst_tile[:, half * 8:(half + 1) * 8, sub * 128:(sub + 1) * 128]
                src = ptb.rearrange("p (k n) -> p k n", k=8)
                if eng == "act":
                    P.op("act", K("activation", out=dst, in_=src, func=AF.Copy), reads=[pb_], writes=[dst_buf])
                else:
                    P.op("dve", K("tensor_copy", out=dst, in_=src), reads=[pb_], writes=[dst_buf])

        A.reset()
        xr = Ring("x", [A.take(D, F32) for _ in range(2)])
        scr_r = Ring("xn", [A.take(D, BF16) for _ in range(2)])
        sm_r = Ring("sm", [A.take(2, F32) for _ in range(2)])
        hr = Ring("ht", [kcv(A.take(16 * 512, BF16), 16) for _ in range(2)])
        hT_v = hT.rearrange("k p t -> p k t")
        for tt in range(8):
            htile, hb = hr.next()
            for sub in range(4):
                xt, xb = xr.next()
                r0 = (tt % 4) * 512 + sub * 128
                srcx = xp if tt < 4 else xo
                P.load(xt, srcx[r0:r0 + 128, :], xb)
                sc, scb = scr_r.next()
                sm, smb = sm_r.next()
                norm_transpose(xt, xb, htile, hb, sub, sc, scb, sm, smb)
                bg1.step(2 if sub % 2 == 0 else 1)
            P.store(hT_v[:, :, tt * 512:(tt + 1) * 512], htile, hb)
        bg1.flush()
        P.barrier()
        if stop_after == 1:
            return finish(nc, P, st, out)

        A.reset()
        Wg = kcv(A.take(16 * 1024, BF16), 16)
        Wgb = Buf("Wg")
        hr = Ring("h2", [kcv(A.take(16 * 512, BF16), 16) for _ in range(2)])
        qs_r = Ring("qs", [A.take(512, BF16) for _ in range(6)])
        t1_r = Ring("t1", [A.take(512, F32) for _ in range(3)])
        t2_r = Ring("t2", [A.take(512, F32) for _ in range(3)])
        sq_r = Ring("sq", [A.take(512, F32) for _ in range(4)])
        rr_r = Ring("rr", [A.take(512, F32) for _ in range(2)])
        cn_r = Ring("cn", [kcv(A.take(4 * 512, BF16), 4) for _ in range(2)])
        wb_in_v = wb_in.rearrange("(k p) c -> p k c", p=128)

        defq = []

        def run_deferred(keep=0):
            while len(defq) > keep:
                defq.pop(0)()

        def rope_epilogue(pt, pb_, nrow, Rm, cosT, sinT, tok0, dst_dram):
            qs, qb = qs_r.next()
            npart = pt.shape[0]
            P.op("act", K("activation", out=qs[0:npart, :], in_=pt, func=AF.Copy), reads=[pb_], writes=[qb])

            def part_b():
                p2, p2b = psring.next()
                P.op("pe", K("matmul", p2[0:nrow, :], lhsT=Rm[0:npart, 0:nrow], rhs=qs[0:npart, :], start=True, stop=True),
                     reads=[qb, cst], writes=[p2b])
                t1, t1b = t1_r.next()
                t2, t2b = t2_r.next()
                P.op("dve", K("tensor_tensor", out=t1[0:nrow, :], in0=p2[0:nrow, :], in1=sinT[0:nrow, tok0:tok0 + 512],
                              op=ALU.mult), reads=[p2b], writes=[t1b])
                P.op("pool", K("tensor_tensor", out=t2[0:nrow, :], in0=qs[0:nrow, :], in1=cosT[0:nrow, tok0:tok0 + 512],
                               op=ALU.mult), reads=[qb], writes=[t2b])
                P.op("dve", K("tensor_tensor", out=qs[0:nrow, :], in0=t1[0:nrow, :], in1=t2[0:nrow, :], op=ALU.add),
                     reads=[t1b, t2b], writes=[qb])
                P.store(dst_dram, qs[0:npart, :], qb)
            defq.append(part_b)
            run_deferred(keep=1)

        def latent_epilogue(cps, tok0, dst3):
            sqs = []
            for (pt, pb_) in cps:
                sq, sqb = sq_r.next()
                P.op("act", K("activation", out=sq, in_=pt, func=AF.Square), reads=[pb_], writes=[sqb])
                sqs.append((sq, sqb))
            p5, p5b = psring.next()
            for i, (sq, sqb) in enumerate(sqs):
                P.op("pe", K("matmul", p5, lhsT=ones_f, rhs=sq, start=(i == 0), stop=(i == 3)),
                     reads=[sqb, cst], writes=[p5b], acc=(i > 0))
            rr, rrb = rr_r.next()
            P.op("act", K("activation", out=rr, in_=p5, func=AF.Sqrt, scale=1.0 / 512, bias=EPS), reads=[p5b], writes=[rrb])
            P.op("dve", K("reciprocal", out=rr, in_=rr), reads=[rrb], writes=[rrb])
            cn, cnb = cn_r.next()
            for i, (pt, pb_) in enumerate(cps):
                P.op("dve", K("tensor_tensor", out=cn[:, i, :], in0=pt, in1=rr, op=ALU.mult), reads=[pb_, rrb], writes=[cnb])
            P.store(dst3, cn, cnb)

        groups = [
            ("aq", 0, 1024, "own"), ("ak", 1024, 1024, "all"), ("av", 2048, 1024, "all"),
            ("cq", 3072, 512, "own"), ("ckv", 3584, 576, "all"),
        ]
        for (gname, c0, ncol, which) in groups:
            P.load(Wg[:, :, 0:ncol], wb_in_v[:, :, c0:c0 + ncol], Wgb)
            tts = range(4, 8) if which == "own" else range(8)
            for tt in tts:
                htile, hb = hr.next()
                P.load(htile, hT_v[:, :, tt * 512:(tt + 1) * 512], hb)
                bg2.step(2)
                tok0 = tt * 512
                otok0 = tok0 - TO
                if gname in ("aq", "ak"):
                    for ct in range(8):
                        pt, pb_ = psring.next()
                        for kc in range(16):
                            P.op("pe", K("matmul", pt, lhsT=Wg[:, kc, ct * 128:(ct + 1) * 128], rhs=htile[:, kc, :],
                                         start=(kc == 0), stop=(kc == 15)), reads=[Wgb, hb], writes=[pb_], acc=(kc > 0))
                        dst = qA[ct, :, otok0:otok0 + 512] if gname == "aq" else kA[ct, :, tok0:tok0 + 512]
                        rope_epilogue(pt, pb_, 32, RA_b, cosA, sinA, tok0, dst)
                elif gname == "av":
                    for sub in range(4):
                        for hf in range(2):
                            pt, pb_ = psring.next()
                            for kc in range(16):
                                P.op("pe", K("matmul", pt, lhsT=htile[:, kc, sub * 128:(sub + 1) * 128],
                                             rhs=Wg[:, kc, hf * 512:(hf + 1) * 512], start=(kc == 0), stop=(kc == 15)),
                                     reads=[Wgb, hb], writes=[pb_], acc=(kc > 0))
                            qs, qb = qs_r.next()
                            eng = "act" if hf == 0 else "dve"
                            if eng == "act":
                                P.op("act", K("activation", out=qs, in_=pt, func=AF.Copy), reads=[pb_], writes=[qb])
                            else:
                                P.op("dve", K("tensor_copy", out=qs, in_=pt), reads=[pb_], writes=[qb])
                            P.store(vA[tok0 + sub * 128:tok0 + (sub + 1) * 128, hf * 512:(hf + 1) * 512], qs, qb)
                else:
                    cps = []
                    for ct in range(4):
                        pt, pb_ = psring.next()
                        for kc in range(16):
                            P.op("pe", K("matmul", pt, lhsT=Wg[:, kc, ct * 128:(ct + 1) * 128], rhs=htile[:, kc, :],
                                         start=(kc == 0), stop=(kc == 15)), reads=[Wgb, hb], writes=[pb_], acc=(kc > 0))
                        cps.append((pt, pb_))
                    if gname == "cq":
                        latent_epilogue(cps, tok0, cq.rearrange("k p t -> p k t")[:, :, otok0:otok0 + 512])
                    else:
                        latent_epilogue(cps, tok0, ckv.rearrange("k p t -> p k t")[:, :, tok0:tok0 + 512])
                        pt, pb_ = psring.next()
                        for kc in range(16):
                            P.op("pe", K("matmul", pt[0:64, :], lhsT=Wg[:, kc, 512:576], rhs=htile[:, kc, :],
                                         start=(kc == 0), stop=(kc == 15)), reads=[Wgb, hb], writes=[pb_], acc=(kc > 0))
                        rope_epilogue(pt[0:64, :], pb_, 64, RB_b, cosB, sinB, tok0, kr[:, tok0:tok0 + 512])
            run_deferred()
        run_deferred()
        P.barrier()
        if stop_after == 2:
            return finish(nc, P, st, out)

        A.reset()
        Wq = kcv(A.take(4 * 1536, BF16), 4)
        Wqb = Buf("Wq")
        Wkv = kcv(A.take(4 * 2048, BF16), 4)
        Wkvb = Buf("Wkv")
        cr = Ring("c3", [kcv(A.take(4 * 512, BF16), 4) for _ in range(2)])
        qs_r = Ring("qs3", [A.take(512, BF16) for _ in range(8)])
        t1_r = Ring("t13", [A.take(512, F32) for _ in range(3)])
        t2_r = Ring("t23", [A.take(512, F32) for _ in range(3)])
        P.load(Wq, wb_uq.rearrange("(k p) c -> p k c", p=128), Wqb)
        P.load(Wkv, wb_ukv.rearrange("(k p) c -> p k c", p=128), Wkvb)
        cq_v = cq.rearrange("k p t -> p k t")
        ckv_v = ckv.rearrange("k p t -> p k t")
        cpy = [0]

        def copy_out(pt, pb_, dst_dram, npart=128):
            qs, qb = qs_r.next()
            cpy[0] += 1
            if cpy[0] % 2:
                P.op("act", K("activation", out=qs[0:npart, :], in_=pt, func=AF.Copy), reads=[pb_], writes=[qb])
            else:
                P.op("dve", K("tensor_copy", out=qs[0:npart, :], in_=pt), reads=[pb_], writes=[qb])
            P.store(dst_dram, qs[0:npart, :], qb)

        for tt in range(4):
            ctile, cb = cr.next()
            P.load(ctile, cq_v[:, :, tt * 512:(tt + 1) * 512], cb)
            bg2.step(2)
            tok0 = TO + tt * 512
            for h in range(NH):
                pt, pb_ = psring.next()
                for kc in range(4):
                    P.op("pe", K("matmul", pt, lhsT=Wq[:, kc, h * 192:h * 192 + 128], rhs=ctile[:, kc, :],
                                 start=(kc == 0), stop=(kc == 3)), reads=[Wqb, cb], writes=[pb_], acc=(kc > 0))
                copy_out(pt, pb_, qn[h, :, tt * 512:(tt + 1) * 512])
                pt, pb_ = psring.next()
                for kc in range(4):
                    P.op("pe", K("matmul", pt[0:64, :], lhsT=Wq[:, kc, h * 192 + 128:h * 192 + 192], rhs=ctile[:, kc, :],
                                 start=(kc == 0), stop=(kc == 3)), reads=[Wqb, cb], writes=[pb_], acc=(kc > 0))
                rope_epilogue(pt[0:64, :], pb_, 64, RB_b, cosB, sinB, tok0, qr[h, :, tt * 512:(tt + 1) * 512])
        run_deferred()
        for tt in range(8):
            ctile, cb = cr.next()
            P.load(ctile, ckv_v[:, :, tt * 512:(tt + 1) * 512], cb)
            bg2.step(2)
            for h in range(NH):
                pt, pb_ = psring.next()
                for kc in range(4):
                    P.op("pe", K("matmul", pt, lhsT=Wkv[:, kc, h * 256:h * 256 + 128], rhs=ctile[:, kc, :],
                                 start=(kc == 0), stop=(kc == 3)), reads=[Wkvb, cb], writes=[pb_], acc=(kc > 0))
                copy_out(pt, pb_, kn[h, :, tt * 512:(tt + 1) * 512])
            for sub in range(4):
                for hf in range(2):
                    pt, pb_ = psring.next()
                    for kc in range(4):
                        rhs = Wkv[:, kc, hf * 1024:(hf + 1) * 1024].rearrange("p (h c) -> p h c", c=256)[:, :, 128:256]
                        P.op("pe", K("matmul", pt.rearrange("p (h c) -> p h c", c=128), lhsT=ctile[:, kc, sub * 128:(sub + 1) * 128],
                                     rhs=rhs, start=(kc == 0), stop=(kc == 3)), reads=[Wkvb, cb], writes=[pb_], acc=(kc > 0))
                    r0 = tt * 512 + sub * 128
                    copy_out(pt, pb_, vB[r0:r0 + 128, hf * 512:(hf + 1) * 512])
        run_deferred()
        P.barrier()
        if stop_after == 3:
            return finish(nc, P, st, out)

        A.base = base_bg
        A.reset()
        Qr = Ring("Qa", [A.take(TO, BF16) for _ in range(2)])
        Kr = Ring("Ka", [A.take(TL, BF16) for _ in range(2)])
        Vr = Ring("Va", [A.take(32 * 128, BF16).rearrange("p (b c) -> p b c", c=128) for _ in range(2)])
        AZr = Ring("AZ", [A.take(2 * TO, F32).rearrange("p (a t) -> p a t", a=2) for _ in range(2)])
        Ptr = Ring("Pt", [A.take(256, BF16) for _ in range(4)])
        Mxr = Ring("Mxa", [A.take(TO, BF16) for _ in range(2)])
        zr_t = A.take(TO, F32)
        zrb = Buf("zr")
        sc_a = float(HD ** -0.5)
        Sring = Ring("S4", [None] * 4)
        Sring.items = [psum[0], psum[1], psum[2], psum[3]]
        Oring = Ring("O4", [None] * 4)
        Oring.items = [psum[4], psum[5], psum[6], psum[7]]
        CFG = (1, 4, 16)
        LOOK = 2

        def load_head4(h):
            Q, Qb = Qr.next()
            Kt, Kb = Kr.next()
            P.load(Q, qA[h], Qb)
            P.load(Kt, kA[h], Kb)
            return (Q, Qb, Kt, Kb)

        def load_v4(h, d):
            Vt, Vb = Vr.next()
            nblk_n = TL // (128 * d)
            srcv = vA[:, h * 128:(h + 1) * 128].rearrange("(n i r) c -> i n r c", i=128, r=d)
            dstv = Vt.rearrange("p (n r) c -> p n r c", r=d)
            if d == 1:
                P.load(Vt, vA[:, h * 128:(h + 1) * 128].rearrange("(n i) c -> i n c", i=128), Vb)
            else:
                for n in range(nblk_n):
                    P.load(dstv[:, n], srcv[:, n], Vb)
            return (Vt, Vb)

        def s_stage4(ctx, d, kb):
            Q, Qb, Kt, Kb = ctx
            has_cur = kb >= 16
            has_next = kb + d <= 31
            n, r = kb // d, kb % d
            ks0 = n * 128 * d + r
            Kblk = Kt[:, ks0:ks0 + 127 * d + 1:d]
            if has_cur and has_next:
                N = 256
                qs0 = ks0 - TO
                msk = mask_b[:, 128:384]
            elif has_cur:
                N = 128
                qs0 = ks0 - TO
                msk = mask_cur
            else:
                N = 128
                qs0 = ks0 + 128 * d - TO
                msk = mask_prev
            Qsl = Q[:, qs0:qs0 + (N - 1) * d + 1:d]
            S, Sb = Sring.next()
            P.op("pe", K("matmul", S[:, 0:N], lhsT=ident, rhs=msk, start=True, stop=False), reads=[cst], writes=[Sb])
            P.op("pe", K("matmul", S[:, 0:N], lhsT=Kblk, rhs=Qsl, start=False, stop=True),
                 reads=[Kb, Qb], writes=[Sb], acc=True)
            Pt, Ptb = Ptr.next()
            P.op("act", K("activation", out=Pt[:, 0:N], in_=S[:, 0:N], func=AF.Exp, scale=sc_a), reads=[Sb], writes=[Ptb])
            return (Pt, Ptb, N, qs0)

        def o_stage4(vt, AZc, d, kb, st_):
            Vt, Vb = vt
            AZ, AZb = AZc
            Pt, Ptb, N, qs0 = st_
            O, Ob = Oring.next()
            P.op("pe", K("matmul", O[:, 0:N], lhsT=Vt[:, kb, :], rhs=Pt[:, 0:N], start=True, stop=True),
                 reads=[Vb, Ptb], writes=[Ob])
            P.op("pe", K("matmul", O[:, 256:256 + N], lhsT=(pv_b if kb < 16 else ones_b), rhs=Pt[:, 0:N],
                         start=True, stop=True, skip_group_check=True), reads=[cst, Ptb], writes=[Ob], acc=True)
            azv = AZ[:, :, qs0:qs0 + (N - 1) * d + 1:d]
            osrc = O[:, 0:512].rearrange("p (a t) -> p a t", a=2)[:, :, 0:N]
            P.op("dve", K("tensor_tensor", out=azv, in0=azv, in1=osrc, op=ALU.add), reads=[Ob, AZb], writes=[AZb])

        fin_q = []

        def fin_head4(h, AZc):
            AZ, AZb = AZc
            Mx, Mxb = Mxr.next()
            NCH = 8
            cw_ = TO // NCH

            def piece(i):
                def f():
                    sl = slice(i * cw_, (i + 1) * cw_)
                    P.op("dve", K("reciprocal", out=zr_t[:, sl], in_=AZ[:, 1, sl]), reads=[AZb], writes=[zrb])
                    P.op("pool", K("tensor_tensor", out=Mx[:, sl], in0=AZ[:, 0, sl], in1=zr_t[:, sl], op=ALU.mult),
                         reads=[AZb, zrb], writes=[Mxb])
                    if i == NCH - 1:
                        P.store(mixT[h], Mx, Mxb)
                return f
            for i in range(NCH):
                fin_q.append(piece(i))

        pend = []
        nxt_ctx = load_head4(0)
        nxt_v = load_v4(0, CFG[0])
        for h in range(NH):
            ctx = nxt_ctx
            AZc = AZr.next()
            P.op("pool", K("memset", AZc[0], 0.0), writes=[AZc[1]])
            for ci_, d in enumerate(CFG):
                vt = nxt_v
                bg2.step(2)
                kbs = list(range(16 - d, 32))
                for i_, kb in enumerate(kbs):
                    if i_ == LOOK + 1:
                        if ci_ < 2:
                            nxt_v = load_v4(h, CFG[ci_ + 1])
                        elif h + 1 < NH:
                            nxt_ctx = load_head4(h + 1)
                            nxt_v = load_v4(h + 1, CFG[0])
                    st_ = s_stage4(ctx, d, kb)
                    pend.append((vt, AZc, d, kb, st_, (h if (ci_ == 2 and kb == 31) else None)))
                    if len(pend) > LOOK:
                        it = pend.pop(0)
                        o_stage4(*it[:5])
                        if it[5] is not None:
                            fin_head4(it[5], it[1])
                        elif fin_q and (i_ % 2 == 0):
                            fin_q.pop(0)()
        while pend:
            it = pend.pop(0)
            o_stage4(*it[:5])
            if it[5] is not None:
                fin_head4(it[5], it[1])
        while fin_q:
            fin_q.pop(0)()
        P.barrier()
        if stop_after == 4:
            return finish(nc, P, st, out)

        A.reset()
        Qnr = Ring("Qn", [A.take(TO, BF16) for _ in range(2)])
        Qrr = Ring("Qr", [A.take(TO, BF16) for _ in range(2)])
        Knr = Ring("Kn", [A.take(TL, BF16) for _ in range(2)])
        KR = A.take(TL, BF16)
        KRb = Buf("KR")
        Vbr = Ring("Vb", [A.take(32 * 132, BF16).rearrange("p (b c) -> p b c", c=132) for _ in range(2)])
        Ptr = Ring("Pt5", [A.take(512, BF16) for _ in range(4)])
        Onr = Ring("On", [A.take(128, BF16) for _ in range(2)])
        rzr = Ring("rz", [A.take(1, F32) for _ in range(2)])
        Mxr = Ring("Mx5", [A.take(512, BF16) for _ in range(2)])
        sc_b = float(192 ** -0.5)
        Opairs = [(psum[0], psum[1]), (psum[2], psum[3])]
        Sring = Ring("S5", [None] * 3)
        Sring.items = [psum[4], psum[5], psum[6]]
        Tring = Ring("T5", [None])
        Tring.items = [psum[7]]
        P.op("pool", K("memset", KR, 0.0), writes=[KRb])
        P.load(KR[0:64, :], kr, KRb)
        for (Qr_t, Qrb) in Qrr.items:
            P.op("pool", K("memset", Qr_t, 0.0), writes=[Qrb])
        for (Vb_t, Vbb) in Vbr.items:
            P.op("pool", K("memset", Vb_t[:, 16:32, 128:129], 1.0), writes=[Vbb])
            P.op("pool", K("tensor_copy", out=Vb_t[:, 0:16, 128:129], in_=pv_b[:, 0:16].rearrange("p (b c) -> p b c", c=1)),
                 reads=[cst], writes=[Vbb])

        def load_head5(h):
            Qn_t, Qnb = Qnr.next()
            Qr_t, Qrb = Qrr.next()
            Kn_t, Knb = Knr.next()
            Vb_t, Vbb = Vbr.next()
            P.load(Qn_t, qn[h], Qnb)
            P.load(Qr_t[0:64, :], qr[h], Qrb)
            P.load(Kn_t, kn[h], Knb)
            srcv = vB[:, h * 128:(h + 1) * 128].rearrange("(b i) c -> i b c", i=128)
            for b4 in range(4):
                P.load(Vb_t[:, b4 * 8:(b4 + 1) * 8, 0:128], srcv[:, b4 * 8:(b4 + 1) * 8, :], Vbb)
            return (Qn_t, Qnb, Qr_t, Qrb, Kn_t, Knb, Vb_t, Vbb)

        def s_stage5(ctx, g, kb):
            Qn_t, Qnb, Qr_t, Qrb, Kn_t, Knb, Vb_t, Vbb = ctx
            j0 = max(0, kb - 16 - 4 * g)
            c0 = j0 * 128
            diag = kb >= 16 + 4 * g
            S, Sb = Sring.next()
            q0 = g * 512 + c0
            if diag:
                P.op("pe", K("matmul", S[:, c0:c0 + 128], lhsT=ident, rhs=mask_cur, start=True, stop=False, skip_group_check=True),
                     reads=[cst], writes=[Sb])
            P.op("pe", K("matmul", S[:, c0:512], lhsT=Kn_t[:, kb * 128:(kb + 1) * 128], rhs=Qn_t[:, q0:g * 512 + 512],
                         start=(not diag), stop=False, skip_group_check=True), reads=[Knb, Qnb], writes=[Sb], acc=diag)
            P.op("pe", K("matmul", S[:, c0:512], lhsT=KR[:, kb * 128:(kb + 1) * 128], rhs=Qr_t[:, q0:g * 512 + 512],
                         start=False, stop=True, skip_group_check=True), reads=[KRb, Qrb], writes=[Sb], acc=True)
            Pt, Ptb = Ptr.next()
            P.op("act", K("activation", out=Pt[:, c0:512], in_=S[:, c0:512], func=AF.Exp, scale=sc_b),
                 reads=[Sb], writes=[Ptb])
            return (Pt, Ptb, j0)

        def pv_stage5(ctx, h, g, kb, opair, Mxc, st_):
            Vb_t, Vbb = ctx[6], ctx[7]
            Pt, Ptb, j0 = st_
            Mx, Mxb = Mxc
            for j in range(j0, 4):
                Ot, Otb = opair[0] if j < 2 else opair[1]
                oc = (j % 2) * 256
                last = (kb == 16 + 4 * g + j)
                P.op("pe", K("matmul", Ot[:, oc:oc + 129], lhsT=Pt[:, j * 128:(j + 1) * 128], rhs=Vb_t[:, kb, 0:129],
                             start=False, stop=last, skip_group_check=True), reads=[Ptb, Vbb], writes=[Otb], acc=True)
                if last:
                    rz, rzb = rzr.next()
                    On, Onb = Onr.next()
                    P.op("dve", K("reciprocal", out=rz, in_=Ot[:, oc + 128:oc + 129]), reads=[Otb], writes=[rzb])
                    P.op("dve", K("tensor_scalar", out=On, in0=Ot[:, oc:oc + 128], scalar1=rz, scalar2=None, op0=ALU.mult),
                         reads=[Otb, rzb], writes=[Onb])
                    T, Tb = Tring.next()
                    Tb16 = T[:, :].bitcast(BF16)
                    P.op("pe", K("transpose", out=Tb16[:, 0:128], in_=On, identity=ident), reads=[Onb, cst], writes=[Tb])
                    P.op("act", K("activation", out=Mx[:, j * 128:(j + 1) * 128], in_=Tb16[:, 0:128], func=AF.Copy),
                         reads=[Tb], writes=[Mxb])
                    if j == 3:
                        P.store(mixT[8 + h, :, g * 512:(g + 1) * 512], Mx, Mxb)

        pend = []
        gi = 0
        nxt_ctx = load_head5(0)
        for h in range(NH):
            ctx = nxt_ctx
            for g in range(4):
                if g == 1 and h + 1 < NH:
                    nxt_ctx = load_head5(h + 1)
                bg2.step(1)
                opair = Opairs[gi % 2]
                gi += 1
                Mxc = Mxr.next()
                for (Ot, Otb) in opair:
                    P.op("pe", K("matmul", Ot, lhsT=zeros_b[:, 0:128], rhs=zeros_b, start=True, stop=False, skip_group_check=True),
                         reads=[cst], writes=[Otb])
                for kb in range(16 + 4 * g + 4):
                    st_ = s_stage5(ctx, g, kb)
                    pend.append((ctx, h, g, kb, opair, Mxc, st_))
                    if len(pend) > LOOK:
                        pv_stage5(*pend.pop(0))
        while pend:
            pv_stage5(*pend.pop(0))
        bg2.flush()
        P.barrier()
        if stop_after == 5:
            return finish(nc, P, st, out)

        A.base = base_const
        A.reset()
        Wo = kcv(A.take(16 * D, BF16), 16)
        Wob = Buf("Wo")
        gp = A.take(D, F32)
        gpb = Buf("gp")
        mr = Ring("m6", [kcv(A.take(16 * 512, BF16), 16) for _ in range(2)])
        xr = Ring("x6", [A.take(D, F32) for _ in range(2)])
        yr = Ring("y6", [A.take(D, F32) for _ in range(2)])
        x1r = Ring("x16", [A.take(D, F32) for _ in range(2)])
        scr_r = Ring("xn6", [A.take(D, BF16) for _ in range(4)])
        sm_r = Ring("sm6", [A.take(8, F32) for _ in range(4)])
        hr = Ring("ht6", [kcv(A.take(16 * 512, BF16), 16) for _ in range(2)])
        P.load(Wo, wb_out.rearrange("(k p) c -> p k c", p=128), Wob)
        P.load(gp, g_post, gpb)
        mixT_v = mixT.rearrange("k p t -> p k t")
        h2T_v = h2T.rearrange("k p t -> p k t")
        defq = []

        def transposes6(sc, scb, htile, hb, sub, tt, last):
            def f():
                for half in range(2):
                    pt, pb_ = psring.next()
                    ptb = pt[:, :].bitcast(BF16)
                    for j in range(8):
                        kc = half * 8 + j
                        P.op("pe", K("transpose", out=ptb[:, j * 128:(j + 1) * 128], in_=sc[:, kc * 128:(kc + 1) * 128],
                                     identity=ident), reads=[scb, cst], writes=[pb_], acc=True)
                    dst = htile[:, half * 8:(half + 1) * 8, sub * 128:(sub + 1) * 128]
                    src = ptb.rearrange("p (k n) -> p k n", k=8)
                    if half == 0:
                        P.op("act", K("activation", out=dst, in_=src, func=AF.Copy), reads=[pb_], writes=[hb])
                    else:
                        P.op("dve", K("tensor_copy", out=dst, in_=src), reads=[pb_], writes=[hb])
                if last:
                    P.store(h2T_v[:, :, tt * 512:(tt + 1) * 512], htile, hb)
            return f

        for tt in range(4):
            mt, mb = mr.next()
            P.load(mt, mixT_v[:, :, tt * 512:(tt + 1) * 512], mb)
            htile, hb = hr.next()
            for sub in range(4):
                r0 = tt * 512 + sub * 128
                xt, xb = xr.next()
                P.load(xt, xo[r0:r0 + 128, :], xb)
                yt, yb = yr.next()
                for dt_ in range(4):
                    pt, pb_ = psring.next()
                    for kc in range(16):
                        P.op("pe", K("matmul", pt, lhsT=mt[:, kc, sub * 128:(sub + 1) * 128], rhs=Wo[:, kc, dt_ * 512:(dt_ + 1) * 512],
                                     start=(kc == 0), stop=(kc == 15)), reads=[mb, Wob], writes=[pb_], acc=(kc > 0))
                    P.op("dve", K("tensor_copy", out=yt[:, dt_ * 512:(dt_ + 1) * 512], in_=pt), reads=[pb_], writes=[yb])
                while len(defq) > 1:
                    defq.pop(0)()
                sm, smb = sm_r.next()
                sc, scb = scr_r.next()
                P.op("act", K("activation", out=sc, in_=yt, func=AF.Square, accum_out=sm[:, 0:1]), reads=[yb], writes=[scb, smb])
                P.op("act", K("activation", out=sm[:, 1:2], in_=sm[:, 0:1], func=AF.Sqrt, scale=1.0 / D, bias=EPS),
                     reads=[smb], writes=[smb])
                P.op("dve", K("reciprocal", out=sm[:, 1:2], in_=sm[:, 1:2]), reads=[smb], writes=[smb])
                P.op("act", K("activation", out=yt, in_=yt, func=AF.Copy, scale=sm[:, 1:2]), reads=[yb, smb], writes=[yb])
                P.op("dve", K("tensor_tensor", out=yt, in0=yt, in1=gp, op=ALU.mult), reads=[yb, gpb], writes=[yb])
                x1t, x1b = x1r.next()
                P.op("pool", K("tensor_tensor", out=x1t, in0=yt, in1=xt, op=ALU.add), reads=[yb, xb], writes=[x1b])
                P.store(x1[r0:r0 + 128, :], x1t, x1b)
                P.op("act", K("activation", out=sc, in_=x1t, func=AF.Square, accum_out=sm[:, 2:3]), reads=[x1b], writes=[scb, smb])
                P.op("act", K("activation", out=sm[:, 3:4], in_=sm[:, 2:3], func=AF.Sqrt, scale=1.0 / D, bias=EPS),
                     reads=[smb], writes=[smb])
                P.op("dve", K("reciprocal", out=sm[:, 3:4], in_=sm[:, 3:4]), reads=[smb], writes=[smb])
                P.op("dve", K("tensor_scalar", out=sc, in0=x1t, scalar1=sm[:, 3:4], scalar2=None, op0=ALU.mult),
                     reads=[x1b, smb], writes=[scb])
                defq.append(transposes6(sc, scb, htile, hb, sub, tt, sub == 3))
        while defq:
            defq.pop(0)()
        P.barrier()
        if stop_after == 6:
            return finish(nc, P, st, out)

        A.reset()
        gp2 = A.take(D, F32)
        gp2b = Buf("gp2")
        P.load(gp2, g_post2, gp2b)
        h2r = Ring("h2t", [kcv(A.take(16 * 512, BF16), 16) for _ in range(2)])
        U = kcv(A.take(64 * 512, BF16), 64)
        Ub = Buf("U")
        Wur = Ring("Wu", [kcv(A.take(16 * 256, BF16), 16) for _ in range(3)])
        Wdr = Ring("Wd", [kcv(A.take(4 * 1024, BF16), 4) for _ in range(3)])
        rl_r = Ring("rl", [A.take(512, F32) for _ in range(3)])
        Y = A.take(4 * D, F32).rearrange("p (s c) -> p s c", s=4)
        Yb = Buf("Y")
        sm7 = A.take(32, F32)
        sm7b = Buf("sm7")
        junk = A.take(1024, BF16)
        junkb = Buf("junk")
        x1r = Ring("x17", [A.take(D, F32) for _ in range(1)])
        wb_up_v = wb_up.rearrange("(k p) c -> p k c", p=128)
        wb_down_v = wb_down.rearrange("(k p) c -> p k c", p=128)
        out_ops = []
        import os as _os
        _ntt = int(_os.environ.get("F7_NTT", "4"))
        _mode = _os.environ.get("F7_MODE", "full")
        nxt_h2 = h2r.next()
        P.load(nxt_h2[0], h2T_v[:, :, 0:512], nxt_h2[1])
        for tt in range(_ntt):
            h2, h2b = nxt_h2
            for fg in range(32):
                Wu, Wub = Wur.next()
                P.load(Wu, wb_up_v[:, :, fg * 256:(fg + 1) * 256], Wub)
                for f in range(2):
                    ft = fg * 2 + f
                    pt, pb_ = psring.next()
                    for kc in range(16):
                        P.op("pe", K("matmul", pt, lhsT=Wu[:, kc, f * 128:(f + 1) * 128], rhs=h2[:, kc, :],
                                     start=(kc == 0), stop=(kc == 15)), reads=[Wub, h2b], writes=[pb_], acc=(kc > 0))
                    rl, rlb = rl_r.next()
                    P.op("act", K("activation", out=rl, in_=pt, func=AF.Relu), reads=[pb_], writes=[rlb])
                    P.op("dve", K("tensor_tensor", out=U[:, ft, :], in0=rl, in1=rl, op=ALU.mult), reads=[rlb], writes=[Ub])
            if tt + 1 < _ntt:
                nxt_h2 = h2r.next()
                P.load(nxt_h2[0], h2T_v[:, :, (tt + 1) * 512:(tt + 2) * 512], nxt_h2[1])
            for dh in range(2 if _mode != "up" else 0):
                accs = [psring.next() for _ in range(8)]
                for fg in range(16):
                    Wd, Wdb = Wdr.next()
                    P.load(Wd, wb_down_v[:, fg * 4:(fg + 1) * 4, dh * 1024:(dh + 1) * 1024], Wdb)
                    for f in range(4):
                        ft = fg * 4 + f
                        for sub in range(4):
                            for dt_ in range(2):
                                pt, pb_ = accs[sub * 2 + dt_]
                                P.op("pe", K("matmul", pt, lhsT=U[:, ft, sub * 128:(sub + 1) * 128],
                                             rhs=Wd[:, f, dt_ * 512:(dt_ + 1) * 512], start=(ft == 0), stop=(ft == 63)),
                                     reads=[Ub, Wdb], writes=[pb_], acc=(ft > 0))
                for sub in range(4):
                    for dt_ in range(2):
                        pt, pb_ = accs[sub * 2 + dt_]
                        col = dh * 2 + dt_
                        _ev = _os.environ.get("F7_EV", "both")
                        P.op("dve", K("tensor_copy", out=Y[:, sub, col * 512:(col + 1) * 512], in_=pt), reads=[pb_], writes=[Yb])
                        P.op("act", K("activation", out=junk[:, 0:512], in_=Y[:, sub, col * 512:(col + 1) * 512], func=AF.Square,
                                      accum_out=sm7[:, sub * 4 + col:sub * 4 + col + 1]), reads=[Yb], writes=[junkb, sm7b])
            for sub in range(4 if _mode == "full" else 0):
                r0 = tt * 512 + sub * 128
                x1t, x1b = x1r.next()
                P.load(x1t, x1[r0:r0 + 128, :], x1b, eng="pool")
                s4 = sm7[:, sub * 4:sub * 4 + 4]
                t2 = sm7[:, 16 + sub * 4:16 + sub * 4 + 2]
                ssum = sm7[:, 16 + sub * 4 + 2:16 + sub * 4 + 3]
                rs7 = sm7[:, 16 + sub * 4 + 3:16 + sub * 4 + 4]
                P.op("dve", K("tensor_tensor", out=t2, in0=s4[:, 0:2], in1=s4[:, 2:4], op=ALU.add), reads=[sm7b], writes=[sm7b])
                P.op("dve", K("tensor_tensor", out=ssum, in0=t2[:, 0:1], in1=t2[:, 1:2], op=ALU.add), reads=[sm7b], writes=[sm7b])
                P.op("act", K("activation", out=rs7, in_=ssum, func=AF.Sqrt, scale=1.0 / D, bias=EPS), reads=[sm7b], writes=[sm7b])
                P.op("dve", K("reciprocal", out=rs7, in_=rs7), reads=[sm7b], writes=[sm7b])
                P.op("act", K("activation", out=Y[:, sub, :], in_=Y[:, sub, :], func=AF.Copy, scale=rs7), reads=[Yb, sm7b], writes=[Yb])
                ee = "dve" if tt == _ntt - 1 else "pool"
                P.op(ee, K("tensor_tensor", out=Y[:, sub, :], in0=Y[:, sub, :], in1=gp2, op=ALU.mult), reads=[Yb, gp2b], writes=[Yb])
                P.op(ee, K("tensor_tensor", out=x1t, in0=Y[:, sub, :], in1=x1t, op=ALU.add), reads=[Yb, x1b], writes=[x1b])
                out_ops.append(P.store(out[r0:r0 + 128, :], x1t, x1b))
        P.barrier()
        return finish(nc, P, st, out)


def finish(nc, P, st, out):
    P.barrier()
    with nc.Block() as block:
        P.emit(block)
    return nc


def _consts():
    j = np.arange(128)[:, None]
    i = np.arange(128)[None, :]
    mask_prev = np.where(j >= i, 0.0, NEG).astype(np.float32)
    mask_cur = np.where(j <= i, 0.0, NEG).astype(np.float32)
    cmask = np.concatenate([mask_prev, mask_cur], axis=1)
    RA = np.zeros((128, 128), np.float32)
    for m in range(16):
        RA[m + 16, m] = -1.0
        RA[m, m + 16] = 1.0
    RB = np.zeros((128, 128), np.float32)
    for m in range(32):
        RB[m + 32, m] = -1.0
        RB[m, m + 32] = 1.0
    p = np.arange(128)
    invA = (THETA ** (-(2.0 * (p % 16)) / 32.0)).astype(np.float32)
    invB = (THETA ** (-(2.0 * (p % 32)) / 64.0)).astype(np.float32)
    invf = np.stack([invA, invB], axis=1).astype(np.float32)
    return cmask, RA, RB, invf


def make_in_maps(x, positions, norm_attn_pre, norm_attn_post, w_in, q_latent_norm, kv_latent_norm,
                 w_uq, w_ukv, w_out, norm_mlp_pre, norm_mlp_post, w_up, w_down):
    x = np.asarray(x, np.float32)
    positions = np.asarray(positions, np.int32)
    cmask, RA, RB, invf = _consts()

    def col(g, k):
        return np.ascontiguousarray(np.asarray(g, np.float32).reshape(k, 128).T)

    def bc(g):
        return np.ascontiguousarray(np.broadcast_to(np.asarray(g, np.float32).reshape(1, -1), (128, D)))

    shared = {
        "cmask": cmask, "cRA": RA, "cRB": RB, "cinvf": invf,
        "g_in": col(norm_attn_pre[0], 16), "g_up": col(norm_mlp_pre[0], 16),
        "g_q": col(q_latent_norm[0], 4), "g_kv": col(kv_latent_norm[0], 4),
        "g_post": bc(norm_attn_post[0]), "g_post2": bc(norm_mlp_post[0]),
        "w_in": np.ascontiguousarray(np.asarray(w_in[0], np.float32)),
        "w_uq": np.ascontiguousarray(np.asarray(w_uq[0], np.float32)),
        "w_ukv": np.ascontiguousarray(np.asarray(w_ukv[0], np.float32)),
        "w_out": np.ascontiguousarray(np.asarray(w_out[0], np.float32)),
        "w_up": np.ascontiguousarray(np.asarray(w_up[0], np.float32)),
        "w_down": np.ascontiguousarray(np.asarray(w_down[0], np.float32)),
    }
    maps = []
    for c in range(8):
        b, half = c // 2, c % 2
        m = dict(shared)
        m["xo"] = np.ascontiguousarray(x[b, half * TO:(half + 1) * TO])
        m["xp"] = np.ascontiguousarray(x[b, 0:TO]) if half else np.zeros((TO, D), np.float32)
        pl = np.concatenate([positions[b, 0:TO], positions[b, half * TO:(half + 1) * TO]])
        m["posb"] = np.ascontiguousarray(np.broadcast_to(pl.reshape(1, TL), (128, TL))).astype(np.int32)
        m["pvt"] = np.full((128, 128), float(half), np.float32)
        maps.append(m)
    return maps


_NC_CACHE = {}


def kernel(**inputs):
    maps = make_in_maps(**inputs)
    if "nc" not in _NC_CACHE:
        _NC_CACHE["nc"] = build_program()
    nc = _NC_CACHE["nc"]
    res = run_bass_kernel_spmd(nc, maps, core_ids=list(range(8)))
    outp = np.empty((NB, SEQ, D), np.float32)
    for c in range(8):
        b, half = c // 2, c % 2
        outp[b, half * TO:(half + 1) * TO] = np.asarray(res.results[c]["out"], np.float32)
    return outp
```

```python
import numpy as np
from contextlib import ExitStack
import concourse.bass as bass
import concourse.mybir as mybir
from concourse.bass_utils import run_bass_kernel_spmd

F32 = mybir.dt.float32
BF16 = mybir.dt.bfloat16
I32 = mybir.dt.int32
AF = mybir.ActivationFunctionType
ALU = mybir.AluOpType
AX = mybir.AxisListType

ENGS = ("pe", "act", "dve", "pool", "sp")

D = 2048
SEQ = 4096
NB = 4
TL = 4096
TO = 2048
HD = 128
NH = 8
DFF = 8192
INC = 4160
EPS = 1e-6
NEG = -30000.0
THETA = 500000.0
TWO_PI = float(2.0 * np.pi)
PI = float(np.pi)


class Buf:
    __slots__ = ("name", "last_w", "readers", "sem", "sem_val", "last_dma", "excl", "_last_was_load")

    def __init__(self, name, excl=False):
        self.name = name
        self._last_was_load = False
        self.excl = excl
        self.last_w = None
        self.readers = []
        self.sem = None
        self.sem_val = 0
        self.last_dma = None


class Op:
    __slots__ = ("eng", "idx", "fn", "deps", "signal", "is_dma", "sem", "val", "waits")

    def __init__(self, eng, idx, fn):
        self.eng = eng
        self.idx = idx
        self.fn = fn
        self.deps = []
        self.signal = False
        self.is_dma = False
        self.sem = None
        self.val = None
        self.waits = None


def K(name, *args, **kw):
    return lambda e: getattr(e, name)(*args, **kw)


class Prog:
    def __init__(self, nc, stack):
        self.nc = nc
        self.stack = stack
        self.ops = {e: [] for e in ENGS}
        self.eng_sem = {e: stack.enter_context(nc.semaphore("S_" + e)) for e in ENGS}
        self.dma_bufs = []
        self.same_engine_sync = True

    def _add(self, eng, fn, reads, writes, acc=False):
        op = Op(eng, len(self.ops[eng]), fn)
        self.ops[eng].append(op)
        deps = []
        for b in reads:
            if b.last_w is not None:
                deps.append(b.last_w)
            if b.excl:
                deps.extend(r for r in b.readers if r.eng != eng)
        for b in writes:
            if b.last_w is not None and not (acc and b.last_w.eng == eng):
                deps.append(b.last_w)
            deps.extend(b.readers)
        op.deps = deps
        for b in reads:
            b.readers.append(op)
        for b in writes:
            b.last_w = op
            b.readers = []
        return op

    def op(self, eng, fn, reads=(), writes=(), acc=False):
        return self._add(eng, fn, list(reads), list(writes), acc)

    def dma(self, eng, out_ap, in_ap, sbuf_buf, reads=(), writes=()):
        b = sbuf_buf
        if b.sem is None:
            b.sem = self.stack.enter_context(self.nc.semaphore("D_" + b.name))
            self.dma_bufs.append(b)
        op = self._add(eng, None, list(reads), list(writes))
        if b.last_dma is not None and not (writes and not reads and b.last_dma.fn is not None and b.last_dma.waits is None and getattr(b, "_last_was_load", False)):
            op.deps.append(b.last_dma)
        b._last_was_load = bool(writes) and not reads
        b.sem_val += 16
        op.is_dma = True
        op.sem = b.sem
        op.val = b.sem_val
        op.fn = (out_ap, in_ap)
        b.last_dma = op
        return op

    def load(self, dst, src, buf, eng="sp"):
        return self.dma(eng, dst, src, buf, writes=[buf])

    def store(self, dst, src, buf, eng="pool"):
        return self.dma(eng, dst, src, buf, reads=[buf])

    def barrier(self):
        deps = []
        for e in ENGS:
            if self.ops[e]:
                deps.append(self.ops[e][-1])
        for b in self.dma_bufs:
            if b.last_dma is not None:
                deps.append(b.last_dma)
        for e in ENGS:
            op = self._add(e, None, [], [])
            op.deps = list(deps)

    def finalize(self):
        for e in ENGS:
            known_idx = {x: -1 for x in ENGS}
            known_dma = {}
            for op in self.ops[e]:
                waits = []
                best = {}
                for d in op.deps:
                    if d.is_dma:
                        k = id(d.sem)
                        if known_dma.get(k, 0) >= d.val:
                            continue
                        known_dma[k] = d.val
                        waits.append(d)
                    else:
                        if d.fn is None:
                            continue
                        if d.eng == e and (e == "pe" or not self.same_engine_sync):
                            continue
                        if d.idx <= known_idx[d.eng]:
                            continue
                        if d.eng not in best or best[d.eng].idx < d.idx:
                            best[d.eng] = d
                for x, d in best.items():
                    known_idx[x] = d.idx
                    d.signal = True
                    waits.append(d)
                op.waits = waits
        for e in ENGS:
            c = 0
            for op in self.ops[e]:
                if not op.is_dma and op.signal:
                    c += 1
                    op.sem = self.eng_sem[e]
                    op.val = c

    def emit(self, block):
        self.finalize()
        P = self

        def run(e, eng):
            for op in P.ops[e]:
                seen = {}
                for d in op.waits:
                    k = id(d.sem)
                    if k not in seen or seen[k][1] < d.val:
                        seen[k] = (d.sem, d.val)
                for sem, val in seen.values():
                    eng.wait_ge(sem, val)
                if op.is_dma:
                    o, i = op.fn
                    eng.dma_start(out=o, in_=i).then_inc(op.sem, 16)
                elif op.fn is not None:
                    ins = op.fn(eng)
                    if op.signal:
                        ins.then_inc(op.sem, 1)

        @block.tensor
        def _(eng):
            run("pe", eng)

        @block.scalar
        def _(eng):
            run("act", eng)

        @block.vector
        def _(eng):
            run("dve", eng)

        @block.gpsimd
        def _(eng):
            run("pool", eng)

        @block.sync
        def _(eng):
            run("sp", eng)


class Ring:
    def __init__(self, name, aps):
        self.items = [(ap, Buf("%s%d" % (name, i))) for i, ap in enumerate(aps)]
        self.i = 0

    def next(self):
        it = self.items[self.i % len(self.items)]
        self.i += 1
        return it


class Arena:
    def __init__(self, t, nbytes):
        self.t = t
        self.n = nbytes
        self.base = 0
        self.off = 0

    def persist(self):
        self.base = self.off

    def reset(self):
        self.off = self.base

    def take(self, nelem, dt, parts=128):
        sz = 2 if dt == BF16 else 4
        nb = (nelem * sz + 63) // 64 * 64
        o = self.off
        self.off += nb
        assert self.off <= self.n, "SBUF arena overflow %d > %d" % (self.off, self.n)
        ap = self.t[0:parts, o // 4:(o + nelem * sz + 3) // 4]
        if dt != F32:
            ap = ap.bitcast(dt)
        return ap


DEBUG_DUMP = False

class BG:
    def __init__(self, P, wst, wob, wlist):
        self.P = P
        self.wst = wst
        self.wob = wob
        self.tiles = []
        for (src, dst, R, C, gcol) in wlist:
            for kc in range(R // 128):
                for c0 in range(0, C, 2048):
                    cw = min(2048, C - c0)
                    self.tiles.append((src[kc * 128:(kc + 1) * 128, c0:c0 + cw], dst[kc * 128:(kc + 1) * 128, c0:c0 + cw], cw,
                                       None if gcol is None else gcol[:, kc:kc + 1]))
        self.il = 0
        self.ic = 0
        self.slots = {}

    def step(self, n=1):
        P = self.P
        for _ in range(n):
            while self.il < min(len(self.tiles), self.ic + 3):
                src, dst, cw, g = self.tiles[self.il]
                sap, sb = self.wst.next()
                self.slots[self.il] = (sap, sb)
                P.load(sap[:, 0:cw], src, sb)
                self.il += 1
            if self.ic < len(self.tiles):
                src, dst, cw, g = self.tiles[self.ic]
                sap, sb = self.slots.pop(self.ic)
                oap, ob = self.wob.next()
                if g is None:
                    P.op("act", K("activation", out=oap[:, 0:cw], in_=sap[:, 0:cw], func=AF.Copy), reads=[sb], writes=[ob])
                else:
                    P.op("act", K("activation", out=oap[:, 0:cw], in_=sap[:, 0:cw], func=AF.Copy, scale=g), reads=[sb], writes=[ob])
                P.store(dst, oap[:, 0:cw], ob)
                self.ic += 1

    def flush(self):
        while self.ic < len(self.tiles):
            self.step()


C1 = 6.28125
C2 = float(2.0 * np.pi - 6.28125)


def rope_table(P, posf, invcol, ang, ni, nf, mm_, sin_out, cos_out, tb):
    w = dict(reads=[tb], writes=[tb])
    P.op("dve", K("tensor_scalar", out=ang, in0=posf, scalar1=invcol, scalar2=None, op0=ALU.mult), **w)
    P.op("dve", K("tensor_scalar", out=ni, in0=ang, scalar1=1.0 / TWO_PI, scalar2=None, op0=ALU.mult), **w)
    P.op("dve", K("tensor_copy", out=nf, in_=ni), **w)
    P.op("dve", K("scalar_tensor_tensor", out=ang, in0=nf, scalar=-C1, in1=ang, op0=ALU.mult, op1=ALU.add), **w)
    P.op("dve", K("scalar_tensor_tensor", out=ang, in0=nf, scalar=-C2, in1=ang, op0=ALU.mult, op1=ALU.add), **w)

    def wrap(x):
        P.op("dve", K("tensor_scalar", out=mm_, in0=x, scalar1=PI, scalar2=-TWO_PI, op0=ALU.is_gt, op1=ALU.mult), **w)
        P.op("dve", K("tensor_tensor", out=x, in0=x, in1=mm_, op=ALU.add), **w)
        P.op("dve", K("tensor_scalar", out=mm_, in0=x, scalar1=-PI, scalar2=TWO_PI, op0=ALU.is_lt, op1=ALU.mult), **w)
        P.op("dve", K("tensor_tensor", out=x, in0=x, in1=mm_, op=ALU.add), **w)
    wrap(ang)
    P.op("act", K("activation", out=sin_out, in_=ang, func=AF.Sin), **w)
    P.op("dve", K("tensor_scalar", out=ang, in0=ang, scalar1=PI / 2, scalar2=None, op0=ALU.add), **w)
    wrap(ang)
    P.op("act", K("activation", out=cos_out, in_=ang, func=AF.Sin), **w)


def build_program(debug=False, stop_after=None):
    nc = bass.Bass("TRN2", target_bir_lowering=False)

    def din(name, shape, dt=F32):
        return nc.dram_tensor(name, list(shape), dt, kind="ExternalInput").ap()

    def dscr(name, shape, dt=BF16):
        if debug:
            return nc.dram_tensor(name, list(shape), dt, kind="ExternalOutput").ap()
        return nc.dram_tensor(name, list(shape), dt).ap()

    xo = din("xo", [TO, D])
    xp = din("xp", [TO, D])
    posb = din("posb", [128, TL], I32)
    pvt = din("pvt", [128, 128])
    cmask = din("cmask", [128, 256])
    cRA = din("cRA", [128, 128])
    cRB = din("cRB", [128, 128])
    cinvf = din("cinvf", [128, 2])
    g_in = din("g_in", [128, 16])
    g_up = din("g_up", [128, 16])
    g_q = din("g_q", [128, 4])
    g_kv = din("g_kv", [128, 4])
    g_post = din("g_post", [128, D])
    g_post2 = din("g_post2", [128, D])
    w_in = din("w_in", [D, INC])
    w_uq = din("w_uq", [512, 1536])
    w_ukv = din("w_ukv", [512, 2048])
    w_out = din("w_out", [D, D])
    w_up = din("w_up", [D, DFF])
    w_down = din("w_down", [DFF, D])
    out = nc.dram_tensor("out", [TO, D], F32, kind="ExternalOutput").ap()

    wb_in = dscr("wb_in", [D, INC])
    wb_uq = dscr("wb_uq", [512, 1536])
    wb_ukv = dscr("wb_ukv", [512, 2048])
    wb_out = dscr("wb_out", [D, D])
    wb_up = dscr("wb_up", [D, DFF])
    wb_down = dscr("wb_down", [DFF, D])
    hT = dscr("hT", [16, 128, TL])
    qA = dscr("qA", [NH, 128, TO])
    kA = dscr("kA", [NH, 128, TL])
    vA = dscr("vA", [TL, NH * HD])
    cq = dscr("cq", [4, 128, TO])
    ckv = dscr("ckv", [4, 128, TL])
    kr = dscr("kr", [64, TL])
    qn = dscr("qn", [NH, 128, TO])
    qr = dscr("qr", [NH, 64, TO])
    kn = dscr("kn", [NH, 128, TL])
    vB = dscr("vB", [TL, NH * HD])
    mixT = dscr("mixT", [16, 128, TO])
    x1 = dscr("x1", [TO, D], F32)
    h2T = dscr("h2T", [16, 128, TO])

    st = ExitStack()
    with st:
        ARENA_BYTES = 212736
        arena_t = st.enter_context(nc.sbuf_tensor("arena", [128, ARENA_BYTES // 4], F32))
        A = Arena(arena_t, ARENA_BYTES)
        psum = []
        for i in range(8):
            t = st.enter_context(nc.psum_tensor("ps%d" % i, [128, 512], F32))
            psum.append((t[:, :], Buf("ps%d" % i, excl=True)))
        P = Prog(nc, st)
        psring = Ring("psr", [None] * 8)
        psring.items = [(t, b) for t, b in psum]

        cst = Buf("const")
        ident = A.take(128, BF16)
        ones_b = A.take(128, BF16)
        zeros_b = A.take(512, BF16)
        pv_b = A.take(128, BF16)
        mask_b = A.take(384, BF16)
        RA_b = A.take(128, BF16)
        RB_b = A.take(128, BF16)
        ones_f = A.take(128, F32)
        invf = A.take(2, F32)
        gc_in = A.take(16, F32)
        gc_up = A.take(16, F32)
        gc_q = A.take(4, F32)
        gc_kv = A.take(4, F32)
        pv_col = A.take(1, F32)
        A.persist()
        stg = A.take(128 * 6, F32)
        stgb = Buf("stg")
        P.load(stg[:, 0:128], pvt, stgb)
        P.load(stg[:, 128:384], cmask, stgb)
        P.load(stg[:, 384:512], cRA, stgb)
        P.load(stg[:, 512:640], cRB, stgb)
        gb = Buf("gl")
        P.load(invf, cinvf, gb)
        P.load(gc_in, g_in, gb)
        P.load(gc_up, g_up, gb)
        P.load(gc_q, g_q, gb)
        P.load(gc_kv, g_kv, gb)
        P.op("pool", K("memset", stg[:, 640:768], 0.0), writes=[cst])
        P.op("pool", K("affine_select", out=stg[:, 640:768], in_=stg[:, 640:768], pattern=[[-1, 128]],
                       compare_op=ALU.not_equal, fill=1.0, base=0, channel_multiplier=1),
             reads=[cst], writes=[cst])
        P.op("pool", K("tensor_copy", out=ident, in_=stg[:, 640:768]), reads=[cst], writes=[cst])
        P.op("pool", K("memset", ones_f, 1.0), writes=[cst])
        P.op("pool", K("memset", ones_b, 1.0), writes=[cst])
        P.op("pool", K("memset", zeros_b, 0.0), writes=[cst])
        P.op("dve", K("tensor_copy", out=pv_b, in_=stg[:, 0:128]), reads=[stgb], writes=[cst])
        P.op("dve", K("tensor_copy", out=pv_col, in_=stg[:, 0:1]), reads=[stgb], writes=[cst])
        P.op("dve", K("tensor_copy", out=mask_b[:, 0:256], in_=stg[:, 128:384]), reads=[stgb], writes=[cst])
        P.op("dve", K("tensor_copy", out=mask_b[:, 256:384], in_=stg[:, 128:256]), reads=[stgb], writes=[cst])
        P.op("dve", K("tensor_copy", out=RA_b, in_=stg[:, 384:512]), reads=[stgb], writes=[cst])
        P.op("dve", K("tensor_copy", out=RB_b, in_=stg[:, 512:640]), reads=[stgb], writes=[cst])
        P.barrier()
        mask_prev = mask_b[:, 0:128]
        mask_cur = mask_b[:, 128:256]

        def kcv(ap, k):
            return ap.rearrange("p (k n) -> p k n", k=k)

        A.reset()
        base_const = A.base
        wst = Ring("wst", [A.take(2048, F32) for _ in range(3)])
        wob = Ring("wob", [A.take(2048, BF16) for _ in range(3)])
        A.persist()
        base_bg = A.base
        bg1 = BG(P, wst, wob, [(w_in[:, 0:2048], wb_in[:, 0:2048], D, 2048, gc_in),
                               (w_in[:, 2048:4096], wb_in[:, 2048:4096], D, 2048, gc_in),
                               (w_in[:, 4096:INC], wb_in[:, 4096:INC], D, INC - 4096, gc_in)])
        bg2 = BG(P, wst, wob, [(w_uq, wb_uq, 512, 1536, gc_q), (w_ukv, wb_ukv, 512, 2048, gc_kv),
                               (w_out, wb_out, D, D, None), (w_up, wb_up, D, DFF, gc_up), (w_down, wb_down, DFF, D, None)])

        cosA = A.take(TL, F32)
        sinA = A.take(TL, F32)
        cosB = A.take(TL, F32)
        sinB = A.take(TL, F32)
        A.persist()
        posi = A.take(TL, I32)
        posf = A.take(TL, F32)
        ang = A.take(TL, F32)
        ni_t = posi
        nf_t = A.take(TL, F32)
        mm_t = A.take(TL, F32)
        tb = Buf("ropet")
        P.load(posi, posb, tb)
        P.op("dve", K("tensor_copy", out=posf, in_=posi), reads=[tb], writes=[tb])
        bg1.step(8)
        rope_table(P, posf, invf[:, 0:1], ang, ni_t, nf_t, mm_t, sinA, cosA, tb)
        bg1.step(8)
        rope_table(P, posf, invf[:, 1:2], ang, ni_t, nf_t, mm_t, sinB, cosB, tb)
        P.barrier()

        def norm_transpose(xt_ap, xb, dst_tile, dst_buf, sub, scr_bf, scr_buf, small, small_buf):
            ss = small[:, 0:1]
            rs = small[:, 1:2]
            P.op("act", K("activation", out=scr_bf, in_=xt_ap, func=AF.Square, accum_out=ss),
                 reads=[xb], writes=[scr_buf, small_buf])
            P.op("act", K("activation", out=rs, in_=ss, func=AF.Sqrt, scale=1.0 / D, bias=EPS),
                 reads=[small_buf], writes=[small_buf])
            P.op("dve", K("reciprocal", out=rs, in_=rs), reads=[small_buf], writes=[small_buf])
            P.op("dve", K("tensor_scalar", out=scr_bf, in0=xt_ap, scalar1=rs, scalar2=None, op0=ALU.mult),
                 reads=[xb, small_buf], writes=[scr_buf])
            for half in range(2):
                pt, pb_ = psring.next()
                ptb = pt[:, :].bitcast(BF16)
                for j in range(8):
                    kc = half * 8 + j
                    P.op("pe", K("transpose", out=ptb[:, j * 128:(j + 1) * 128], in_=scr_bf[:, kc * 128:(kc + 1) * 128],
                                 identity=ident), reads=[scr_buf, cst], writes=[pb_], acc=True)
                eng = "act" if half == 0 else "dve"
                dst = dst_tile[:, half * 8:(half + 1) * 8, sub * 128:(sub + 1) * 128]
                src = ptb.rearrange("p (k n) -> p k n", k=8)
                if eng == "act":
                    P.op("act", K("activation", out=dst, in_=src, func=AF.Copy), reads=[pb_], writes=[dst_buf])
                else:
                    P.op("dve", K("tensor_copy", out=dst, in_=src), reads=[pb_], writes=[dst_buf])

        A.reset()
        Wg = kcv(A.take(16 * 1024, BF16), 16)
        Wgb = Buf("Wg")
        hr = Ring("h2", [kcv(A.take(16 * 512, BF16), 16) for _ in range(2)])
        qs_r = Ring("qs", [A.take(512, BF16) for _ in range(6)])
        t1_r = Ring("t1", [A.take(512, F32) for _ in range(3)])
        t2_r = Ring("t2", [A.take(512, F32) for _ in range(3)])
        mark2 = A.off
        xr = Ring("x", [A.take(D, F32) for _ in range(2)])
        scr_r = Ring("xn", [A.take(D, BF16) for _ in range(1)])
        sm_r = Ring("sm", [A.take(2, F32) for _ in range(2)])
        wb_in_v = wb_in.rearrange("(k p) c -> p k c", p=128)
        hT_v = hT.rearrange("k p t -> p k t")

        defq = []

        def run_deferred(keep=0):
            while len(defq) > keep:
                defq.pop(0)()

        def rope_epilogue(pt, pb_, nrow, Rm, cosT, sinT, tok0, dst_dram):
            qs, qb = qs_r.next()
            npart = pt.shape[0]
            P.op("act", K("activation", out=qs[0:npart, :], in_=pt, func=AF.Copy), reads=[pb_], writes=[qb])

            def part_b():
                p2, p2b = psring.next()
                P.op("pe", K("matmul", p2[0:nrow, :], lhsT=Rm[0:npart, 0:nrow], rhs=qs[0:npart, :], start=True, stop=True),
                     reads=[qb, cst], writes=[p2b])
                t1, t1b = t1_r.next()
                t2, t2b = t2_r.next()
                P.op("dve", K("tensor_tensor", out=t1[0:nrow, :], in0=p2[0:nrow, :], in1=sinT[0:nrow, tok0:tok0 + 512],
                              op=ALU.mult), reads=[p2b], writes=[t1b])
                P.op("pool", K("tensor_tensor", out=t2[0:nrow, :], in0=qs[0:nrow, :], in1=cosT[0:nrow, tok0:tok0 + 512],
                               op=ALU.mult), reads=[qb], writes=[t2b])
                P.op("dve", K("tensor_tensor", out=qs[0:nrow, :], in0=t1[0:nrow, :], in1=t2[0:nrow, :], op=ALU.add),
                     reads=[t1b, t2b], writes=[qb])
                P.store(dst_dram, qs[0:npart, :], qb)
            defq.append(part_b)
            run_deferred(keep=1)

        def latent_epilogue(cps, tok0, dst3):
            sqs = []
            for (pt, pb_) in cps:
                sq, sqb = sq_r.next()
                P.op("act", K("activation", out=sq, in_=pt, func=AF.Square), reads=[pb_], writes=[sqb])
                sqs.append((sq, sqb))
            p5, p5b = psring.next()
            for i, (sq, sqb) in enumerate(sqs):
                P.op("pe", K("matmul", p5, lhsT=ones_f, rhs=sq, start=(i == 0), stop=(i == 3)),
                     reads=[sqb, cst], writes=[p5b], acc=(i > 0))
            rr, rrb = rr_r.next()
            P.op("act", K("activation", out=rr, in_=p5, func=AF.Sqrt, scale=1.0 / 512, bias=EPS), reads=[p5b], writes=[rrb])
            P.op("dve", K("reciprocal", out=rr, in_=rr), reads=[rrb], writes=[rrb])
            cn, cnb = cn_r.next()
            for i, (pt, pb_) in enumerate(cps):
                P.op("dve", K("tensor_tensor", out=cn[:, i, :], in0=pt, in1=rr, op=ALU.mult), reads=[pb_, rrb], writes=[cnb])
            P.store(dst3, cn, cnb)

        P.load(Wg[:, :, 0:1024], wb_in_v[:, :, 1024:2048], Wgb)

        def ak_tile(htile, hb, tt):
            tok0 = tt * 512
            for ct in range(8):
                pt, pb_ = psring.next()
                for kc in range(16):
                    P.op("pe", K("matmul", pt, lhsT=Wg[:, kc, ct * 128:(ct + 1) * 128], rhs=htile[:, kc, :],
                                 start=(kc == 0), stop=(kc == 15)), reads=[Wgb, hb], writes=[pb_], acc=(kc > 0))
                rope_epilogue(pt, pb_, 32, RA_b, cosA, sinA, tok0, kA[ct, :, tok0:tok0 + 512])

        prev_tile = None
        for tt in range(8):
            htile, hb = hr.next()
            for sub in range(4):
                xt, xb = xr.next()
                r0 = (tt % 4) * 512 + sub * 128
                srcx = xp if tt < 4 else xo
                P.load(xt, srcx[r0:r0 + 128, :], xb)
                sc, scb = scr_r.next()
                sm, smb = sm_r.next()
                norm_transpose(xt, xb, htile, hb, sub, sc, scb, sm, smb)
                bg1.step(1)
            P.store(hT_v[:, :, tt * 512:(tt + 1) * 512], htile, hb)
            if prev_tile is not None:
                ak_tile(*prev_tile)
            prev_tile = (htile, hb, tt)
        ak_tile(*prev_tile)
        run_deferred()
        bg1.flush()
        P.barrier()
        if stop_after == 1:
            return finish(nc, P, st, out)
        A.off = mark2
        sq_r = Ring("sq", [A.take(512, F32) for _ in range(4)])
        rr_r = Ring("rr", [A.take(512, F32) for _ in range(2)])
        cn_r = Ring("cn", [kcv(A.take(4 * 512, BF16), 4) for _ in range(2)])

        groups = [
            ("aq", 0, 1024, "own"), ("av", 2048, 1024, "all"),
            ("cq", 3072, 512, "own"), ("ckv", 3584, 576, "all"),
        ]
        for (gname, c0, ncol, which) in groups:
            P.load(Wg[:, :, 0:ncol], wb_in_v[:, :, c0:c0 + ncol], Wgb)
            tts = range(4, 8) if which == "own" else range(8)
            for tt in tts:
                htile, hb = hr.next()
                P.load(htile, hT_v[:, :, tt * 512:(tt + 1) * 512], hb)
                bg2.step(2)
                tok0 = tt * 512
                otok0 = tok0 - TO
                if gname in ("aq", "ak"):
                    for ct in range(8):
                        pt, pb_ = psring.next()
                        for kc in range(16):
                            P.op("pe", K("matmul", pt, lhsT=Wg[:, kc, ct * 128:(ct + 1) * 128], rhs=htile[:, kc, :],
                                         start=(kc == 0), stop=(kc == 15)), reads=[Wgb, hb], writes=[pb_], acc=(kc > 0))
                        dst = qA[ct, :, otok0:otok0 + 512] if gname == "aq" else kA[ct, :, tok0:tok0 + 512]
                        rope_epilogue(pt, pb_, 32, RA_b, cosA, sinA, tok0, dst)
                elif gname == "av":
                    for sub in range(4):
                        for hf in range(2):
                            pt, pb_ = psring.next()
                            for kc in range(16):
                                P.op("pe", K("matmul", pt, lhsT=htile[:, kc, sub * 128:(sub + 1) * 128],
                                             rhs=Wg[:, kc, hf * 512:(hf + 1) * 512], start=(kc == 0), stop=(kc == 15)),
                                     reads=[Wgb, hb], writes=[pb_], acc=(kc > 0))
                            qs, qb = qs_r.next()
                            eng = "act" if hf == 0 else "dve"
                            if eng == "act":
                                P.op("act", K("activation", out=qs, in_=pt, func=AF.Copy), reads=[pb_], writes=[qb])
                            else:
                                P.op("dve", K("tensor_copy", out=qs, in_=pt), reads=[pb_], writes=[qb])
                            P.store(vA[tok0 + sub * 128:tok0 + (sub + 1) * 128, hf * 512:(hf + 1) * 512], qs, qb)
                else:
                    cps = []
                    for ct in range(4):
                        pt, pb_ = psring.next()
                        for kc in range(16):
                            P.op("pe", K("matmul", pt, lhsT=Wg[:, kc, ct * 128:(ct + 1) * 128], rhs=htile[:, kc, :],
                                         start=(kc == 0), stop=(kc == 15)), reads=[Wgb, hb], writes=[pb_], acc=(kc > 0))
                        cps.append((pt, pb_))
                    if gname == "cq":
                        latent_epilogue(cps, tok0, cq.rearrange("k p t -> p k t")[:, :, otok0:otok0 + 512])
                    else:
                        latent_epilogue(cps, tok0, ckv.rearrange("k p t -> p k t")[:, :, tok0:tok0 + 512])
                        pt, pb_ = psring.next()
                        for kc in range(16):
                            P.op("pe", K("matmul", pt[0:64, :], lhsT=Wg[:, kc, 512:576], rhs=htile[:, kc, :],
                                         start=(kc == 0), stop=(kc == 15)), reads=[Wgb, hb], writes=[pb_], acc=(kc > 0))
                        rope_epilogue(pt[0:64, :], pb_, 64, RB_b, cosB, sinB, tok0, kr[:, tok0:tok0 + 512])
            run_deferred()
        run_deferred()
        P.barrier()
        if stop_after == 2:
            return finish(nc, P, st, out)

        A.reset()
        Wq = kcv(A.take(4 * 1536, BF16), 4)
        Wqb = Buf("Wq")
        Wkv = kcv(A.take(4 * 2048, BF16), 4)
        Wkvb = Buf("Wkv")
        cr = Ring("c3", [kcv(A.take(4 * 512, BF16), 4) for _ in range(2)])
        qs_r = Ring("qs3", [A.take(512, BF16) for _ in range(8)])
        t1_r = Ring("t13", [A.take(512, F32) for _ in range(3)])
        t2_r = Ring("t23", [A.take(512, F32) for _ in range(3)])
        P.load(Wq, wb_uq.rearrange("(k p) c -> p k c", p=128), Wqb)
        P.load(Wkv, wb_ukv.rearrange("(k p) c -> p k c", p=128), Wkvb)
        cq_v = cq.rearrange("k p t -> p k t")
        ckv_v = ckv.rearrange("k p t -> p k t")
        cpy = [0]

        def copy_out(pt, pb_, dst_dram, npart=128):
            qs, qb = qs_r.next()
            cpy[0] += 1
            if cpy[0] % 2:
                P.op("act", K("activation", out=qs[0:npart, :], in_=pt, func=AF.Copy), reads=[pb_], writes=[qb])
            else:
                P.op("dve", K("tensor_copy", out=qs[0:npart, :], in_=pt), reads=[pb_], writes=[qb])
            P.store(dst_dram, qs[0:npart, :], qb)

        for tt in range(4):
            ctile, cb = cr.next()
            P.load(ctile, cq_v[:, :, tt * 512:(tt + 1) * 512], cb)
            bg2.step(2)
            tok0 = TO + tt * 512
            for h in range(NH):
                pt, pb_ = psring.next()
                for kc in range(4):
                    P.op("pe", K("matmul", pt, lhsT=Wq[:, kc, h * 192:h * 192 + 128], rhs=ctile[:, kc, :],
                                 start=(kc == 0), stop=(kc == 3)), reads=[Wqb, cb], writes=[pb_], acc=(kc > 0))
                copy_out(pt, pb_, qn[h, :, tt * 512:(tt + 1) * 512])
                pt, pb_ = psring.next()
                for kc in range(4):
                    P.op("pe", K("matmul", pt[0:64, :], lhsT=Wq[:, kc, h * 192 + 128:h * 192 + 192], rhs=ctile[:, kc, :],
                                 start=(kc == 0), stop=(kc == 3)), reads=[Wqb, cb], writes=[pb_], acc=(kc > 0))
                rope_epilogue(pt[0:64, :], pb_, 64, RB_b, cosB, sinB, tok0, qr[h, :, tt * 512:(tt + 1) * 512])
        run_deferred()
        for tt in range(8):
            ctile, cb = cr.next()
            P.load(ctile, ckv_v[:, :, tt * 512:(tt + 1) * 512], cb)
            bg2.step(2)
            for h in range(NH):
                pt, pb_ = psring.next()
                for kc in range(4):
                    P.op("pe", K("matmul", pt, lhsT=Wkv[:, kc, h * 256:h * 256 + 128], rhs=ctile[:, kc, :],
                                 start=(kc == 0), stop=(kc == 3)), reads=[Wkvb, cb], writes=[pb_], acc=(kc > 0))
                copy_out(pt, pb_, kn[h, :, tt * 512:(tt + 1) * 512])
            for sub in range(4):
                for hf in range(2):
                    pt, pb_ = psring.next()
                    for kc in range(4):
                        rhs = Wkv[:, kc, hf * 1024:(hf + 1) * 1024].rearrange("p (h c) -> p h c", c=256)[:, :, 128:256]
                        P.op("pe", K("matmul", pt.rearrange("p (h c) -> p h c", c=128), lhsT=ctile[:, kc, sub * 128:(sub + 1) * 128],
                                     rhs=rhs, start=(kc == 0), stop=(kc == 3)), reads=[Wkvb, cb], writes=[pb_], acc=(kc > 0))
                    r0 = tt * 512 + sub * 128
                    copy_out(pt, pb_, vB[r0:r0 + 128, hf * 512:(hf + 1) * 512])
        run_deferred()
        P.barrier()
        if stop_after == 3:
            return finish(nc, P, st, out)

        A.base = base_bg
        A.reset()
        Qr = Ring("Qa", [A.take(TO, BF16) for _ in range(2)])
        Kr = Ring("Ka", [A.take(TL, BF16) for _ in range(2)])
        Vr = Ring("Va", [A.take(32 * 128, BF16).rearrange("p (b c) -> p b c", c=128) for _ in range(2)])
        AZr = Ring("AZ", [A.take(2 * TO, F32).rearrange("p (a t) -> p a t", a=2) for _ in range(2)])
        Ptr = Ring("Pt", [A.take(256, BF16) for _ in range(4)])
        Mxr = Ring("Mxa", [A.take(TO, BF16) for _ in range(2)])
        zr_t = A.take(TO, F32)
        zrb = Buf("zr")
        sc_a = float(HD ** -0.5)
        Sring = Ring("S4", [None] * 4)
        Sring.items = [psum[0], psum[1], psum[2], psum[3]]
        Oring = Ring("O4", [None] * 4)
        Oring.items = [psum[4], psum[5], psum[6], psum[7]]
        CFG = (1, 4, 16)
        LOOK = 2

        def load_head4(h):
            Q, Qb = Qr.next()
            Kt, Kb = Kr.next()
            P.load(Q, qA[h], Qb)
            P.load(Kt, kA[h], Kb)
            return (Q, Qb, Kt, Kb)

        def load_v4(h, d):
            Vt, Vb = Vr.next()
            nblk_n = TL // (128 * d)
            srcv = vA[:, h * 128:(h + 1) * 128].rearrange("(n i r) c -> i n r c", i=128, r=d)
            dstv = Vt.rearrange("p (n r) c -> p n r c", r=d)
            if d == 1:
                P.load(Vt, vA[:, h * 128:(h + 1) * 128].rearrange("(n i) c -> i n c", i=128), Vb)
            else:
                for n in range(nblk_n):
                    P.load(dstv[:, n], srcv[:, n], Vb)
            return (Vt, Vb)

        def s_stage4(ctx, d, kb):
            Q, Qb, Kt, Kb = ctx
            has_cur = kb >= 16
            has_next = kb + d <= 31
            n, r = kb // d, kb % d
            ks0 = n * 128 * d + r
            Kblk = Kt[:, ks0:ks0 + 127 * d + 1:d]
            if has_cur and has_next:
                N = 256
                qs0 = ks0 - TO
                msk = mask_b[:, 128:384]
            elif has_cur:
                N = 128
                qs0 = ks0 - TO
                msk = mask_cur
            else:
                N = 128
                qs0 = ks0 + 128 * d - TO
                msk = mask_prev
            Qsl = Q[:, qs0:qs0 + (N - 1) * d + 1:d]
            S, Sb = Sring.next()
            P.op("pe", K("matmul", S[:, 0:N], lhsT=ident, rhs=msk, start=True, stop=False), reads=[cst], writes=[Sb])
            P.op("pe", K("matmul", S[:, 0:N], lhsT=Kblk, rhs=Qsl, start=False, stop=True),
                 reads=[Kb, Qb], writes=[Sb], acc=True)
            Pt, Ptb = Ptr.next()
            P.op("act", K("activation", out=Pt[:, 0:N], in_=S[:, 0:N], func=AF.Exp, scale=sc_a), reads=[Sb], writes=[Ptb])
            return (Pt, Ptb, N, qs0)

        def o_stage4(vt, AZc, d, kb, st_):
            Vt, Vb = vt
            AZ, AZb = AZc
            Pt, Ptb, N, qs0 = st_
            O, Ob = Oring.next()
            P.op("pe", K("matmul", O[:, 0:N], lhsT=Vt[:, kb, :], rhs=Pt[:, 0:N], start=True, stop=True),
                 reads=[Vb, Ptb], writes=[Ob])
            P.op("pe", K("matmul", O[:, 256:256 + N], lhsT=(pv_b if kb < 16 else ones_b), rhs=Pt[:, 0:N],
                         start=True, stop=True, skip_group_check=True), reads=[cst, Ptb], writes=[Ob], acc=True)
            azv = AZ[:, :, qs0:qs0 + (N - 1) * d + 1:d]
            osrc = O[:, 0:512].rearrange("p (a t) -> p a t", a=2)[:, :, 0:N]
            P.op("dve", K("tensor_tensor", out=azv, in0=azv, in1=osrc, op=ALU.add), reads=[Ob, AZb], writes=[AZb])

        fin_q = []

        def fin_head4(h, AZc):
            AZ, AZb = AZc
            Mx, Mxb = Mxr.next()
            NCH = 8
            cw_ = TO // NCH

            def piece(i):
                def f():
                    sl = slice(i * cw_, (i + 1) * cw_)
                    P.op("dve", K("reciprocal", out=zr_t[:, sl], in_=AZ[:, 1, sl]), reads=[AZb], writes=[zrb])
                    P.op("pool", K("tensor_tensor", out=Mx[:, sl], in0=AZ[:, 0, sl], in1=zr_t[:, sl], op=ALU.mult),
                         reads=[AZb, zrb], writes=[Mxb])
                    if i == NCH - 1:
                        P.store(mixT[h], Mx, Mxb)
                return f
            for i in range(NCH):
                fin_q.append(piece(i))

        pend = []
        nxt_ctx = load_head4(0)
        nxt_v = load_v4(0, CFG[0])
        for h in range(NH):
            ctx = nxt_ctx
            AZc = AZr.next()
            P.op("pool", K("memset", AZc[0], 0.0), writes=[AZc[1]])
            for ci_, d in enumerate(CFG):
                vt = nxt_v
                bg2.step(2)
                kbs = list(range(16 - d, 32))
                for i_, kb in enumerate(kbs):
                    if i_ == LOOK + 1:
                        if ci_ < 2:
                            nxt_v = load_v4(h, CFG[ci_ + 1])
                        elif h + 1 < NH:
                            nxt_ctx = load_head4(h + 1)
                            nxt_v = load_v4(h + 1, CFG[0])
                    st_ = s_stage4(ctx, d, kb)
                    pend.append((vt, AZc, d, kb, st_, (h if (ci_ == 2 and kb == 31) else None)))
                    if len(pend) > LOOK:
                        it = pend.pop(0)
                        o_stage4(*it[:5])
                        if it[5] is not None:
                            fin_head4(it[5], it[1])
                        elif fin_q and (i_ % 2 == 0):
                            fin_q.pop(0)()
        while pend:
            it = pend.pop(0)
            o_stage4(*it[:5])
            if it[5] is not None:
                fin_head4(it[5], it[1])
        while fin_q:
            fin_q.pop(0)()
        P.barrier()
        if stop_after == 4:
            return finish(nc, P, st, out)

        A.reset()
        Qnr = Ring("Qn", [A.take(TO, BF16) for _ in range(2)])
        Qrr = Ring("Qr", [A.take(TO, BF16) for _ in range(2)])
        Knr = Ring("Kn", [A.take(TL, BF16) for _ in range(2)])
        KR = A.take(TL, BF16)
        KRb = Buf("KR")
        Vbr = Ring("Vb", [A.take(32 * 132, BF16).rearrange("p (b c) -> p b c", c=132) for _ in range(2)])
        Ptr = Ring("Pt5", [A.take(512, BF16) for _ in range(4)])
        Onr = Ring("On", [A.take(128, BF16) for _ in range(2)])
        rzr = Ring("rz", [A.take(1, F32) for _ in range(2)])
        Mxr = Ring("Mx5", [A.take(512, BF16) for _ in range(2)])
        sc_b = float(192 ** -0.5)
        Opairs = [(psum[0], psum[1]), (psum[2], psum[3])]
        Sring = Ring("S5", [None] * 3)
        Sring.items = [psum[4], psum[5], psum[6]]
        Tring = Ring("T5", [None])
        Tring.items = [psum[7]]
        P.op("pool", K("memset", KR, 0.0), writes=[KRb])
        P.load(KR[0:64, :], kr, KRb)
        for (Qr_t, Qrb) in Qrr.items:
            P.op("pool", K("memset", Qr_t, 0.0), writes=[Qrb])
        for (Vb_t, Vbb) in Vbr.items:
            P.op("pool", K("memset", Vb_t[:, 16:32, 128:129], 1.0), writes=[Vbb])
            P.op("pool", K("tensor_copy", out=Vb_t[:, 0:16, 128:129], in_=pv_b[:, 0:16].rearrange("p (b c) -> p b c", c=1)),
                 reads=[cst], writes=[Vbb])

        def load_head5(h):
            Qn_t, Qnb = Qnr.next()
            Qr_t, Qrb = Qrr.next()
            Kn_t, Knb = Knr.next()
            Vb_t, Vbb = Vbr.next()
            P.load(Qn_t, qn[h], Qnb)
            P.load(Qr_t[0:64, :], qr[h], Qrb)
            P.load(Kn_t, kn[h], Knb)
            srcv = vB[:, h * 128:(h + 1) * 128].rearrange("(b i) c -> i b c", i=128)
            for b4 in range(4):
                P.load(Vb_t[:, b4 * 8:(b4 + 1) * 8, 0:128], srcv[:, b4 * 8:(b4 + 1) * 8, :], Vbb)
            return (Qn_t, Qnb, Qr_t, Qrb, Kn_t, Knb, Vb_t, Vbb)

        def s_stage5(ctx, g, kb):
            Qn_t, Qnb, Qr_t, Qrb, Kn_t, Knb, Vb_t, Vbb = ctx
            j0 = max(0, kb - 16 - 4 * g)
            c0 = j0 * 128
            diag = kb >= 16 + 4 * g
            S, Sb = Sring.next()
            q0 = g * 512 + c0
            if diag:
                P.op("pe", K("matmul", S[:, c0:c0 + 128], lhsT=ident, rhs=mask_cur, start=True, stop=False, skip_group_check=True),
                     reads=[cst], writes=[Sb])
            P.op("pe", K("matmul", S[:, c0:512], lhsT=Kn_t[:, kb * 128:(kb + 1) * 128], rhs=Qn_t[:, q0:g * 512 + 512],
                         start=(not diag), stop=False, skip_group_check=True), reads=[Knb, Qnb], writes=[Sb], acc=diag)
            P.op("pe", K("matmul", S[:, c0:512], lhsT=KR[:, kb * 128:(kb + 1) * 128], rhs=Qr_t[:, q0:g * 512 + 512],
                         start=False, stop=True, skip_group_check=True), reads=[KRb, Qrb], writes=[Sb], acc=True)
            Pt, Ptb = Ptr.next()
            P.op("act", K("activation", out=Pt[:, c0:512], in_=S[:, c0:512], func=AF.Exp, scale=sc_b),
                 reads=[Sb], writes=[Ptb])
            return (Pt, Ptb, j0)

        def pv_stage5(ctx, h, g, kb, opair, Mxc, st_):
            Vb_t, Vbb = ctx[6], ctx[7]
            Pt, Ptb, j0 = st_
            Mx, Mxb = Mxc
            for j in range(j0, 4):
                Ot, Otb = opair[0] if j < 2 else opair[1]
                oc = (j % 2) * 256
                last = (kb == 16 + 4 * g + j)
                P.op("pe", K("matmul", Ot[:, oc:oc + 129], lhsT=Pt[:, j * 128:(j + 1) * 128], rhs=Vb_t[:, kb, 0:129],
                             start=False, stop=last, skip_group_check=True), reads=[Ptb, Vbb], writes=[Otb], acc=True)
                if last:
                    rz, rzb = rzr.next()
                    On, Onb = Onr.next()
                    P.op("dve", K("reciprocal", out=rz, in_=Ot[:, oc + 128:oc + 129]), reads=[Otb], writes=[rzb])
                    P.op("dve", K("tensor_scalar", out=On, in0=Ot[:, oc:oc + 128], scalar1=rz, scalar2=None, op0=ALU.mult),
                         reads=[Otb, rzb], writes=[Onb])
                    T, Tb = Tring.next()
                    Tb16 = T[:, :].bitcast(BF16)
                    P.op("pe", K("transpose", out=Tb16[:, 0:128], in_=On, identity=ident), reads=[Onb, cst], writes=[Tb])
                    P.op("act", K("activation", out=Mx[:, j * 128:(j + 1) * 128], in_=Tb16[:, 0:128], func=AF.Copy),
                         reads=[Tb], writes=[Mxb])
                    if j == 3:
                        P.store(mixT[8 + h, :, g * 512:(g + 1) * 512], Mx, Mxb)

        pend = []
        gi = 0
        nxt_ctx = load_head5(0)
        for h in range(NH):
            ctx = nxt_ctx
            for g in range(4):
                if g == 1 and h + 1 < NH:
                    nxt_ctx = load_head5(h + 1)
                bg2.step(1)
                opair = Opairs[gi % 2]
                gi += 1
                Mxc = Mxr.next()
                for (Ot, Otb) in opair:
                    P.op("pe", K("matmul", Ot, lhsT=zeros_b[:, 0:128], rhs=zeros_b, start=True, stop=False, skip_group_check=True),
                         reads=[cst], writes=[Otb])
                for kb in range(16 + 4 * g + 4):
                    st_ = s_stage5(ctx, g, kb)
                    pend.append((ctx, h, g, kb, opair, Mxc, st_))
                    if len(pend) > LOOK:
                        pv_stage5(*pend.pop(0))
        while pend:
            pv_stage5(*pend.pop(0))
        bg2.flush()
        P.barrier()
        if stop_after == 5:
            return finish(nc, P, st, out)

        A.base = base_const
        A.reset()
        Wo = kcv(A.take(16 * D, BF16), 16)
        Wob = Buf("Wo")
        gp = A.take(D, F32)
        gpb = Buf("gp")
        mr = Ring("m6", [kcv(A.take(16 * 512, BF16), 16) for _ in range(2)])
        xr = Ring("x6", [A.take(D, F32) for _ in range(2)])
        yr = Ring("y6", [A.take(D, F32) for _ in range(2)])
        x1r = Ring("x16", [A.take(D, F32) for _ in range(2)])
        scr_r = Ring("xn6", [A.take(D, BF16) for _ in range(4)])
        sm_r = Ring("sm6", [A.take(8, F32) for _ in range(4)])
        hr = Ring("ht6", [kcv(A.take(16 * 512, BF16), 16) for _ in range(2)])
        P.load(Wo, wb_out.rearrange("(k p) c -> p k c", p=128), Wob)
        P.load(gp, g_post, gpb)
        mixT_v = mixT.rearrange("k p t -> p k t")
        h2T_v = h2T.rearrange("k p t -> p k t")
        defq = []

        def transposes6(sc, scb, htile, hb, sub, tt, last):
            def f():
                for half in range(2):
                    pt, pb_ = psring.next()
                    ptb = pt[:, :].bitcast(BF16)
                    for j in range(8):
                        kc = half * 8 + j
                        P.op("pe", K("transpose", out=ptb[:, j * 128:(j + 1) * 128], in_=sc[:, kc * 128:(kc + 1) * 128],
                                     identity=ident), reads=[scb, cst], writes=[pb_], acc=True)
                    dst = htile[:, half * 8:(half + 1) * 8, sub * 128:(sub + 1) * 128]
                    src = ptb.rearrange("p (k n) -> p k n", k=8)
                    if half == 0:
                        P.op("act", K("activation", out=dst, in_=src, func=AF.Copy), reads=[pb_], writes=[hb])
                    else:
                        P.op("dve", K("tensor_copy", out=dst, in_=src), reads=[pb_], writes=[hb])
                if last:
                    P.store(h2T_v[:, :, tt * 512:(tt + 1) * 512], htile, hb)
            return f

        for tt in range(4):
            mt, mb = mr.next()
            P.load(mt, mixT_v[:, :, tt * 512:(tt + 1) * 512], mb)
            htile, hb = hr.next()
            for sub in range(4):
                r0 = tt * 512 + sub * 128
                xt, xb = xr.next()
                P.load(xt, xo[r0:r0 + 128, :], xb)
                yt, yb = yr.next()
                for dt_ in range(4):
                    pt, pb_ = psring.next()
                    for kc in range(16):
                        P.op("pe", K("matmul", pt, lhsT=mt[:, kc, sub * 128:(sub + 1) * 128], rhs=Wo[:, kc, dt_ * 512:(dt_ + 1) * 512],
                                     start=(kc == 0), stop=(kc == 15)), reads=[mb, Wob], writes=[pb_], acc=(kc > 0))
                    P.op("dve", K("tensor_copy", out=yt[:, dt_ * 512:(dt_ + 1) * 512], in_=pt), reads=[pb_], writes=[yb])
                while len(defq) > 1:
                    defq.pop(0)()
                sm, smb = sm_r.next()
                sc, scb = scr_r.next()
                P.op("act", K("activation", out=sc, in_=yt, func=AF.Square, accum_out=sm[:, 0:1]), reads=[yb], writes=[scb, smb])
                P.op("act", K("activation", out=sm[:, 1:2], in_=sm[:, 0:1], func=AF.Sqrt, scale=1.0 / D, bias=EPS),
                     reads=[smb], writes=[smb])
                P.op("dve", K("reciprocal", out=sm[:, 1:2], in_=sm[:, 1:2]), reads=[smb], writes=[smb])
                P.op("act", K("activation", out=yt, in_=yt, func=AF.Copy, scale=sm[:, 1:2]), reads=[yb, smb], writes=[yb])
                P.op("dve", K("tensor_tensor", out=yt, in0=yt, in1=gp, op=ALU.mult), reads=[yb, gpb], writes=[yb])
                x1t, x1b = x1r.next()
                P.op("pool", K("tensor_tensor", out=x1t, in0=yt, in1=xt, op=ALU.add), reads=[yb, xb], writes=[x1b])
                P.store(x1[r0:r0 + 128, :], x1t, x1b)
                P.op("act", K("activation", out=sc, in_=x1t, func=AF.Square, accum_out=sm[:, 2:3]), reads=[x1b], writes=[scb, smb])
                P.op("act", K("activation", out=sm[:, 3:4], in_=sm[:, 2:3], func=AF.Sqrt, scale=1.0 / D, bias=EPS),
                     reads=[smb], writes=[smb])
                P.op("dve", K("reciprocal", out=sm[:, 3:4], in_=sm[:, 3:4]), reads=[smb], writes=[smb])
                P.op("dve", K("tensor_scalar", out=sc, in0=x1t, scalar1=sm[:, 3:4], scalar2=None, op0=ALU.mult),
                     reads=[x1b, smb], writes=[scb])
                defq.append(transposes6(sc, scb, htile, hb, sub, tt, sub == 3))
        while defq:
            defq.pop(0)()
        P.barrier()
        if stop_after == 6:
            return finish(nc, P, st, out)

        A.reset()
        gp2 = A.take(D, F32)
        gp2b = Buf("gp2")
        P.load(gp2, g_post2, gp2b)
        h2r = Ring("h2t", [kcv(A.take(16 * 512, BF16), 16) for _ in range(2)])
        U = kcv(A.take(64 * 512, BF16), 64)
        Ub = Buf("U")
        Wur = Ring("Wu", [kcv(A.take(16 * 256, BF16), 16) for _ in range(3)])
        Wdr = Ring("Wd", [kcv(A.take(4 * 1024, BF16), 4) for _ in range(3)])
        rl_r = Ring("rl", [A.take(512, F32) for _ in range(3)])
        Y = A.take(4 * D, F32).rearrange("p (s c) -> p s c", s=4)
        Yb = Buf("Y")
        sm7 = A.take(32, F32)
        sm7b = Buf("sm7")
        junk = A.take(1024, BF16)
        junkb = Buf("junk")
        x1r = Ring("x17", [A.take(D, F32) for _ in range(1)])
        wb_up_v = wb_up.rearrange("(k p) c -> p k c", p=128)
        wb_down_v = wb_down.rearrange("(k p) c -> p k c", p=128)
        out_ops = []
        import os as _os
        _ntt = int(_os.environ.get("F7_NTT", "4"))
        _mode = _os.environ.get("F7_MODE", "full")
        nxt_h2 = h2r.next()
        P.load(nxt_h2[0], h2T_v[:, :, 0:512], nxt_h2[1])
        for tt in range(_ntt):
            h2, h2b = nxt_h2
            for fg in range(32):
                Wu, Wub = Wur.next()
                P.load(Wu, wb_up_v[:, :, fg * 256:(fg + 1) * 256], Wub)
                for f in range(2):
                    ft = fg * 2 + f
                    pt, pb_ = psring.next()
                    for kc in range(16):
                        P.op("pe", K("matmul", pt, lhsT=Wu[:, kc, f * 128:(f + 1) * 128], rhs=h2[:, kc, :],
                                     start=(kc == 0), stop=(kc == 15)), reads=[Wub, h2b], writes=[pb_], acc=(kc > 0))
                    rl, rlb = rl_r.next()
                    P.op("act", K("activation", out=rl, in_=pt, func=AF.Relu), reads=[pb_], writes=[rlb])
                    P.op("dve", K("tensor_tensor", out=U[:, ft, :], in0=rl, in1=rl, op=ALU.mult), reads=[rlb], writes=[Ub])
            if tt + 1 < _ntt:
                nxt_h2 = h2r.next()
                P.load(nxt_h2[0], h2T_v[:, :, (tt + 1) * 512:(tt + 2) * 512], nxt_h2[1])
            for dh in range(2 if _mode != "up" else 0):
                accs = [psring.next() for _ in range(8)]
                for fg in range(16):
                    Wd, Wdb = Wdr.next()
                    P.load(Wd, wb_down_v[:, fg * 4:(fg + 1) * 4, dh * 1024:(dh + 1) * 1024], Wdb)
                    for f in range(4):
                        ft = fg * 4 + f
                        for sub in range(4):
                            for dt_ in range(2):
                                pt, pb_ = accs[sub * 2 + dt_]
                                P.op("pe", K("matmul", pt, lhsT=U[:, ft, sub * 128:(sub + 1) * 128],
                                             rhs=Wd[:, f, dt_ * 512:(dt_ + 1) * 512], start=(ft == 0), stop=(ft == 63)),
                                     reads=[Ub, Wdb], writes=[pb_], acc=(ft > 0))
                for sub in range(4):
                    for dt_ in range(2):
                        pt, pb_ = accs[sub * 2 + dt_]
                        col = dh * 2 + dt_
                        _ev = _os.environ.get("F7_EV", "both")
                        P.op("dve", K("tensor_copy", out=Y[:, sub, col * 512:(col + 1) * 512], in_=pt), reads=[pb_], writes=[Yb])
                        P.op("act", K("activation", out=junk[:, 0:512], in_=Y[:, sub, col * 512:(col + 1) * 512], func=AF.Square,
                                      accum_out=sm7[:, sub * 4 + col:sub * 4 + col + 1]), reads=[Yb], writes=[junkb, sm7b])
            for sub in range(4 if _mode == "full" else 0):
                r0 = tt * 512 + sub * 128
                x1t, x1b = x1r.next()
                P.load(x1t, x1[r0:r0 + 128, :], x1b, eng="pool")
                s4 = sm7[:, sub * 4:sub * 4 + 4]
                t2 = sm7[:, 16 + sub * 4:16 + sub * 4 + 2]
                ssum = sm7[:, 16 + sub * 4 + 2:16 + sub * 4 + 3]
                rs7 = sm7[:, 16 + sub * 4 + 3:16 + sub * 4 + 4]
                P.op("dve", K("tensor_tensor", out=t2, in0=s4[:, 0:2], in1=s4[:, 2:4], op=ALU.add), reads=[sm7b], writes=[sm7b])
                P.op("dve", K("tensor_tensor", out=ssum, in0=t2[:, 0:1], in1=t2[:, 1:2], op=ALU.add), reads=[sm7b], writes=[sm7b])
                P.op("act", K("activation", out=rs7, in_=ssum, func=AF.Sqrt, scale=1.0 / D, bias=EPS), reads=[sm7b], writes=[sm7b])
                P.op("dve", K("reciprocal", out=rs7, in_=rs7), reads=[sm7b], writes=[sm7b])
                P.op("act", K("activation", out=Y[:, sub, :], in_=Y[:, sub, :], func=AF.Copy, scale=rs7), reads=[Yb, sm7b], writes=[Yb])
                ee = "dve" if tt == _ntt - 1 else "pool"
                P.op(ee, K("tensor_tensor", out=Y[:, sub, :], in0=Y[:, sub, :], in1=gp2, op=ALU.mult), reads=[Yb, gp2b], writes=[Yb])
                P.op(ee, K("tensor_tensor", out=x1t, in0=Y[:, sub, :], in1=x1t, op=ALU.add), reads=[Yb, x1b], writes=[x1b])
                out_ops.append(P.store(out[r0:r0 + 128, :], x1t, x1b))
        P.barrier()
        return finish(nc, P, st, out)


def finish(nc, P, st, out):
    P.barrier()
    with nc.Block() as block:
        P.emit(block)
    return nc


def _consts():
    j = np.arange(128)[:, None]
    i = np.arange(128)[None, :]
    mask_prev = np.where(j >= i, 0.0, NEG).astype(np.float32)
    mask_cur = np.where(j <= i, 0.0, NEG).astype(np.float32)
    cmask = np.concatenate([mask_prev, mask_cur], axis=1)
    RA = np.zeros((128, 128), np.float32)
    for m in range(16):
        RA[m + 16, m] = -1.0
        RA[m, m + 16] = 1.0
    RB = np.zeros((128, 128), np.float32)
    for m in range(32):
        RB[m + 32, m] = -1.0
        RB[m, m + 32] = 1.0
    p = np.arange(128)
    invA = (THETA ** (-(2.0 * (p % 16)) / 32.0)).astype(np.float32)
    invB = (THETA ** (-(2.0 * (p % 32)) / 64.0)).astype(np.float32)
    invf = np.stack([invA, invB], axis=1).astype(np.float32)
    return cmask, RA, RB, invf


def make_in_maps(x, positions, norm_attn_pre, norm_attn_post, w_in, q_latent_norm, kv_latent_norm,
                 w_uq, w_ukv, w_out, norm_mlp_pre, norm_mlp_post, w_up, w_down):
    x = np.asarray(x, np.float32)
    positions = np.asarray(positions, np.int32)
    cmask, RA, RB, invf = _consts()

    def col(g, k):
        return np.ascontiguousarray(np.asarray(g, np.float32).reshape(k, 128).T)

    def bc(g):
        return np.ascontiguousarray(np.broadcast_to(np.asarray(g, np.float32).reshape(1, -1), (128, D)))

    shared = {
        "cmask": cmask, "cRA": RA, "cRB": RB, "cinvf": invf,
        "g_in": col(norm_attn_pre[0], 16), "g_up": col(norm_mlp_pre[0], 16),
        "g_q": col(q_latent_norm[0], 4), "g_kv": col(kv_latent_norm[0], 4),
        "g_post": bc(norm_attn_post[0]), "g_post2": bc(norm_mlp_post[0]),
        "w_in": np.ascontiguousarray(np.asarray(w_in[0], np.float32)),
        "w_uq": np.ascontiguousarray(np.asarray(w_uq[0], np.float32)),
        "w_ukv": np.ascontiguousarray(np.asarray(w_ukv[0], np.float32)),
        "w_out": np.ascontiguousarray(np.asarray(w_out[0], np.float32)),
        "w_up": np.ascontiguousarray(np.asarray(w_up[0], np.float32)),
        "w_down": np.ascontiguousarray(np.asarray(w_down[0], np.float32)),
    }
    maps = []
    for c in range(8):
        b, half = c // 2, c % 2
        m = dict(shared)
        m["xo"] = np.ascontiguousarray(x[b, half * TO:(half + 1) * TO])
        m["xp"] = np.ascontiguousarray(x[b, 0:TO]) if half else np.zeros((TO, D), np.float32)
        pl = np.concatenate([positions[b, 0:TO], positions[b, half * TO:(half + 1) * TO]])
        m["posb"] = np.ascontiguousarray(np.broadcast_to(pl.reshape(1, TL), (128, TL))).astype(np.int32)
        m["pvt"] = np.full((128, 128), float(half), np.float32)
        maps.append(m)
    return maps


_NC_CACHE = {}


def kernel(**inputs):
    maps = make_in_maps(**inputs)
    if "nc" not in _NC_CACHE:
        _NC_CACHE["nc"] = build_program()
    nc = _NC_CACHE["nc"]
    res = run_bass_kernel_spmd(nc, maps, core_ids=list(range(8)))
    outp = np.empty((NB, SEQ, D), np.float32)
    for c in range(8):
        b, half = c // 2, c % 2
        outp[b, half * TO:(half + 1) * TO] = np.asarray(res.results[c]["out"], np.float32)
    return outp
```
